# Optimizing a Trainium2 kernel written in Bass

```python
import jax, jax.numpy as jnp
from jax import lax
import numpy as np

D_MODEL = 1024
BATCH = 32
SEQ = 2048
DEPTH = 4

N_MIXERS = 3
N_A = (DEPTH + 2) // 3
N_B = (DEPTH + 1) // 3
N_C = DEPTH // 3

ATTN_GROUPS = ((128, 1), (512, 4), (2048, 16))
N_GROUPS = len(ATTN_GROUPS)
HEADS_PER_GROUP = 16
HEAD_DIM = 64
ROT_DIM = HEAD_DIM // 4
ROPE_THETA = 500000.0
QKV_WIDTH = 3 * N_GROUPS * HEADS_PER_GROUP * HEAD_DIM
ATTN_OUT_WIDTH = HEADS_PER_GROUP * HEAD_DIM

SC_WIDTH = 3
CONF_CONV_WIDTH = 31
FFN_DIM = 2816
FFN_CONV_WIDTH = 3
EPS_RMS = 1e-6
EPS_LN = 1e-5
NEG_INF = -1e30

kernel_name = "hybrid_dilated_attn_shortconv_conformer_convffn"


def rms_norm(x, g):
    xf = x.astype(jnp.float32)
    y = xf * lax.rsqrt(jnp.mean(xf * xf, axis=-1, keepdims=True) + EPS_RMS)
    return (y * g.astype(jnp.float32)).astype(x.dtype)


def layer_norm(x, g, b):
    xf = x.astype(jnp.float32)
    mu = jnp.mean(xf, axis=-1, keepdims=True)
    var = jnp.mean(jnp.square(xf - mu), axis=-1, keepdims=True)
    y = (xf - mu) * lax.rsqrt(var + EPS_LN)
    return (y * g.astype(jnp.float32) + b.astype(jnp.float32)).astype(x.dtype)


def causal_dwconv(x, w):
    k_width, ch = w.shape
    return lax.conv_general_dilated(
        x, w[:, None, :].astype(x.dtype), window_strides=(1,), padding=[(k_width - 1, 0)],
        dimension_numbers=("NWC", "WIO", "NWC"), feature_group_count=ch)


def rope_partial(t, cos, sin):
    half = ROT_DIM // 2
    tr = t[..., :ROT_DIM].astype(jnp.float32)
    t1, t2 = tr[..., :half], tr[..., half:]
    rot = jnp.concatenate([t1 * cos - t2 * sin, t2 * cos + t1 * sin], axis=-1)
    return jnp.concatenate([rot.astype(t.dtype), t[..., ROT_DIM:]], axis=-1)


def dilated_group_attention(q, k, v, window, dilation):
    B, S, H, Dh = q.shape
    L = S // dilation
    W = window // dilation
    nb = -(-L // W)
    Lp = nb * W

    def to_sub(t):
        t = t.reshape(B, L, dilation, H, Dh).transpose(0, 2, 1, 3, 4).reshape(B * dilation, L, H, Dh)
        t = jnp.pad(t, ((0, 0), (0, Lp - L), (0, 0), (0, 0)))
        return t.reshape(B * dilation, nb, W, H, Dh)

    def with_prev(t):
        prev = jnp.pad(t[:, :-1], ((0, 0), (1, 0), (0, 0), (0, 0), (0, 0)))
        return jnp.concatenate([prev, t], axis=2)

    qb, kb, vb = to_sub(q), to_sub(k), to_sub(v)
    kk, vv = with_prev(kb), with_prev(vb)
    scale = 1.0 / np.sqrt(Dh).astype(np.float32)
    s = jnp.einsum("bnqhd,bnkhd->bnhqk", qb, kk, preferred_element_type=jnp.float32) * scale
    qi = jnp.arange(W)[:, None]
    kj = jnp.arange(2 * W)[None, :]
    dist = qi - kj + W
    blk = jnp.arange(nb)[:, None, None]
    valid = (dist >= 0) & (dist <= W) & ((blk > 0) | (kj >= W))
    s = jnp.where(valid[None, :, None], s, NEG_INF)
    lse = jax.nn.logsumexp(s, axis=-1)
    p = jnp.exp(s - lse[..., None])
    o = jnp.einsum("bnhqk,bnkhd->bnqhd", p.astype(v.dtype), vv)
    o = o.reshape(B * dilation, Lp, H, Dh)[:, :L]
    o = o.reshape(B, dilation, L, H, Dh).transpose(0, 2, 1, 3, 4).reshape(B, S, H, Dh)
    lse = lse.transpose(0, 1, 3, 2).reshape(B * dilation, Lp, H)[:, :L]
    lse = lse.reshape(B, dilation, L, H).transpose(0, 2, 1, 3).reshape(B, S, H)
    return o, lse


def dilated_attention_mixer(x, positions, w_qkv, w_o):
    B, S, _ = x.shape
    GH = N_GROUPS * HEADS_PER_GROUP
    qkv = (x @ w_qkv).reshape(B, S, 3, GH, HEAD_DIM)
    inv_freq = ROPE_THETA ** (-jnp.arange(0, ROT_DIM, 2, dtype=jnp.float32) / ROT_DIM)
    ang = positions.astype(jnp.float32)[..., None] * inv_freq
    cos, sin = jnp.cos(ang)[:, :, None, :], jnp.sin(ang)[:, :, None, :]
    q = rope_partial(qkv[:, :, 0], cos, sin)
    k = rope_partial(qkv[:, :, 1], cos, sin)
    v = qkv[:, :, 2]
    outs, lses = [], []
    for g, (window, dilation) in enumerate(ATTN_GROUPS):
        sl = slice(g * HEADS_PER_GROUP, (g + 1) * HEADS_PER_GROUP)
        o, l = dilated_group_attention(q[:, :, sl], k[:, :, sl], v[:, :, sl], window, dilation)
        outs.append(o)
        lses.append(l)
    alpha = jax.nn.softmax(jnp.stack(lses, axis=0), axis=0)
    o = jnp.sum(alpha[..., None] * jnp.stack(outs, axis=0).astype(jnp.float32), axis=0)
    return o.astype(x.dtype).reshape(B, S, ATTN_OUT_WIDTH) @ w_o


def short_conv_mixer(x, w_in, w_conv, w_out):
    b_gate, c_gate, h = jnp.split(x @ w_in, 3, axis=-1)
    return (b_gate * causal_dwconv(c_gate * h, w_conv)) @ w_out


def conformer_conv_mixer(x, w_pw1, b_pw1, w_dw, b_dw, ln_g, ln_b, w_pw2, b_pw2):
    a, g = jnp.split(x @ w_pw1 + b_pw1, 2, axis=-1)
    h = a * jax.nn.sigmoid(g)
    h = causal_dwconv(h, w_dw) + b_dw
    h = jax.nn.silu(layer_norm(h, ln_g, ln_b))
    return h @ w_pw2 + b_pw2


def conv_ffn(x, w_in, w_conv, w_out):
    u = causal_dwconv(x @ w_in, w_conv)
    g, v = jnp.split(u, 2, axis=-1)
    return (jax.nn.silu(g) * v) @ w_out


def setup_inputs(seed: int = 0) -> dict:
    key = jax.random.key(seed)
    ks = jax.random.split(key, 24)
    f32 = jnp.float32
    D, F = D_MODEL, FFN_DIM

    def nrm(k, shape, scale):
        return jax.random.normal(k, shape, f32) * scale

    def gain(k, shape):
        return 1.0 + 0.1 * jax.random.normal(k, shape, f32)

    return {
        "x": jax.random.normal(ks[0], (BATCH, SEQ, D), f32),
        "positions": jnp.broadcast_to(jnp.arange(SEQ, dtype=jnp.int32)[None, :], (BATCH, SEQ)),
        "mix_norm_pre": gain(ks[1], (DEPTH, D)),
        "mix_norm_post": gain(ks[2], (DEPTH, D)),
        "ffn_norm_pre": gain(ks[3], (DEPTH, D)),
        "ffn_norm_post": gain(ks[4], (DEPTH, D)),
        "attn_w_qkv": nrm(ks[5], (N_A, D, QKV_WIDTH), D ** -0.5),
        "attn_w_o": nrm(ks[6], (N_A, ATTN_OUT_WIDTH, D), ATTN_OUT_WIDTH ** -0.5),
        "sc_w_in": nrm(ks[7], (N_B, D, 3 * D), D ** -0.5),
        "sc_w_conv": nrm(ks[8], (N_B, SC_WIDTH, D), SC_WIDTH ** -0.5),
        "sc_w_out": nrm(ks[9], (N_B, D, D), D ** -0.5),
        "cc_w_pw1": nrm(ks[10], (N_C, D, 2 * D), D ** -0.5),
        "cc_b_pw1": nrm(ks[11], (N_C, 2 * D), 0.01),
        "cc_w_dw": nrm(ks[12], (N_C, CONF_CONV_WIDTH, D), CONF_CONV_WIDTH ** -0.5),
        "cc_b_dw": nrm(ks[13], (N_C, D), 0.01),
        "cc_ln_g": gain(ks[14], (N_C, D)),
        "cc_ln_b": nrm(ks[15], (N_C, D), 0.01),
        "cc_w_pw2": nrm(ks[16], (N_C, D, D), D ** -0.5),
        "cc_b_pw2": nrm(ks[17], (N_C, D), 0.01),
        "ffn_w_in": nrm(ks[18], (DEPTH, D, 2 * F), D ** -0.5),
        "ffn_w_conv": nrm(ks[19], (DEPTH, FFN_CONV_WIDTH, 2 * F), FFN_CONV_WIDTH ** -0.5),
        "ffn_w_out": nrm(ks[20], (DEPTH, F, D), F ** -0.5),
    }


def reference(x, positions, mix_norm_pre, mix_norm_post, ffn_norm_pre, ffn_norm_post,
              attn_w_qkv, attn_w_o, sc_w_in, sc_w_conv, sc_w_out,
              cc_w_pw1, cc_b_pw1, cc_w_dw, cc_b_dw, cc_ln_g, cc_ln_b, cc_w_pw2, cc_b_pw2,
              ffn_w_in, ffn_w_conv, ffn_w_out):
    h = x
    for i in range(DEPTH):
        kind, j = i % N_MIXERS, i // N_MIXERS
        a = rms_norm(h, mix_norm_pre[i])
        if kind == 0:
            m = dilated_attention_mixer(a, positions, attn_w_qkv[j], attn_w_o[j])
        elif kind == 1:
            m = short_conv_mixer(a, sc_w_in[j], sc_w_conv[j], sc_w_out[j])
        else:
            m = conformer_conv_mixer(a, cc_w_pw1[j], cc_b_pw1[j], cc_w_dw[j], cc_b_dw[j],
                                     cc_ln_g[j], cc_ln_b[j], cc_w_pw2[j], cc_b_pw2[j])
        h = h + rms_norm(m, mix_norm_post[i])
        f = conv_ffn(rms_norm(h, ffn_norm_pre[i]), ffn_w_in[i], ffn_w_conv[i], ffn_w_out[i])
        h = h + rms_norm(f, ffn_norm_post[i])
    return h
```

```python
import contextlib
import math
import numpy as np
import concourse.bass as bass
import concourse.mybir as mybir
from concourse.bass_utils import run_bass_kernel_spmd

F32 = mybir.dt.float32
BF16 = mybir.dt.bfloat16
I32 = mybir.dt.int32
AF = mybir.ActivationFunctionType
ALU = mybir.AluOpType

S = 2048
D = 1024
FF = 2816
NCORES = 8
SEQ_PER_CORE = 4
DEPTH = 4
GROUPS = ((128, 1), (512, 4), (2048, 16))

R_MIXPRE, R_MIXPOST, R_FFNPRE, R_FFNPOST = 0, 32, 64, 96
R_SCCONV = 128
R_CCBPW1 = 152
R_CCWDW = 168
R_CCBDW, R_CCLNG, R_CCLNB, R_CCBPW2 = 416, 424, 432, 440
R_FFNCONV = 448
C_ID, C_SMAT, C_MASK, C_RC = 0, 128, 256, 512
NCST = 640

SEM_CAP = 30000
ENGS = ("pe", "act", "dve", "pool", "sp")


class T:
    __slots__ = ("w", "r")

    def __init__(self):
        self.w = None
        self.r = {}


class TP(T):
    __slots__ = ()


class Prog:
    def __init__(self, nc, stack):
        self.nc = nc
        self.stack = stack
        self.q = {e: [] for e in ENGS}
        self.cnt = {}
        self.seen = {e: {} for e in ENGS}
        self.sems = {}

    def _sem(self, chan, idx):
        lst = self.sems.setdefault(chan, [])
        while len(lst) <= idx:
            lst.append(self.stack.enter_context(self.nc.semaphore(f"s_{chan}_{len(lst)}")))
        return lst[idx]

    def _ev(self, chan, c, step):
        per = SEM_CAP // step
        return self._sem(chan, (c - 1) // per), ((c - 1) % per + 1) * step

    def _deps(self, eng, reads, writes, skip=None):
        deps = {}

        def add(ev):
            if ev is None:
                return
            ch, c = ev
            if (ch == "pe" and eng == "pe") or ch == skip:
                return
            if ch not in ENGS:
                c = self.cnt[ch]
            if deps.get(ch, 0) < c:
                deps[ch] = c
        for t in reads:
            add(t.w)
        for t in writes:
            add(t.w)
            for ch, c in t.r.items():
                add((ch, c))
        out = []
        for ch, c in deps.items():
            if self.seen[eng].get(ch, 0) < c:
                self.seen[eng][ch] = c
                out.append((ch, c))
        return out

    def _mark(self, ev, reads, writes):
        ch, c = ev
        for t in writes:
            t.w = ev
            t.r = {}
        for t in reads:
            if t.r.get(ch, 0) < c:
                t.r[ch] = c

    def op(self, eng, fn, reads=(), writes=()):
        if any(isinstance(t, TP) for t in reads):
            writes = list(writes) + [t for t in reads if isinstance(t, TP)]
            reads = [t for t in reads if not isinstance(t, TP)]
        waits = self._deps(eng, reads, writes)
        c = self.cnt.get(eng, 0) + 1
        self.cnt[eng] = c
        self.q[eng].append((waits, fn, (eng, c, 1)))
        self._mark((eng, c), reads, writes)

    def dma(self, eng, chan, fn, reads=(), writes=()):
        waits = self._deps(eng, reads, writes, skip=chan)
        c = self.cnt.get(chan, 0) + 1
        self.cnt[chan] = c
        self.q[eng].append((waits, fn, (chan, c, 16)))
        self._mark((chan, c), reads, writes)

    def barrier(self):
        for e in ENGS:
            waits = []
            for ch, c in self.cnt.items():
                if ch == e or ch.startswith("w"):
                    continue
                if self.seen[e].get(ch, 0) < c:
                    self.seen[e][ch] = c
                    waits.append((ch, c))
            if waits:
                self.q[e].append((waits, None, None))

    def wait_all(self, eng, tiles):
        waits = self._deps(eng, tiles, ())
        self.q[eng].append((waits, None, None))

    def emit(self, block):
        handles = {"pe": block.tensor, "act": block.scalar, "dve": block.vector,
                   "pool": block.gpsimd, "sp": block.sync}
        for e in ENGS:
            q = self.q[e]
            if not q:
                continue

            def body(h, q=q):
                for waits, fn, inc in q:
                    for ch, c in waits:
                        s, v = self._ev(ch, c, 1 if ch in ENGS else 16)
                        h.wait_ge(s, v)
                    if fn is None:
                        continue
                    ins = None
                    for name, kw in (fn if isinstance(fn, list) else [fn]):
                        ins = getattr(h, name)(**kw)
                    ch, c, step = inc
                    s, v = self._ev(ch, c, step)
                    ins.then_inc(s, step)
            handles[e](body)


def _wkeys(layers, nseq, do_mixer, do_ffn):
    for _ in range(nseq):
        for l in layers:
            kind = l % 3
            if do_mixer:
                if kind == 0:
                    for p in range(8):
                        for g in range(3):
                            yield ("qkv", l, p, g)
                elif kind == 1:
                    for i in range(8):
                        yield ("scin", l, i)
                else:
                    for i in range(8):
                        yield ("pw1", l, i)
                for tt in range(4):
                    for hh in range(2):
                        yield ("mo", l, tt, hh)
            if do_ffn:
                for hf in range(2):
                    for j in range(11):
                        yield ("fin", l, hf, j)
                    for tq in range(2):
                        for o in range(8):
                            yield ("fout", l, hf, tq, o)


def I(name, **kw):
    return (name, kw)


def build(layers, nseq, do_mixer=True, do_ffn=True):
    nc = bass.Bass("TRN2", target_bir_lowering=False)
    dt = nc.dram_tensor
    x = dt("x", [nseq * S, D], F32, kind="ExternalInput").ap()
    y = dt("y", [nseq * S, D], F32, kind="ExternalOutput").ap()
    pos = dt("pos", [nseq, S], I32, kind="ExternalInput").ap()
    pth = dt("pth", [1024, 128], F32, kind="ExternalInput").ap()
    cst = dt("cst", [128, NCST], F32, kind="ExternalInput").ap()
    W = {}
    has_attn = False
    for l in layers:
        kind = l % 3
        if kind == 0:
            has_attn = True
            W[l, "min"] = dt(f"w{l}_min", [D, 9216], F32, kind="ExternalInput").ap()
        elif kind == 1:
            W[l, "min"] = dt(f"w{l}_min", [D, 3072], F32, kind="ExternalInput").ap()
        else:
            W[l, "min"] = dt(f"w{l}_min", [D, 2048], F32, kind="ExternalInput").ap()
        W[l, "mout"] = dt(f"w{l}_mout", [D, D], F32, kind="ExternalInput").ap()
        W[l, "fin"] = dt(f"w{l}_fin", [D, 2 * FF], F32, kind="ExternalInput").ap()
        W[l, "fout"] = dt(f"w{l}_fout", [FF, D], F32, kind="ExternalInput").ap()
    hpark = dt("hpark", [128, 8 * S], F32, kind="Internal").ap() if has_attn else None

    with contextlib.ExitStack() as st:
        ARENA_BYTES = 212480
        arena = st.enter_context(nc.sbuf_tensor("arena", [128, ARENA_BYTES // 4], F32))
        arena_bf = arena.bitcast(BF16)
        arena_i = arena.bitcast(I32)
        pb = [st.enter_context(nc.psum_tensor(f"pb{i}", [128, 512], F32)) for i in range(8)]
        pb_bf = [b.bitcast(BF16) for b in pb]
        block = st.enter_context(nc.Block())
        P = Prog(nc, st)

        def carve(off, n, dtype=F32):
            if dtype == BF16:
                assert off % 2 == 0
                return arena_bf[:, off // 2: off // 2 + n]
            assert off % 4 == 0
            v = arena if dtype == F32 else arena_i
            return v[:, off // 4: off // 4 + n]

        tiles = {}

        def t(*key):
            r = tiles.get(key)
            if r is None:
                r = tiles[key] = T()
            return r

        ptiles = [TP() for _ in range(8)]

        def pbt(i, q0=0, q1=4):
            return [ptiles[i]]

        PTAB = carve(0, 1024)
        IDF = carve(4096, 128)
        ONESB = carve(4608, 128, BF16)
        IDB = carve(4864, 128, BF16)
        SMAT = carve(5120, 128, BF16)
        MASK2 = carve(5376, 256, BF16)
        MASK = MASK2.rearrange("p (j q) -> p j q", j=2)
        RC = carve(5888, 8)
        ONESF = carve(6144, 128)
        BG = carve(6656, 8)
        WS0 = 7168
        NW = 3
        wslots = [carve(WS0 + i * 8192, 4096, BF16) for i in range(NW)]
        A_OFF = WS0 + NW * 8192
        A = carve(A_OFF, 8 * S, BF16).rearrange("p (c t) -> p c t", c=8)
        H_OFF = A_OFF + 32768
        H = carve(H_OFF, 8 * S).rearrange("p (c t) -> p c t", c=8)
        PH = H_OFF + 65536
        assert ARENA_BYTES - PH >= 79904
        EPS_RMS = RC[:, 1:2]
        EPS_LN = RC[:, 2:3]

        def col(r):
            return PTAB[:, r:r + 1]

        def tA(c, tt):
            return t("A", c, tt)

        def tH(c, tt):
            return t("H", c, tt)

        keygen = _wkeys(layers, nseq, do_mixer, do_ffn)
        WS = {"pending": [], "nxt": 0}
        wT = [T() for _ in range(NW)]

        def w_issue():
            key = next(keygen, None)
            if key is None:
                return
            s = WS["nxt"]
            WS["nxt"] = (s + 1) % NW
            slot = wslots[s]
            kind, l = key[0], key[1]
            dmas = []
            if kind == "fin":
                j = key[3]
                v = slot.rearrange("p (c s n) -> p c s n", c=8, s=2)
                src = W[l, "fin"].rearrange("(c p) n -> p c n", p=128)
                for h in range(2):
                    dmas.append((v[:, :, h, :], src[:, :, h * FF + j * 256: h * FF + (j + 1) * 256]))
            elif kind == "fout":
                o = key[4]
                v = slot[:, 0:2816].rearrange("p (k n) -> p k n", k=22)
                src = W[l, "fout"].rearrange("(k p) n -> p k n", p=128)
                dmas.append((v, src[:, :, o * 128:(o + 1) * 128]))
            elif kind in ("qkv", "scin", "pw1"):
                ns = 2 if kind == "pw1" else 3
                v = slot[:, 0:ns * 1024].rearrange("p (c s n) -> p c s n", c=8, s=ns)
                src = W[l, "min"].rearrange("(c p) n -> p c n", p=128)
                for s_ in range(ns):
                    if kind == "qkv":
                        cl = s_ * 3072 + key[3] * 1024 + key[2] * 128
                    else:
                        cl = s_ * 1024 + key[2] * 128
                    dmas.append((v[:, :, s_, :], src[:, :, cl:cl + 128]))
            elif kind == "mo":
                hh = key[3]
                v = slot.rearrange("p (c n) -> p c n", c=8)
                src = W[l, "mout"].rearrange("(c p) n -> p c n", p=128)
                dmas.append((v, src[:, :, hh * 512:(hh + 1) * 512]))
            else:
                raise AssertionError(key)
            for (o_, i_) in dmas:
                P.dma("pool", f"w{s}", I("dma_start", out=o_, in_=i_), writes=[wT[s]])
            WS["pending"].append((key, s))

        def w_acquire(key):
            k, s = WS["pending"].pop(0)
            assert k == key, (k, key)
            return wslots[s], wT[s]

        def w_release():
            w_issue()

        def mm(out, pairs, reads, writes):
            n = len(pairs)
            P.op("pe", [I("matmul", out=out, lhsT=l_, rhs=r_, start=(i == 0), stop=(i == n - 1))
                        for i, (l_, r_) in enumerate(pairs)], reads, writes)

        ring = {"mm": [0, 1, 2, 3], "i": 0}

        def next_bank():
            b = ring["mm"][ring["i"] % len(ring["mm"])]
            ring["i"] += 1
            return b

        flip = {"n": 0}

        def alt_engine():
            flip["n"] += 1
            return "act" if flip["n"] % 2 else "dve"

        def copy_op(eng, out, in_, reads, writes):
            if eng == "act":
                P.op("act", I("activation", out=out, in_=in_, func=AF.Copy), reads, writes)
            else:
                P.op(eng, I("tensor_copy", out=out, in_=in_), reads, writes)

        def rstd_inplace(bank, eps_ap):
            bt = pbt(bank)
            P.op("act", I("activation", out=pb[bank][:], in_=pb[bank][:], func=AF.Ln, bias=eps_ap, scale=1.0 / D),
                 bt + [t("cst")], bt)
            P.op("act", I("activation", out=pb[bank][:], in_=pb[bank][:], func=AF.Exp, scale=-0.5), bt, bt)

        def prenorm(row, sq_off):
            sqs = [carve(sq_off + i * 8192, 4096, BF16).rearrange("p (c n) -> p c n", c=8) for i in range(2)]
            for tt in range(4):
                sq = sqs[tt % 2]
                tsq = t("sqpre", tt % 2)
                sl = slice(tt * 512, (tt + 1) * 512)
                hts = [tH(c, tt) for c in range(8)]
                P.op("act", I("activation", out=sq, in_=H[:, :, sl], func=AF.Square), hts, [tsq])
                bank = 6 + tt % 2
                mm(pb[bank][:], [(ONESB, sq[:, c, :]) for c in range(8)], [tsq, t("cst")], pbt(bank))
                rstd_inplace(bank, EPS_RMS)
                for c in range(8):
                    P.op("dve", I("scalar_tensor_tensor", out=A[:, c, sl], in0=H[:, c, sl], scalar=col(row + c),
                                  in1=pb[bank][:], op0=ALU.mult, op1=ALU.mult),
                         [tH(c, tt), t("ptab")] + pbt(bank), [tA(c, tt)])

        Post = {}

        def post_setup(mb_off, sq_off):
            Post["Mb"] = carve(mb_off, 4096).rearrange("p (o n) -> p o n", o=8)
            Post["sq"] = [carve(sq_off + i * 1024, 512, BF16) for i in range(2)]
            Post["n"] = 0

        def post_evac(o, bank, tt, grow, brow=None):
            Mb = Post["Mb"]
            sqb = Post["sq"][Post["n"] % 2]
            tsq = t("sqpost", Post["n"] % 2)
            Post["n"] += 1
            sbank = 6 + tt % 2
            bt = pbt(bank)
            if brow is None:
                P.op("act", I("activation", out=sqb, in_=pb[bank][:], func=AF.Square), bt, [tsq])
                P.op("act", I("activation", out=Mb[:, o, :], in_=pb[bank][:], func=AF.Copy, scale=col(grow + o)),
                     bt + [t("ptab")], [t("Mb", o)])
            else:
                P.op("act", I("activation", out=sqb, in_=pb[bank][:], func=AF.Square, bias=col(brow + o)),
                     bt + [t("ptab")], [tsq])
                P.op("act", I("activation", out=Mb[:, o, :], in_=pb[bank][:], func=AF.Identity,
                              scale=col(grow + o), bias=BG[:, o:o + 1]),
                     bt + [t("ptab"), t("bg")], [t("Mb", o)])
            P.op("pe", I("matmul", out=pb[sbank][:], lhsT=ONESB, rhs=sqb, start=(o == 0), stop=(o == 7)),
                 [tsq, t("cst")], pbt(sbank))

        def post_finish(tt):
            Mb = Post["Mb"]
            sbank = 6 + tt % 2
            rstd_inplace(sbank, EPS_RMS)
            sl = slice(tt * 512, (tt + 1) * 512)
            mbt = [t("Mb", o) for o in range(8)]
            P.op("dve", I("tensor_tensor", out=Mb, in0=Mb,
                          in1=pb[sbank][:].unsqueeze(1).broadcast_to([128, 8, 512]), op=ALU.mult),
                 mbt + pbt(sbank), mbt)
            hts = [tH(c, tt) for c in range(8)]
            P.op("pool", I("tensor_tensor", out=H[:, :, sl], in0=H[:, :, sl], in1=Mb, op=ALU.add), mbt + hts, hts)

        def mixer_out(l, Y, ytile, brow=None):
            grow = R_MIXPOST + l * 8
            for tt in range(4):
                sl = slice(tt * 512, (tt + 1) * 512)
                for hh in range(2):
                    slot, wt = w_acquire(("mo", l, tt, hh))
                    v = slot.rearrange("p (c n) -> p c n", c=8)
                    for o4 in range(4):
                        o = hh * 4 + o4
                        bank = next_bank()
                        mm(pb[bank][:], [(v[:, c, o4 * 128:(o4 + 1) * 128], Y[:, c, sl]) for c in range(8)],
                           [wt] + [ytile(c, tt) for c in range(8)], pbt(bank))
                        post_evac(o, bank, tt, grow, brow)
                    w_release()
                post_finish(tt)

        tc_ = t("cst")
        P.dma("sp", "cst", I("dma_start", out=IDF, in_=cst[:, C_ID:C_ID + 128]), writes=[tc_])
        P.dma("sp", "cst", I("dma_start", out=RC, in_=cst[:, C_RC:C_RC + 8]), writes=[tc_])
        P.dma("pool", "cst", I("dma_start", out=IDB, in_=cst[:, C_ID:C_ID + 128]), writes=[tc_])
        P.dma("pool", "cst", I("dma_start", out=SMAT, in_=cst[:, C_SMAT:C_SMAT + 128]), writes=[tc_])
        P.dma("pool", "cst", I("dma_start", out=MASK2, in_=cst[:, C_MASK:C_MASK + 256]), writes=[tc_])
        P.op("dve", I("memset", ap=ONESB, constant=1.0), [], [t("ones")])
        P.op("dve", I("memset", ap=ONESF, constant=1.0), [], [t("ones")])
        pst = carve(PH, 1024).rearrange("p (k f) -> p k f", k=8)
        P.dma("sp", "ld0", I("dma_start", out=pst, in_=pth.rearrange("(k r) f -> r k f", r=128)), writes=[t("pst")])
        for hb in range(2):
            P.op("pe", [I("transpose", out=pb[hb][:, q * 128:(q + 1) * 128], in_=pst[:, hb * 4 + q, :], identity=IDF)
                        for q in range(4)], [t("pst"), tc_], pbt(hb))
            P.op("act", I("activation", out=PTAB[:, hb * 512:(hb + 1) * 512], in_=pb[hb][:], func=AF.Copy),
                 pbt(hb), [t("ptab")])
        for _ in range(NW):
            w_issue()
        P.barrier()

        def load_x(sq_i):
            stg = [carve(PH + i * 4096, 1024) for i in range(2)]
            for j in range(16):
                sb_ = stg[j % 2]
                ts_ = t("stin", j % 2)
                row0 = (sq_i * 16 + j) * 128
                P.dma("sp", f"ld{j % 2}", I("dma_start", out=sb_, in_=x[row0:row0 + 128, :]), writes=[ts_])
                for hb in range(2):
                    bank = next_bank()
                    P.op("pe", [I("transpose", out=pb[bank][:, q * 128:(q + 1) * 128],
                                  in_=sb_[:, (hb * 4 + q) * 128:(hb * 4 + q + 1) * 128], identity=IDF)
                                for q in range(4)], [ts_, tc_], pbt(bank))
                    copy_op(alt_engine(), H[:, hb * 4:hb * 4 + 4, j * 128:(j + 1) * 128],
                            pb[bank][:].rearrange("p (c n) -> p c n", c=4), pbt(bank),
                            [tH(c, j // 4) for c in range(hb * 4, hb * 4 + 4)])

        def store_y(sq_i):
            stg = [carve(PH + 8192 + i * 4096, 1024) for i in range(2)]
            outs = []
            for j in range(16):
                sb_ = stg[j % 2]
                ts_ = t("stout", j % 2)
                for hb in range(2):
                    bank = next_bank()
                    P.op("pe", [I("transpose", out=pb[bank][:, q * 128:(q + 1) * 128],
                                  in_=H[:, hb * 4 + q, j * 128:(j + 1) * 128], identity=IDF) for q in range(4)],
                         [tH(c, j // 4) for c in range(hb * 4, hb * 4 + 4)] + [tc_], pbt(bank))
                    copy_op(alt_engine(), sb_[:, hb * 512:(hb + 1) * 512], pb[bank][:], pbt(bank), [ts_])
                row0 = (sq_i * 16 + j) * 128
                ty = t("yout", sq_i, j)
                P.dma("sp", f"st{j % 2}", I("dma_start", out=y[row0:row0 + 128, :], in_=sb_), reads=[ts_], writes=[ty])
                outs.append(ty)
            return outs

        def ffn(l):
            ring["mm"] = [0, 1, 2, 3, 4, 5]
            Z = carve(PH, 22 * 1024, BF16).rearrange("p (k n) -> p k n", k=22)
            ACC = [[carve(PH + 45056 + (kd * 2 + b) * 4096, 1024) for b in range(2)] for kd in range(2)]
            SG = [carve(PH + 61440 + b * 4096, 1024) for b in range(2)]
            HALO = carve(PH + 69632, 88).rearrange("p (f n) -> p f n", f=44)
            prenorm(R_FFNPRE + l * 8, PH)
            P.barrier()
            npair = 0
            for hf in range(2):
                for j in range(11):
                    slot, wt = w_acquire(("fin", l, hf, j))
                    v = slot.rearrange("p (c s n) -> p c s n", c=8, s=2)
                    for ii in range(2):
                        i = 2 * j + ii
                        buf = npair % 2
                        npair += 1
                        for kd in range(2):
                            ft = kd * 22 + i
                            acc = ACC[kd][buf]
                            tacc = t("acc", kd, buf)
                            r0 = R_FFNCONV + l * 132 + ft
                            r1 = r0 + 44
                            r2 = r0 + 88
                            th = t("halo", ft)
                            for tq in range(2):
                                tt = hf * 2 + tq
                                sl = slice(tt * 512, (tt + 1) * 512)
                                bank = next_bank()
                                bt = pbt(bank)
                                bk = pb[bank]
                                mm(bk[:], [(v[:, c, kd, ii * 128:(ii + 1) * 128], A[:, c, sl]) for c in range(8)],
                                   [wt] + [tA(c, tt) for c in range(8)], bt)
                                a0 = tq * 512
                                P.op("act", I("activation", out=acc[:, a0:a0 + 512], in_=bk[:], func=AF.Copy,
                                              scale=col(r2)), bt + [t("ptab")], [tacc])
                                P.op("dve", I("scalar_tensor_tensor", out=acc[:, a0 + 1:a0 + 512], in0=bk[:, 0:511],
                                              scalar=col(r1), in1=acc[:, a0 + 1:a0 + 512], op0=ALU.mult, op1=ALU.add),
                                     bt + [tacc, t("ptab")], [tacc])
                                P.op("dve", I("scalar_tensor_tensor", out=acc[:, a0 + 2:a0 + 512], in0=bk[:, 0:510],
                                              scalar=col(r0), in1=acc[:, a0 + 2:a0 + 512], op0=ALU.mult, op1=ALU.add),
                                     bt + [tacc, t("ptab")], [tacc])
                                if tt > 0:
                                    P.op("dve", I("scalar_tensor_tensor", out=acc[:, a0:a0 + 2], in0=HALO[:, ft, 0:2],
                                                  scalar=col(r0), in1=acc[:, a0:a0 + 2], op0=ALU.mult, op1=ALU.add),
                                         [tacc, th, t("ptab")], [tacc])
                                    P.op("dve", I("scalar_tensor_tensor", out=acc[:, a0:a0 + 1], in0=HALO[:, ft, 1:2],
                                                  scalar=col(r1), in1=acc[:, a0:a0 + 1], op0=ALU.mult, op1=ALU.add),
                                         [tacc, th, t("ptab")], [tacc])
                                if tt < 3:
                                    P.op("act", I("activation", out=HALO[:, ft, :], in_=bk[:, 510:512], func=AF.Copy),
                                         bt, [th])
                        sg = SG[buf]
                        tsg = t("sg", buf)
                        P.op("act", I("activation", out=sg, in_=ACC[0][buf], func=AF.Silu), [t("acc", 0, buf)], [tsg])
                        P.op("pool", I("tensor_tensor", out=Z[:, i, :], in0=sg, in1=ACC[1][buf], op=ALU.mult),
                             [tsg, t("acc", 1, buf)], [t("Z", i)])
                    w_release()
                P.barrier()
                post_setup(PH + 45056, PH + 61440)
                for tq in range(2):
                    tt = hf * 2 + tq
                    for o in range(8):
                        slot, wt = w_acquire(("fout", l, hf, tq, o))
                        v = slot[:, 0:2816].rearrange("p (k n) -> p k n", k=22)
                        bank = next_bank()
                        mm(pb[bank][:], [(v[:, k, :], Z[:, k, tq * 512:(tq + 1) * 512]) for k in range(22)],
                           [wt] + [t("Z", k) for k in range(22)], pbt(bank))
                        post_evac(o, bank, tt, R_FFNPOST + l * 8)
                        w_release()
                    post_finish(tt)
                P.barrier()

        def sc_mixer(l):
            ring["mm"] = [0, 1, 2, 3, 4, 5]
            Y = carve(PH, 8 * S, BF16).rearrange("p (c t) -> p c t", c=8)
            CH = carve(PH + 32768, 2056)
            BSB = carve(PH + 40992, 2048)
            HSB = [carve(PH + 49184 + b * 2048, 512) for b in range(2)]
            ACC = carve(PH + 53280, 2048)
            prenorm(R_MIXPRE + l * 8, PH)
            P.barrier()
            P.op("dve", I("memset", ap=CH[:, 0:2], constant=0.0), [], [t("ch")])
            nh = 0
            for i in range(8):
                slot, wt = w_acquire(("scin", l, i))
                v = slot[:, 0:3072].rearrange("p (c s n) -> p c s n", c=8, s=3)
                for tt in range(4):
                    sl = slice(tt * 512, (tt + 1) * 512)
                    banks = [next_bank() for _ in range(3)]
                    for s_ in range(3):
                        mm(pb[banks[s_]][:], [(v[:, c, s_, :], A[:, c, sl]) for c in range(8)],
                           [wt] + [tA(c, tt) for c in range(8)], pbt(banks[s_]))
                    hs = HSB[nh % 2]
                    ths = t("hsb", nh % 2)
                    nh += 1
                    P.op("act", I("activation", out=hs, in_=pb[banks[2]][:], func=AF.Copy), pbt(banks[2]), [ths])
                    P.op("dve", I("tensor_tensor", out=CH[:, 2 + tt * 512:2 + (tt + 1) * 512], in0=pb[banks[1]][:],
                                  in1=hs, op=ALU.mult), pbt(banks[1]) + [ths], [t("ch")])
                    P.op("act", I("activation", out=BSB[:, sl], in_=pb[banks[0]][:], func=AF.Copy),
                         pbt(banks[0]), [t("bsb")])
                w_release()
                r0 = R_SCCONV + i
                P.op("act", I("activation", out=ACC, in_=CH[:, 2:2050], func=AF.Copy, scale=col(r0 + 16)),
                     [t("ch"), t("ptab")], [t("scacc")])
                P.op("dve", I("scalar_tensor_tensor", out=ACC, in0=CH[:, 1:2049], scalar=col(r0 + 8), in1=ACC,
                              op0=ALU.mult, op1=ALU.add), [t("ch"), t("scacc"), t("ptab")], [t("scacc")])
                P.op("dve", I("scalar_tensor_tensor", out=ACC, in0=CH[:, 0:2048], scalar=col(r0), in1=ACC,
                              op0=ALU.mult, op1=ALU.add), [t("ch"), t("scacc"), t("ptab")], [t("scacc")])
                P.op("pool", I("tensor_tensor", out=Y[:, i, :], in0=BSB, in1=ACC, op=ALU.mult),
                     [t("bsb"), t("scacc")], [t("Y", i)])
            post_setup(PH + 61472, PH + 77856)
            mixer_out(l, Y, lambda c, tt: t("Y", c))
            P.barrier()

        def cc_mixer(l):
            ring["mm"] = [0, 1, 2, 3]
            X = carve(PH, 8 * S).rearrange("p (c t) -> p c t", c=8)
            GLU = carve(PH + 65536, 2080)
            SIG = [carve(PH + 73856 + b * 2048, 512) for b in range(2)]
            Yc = A
            prenorm(R_MIXPRE + l * 8, PH)
            P.barrier()
            P.op("dve", I("memset", ap=GLU[:, 0:30], constant=0.0), [], [t("glu")])
            P.op("dve", I("tensor_tensor", out=BG, in0=PTAB[:, R_CCBPW2:R_CCBPW2 + 8],
                          in1=PTAB[:, R_MIXPOST + l * 8:R_MIXPOST + l * 8 + 8], op=ALU.mult), [t("ptab")], [t("bg")])
            ns = 0
            for i in range(8):
                slot, wt = w_acquire(("pw1", l, i))
                v = slot[:, 0:2048].rearrange("p (c s n) -> p c s n", c=8, s=2)
                for tt in range(4):
                    sl = slice(tt * 512, (tt + 1) * 512)
                    ba, bg_ = next_bank(), next_bank()
                    mm(pb[ba][:], [(v[:, c, 0, :], A[:, c, sl]) for c in range(8)],
                       [wt] + [tA(c, tt) for c in range(8)], pbt(ba))
                    mm(pb[bg_][:], [(v[:, c, 1, :], A[:, c, sl]) for c in range(8)],
                       [wt] + [tA(c, tt) for c in range(8)], pbt(bg_))
                    sg = SIG[ns % 2]
                    tsg = t("sig", ns % 2)
                    ns += 1
                    P.op("act", I("activation", out=sg, in_=pb[bg_][:], func=AF.Sigmoid, bias=col(R_CCBPW1 + 8 + i)),
                         pbt(bg_) + [t("ptab")], [tsg])
                    P.op("dve", I("scalar_tensor_tensor", out=GLU[:, 30 + tt * 512:30 + (tt + 1) * 512], in0=pb[ba][:],
                                  scalar=col(R_CCBPW1 + i), in1=sg, op0=ALU.add, op1=ALU.mult),
                         pbt(ba) + [tsg, t("ptab")], [t("glu")])
                w_release()
                tx = [t("X", i, tt) for tt in range(4)]
                P.op("act", I("activation", out=X[:, i, :], in_=GLU[:, 30:30 + S], func=AF.Identity,
                              scale=col(R_CCWDW + 240 + i), bias=col(R_CCBDW + i)), [t("glu"), t("ptab")], tx)
                for k in range(30):
                    P.op("dve", I("scalar_tensor_tensor", out=X[:, i, :], in0=GLU[:, k:k + S],
                                  scalar=col(R_CCWDW + k * 8 + i), in1=X[:, i, :], op0=ALU.mult, op1=ALU.add),
                         [t("glu"), t("ptab")] + tx, tx)
            P.barrier()
            SQ = carve(PH + 65536, 4096, BF16).rearrange("p (c n) -> p c n", c=8)
            MEAN = carve(PH + 73728, 512)
            VAR = carve(PH + 75776, 512)
            for tt in range(4):
                sl = slice(tt * 512, (tt + 1) * 512)
                txs = [t("X", c, tt) for c in range(8)]
                P.op("act", I("activation", out=SQ, in_=X[:, :, sl], func=AF.Square), txs, [t("lnsq")])
                b1, b2 = next_bank(), next_bank()
                mm(pb[b1][:], [(ONESF, X[:, c, sl]) for c in range(8)], txs + [t("ones")], pbt(b1))
                mm(pb[b2][:], [(ONESB, SQ[:, c, :]) for c in range(8)], [t("lnsq"), t("ones")], pbt(b2))
                P.op("dve", I("tensor_scalar", out=MEAN, in0=pb[b1][:], scalar1=1.0 / D, scalar2=None, op0=ALU.mult),
                     pbt(b1), [t("mean")])
                P.op("dve", I("tensor_tensor", out=VAR, in0=MEAN, in1=MEAN, op=ALU.mult), [t("mean")], [t("var")])
                P.op("dve", I("scalar_tensor_tensor", out=VAR, in0=pb[b2][:], scalar=1.0 / D, in1=VAR,
                              op0=ALU.mult, op1=ALU.subtract), pbt(b2) + [t("var")], [t("var")])
                P.op("act", I("activation", out=VAR, in_=VAR, func=AF.Ln, bias=EPS_LN, scale=1.0),
                     [t("var"), tc_], [t("var")])
                P.op("act", I("activation", out=VAR, in_=VAR, func=AF.Exp, scale=-0.5), [t("var")], [t("var")])
                P.op("dve", I("tensor_tensor", out=X[:, :, sl], in0=X[:, :, sl],
                              in1=MEAN.unsqueeze(1).broadcast_to([128, 8, 512]), op=ALU.subtract),
                     txs + [t("mean")], txs)
                P.op("dve", I("tensor_tensor", out=X[:, :, sl], in0=X[:, :, sl],
                              in1=VAR.unsqueeze(1).broadcast_to([128, 8, 512]), op=ALU.mult), txs + [t("var")], txs)
                for c in range(8):
                    P.op("act", I("activation", out=Yc[:, c, sl], in_=X[:, c, sl], func=AF.Silu,
                                  scale=col(R_CCLNG + c), bias=col(R_CCLNB + c)), [t("X", c, tt), t("ptab")], [tA(c, tt)])
            P.barrier()
            post_setup(PH, PH + 16384)
            mixer_out(l, Yc, tA, brow=R_CCBPW2)
            P.barrier()

        def attn_pair_group(l, p, g, set_, QKV, VAB, UAB, PTR, T12, COS, SIN, cnt):
            dil = GROUPS[g][1]
            L = S // dil
            nb = L // 128
            QT, KT, VT = QKV[set_]
            VA, VB = VAB[set_]
            tq_ = [t("qkv", set_, k) for k in range(3)]
            slot, wt = w_acquire(("qkv", l, p, g))
            v = slot[:, 0:3072].rearrange("p (c s n) -> p c s n", c=8, s=3)
            for s3 in range(3):
                dst = QKV[set_][s3]
                for tt in range(4):
                    sl = slice(tt * 512, (tt + 1) * 512)
                    bank = next_bank()
                    bt = pbt(bank)
                    mm(pb[bank][:], [(v[:, c, s3, :], A[:, c, sl]) for c in range(8)],
                       [wt] + [tA(c, tt) for c in range(8)], bt)
                    P.op("act", I("activation", out=dst[:, sl], in_=pb[bank][:], func=AF.Copy), bt, [tq_[s3]])
                    if s3 < 2:
                        nr = cnt["rope"]
                        cnt["rope"] += 1
                        pbk = 2
                        t1, t2 = T12[nr % 2]
                        tt1, tt2 = t("t1", nr % 2), t("t2", nr % 2)
                        mm(pb[pbk][0:80, :], [(SMAT[:, 0:80], dst[:, sl])], [tq_[s3], tc_], pbt(pbk))
                        P.op("dve", I("tensor_tensor", out=t1[0:80, :], in0=pb[bank][0:80, :], in1=COS[0:80, sl],
                                      op=ALU.mult), bt + [t("cos")], [tt1])
                        P.op("dve", I("tensor_tensor", out=t2[0:80, :], in0=pb[pbk][0:80, :], in1=SIN[0:80, sl],
                                      op=ALU.mult), pbt(pbk) + [t("sin")], [tt2])
                        P.op("pool", I("tensor_tensor", out=dst[0:80, sl], in0=t1[0:80, :], in1=t2[0:80, :],
                                       op=ALU.add), [tt1, tt2], [tq_[s3]])
            w_release()
            QTv = QT.rearrange("p (m d) -> p d m", d=dil)
            KTv = KT.rearrange("p (m d) -> p d m", d=dil)
            VTv = VT.rearrange("p (m d) -> p d m", d=dil)
            for h8 in range(2):
                bank = next_bank()
                ins = []
                for b8 in range(8):
                    blk = h8 * 8 + b8
                    r, n = blk // nb, blk % nb
                    ins.append(I("transpose", out=pb_bf[bank][:, b8 * 128:(b8 + 1) * 128],
                                 in_=VTv[:, r, n * 128:(n + 1) * 128], identity=IDB))
                P.op("pe", ins, [tq_[2], tc_], pbt(bank))
                src = pb_bf[bank][:].rearrange("p (b n) -> p b n", b=8)
                P.op("dve", I("tensor_copy", out=VA[:, h8 * 8:h8 * 8 + 8, 0:64], in_=src[:, :, 0:64]),
                     pbt(bank), [t("va", set_)])
                P.op("act", I("activation", out=VB[:, h8 * 8:h8 * 8 + 8, 64:128], in_=src[:, :, 64:128], func=AF.Copy),
                     pbt(bank), [t("vb", set_)])
            items = [(r, n, hd) for r in range(dil) for n in range(nb) for hd in range(2)]
            st_ = {}

            def scores(idx, it):
                r, n, hd = it
                hb = hd * 64
                k4 = idx % 4
                sbank = 3 + idx % 3
                stv = pb[sbank][:, 0:256].rearrange("p (j q) -> p j q", j=2)
                stt = pbt(sbank)
                j0 = 0 if n > 0 else 1
                P.op("pe", [I("matmul", out=stv[:, jj, :], lhsT=KTv[hb:hb + 64, r, (n - 1 + jj) * 128:(n + jj) * 128],
                              rhs=QTv[hb:hb + 64, r, n * 128:(n + 1) * 128], start=True, stop=True)
                            for jj in range(j0, 2)], [tq_[0], tq_[1]], stt)
                pt = PTR[k4]
                tpt = t("pt", k4)
                P.op("act", I("activation", out=pt[:, j0:2, :], in_=stv[:, j0:2, :], func=AF.Exp, scale=0.125),
                     stt, [tpt])
                P.op("pool", I("tensor_tensor", out=pt[:, j0:2, :], in0=pt[:, j0:2, :], in1=MASK[:, j0:2, :],
                               op=ALU.mult), [tpt, tc_], [tpt])
                st_[idx] = (pt, tpt, j0)

            def pv(idx, it):
                r, n, hd = it
                pt, tpt, j0 = st_.pop(idx)
                k4 = idx % 4
                xq = pb[6 + idx % 2][:, 0:128]
                xt = pbt(6 + idx % 2)
                Vh = VA if hd == 0 else VB
                tv = t("va", set_) if hd == 0 else t("vb", set_)
                P.op("pe", [I("matmul", out=xq, lhsT=Vh[:, r * nb + n - 1 + jj, :], rhs=pt[:, jj, :],
                              start=(jj == j0), stop=(jj == 1)) for jj in range(j0, 2)], [tpt, tv], xt)
                tu = t("u", hd)
                Uv = UAB[hd].rearrange("p (m d) -> p d m", d=dil)[:, r, n * 128:(n + 1) * 128]
                if g == 0:
                    P.op("dve", I("tensor_copy", out=Uv, in_=xq), xt, [tu])
                else:
                    P.op("dve", I("tensor_tensor", out=Uv, in0=xq, in1=Uv, op=ALU.add), xt + [tu], [tu])

            for idx in range(len(items) + 2):
                if idx < len(items):
                    scores(idx, items[idx])
                if idx >= 2:
                    pv(idx - 2, items[idx - 2])

        def attn_mixer(l, sq_i):
            ring["mm"] = [0, 1]
            HB = H_OFF
            QKV = [[carve(HB + s_ * 12288 + k * 4096, S, BF16) for k in range(3)] for s_ in range(2)]
            VAB = [[carve(HB + 24576 + s_ * 8192 + k * 4096, 2048, BF16).rearrange("p (b n) -> p b n", b=16)
                    for k in range(2)] for s_ in range(2)]
            UAB = [carve(HB + 40960 + k * 8192, S) for k in range(2)]
            ZN = carve(HB + 57344, S)
            OT = carve(PH, 8 * S, BF16).rearrange("p (c t) -> p c t", c=8)
            COS = carve(PH + 32768, S)
            SIN = carve(PH + 40960, S)
            PTR = [carve(PH + 49152 + i * 512, 256, BF16).rearrange("p (j q) -> p j q", j=2) for i in range(4)]
            T12 = [[carve(PH + 51200 + (b * 2 + k) * 2048, 512) for k in range(2)] for b in range(2)]
            POSI = carve(PH + 59392, S, I32)
            XF = carve(PH + 67584, S)
            prenorm(R_MIXPRE + l * 8, PH)
            P.barrier()
            for c in range(8):
                P.dma("sp", "park", I("dma_start", out=hpark[:, c * S:(c + 1) * S], in_=H[:, c, :]),
                      reads=[tH(c, tt) for tt in range(4)], writes=[t("hpark", c)])
            P.dma("sp", "pos", I("dma_start", out=POSI, in_=pos[sq_i:sq_i + 1, :].broadcast_to([128, S])),
                  writes=[t("posi")])
            P.op("dve", I("tensor_copy", out=XF, in_=POSI), [t("posi")], [t("xf")])
            P.op("dve", I("tensor_scalar", out=XF, in0=XF, scalar1=RC[:, 0:1], scalar2=None, op0=ALU.mult),
                 [t("xf"), tc_], [t("xf")])
            P.op("dve", I("tensor_copy", out=POSI, in_=XF), [t("xf")], [t("posi")])
            P.op("dve", I("tensor_copy", out=COS, in_=POSI), [t("posi")], [t("cos")])
            P.op("dve", I("tensor_tensor", out=XF, in0=XF, in1=COS, op=ALU.subtract), [t("xf"), t("cos")], [t("xf")])
            P.op("dve", I("scalar_tensor_tensor", out=SIN, in0=XF, scalar=0.5, in1=XF, op0=ALU.is_gt, op1=ALU.subtract),
                 [t("xf")], [t("sin")])
            P.op("dve", I("scalar_tensor_tensor", out=SIN, in0=SIN, scalar=0.5, in1=SIN, op0=ALU.is_gt, op1=ALU.subtract),
                 [t("sin")], [t("sin")])
            P.op("dve", I("tensor_scalar", out=XF, in0=XF, scalar1=0.25, scalar2=None, op0=ALU.add), [t("xf")], [t("xf")])
            P.op("dve", I("scalar_tensor_tensor", out=COS, in0=XF, scalar=0.5, in1=XF, op0=ALU.is_gt, op1=ALU.subtract),
                 [t("xf"), t("cos")], [t("cos")])
            P.op("dve", I("scalar_tensor_tensor", out=COS, in0=COS, scalar=0.5, in1=COS, op0=ALU.is_gt, op1=ALU.subtract),
                 [t("cos")], [t("cos")])
            TWO_PI = 2.0 * math.pi * (1.0 - 2e-7)
            P.op("act", I("activation", out=SIN, in_=SIN, func=AF.Sin, scale=TWO_PI), [t("sin")], [t("sin")])
            P.op("act", I("activation", out=COS, in_=COS, func=AF.Sin, scale=TWO_PI), [t("cos")], [t("cos")])
            P.barrier()
            for s_ in range(2):
                P.op("pool", I("memset", ap=VAB[s_][0][:, :, 64:128], constant=1.0), [], [t("va", s_)])
                P.op("pool", I("memset", ap=VAB[s_][1][:, :, 0:64], constant=1.0), [], [t("vb", s_)])
            cnt = {"rope": 0}
            npg = 0
            for p in range(8):
                for g in range(3):
                    attn_pair_group(l, p, g, npg % 2, QKV, VAB, UAB, PTR, T12, COS, SIN, cnt)
                    npg += 1
                P.dma("sp", "zsw", I("dma_start", out=ZN[0:64, :], in_=UAB[0][64:128, :]),
                      reads=[t("u", 0)], writes=[t("zn")])
                P.dma("sp", "zsw", I("dma_start", out=ZN[64:128, :], in_=UAB[1][0:64, :]),
                      reads=[t("u", 1)], writes=[t("zn")])
                P.op("act", I("activation", out=ZN, in_=ZN, func=AF.Ln), [t("zn")], [t("zn")])
                P.op("act", I("activation", out=ZN, in_=ZN, func=AF.Exp, scale=-1.0), [t("zn")], [t("zn")])
                P.op("dve", I("tensor_tensor", out=OT[0:64, p, :], in0=UAB[0][0:64, :], in1=ZN[0:64, :], op=ALU.mult),
                     [t("u", 0), t("zn")], [t("ot", p)])
                P.op("dve", I("tensor_tensor", out=OT[64:128, p, :], in0=UAB[1][64:128, :], in1=ZN[64:128, :],
                              op=ALU.mult), [t("u", 1), t("zn")], [t("ot", p)])
            P.barrier()
            for c in range(8):
                P.dma("sp", "unpark", I("dma_start", out=H[:, c, :], in_=hpark[:, c * S:(c + 1) * S]),
                      reads=[t("hpark", c)], writes=[tH(c, tt) for tt in range(4)])
            ring["mm"] = [0, 1, 2, 3, 4, 5]
            post_setup(PH + 32768, PH + 49152)
            mixer_out(l, OT, lambda c, tt: t("ot", c))
            P.barrier()

        outs = []
        for sq_i in range(nseq):
            load_x(sq_i)
            P.barrier()
            for l in layers:
                kind = l % 3
                if do_mixer:
                    if kind == 0:
                        attn_mixer(l, sq_i)
                    elif kind == 1:
                        sc_mixer(l)
                    else:
                        cc_mixer(l)
                if do_ffn:
                    ffn(l)
            ring["mm"] = [0, 1, 2, 3]
            outs += store_y(sq_i)
            P.barrier()
        P.wait_all("sp", outs)
        assert next(keygen, None) is None and not WS["pending"]
        P.emit(block)
    return nc


def _consts():
    c = np.zeros((128, NCST), np.float32)
    c[:, C_ID:C_ID + 128] = np.eye(128, dtype=np.float32)
    sm = np.zeros((128, 128), np.float32)
    for base in (0, 64):
        for m in range(8):
            sm[base + m + 8, base + m] = -1.0
            sm[base + m, base + m + 8] = 1.0
    c[:, C_SMAT:C_SMAT + 128] = sm
    k = np.arange(128)[:, None]
    q = np.arange(128)[None, :]
    c[:, C_MASK:C_MASK + 128] = (k >= q).astype(np.float32)
    c[:, C_MASK + 128:C_MASK + 256] = (k <= q).astype(np.float32)
    inv = (500000.0 ** (-np.arange(0, 16, 2, dtype=np.float32) / 16.0)).astype(np.float32)
    for p in range(128):
        if p % 64 < 16:
            c[p, C_RC] = inv[p % 8] / np.float32(2.0 * math.pi)
    c[:, C_RC + 1] = 1e-6
    c[:, C_RC + 2] = 1e-5
    return c


def _pack_params(inp):
    f = lambda a: np.ascontiguousarray(np.asarray(a, np.float32)).reshape(-1, 128)
    rows = np.concatenate([
        f(inp["mix_norm_pre"]), f(inp["mix_norm_post"]), f(inp["ffn_norm_pre"]), f(inp["ffn_norm_post"]),
        f(inp["sc_w_conv"]), f(inp["cc_b_pw1"]), f(inp["cc_w_dw"]), f(inp["cc_b_dw"]),
        f(inp["cc_ln_g"]), f(inp["cc_ln_b"]), f(inp["cc_b_pw2"]), f(inp["ffn_w_conv"])], axis=0)
    assert rows.shape[0] == 976
    out = np.zeros((1024, 128), np.float32)
    out[:976] = rows
    return out


def _layer_weights(inp, l):
    kind, j = l % 3, l // 3
    a = lambda k, i: np.ascontiguousarray(np.asarray(inp[k][i], np.float32))
    d = {}
    if kind == 0:
        d[f"w{l}_min"] = a("attn_w_qkv", j)
        d[f"w{l}_mout"] = a("attn_w_o", j)
    elif kind == 1:
        d[f"w{l}_min"] = a("sc_w_in", j)
        d[f"w{l}_mout"] = a("sc_w_out", j)
    else:
        d[f"w{l}_min"] = a("cc_w_pw1", j)
        d[f"w{l}_mout"] = a("cc_w_pw2", j)
    d[f"w{l}_fin"] = a("ffn_w_in", l)
    d[f"w{l}_fout"] = a("ffn_w_out", l)
    return d


LAUNCH_GROUPS = [[0, 1, 2, 3]]
_NC_CACHE = {}


def kernel(**inp):
    x = np.ascontiguousarray(np.asarray(inp["x"], np.float32))
    positions = np.ascontiguousarray(np.asarray(inp["positions"], np.int32))
    pth = _pack_params(inp)
    cst = _consts()
    cur = [x[c * SEQ_PER_CORE:(c + 1) * SEQ_PER_CORE].reshape(SEQ_PER_CORE * S, D) for c in range(NCORES)]
    for grp in LAUNCH_GROUPS:
        key = tuple(grp)
        if key not in _NC_CACHE:
            _NC_CACHE[key] = build(list(grp), SEQ_PER_CORE)
        nc = _NC_CACHE[key]
        wd = {}
        for l in grp:
            wd.update(_layer_weights(inp, l))
        in_maps = []
        for c in range(NCORES):
            m = {"x": cur[c], "pos": positions[c * SEQ_PER_CORE:(c + 1) * SEQ_PER_CORE], "pth": pth, "cst": cst}
            m.update(wd)
            in_maps.append(m)
        res = run_bass_kernel_spmd(nc, in_maps, core_ids=list(range(NCORES)))
        cur = [np.asarray(r["y"], np.float32) for r in res.results]
    out = np.stack([c.reshape(SEQ_PER_CORE, S, D) for c in cur], axis=0).reshape(NCORES * SEQ_PER_CORE, S, D)
    return out
```

```python
import contextlib
import math
import numpy as np
import concourse.bass as bass
import concourse.mybir as mybir
from concourse.bass_utils import run_bass_kernel_spmd

F32 = mybir.dt.float32
BF16 = mybir.dt.bfloat16
I32 = mybir.dt.int32
AF = mybir.ActivationFunctionType
ALU = mybir.AluOpType

S = 2048
D = 1024
FF = 2816
NCORES = 8
SEQ_PER_CORE = 4
DEPTH = 4
GROUPS = ((128, 1), (512, 4), (2048, 16))

R_MIXPRE, R_MIXPOST, R_FFNPRE, R_FFNPOST = 0, 32, 64, 96
R_SCCONV = 128
R_CCBPW1 = 152
R_CCWDW = 168
R_CCBDW, R_CCLNG, R_CCLNB, R_CCBPW2 = 416, 424, 432, 440
R_FFNCONV = 448
C_ID, C_SMAT, C_MASK, C_RC = 0, 128, 256, 512
NCST = 640

SEM_CAP = 30000
ENGS = ("pe", "act", "dve", "pool", "sp")


class T:
    __slots__ = ("w", "r")

    def __init__(self):
        self.w = None
        self.r = {}


class TP(T):
    __slots__ = ()


class Prog:
    def __init__(self, nc, stack):
        self.nc = nc
        self.stack = stack
        self.q = {e: [] for e in ENGS}
        self.cnt = {}
        self.seen = {e: {} for e in ENGS}
        self.sems = {}

    def _sem(self, chan, idx):
        lst = self.sems.setdefault(chan, [])
        while len(lst) <= idx:
            lst.append(self.stack.enter_context(self.nc.semaphore(f"s_{chan}_{len(lst)}")))
        return lst[idx]

    def _ev(self, chan, c, step):
        per = SEM_CAP // step
        return self._sem(chan, (c - 1) // per), ((c - 1) % per + 1) * step

    def _deps(self, eng, reads, writes, skip=None):
        deps = {}

        def add(ev):
            if ev is None:
                return
            ch, c = ev
            if (ch == "pe" and eng == "pe") or ch == skip:
                return
            if ch not in ENGS:
                c = self.cnt[ch]
            if deps.get(ch, 0) < c:
                deps[ch] = c
        for t in reads:
            add(t.w)
        for t in writes:
            add(t.w)
            for ch, c in t.r.items():
                add((ch, c))
        out = []
        for ch, c in deps.items():
            if self.seen[eng].get(ch, 0) < c:
                self.seen[eng][ch] = c
                out.append((ch, c))
        return out

    def _mark(self, ev, reads, writes):
        ch, c = ev
        for t in writes:
            t.w = ev
            t.r = {}
        for t in reads:
            if t.r.get(ch, 0) < c:
                t.r[ch] = c

    def op(self, eng, fn, reads=(), writes=()):
        if any(isinstance(t, TP) for t in reads):
            writes = list(writes) + [t for t in reads if isinstance(t, TP)]
            reads = [t for t in reads if not isinstance(t, TP)]
        waits = self._deps(eng, reads, writes)
        c = self.cnt.get(eng, 0) + 1
        self.cnt[eng] = c
        self.q[eng].append((waits, fn, (eng, c, 1)))
        self._mark((eng, c), reads, writes)

    def dma(self, eng, chan, fn, reads=(), writes=()):
        waits = self._deps(eng, reads, writes, skip=chan)
        c = self.cnt.get(chan, 0) + 1
        self.cnt[chan] = c
        self.q[eng].append((waits, fn, (chan, c, 16)))
        self._mark((chan, c), reads, writes)

    def barrier(self):
        for e in ENGS:
            waits = []
            for ch, c in self.cnt.items():
                if ch == e or ch.startswith("w"):
                    continue
                if self.seen[e].get(ch, 0) < c:
                    self.seen[e][ch] = c
                    waits.append((ch, c))
            if waits:
                self.q[e].append((waits, None, None))

    def wait_all(self, eng, tiles):
        waits = self._deps(eng, tiles, ())
        self.q[eng].append((waits, None, None))

    def emit(self, block):
        handles = {"pe": block.tensor, "act": block.scalar, "dve": block.vector,
                   "pool": block.gpsimd, "sp": block.sync}
        for e in ENGS:
            q = self.q[e]
            if not q:
                continue

            def body(h, q=q):
                for waits, fn, inc in q:
                    for ch, c in waits:
                        s, v = self._ev(ch, c, 1 if ch in ENGS else 16)
                        h.wait_ge(s, v)
                    if fn is None:
                        continue
                    ins = None
                    for name, kw in (fn if isinstance(fn, list) else [fn]):
                        ins = getattr(h, name)(**kw)
                    ch, c, step = inc
                    s, v = self._ev(ch, c, step)
                    ins.then_inc(s, step)
            handles[e](body)


def _wkeys(layers, nseq, do_mixer, do_ffn):
    for _ in range(nseq):
        for l in layers:
            kind = l % 3
            if do_mixer:
                if kind == 0:
                    for p in range(8):
                        for g in range(3):
                            yield ("qkv", l, p, g)
                elif kind == 1:
                    for i in range(8):
                        yield ("scin", l, i)
                else:
                    for i in range(8):
                        yield ("pw1", l, i)
                for tt in range(4):
                    for hh in range(2):
                        yield ("mo", l, tt, hh)
            if do_ffn:
                for hf in range(2):
                    for j in range(11):
                        yield ("fin", l, hf, j)
                    for tq in range(2):
                        for o in range(8):
                            yield ("fout", l, hf, tq, o)


def I(name, **kw):
    return (name, kw)


def build(layers, nseq, do_mixer=True, do_ffn=True):
    nc = bass.Bass("TRN2", target_bir_lowering=False)
    dt = nc.dram_tensor
    x = dt("x", [nseq * S, D], F32, kind="ExternalInput").ap()
    y = dt("y", [nseq * S, D], F32, kind="ExternalOutput").ap()
    pos = dt("pos", [nseq, S], I32, kind="ExternalInput").ap()
    pth = dt("pth", [1024, 128], F32, kind="ExternalInput").ap()
    cst = dt("cst", [128, NCST], F32, kind="ExternalInput").ap()
    W = {}
    has_attn = False
    for l in layers:
        kind = l % 3
        if kind == 0:
            has_attn = True
            W[l, "min"] = dt(f"w{l}_min", [D, 9216], F32, kind="ExternalInput").ap()
        elif kind == 1:
            W[l, "min"] = dt(f"w{l}_min", [D, 3072], F32, kind="ExternalInput").ap()
        else:
            W[l, "min"] = dt(f"w{l}_min", [D, 2048], F32, kind="ExternalInput").ap()
        W[l, "mout"] = dt(f"w{l}_mout", [D, D], F32, kind="ExternalInput").ap()
        W[l, "fin"] = dt(f"w{l}_fin", [D, 2 * FF], F32, kind="ExternalInput").ap()
        W[l, "fout"] = dt(f"w{l}_fout", [FF, D], F32, kind="ExternalInput").ap()
    hpark = dt("hpark", [128, 8 * S], F32, kind="Internal").ap() if has_attn else None

    with contextlib.ExitStack() as st:
        ARENA_BYTES = 212480
        arena = st.enter_context(nc.sbuf_tensor("arena", [128, ARENA_BYTES // 4], F32))
        arena_bf = arena.bitcast(BF16)
        arena_i = arena.bitcast(I32)
        pb = [st.enter_context(nc.psum_tensor(f"pb{i}", [128, 512], F32)) for i in range(8)]
        pb_bf = [b.bitcast(BF16) for b in pb]
        block = st.enter_context(nc.Block())
        P = Prog(nc, st)

        def carve(off, n, dtype=F32):
            if dtype == BF16:
                assert off % 2 == 0
                return arena_bf[:, off // 2: off // 2 + n]
            assert off % 4 == 0
            v = arena if dtype == F32 else arena_i
            return v[:, off // 4: off // 4 + n]

        tiles = {}

        def t(*key):
            r = tiles.get(key)
            if r is None:
                r = tiles[key] = T()
            return r

        ptiles = [TP() for _ in range(8)]

        def pbt(i, q0=0, q1=4):
            return [ptiles[i]]

        PTAB = carve(0, 1024)
        IDF = carve(4096, 128)
        ONESB = carve(4608, 128, BF16)
        IDB = carve(4864, 128, BF16)
        SMAT = carve(5120, 128, BF16)
        MASK2 = carve(5376, 256, BF16)
        MASK = MASK2.rearrange("p (j q) -> p j q", j=2)
        RC = carve(5888, 8)
        ONESF = carve(6144, 128)
        BG = carve(6656, 8)
        WS0 = 7168
        NW = 3
        wslots = [carve(WS0 + i * 8192, 4096, BF16) for i in range(NW)]
        A_OFF = WS0 + NW * 8192
        A = carve(A_OFF, 8 * S, BF16).rearrange("p (c t) -> p c t", c=8)
        H_OFF = A_OFF + 32768
        H = carve(H_OFF, 8 * S).rearrange("p (c t) -> p c t", c=8)
        PH = H_OFF + 65536
        assert ARENA_BYTES - PH >= 79904
        EPS_RMS = RC[:, 1:2]
        EPS_LN = RC[:, 2:3]

        def col(r):
            return PTAB[:, r:r + 1]

        def tA(c, tt):
            return t("A", c, tt)

        def tH(c, tt):
            return t("H", c, tt)

        keygen = _wkeys(layers, nseq, do_mixer, do_ffn)
        WS = {"pending": [], "nxt": 0}
        wT = [T() for _ in range(NW)]

        def w_issue():
            key = next(keygen, None)
            if key is None:
                return
            s = WS["nxt"]
            WS["nxt"] = (s + 1) % NW
            slot = wslots[s]
            kind, l = key[0], key[1]
            dmas = []
            if kind == "fin":
                j = key[3]
                v = slot.rearrange("p (c s n) -> p c s n", c=8, s=2)
                src = W[l, "fin"].rearrange("(c p) n -> p c n", p=128)
                for h in range(2):
                    dmas.append((v[:, :, h, :], src[:, :, h * FF + j * 256: h * FF + (j + 1) * 256]))
            elif kind == "fout":
                o = key[4]
                v = slot[:, 0:2816].rearrange("p (k n) -> p k n", k=22)
                src = W[l, "fout"].rearrange("(k p) n -> p k n", p=128)
                dmas.append((v, src[:, :, o * 128:(o + 1) * 128]))
            elif kind in ("qkv", "scin", "pw1"):
                ns = 2 if kind == "pw1" else 3
                v = slot[:, 0:ns * 1024].rearrange("p (c s n) -> p c s n", c=8, s=ns)
                src = W[l, "min"].rearrange("(c p) n -> p c n", p=128)
                for s_ in range(ns):
                    if kind == "qkv":
                        cl = s_ * 3072 + key[3] * 1024 + key[2] * 128
                    else:
                        cl = s_ * 1024 + key[2] * 128
                    dmas.append((v[:, :, s_, :], src[:, :, cl:cl + 128]))
            elif kind == "mo":
                hh = key[3]
                v = slot.rearrange("p (c n) -> p c n", c=8)
                src = W[l, "mout"].rearrange("(c p) n -> p c n", p=128)
                dmas.append((v, src[:, :, hh * 512:(hh + 1) * 512]))
            else:
                raise AssertionError(key)
            for (o_, i_) in dmas:
                P.dma("pool", f"w{s}", I("dma_start", out=o_, in_=i_), writes=[wT[s]])
            WS["pending"].append((key, s))

        def w_acquire(key):
            k, s = WS["pending"].pop(0)
            assert k == key, (k, key)
            return wslots[s], wT[s]

        def w_release():
            w_issue()

        def mm(out, pairs, reads, writes):
            n = len(pairs)
            P.op("pe", [I("matmul", out=out, lhsT=l_, rhs=r_, start=(i == 0), stop=(i == n - 1))
                        for i, (l_, r_) in enumerate(pairs)], reads, writes)

        ring = {"mm": [0, 1, 2, 3], "i": 0}

        def next_bank():
            b = ring["mm"][ring["i"] % len(ring["mm"])]
            ring["i"] += 1
            return b

        flip = {"n": 0}

        def alt_engine():
            flip["n"] += 1
            return "act" if flip["n"] % 2 else "dve"

        def copy_op(eng, out, in_, reads, writes):
            if eng == "act":
                P.op("act", I("activation", out=out, in_=in_, func=AF.Copy), reads, writes)
            else:
                P.op(eng, I("tensor_copy", out=out, in_=in_), reads, writes)

        def rstd_inplace(bank, eps_ap):
            bt = pbt(bank)
            P.op("act", I("activation", out=pb[bank][:], in_=pb[bank][:], func=AF.Ln, bias=eps_ap, scale=1.0 / D),
                 bt + [t("cst")], bt)
            P.op("act", I("activation", out=pb[bank][:], in_=pb[bank][:], func=AF.Exp, scale=-0.5), bt, bt)

        def prenorm(row, sq_off):
            sqs = [carve(sq_off + i * 8192, 4096, BF16).rearrange("p (c n) -> p c n", c=8) for i in range(2)]

            def square(tt):
                sl = slice(tt * 512, (tt + 1) * 512)
                P.op("act", I("activation", out=sqs[tt % 2], in_=H[:, :, sl], func=AF.Square),
                     [tH(c, tt) for c in range(8)], [t("sqpre", tt % 2)])
            square(0)
            square(1)
            for tt in range(4):
                sq = sqs[tt % 2]
                tsq = t("sqpre", tt % 2)
                sl = slice(tt * 512, (tt + 1) * 512)
                bank = 6 + tt % 2
                mm(pb[bank][:], [(ONESB, sq[:, c, :]) for c in range(8)], [tsq, t("cst")], pbt(bank))
                rstd_inplace(bank, EPS_RMS)
                if tt + 2 < 4:
                    square(tt + 2)
                for c in range(8):
                    P.op("dve", I("scalar_tensor_tensor", out=A[:, c, sl], in0=H[:, c, sl], scalar=col(row + c),
                                  in1=pb[bank][:], op0=ALU.mult, op1=ALU.mult),
                         [tH(c, tt), t("ptab")] + pbt(bank), [tA(c, tt)])

        Post = {}

        def post_setup(mb_off, sq_off, mb_off2=None):
            offs = [mb_off, mb_off if mb_off2 is None else mb_off2]
            Post["Mb"] = [carve(o_, 4096).rearrange("p (o n) -> p o n", o=8) for o_ in offs]
            Post["two"] = mb_off2 is not None
            Post["sq"] = [carve(sq_off + i * 1024, 512, BF16) for i in range(2)]
            Post["n"] = 0

        def post_evac(o, bank, tt, grow, brow=None):
            mi = tt % 2 if Post["two"] else 0
            Mb = Post["Mb"][mi]
            sqb = Post["sq"][Post["n"] % 2]
            tsq = t("sqpost", Post["n"] % 2)
            Post["n"] += 1
            sbank = 6 + tt % 2
            bt = pbt(bank)
            if brow is None:
                P.op("act", I("activation", out=sqb, in_=pb[bank][:], func=AF.Square), bt, [tsq])
                P.op("act", I("activation", out=Mb[:, o, :], in_=pb[bank][:], func=AF.Copy, scale=col(grow + o)),
                     bt + [t("ptab")], [t("Mb", mi, o)])
            else:
                P.op("act", I("activation", out=sqb, in_=pb[bank][:], func=AF.Square, bias=col(brow + o)),
                     bt + [t("ptab")], [tsq])
                P.op("act", I("activation", out=Mb[:, o, :], in_=pb[bank][:], func=AF.Identity,
                              scale=col(grow + o), bias=BG[:, o:o + 1]),
                     bt + [t("ptab"), t("bg")], [t("Mb", mi, o)])
            P.op("pe", I("matmul", out=pb[sbank][:], lhsT=ONESB, rhs=sqb, start=(o == 0), stop=(o == 7)),
                 [tsq, t("cst")], pbt(sbank))

        def post_finish(tt):
            mi = tt % 2 if Post["two"] else 0
            Mb = Post["Mb"][mi]
            sbank = 6 + tt % 2
            rstd_inplace(sbank, EPS_RMS)
            sl = slice(tt * 512, (tt + 1) * 512)
            mbt = [t("Mb", mi, o) for o in range(8)]
            P.op("dve", I("tensor_tensor", out=Mb, in0=Mb,
                          in1=pb[sbank][:].unsqueeze(1).broadcast_to([128, 8, 512]), op=ALU.mult),
                 mbt + pbt(sbank), mbt)
            hts = [tH(c, tt) for c in range(8)]
            P.op("pool", I("tensor_tensor", out=H[:, :, sl], in0=H[:, :, sl], in1=Mb, op=ALU.add), mbt + hts, hts)

        def mixer_out(l, Y, ytile, brow=None):
            grow = R_MIXPOST + l * 8
            for tt in range(4):
                sl = slice(tt * 512, (tt + 1) * 512)
                for hh in range(2):
                    slot, wt = w_acquire(("mo", l, tt, hh))
                    v = slot.rearrange("p (c n) -> p c n", c=8)
                    for o4 in range(4):
                        o = hh * 4 + o4
                        bank = next_bank()
                        mm(pb[bank][:], [(v[:, c, o4 * 128:(o4 + 1) * 128], Y[:, c, sl]) for c in range(8)],
                           [wt] + [ytile(c, tt) for c in range(8)], pbt(bank))
                        post_evac(o, bank, tt, grow, brow)
                    w_release()
                post_finish(tt)

        tc_ = t("cst")
        P.dma("sp", "cst", I("dma_start", out=IDF, in_=cst[:, C_ID:C_ID + 128]), writes=[tc_])
        P.dma("sp", "cst", I("dma_start", out=RC, in_=cst[:, C_RC:C_RC + 8]), writes=[tc_])
        P.dma("pool", "cst", I("dma_start", out=IDB, in_=cst[:, C_ID:C_ID + 128]), writes=[tc_])
        P.dma("pool", "cst", I("dma_start", out=SMAT, in_=cst[:, C_SMAT:C_SMAT + 128]), writes=[tc_])
        P.dma("pool", "cst", I("dma_start", out=MASK2, in_=cst[:, C_MASK:C_MASK + 256]), writes=[tc_])
        P.op("dve", I("memset", ap=ONESB, constant=1.0), [], [t("ones")])
        P.op("dve", I("memset", ap=ONESF, constant=1.0), [], [t("ones")])
        pst = carve(PH, 1024).rearrange("p (k f) -> p k f", k=8)
        P.dma("sp", "ld0", I("dma_start", out=pst, in_=pth.rearrange("(k r) f -> r k f", r=128)), writes=[t("pst")])
        for hb in range(2):
            P.op("pe", [I("transpose", out=pb[hb][:, q * 128:(q + 1) * 128], in_=pst[:, hb * 4 + q, :], identity=IDF)
                        for q in range(4)], [t("pst"), tc_], pbt(hb))
            P.op("act", I("activation", out=PTAB[:, hb * 512:(hb + 1) * 512], in_=pb[hb][:], func=AF.Copy),
                 pbt(hb), [t("ptab")])
        for _ in range(NW):
            w_issue()
        P.barrier()

        def load_x(sq_i):
            stg = [carve(PH + i * 4096, 1024) for i in range(2)]
            for j in range(16):
                sb_ = stg[j % 2]
                ts_ = t("stin", j % 2)
                row0 = (sq_i * 16 + j) * 128
                P.dma("sp", f"ld{j % 2}", I("dma_start", out=sb_, in_=x[row0:row0 + 128, :]), writes=[ts_])
                for hb in range(2):
                    bank = next_bank()
                    P.op("pe", [I("transpose", out=pb[bank][:, q * 128:(q + 1) * 128],
                                  in_=sb_[:, (hb * 4 + q) * 128:(hb * 4 + q + 1) * 128], identity=IDF)
                                for q in range(4)], [ts_, tc_], pbt(bank))
                    copy_op(alt_engine(), H[:, hb * 4:hb * 4 + 4, j * 128:(j + 1) * 128],
                            pb[bank][:].rearrange("p (c n) -> p c n", c=4), pbt(bank),
                            [tH(c, j // 4) for c in range(hb * 4, hb * 4 + 4)])

        def store_y(sq_i):
            stg = [carve(PH + 8192 + i * 4096, 1024) for i in range(2)]
            outs = []
            for j in range(16):
                sb_ = stg[j % 2]
                ts_ = t("stout", j % 2)
                for hb in range(2):
                    bank = next_bank()
                    P.op("pe", [I("transpose", out=pb[bank][:, q * 128:(q + 1) * 128],
                                  in_=H[:, hb * 4 + q, j * 128:(j + 1) * 128], identity=IDF) for q in range(4)],
                         [tH(c, j // 4) for c in range(hb * 4, hb * 4 + 4)] + [tc_], pbt(bank))
                    copy_op(alt_engine(), sb_[:, hb * 512:(hb + 1) * 512], pb[bank][:], pbt(bank), [ts_])
                row0 = (sq_i * 16 + j) * 128
                ty = t("yout", sq_i, j)
                P.dma("sp", f"st{j % 2}", I("dma_start", out=y[row0:row0 + 128, :], in_=sb_), reads=[ts_], writes=[ty])
                outs.append(ty)
            return outs

        def ffn(l):
            ring["mm"] = [0, 1, 2, 3, 4, 5]
            Z = carve(PH, 22 * 1024, BF16).rearrange("p (k n) -> p k n", k=22)
            ACC = [[carve(PH + 45056 + (kd * 2 + b) * 4096, 1024) for b in range(2)] for kd in range(2)]
            SG = [carve(PH + 61440 + b * 4096, 1024) for b in range(2)]
            HALO = carve(PH + 79872, 176).rearrange("p (q f n) -> p q f n", q=2, f=44)
            prenorm(R_FFNPRE + l * 8, PH)
            P.barrier()
            npair = 0
            for hf in range(2):
                for j in range(11):
                    slot, wt = w_acquire(("fin", l, hf, j))
                    v = slot.rearrange("p (c s n) -> p c s n", c=8, s=2)
                    for ii in range(2):
                        i = 2 * j + ii
                        buf = npair % 2
                        npair += 1
                        for kd in range(2):
                            ft = kd * 22 + i
                            acc = ACC[kd][buf]
                            tacc = t("acc", kd, buf)
                            r0 = R_FFNCONV + l * 132 + ft
                            r1 = r0 + 44
                            r2 = r0 + 88
                            for tq in range(2):
                                tt = hf * 2 + tq
                                sl = slice(tt * 512, (tt + 1) * 512)
                                bank = next_bank()
                                bt = pbt(bank)
                                bk = pb[bank]
                                mm(bk[:], [(v[:, c, kd, ii * 128:(ii + 1) * 128], A[:, c, sl]) for c in range(8)],
                                   [wt] + [tA(c, tt) for c in range(8)], bt)
                                a0 = tq * 512
                                P.op("act", I("activation", out=acc[:, a0:a0 + 512], in_=bk[:], func=AF.Copy,
                                              scale=col(r2)), bt + [t("ptab")], [tacc])
                                if tt < 3:
                                    P.op("act", I("activation", out=HALO[:, tt % 2, ft, :], in_=bk[:, 510:512],
                                                  func=AF.Copy), bt, [t("halo", tt % 2, ft)])
                                P.op("dve", I("scalar_tensor_tensor", out=acc[:, a0 + 1:a0 + 512], in0=bk[:, 0:511],
                                              scalar=col(r1), in1=acc[:, a0 + 1:a0 + 512], op0=ALU.mult, op1=ALU.add),
                                     bt + [tacc, t("ptab")], [tacc])
                                P.op("dve", I("scalar_tensor_tensor", out=acc[:, a0 + 2:a0 + 512], in0=bk[:, 0:510],
                                              scalar=col(r0), in1=acc[:, a0 + 2:a0 + 512], op0=ALU.mult, op1=ALU.add),
                                     bt + [tacc, t("ptab")], [tacc])
                                if tt > 0:
                                    hq = (tt - 1) % 2
                                    th = t("halo", hq, ft)
                                    P.op("dve", I("scalar_tensor_tensor", out=acc[:, a0:a0 + 2], in0=HALO[:, hq, ft, 0:2],
                                                  scalar=col(r0), in1=acc[:, a0:a0 + 2], op0=ALU.mult, op1=ALU.add),
                                         [tacc, th, t("ptab")], [tacc])
                                    P.op("dve", I("scalar_tensor_tensor", out=acc[:, a0:a0 + 1], in0=HALO[:, hq, ft, 1:2],
                                                  scalar=col(r1), in1=acc[:, a0:a0 + 1], op0=ALU.mult, op1=ALU.add),
                                         [tacc, th, t("ptab")], [tacc])
                        sg = SG[buf]
                        tsg = t("sg", buf)
                        P.op("act", I("activation", out=sg, in_=ACC[0][buf], func=AF.Silu), [t("acc", 0, buf)], [tsg])
                        P.op("pool", I("tensor_tensor", out=Z[:, i, :], in0=sg, in1=ACC[1][buf], op=ALU.mult),
                             [tsg, t("acc", 1, buf)], [t("Z", i)])
                    w_release()
                P.barrier()
                post_setup(PH + 45056, PH + 61440, PH + 63488)
                for tq in range(2):
                    tt = hf * 2 + tq
                    for o in range(8):
                        slot, wt = w_acquire(("fout", l, hf, tq, o))
                        v = slot[:, 0:2816].rearrange("p (k n) -> p k n", k=22)
                        bank = next_bank()
                        mm(pb[bank][:], [(v[:, k, :], Z[:, k, tq * 512:(tq + 1) * 512]) for k in range(22)],
                           [wt] + [t("Z", k) for k in range(22)], pbt(bank))
                        post_evac(o, bank, tt, R_FFNPOST + l * 8)
                        w_release()
                    post_finish(tt)
                P.barrier()

        def sc_mixer(l):
            ring["mm"] = [0, 1, 2, 3, 4, 5]
            Y = carve(PH, 8 * S, BF16).rearrange("p (c t) -> p c t", c=8)
            CH = carve(PH + 32768, 2056)
            BSB = carve(PH + 40992, 2048)
            HSB = [carve(PH + 49184 + b * 2048, 512) for b in range(2)]
            ACC = carve(PH + 53280, 2048)
            prenorm(R_MIXPRE + l * 8, PH)
            P.barrier()
            P.op("dve", I("memset", ap=CH[:, 0:2], constant=0.0), [], [t("ch")])
            nh = 0
            for i in range(8):
                slot, wt = w_acquire(("scin", l, i))
                v = slot[:, 0:3072].rearrange("p (c s n) -> p c s n", c=8, s=3)
                for tt in range(4):
                    sl = slice(tt * 512, (tt + 1) * 512)
                    banks = [next_bank() for _ in range(3)]
                    for s_ in range(3):
                        mm(pb[banks[s_]][:], [(v[:, c, s_, :], A[:, c, sl]) for c in range(8)],
                           [wt] + [tA(c, tt) for c in range(8)], pbt(banks[s_]))
                    hs = HSB[nh % 2]
                    ths = t("hsb", nh % 2)
                    nh += 1
                    P.op("act", I("activation", out=hs, in_=pb[banks[2]][:], func=AF.Copy), pbt(banks[2]), [ths])
                    P.op("dve", I("tensor_tensor", out=CH[:, 2 + tt * 512:2 + (tt + 1) * 512], in0=pb[banks[1]][:],
                                  in1=hs, op=ALU.mult), pbt(banks[1]) + [ths], [t("ch")])
                    P.op("act", I("activation", out=BSB[:, sl], in_=pb[banks[0]][:], func=AF.Copy),
                         pbt(banks[0]), [t("bsb")])
                w_release()
                r0 = R_SCCONV + i
                P.op("act", I("activation", out=ACC, in_=CH[:, 2:2050], func=AF.Copy, scale=col(r0 + 16)),
                     [t("ch"), t("ptab")], [t("scacc")])
                P.op("dve", I("scalar_tensor_tensor", out=ACC, in0=CH[:, 1:2049], scalar=col(r0 + 8), in1=ACC,
                              op0=ALU.mult, op1=ALU.add), [t("ch"), t("scacc"), t("ptab")], [t("scacc")])
                P.op("dve", I("scalar_tensor_tensor", out=ACC, in0=CH[:, 0:2048], scalar=col(r0), in1=ACC,
                              op0=ALU.mult, op1=ALU.add), [t("ch"), t("scacc"), t("ptab")], [t("scacc")])
                P.op("pool", I("tensor_tensor", out=Y[:, i, :], in0=BSB, in1=ACC, op=ALU.mult),
                     [t("bsb"), t("scacc")], [t("Y", i)])
            P.barrier()
            post_setup(PH + 61472, PH + 77856, PH + 32768)
            mixer_out(l, Y, lambda c, tt: t("Y", c))
            P.barrier()

        def cc_mixer(l):
            ring["mm"] = [0, 1, 2, 3]
            X = carve(PH, 8 * S).rearrange("p (c t) -> p c t", c=8)
            GLU = carve(PH + 65536, 2080)
            SIG = [carve(PH + 73856 + b * 2048, 512) for b in range(2)]
            Yc = A
            prenorm(R_MIXPRE + l * 8, PH)
            P.barrier()
            P.op("dve", I("memset", ap=GLU[:, 0:30], constant=0.0), [], [t("glu")])
            P.op("dve", I("tensor_tensor", out=BG, in0=PTAB[:, R_CCBPW2:R_CCBPW2 + 8],
                          in1=PTAB[:, R_MIXPOST + l * 8:R_MIXPOST + l * 8 + 8], op=ALU.mult), [t("ptab")], [t("bg")])
            ns = 0
            for i in range(8):
                slot, wt = w_acquire(("pw1", l, i))
                v = slot[:, 0:2048].rearrange("p (c s n) -> p c s n", c=8, s=2)
                for tt in range(4):
                    sl = slice(tt * 512, (tt + 1) * 512)
                    ba, bg_ = next_bank(), next_bank()
                    mm(pb[ba][:], [(v[:, c, 0, :], A[:, c, sl]) for c in range(8)],
                       [wt] + [tA(c, tt) for c in range(8)], pbt(ba))
                    mm(pb[bg_][:], [(v[:, c, 1, :], A[:, c, sl]) for c in range(8)],
                       [wt] + [tA(c, tt) for c in range(8)], pbt(bg_))
                    sg = SIG[ns % 2]
                    tsg = t("sig", ns % 2)
                    ns += 1
                    P.op("act", I("activation", out=sg, in_=pb[bg_][:], func=AF.Sigmoid, bias=col(R_CCBPW1 + 8 + i)),
                         pbt(bg_) + [t("ptab")], [tsg])
                    P.op("dve", I("scalar_tensor_tensor", out=GLU[:, 30 + tt * 512:30 + (tt + 1) * 512], in0=pb[ba][:],
                                  scalar=col(R_CCBPW1 + i), in1=sg, op0=ALU.add, op1=ALU.mult),
                         pbt(ba) + [tsg, t("ptab")], [t("glu")])
                w_release()
                tx = [t("X", i, tt) for tt in range(4)]
                P.op("act", I("activation", out=X[:, i, :], in_=GLU[:, 30:30 + S], func=AF.Identity,
                              scale=col(R_CCWDW + 240 + i), bias=col(R_CCBDW + i)), [t("glu"), t("ptab")], tx)
                for k in range(30):
                    P.op("dve", I("scalar_tensor_tensor", out=X[:, i, :], in0=GLU[:, k:k + S],
                                  scalar=col(R_CCWDW + k * 8 + i), in1=X[:, i, :], op0=ALU.mult, op1=ALU.add),
                         [t("glu"), t("ptab")] + tx, tx)
            P.barrier()
            SQ = carve(PH + 65536, 4096, BF16).rearrange("p (c n) -> p c n", c=8)
            MEAN = carve(PH + 73728, 512)
            VAR = carve(PH + 75776, 512)
            for tt in range(4):
                sl = slice(tt * 512, (tt + 1) * 512)
                txs = [t("X", c, tt) for c in range(8)]
                P.op("act", I("activation", out=SQ, in_=X[:, :, sl], func=AF.Square), txs, [t("lnsq")])
                b1, b2 = next_bank(), next_bank()
                mm(pb[b1][:], [(ONESF, X[:, c, sl]) for c in range(8)], txs + [t("ones")], pbt(b1))
                mm(pb[b2][:], [(ONESB, SQ[:, c, :]) for c in range(8)], [t("lnsq"), t("ones")], pbt(b2))
                P.op("dve", I("tensor_scalar", out=MEAN, in0=pb[b1][:], scalar1=1.0 / D, scalar2=None, op0=ALU.mult),
                     pbt(b1), [t("mean")])
                P.op("dve", I("tensor_tensor", out=VAR, in0=MEAN, in1=MEAN, op=ALU.mult), [t("mean")], [t("var")])
                P.op("dve", I("scalar_tensor_tensor", out=VAR, in0=pb[b2][:], scalar=1.0 / D, in1=VAR,
                              op0=ALU.mult, op1=ALU.subtract), pbt(b2) + [t("var")], [t("var")])
                P.op("act", I("activation", out=VAR, in_=VAR, func=AF.Ln, bias=EPS_LN, scale=1.0),
                     [t("var"), tc_], [t("var")])
                P.op("act", I("activation", out=VAR, in_=VAR, func=AF.Exp, scale=-0.5), [t("var")], [t("var")])
                P.op("dve", I("tensor_tensor", out=X[:, :, sl], in0=X[:, :, sl],
                              in1=MEAN.unsqueeze(1).broadcast_to([128, 8, 512]), op=ALU.subtract),
                     txs + [t("mean")], txs)
                P.op("dve", I("tensor_tensor", out=X[:, :, sl], in0=X[:, :, sl],
                              in1=VAR.unsqueeze(1).broadcast_to([128, 8, 512]), op=ALU.mult), txs + [t("var")], txs)
                for c in range(8):
                    P.op("act", I("activation", out=Yc[:, c, sl], in_=X[:, c, sl], func=AF.Silu,
                                  scale=col(R_CCLNG + c), bias=col(R_CCLNB + c)), [t("X", c, tt), t("ptab")], [tA(c, tt)])
            P.barrier()
            post_setup(PH, PH + 16384, PH + 18432)
            mixer_out(l, Yc, tA, brow=R_CCBPW2)
            P.barrier()

        def attn_mixer(l, sq_i):
            ring["mm"] = [0, 1, 2]
            HB = H_OFF
            QKV = [[carve(HB + s_ * 12288 + k * 4096, S, BF16) for k in range(3)] for s_ in range(2)]
            VAB = [[carve(HB + 24576 + s_ * 8192 + k * 4096, 2048, BF16).rearrange("p (b n) -> p b n", b=16)
                    for k in range(2)] for s_ in range(2)]
            OT = carve(PH, 8 * S, BF16).rearrange("p (c t) -> p c t", c=8)
            COS = carve(PH + 32768, S)
            SIN = carve(PH + 40960, S)
            PTR = [carve(PH + 49152 + i * 512, 256, BF16).rearrange("p (j q) -> p j q", j=2) for i in range(4)]
            T12 = [[carve(PH + 51200 + (b * 2 + k) * 2048, 512) for k in range(2)] for b in range(2)]
            POSI = carve(PH + 59392, S, I32)
            XF = carve(PH + 67584, S)
            ZN = carve(PH + 59392, S)
            UAB = [[carve(HB + 40960, S), carve(HB + 49152, S)],
                   [carve(HB + 57344, S), carve(PH + 67584, S)]]
            for c in range(8):
                P.dma("sp", "park", I("dma_start", out=hpark[:, c * S:(c + 1) * S], in_=H[:, c, :]),
                      reads=[tH(c, tt) for tt in range(4)], writes=[t("hpark", c)])
            P.dma("sp", "pos", I("dma_start", out=POSI, in_=pos[sq_i:sq_i + 1, :].broadcast_to([128, S])),
                  writes=[t("posi")])
            P.op("dve", I("tensor_copy", out=XF, in_=POSI), [t("posi")], [t("xf")])
            P.op("dve", I("tensor_scalar", out=XF, in0=XF, scalar1=RC[:, 0:1], scalar2=None, op0=ALU.mult),
                 [t("xf"), tc_], [t("xf")])
            P.op("dve", I("tensor_copy", out=POSI, in_=XF), [t("xf")], [t("posi")])
            P.op("dve", I("tensor_copy", out=COS, in_=POSI), [t("posi")], [t("cos")])
            P.op("dve", I("tensor_tensor", out=XF, in0=XF, in1=COS, op=ALU.subtract), [t("xf"), t("cos")], [t("xf")])
            P.op("dve", I("scalar_tensor_tensor", out=SIN, in0=XF, scalar=0.5, in1=XF, op0=ALU.is_gt, op1=ALU.subtract),
                 [t("xf")], [t("sin")])
            P.op("dve", I("scalar_tensor_tensor", out=SIN, in0=SIN, scalar=0.5, in1=SIN, op0=ALU.is_gt, op1=ALU.subtract),
                 [t("sin")], [t("sin")])
            P.op("dve", I("tensor_scalar", out=XF, in0=XF, scalar1=0.25, scalar2=None, op0=ALU.add), [t("xf")], [t("xf")])
            P.op("dve", I("scalar_tensor_tensor", out=COS, in0=XF, scalar=0.5, in1=XF, op0=ALU.is_gt, op1=ALU.subtract),
                 [t("xf"), t("cos")], [t("cos")])
            P.op("dve", I("scalar_tensor_tensor", out=COS, in0=COS, scalar=0.5, in1=COS, op0=ALU.is_gt, op1=ALU.subtract),
                 [t("cos")], [t("cos")])
            TWO_PI = 2.0 * math.pi * (1.0 - 2e-7)
            P.op("act", I("activation", out=SIN, in_=SIN, func=AF.Sin, scale=TWO_PI), [t("sin")], [t("sin")])
            P.op("act", I("activation", out=COS, in_=COS, func=AF.Sin, scale=TWO_PI), [t("cos")], [t("cos")])
            prenorm(R_MIXPRE + l * 8, PH)
            P.barrier()
            for s_ in range(2):
                P.op("pool", I("memset", ap=VAB[s_][0][:, :, 64:128], constant=1.0), [], [t("va", s_)])
                P.op("pool", I("memset", ap=VAB[s_][1][:, :, 0:64], constant=1.0), [], [t("vb", s_)])
            cnt = {"rope": 0, "idx": 0}

            def proj_gen(p, g, set_):
                dil = GROUPS[g][1]
                nb = (S // dil) // 128
                VA, VB = VAB[set_]
                tq_ = [t("qkv", set_, k) for k in range(3)]
                slot, wt = w_acquire(("qkv", l, p, g))
                v = slot[:, 0:3072].rearrange("p (c s n) -> p c s n", c=8, s=3)
                pend = None
                for s3 in range(3):
                    dst = QKV[set_][s3]
                    for tt in range(4):
                        sl = slice(tt * 512, (tt + 1) * 512)
                        bank = next_bank()
                        bt = pbt(bank)
                        mm(pb[bank][:], [(v[:, c, s3, :], A[:, c, sl]) for c in range(8)],
                           [wt] + [tA(c, tt) for c in range(8)], bt)
                        if pend is not None:
                            pend()
                            pend = None
                        P.op("act", I("activation", out=dst[:, sl], in_=pb[bank][:], func=AF.Copy), bt, [tq_[s3]])
                        if s3 < 2:
                            nr = cnt["rope"]
                            cnt["rope"] += 1
                            t1, t2 = T12[nr % 2]
                            tt1, tt2 = t("t1", nr % 2), t("t2", nr % 2)
                            P.op("dve", I("tensor_tensor", out=t1[0:80, :], in0=pb[bank][0:80, :], in1=COS[0:80, sl],
                                          op=ALU.mult), bt + [t("cos")], [tt1])

                            def rope_tail(dst=dst, sl=sl, t1=t1, t2=t2, tt1=tt1, tt2=tt2, tq=tq_[s3]):
                                mm(pb[3][0:80, :], [(SMAT[:, 0:80], dst[:, sl])], [tq, tc_], pbt(3))
                                P.op("dve", I("tensor_tensor", out=t2[0:80, :], in0=pb[3][0:80, :], in1=SIN[0:80, sl],
                                              op=ALU.mult), pbt(3) + [t("sin")], [tt2])
                                P.op("pool", I("tensor_tensor", out=dst[0:80, sl], in0=t1[0:80, :], in1=t2[0:80, :],
                                               op=ALU.add), [tt1, tt2], [tq])
                            pend = rope_tail
                        yield
                if pend is not None:
                    pend()
                w_release()
                VTv = QKV[set_][2].rearrange("p (m d) -> p d m", d=dil)
                for h8 in range(2):
                    bank = next_bank()
                    ins = []
                    for b8 in range(8):
                        blk = h8 * 8 + b8
                        r, n = blk // nb, blk % nb
                        ins.append(I("transpose", out=pb_bf[bank][:, b8 * 128:(b8 + 1) * 128],
                                     in_=VTv[:, r, n * 128:(n + 1) * 128], identity=IDB))
                    P.op("pe", ins, [tq_[2], tc_], pbt(bank))
                    src = pb_bf[bank][:].rearrange("p (b n) -> p b n", b=8)
                    P.op("dve", I("tensor_copy", out=VA[:, h8 * 8:h8 * 8 + 8, 0:64], in_=src[:, :, 0:64]),
                         pbt(bank), [t("va", set_)])
                    P.op("act", I("activation", out=VB[:, h8 * 8:h8 * 8 + 8, 64:128], in_=src[:, :, 64:128],
                                  func=AF.Copy), pbt(bank), [t("vb", set_)])
                    yield

            def core_gen(p, g, set_):
                dil = GROUPS[g][1]
                nb = (S // dil) // 128
                QT, KT, VT = QKV[set_]
                VA, VB = VAB[set_]
                U2 = UAB[p % 2]
                tq_ = [t("qkv", set_, k) for k in range(3)]
                QTv = QT.rearrange("p (m d) -> p d m", d=dil)
                KTv = KT.rearrange("p (m d) -> p d m", d=dil)
                items = [(r, n, hd) for r in range(dil) for n in range(nb) for hd in range(2)]
                st_ = {}

                def scores(it):
                    r, n, hd = it
                    hb = hd * 64
                    gi = cnt["idx"]
                    cnt["idx"] += 1
                    sbank = 4 + gi % 2
                    stv = pb[sbank][:, 0:256].rearrange("p (j q) -> p j q", j=2)
                    stt = pbt(sbank)
                    j0 = 0 if n > 0 else 1
                    P.op("pe", [I("matmul", out=stv[:, jj, :], lhsT=KTv[hb:hb + 64, r, (n - 1 + jj) * 128:(n + jj) * 128],
                                  rhs=QTv[hb:hb + 64, r, n * 128:(n + 1) * 128], start=True, stop=True)
                                for jj in range(j0, 2)], [tq_[0], tq_[1]], stt)
                    pt = PTR[gi % 4]
                    tpt = t("pt", gi % 4)
                    P.op("act", I("activation", out=pt[:, j0:2, :], in_=stv[:, j0:2, :], func=AF.Exp, scale=0.125),
                         stt, [tpt])
                    P.op("pool", I("tensor_tensor", out=pt[:, j0:2, :], in0=pt[:, j0:2, :], in1=MASK[:, j0:2, :],
                                   op=ALU.mult), [tpt, tc_], [tpt])
                    return (pt, tpt, j0, gi)

                def pv(it, stf):
                    r, n, hd = it
                    pt, tpt, j0, gi = stf
                    xb = 6 + gi % 2
                    xq = pb[xb][:, 0:128]
                    xt = pbt(xb)
                    Vh = VA if hd == 0 else VB
                    tv = t("va", set_) if hd == 0 else t("vb", set_)
                    P.op("pe", [I("matmul", out=xq, lhsT=Vh[:, r * nb + n - 1 + jj, :], rhs=pt[:, jj, :],
                                  start=(jj == j0), stop=(jj == 1)) for jj in range(j0, 2)], [tpt, tv], xt)
                    tu = t("u", p % 2, hd)
                    Uv = U2[hd].rearrange("p (m d) -> p d m", d=dil)[:, r, n * 128:(n + 1) * 128]
                    if g == 0:
                        P.op("dve", I("tensor_copy", out=Uv, in_=xq), xt, [tu])
                    else:
                        P.op("dve", I("tensor_tensor", out=Uv, in0=xq, in1=Uv, op=ALU.add), xt + [tu], [tu])

                for idx in range(len(items) + 2):
                    if idx < len(items):
                        st_[idx] = scores(items[idx])
                    if idx >= 2:
                        pv(items[idx - 2], st_.pop(idx - 2))
                    yield

            def normalize(p):
                U2 = UAB[p % 2]
                tu0, tu1 = t("u", p % 2, 0), t("u", p % 2, 1)
                P.dma("sp", "zsw", I("dma_start", out=ZN[0:64, :], in_=U2[0][64:128, :]), reads=[tu0], writes=[t("zn")])
                P.dma("sp", "zsw", I("dma_start", out=ZN[64:128, :], in_=U2[1][0:64, :]), reads=[tu1], writes=[t("zn")])
                P.op("act", I("activation", out=ZN, in_=ZN, func=AF.Ln), [t("zn")], [t("zn")])
                P.op("act", I("activation", out=ZN, in_=ZN, func=AF.Exp, scale=-1.0), [t("zn")], [t("zn")])
                P.op("dve", I("tensor_tensor", out=OT[0:64, p, :], in0=U2[0][0:64, :], in1=ZN[0:64, :], op=ALU.mult),
                     [tu0, t("zn")], [t("ot", p)])
                P.op("dve", I("tensor_tensor", out=OT[64:128, p, :], in0=U2[1][64:128, :], in1=ZN[64:128, :],
                              op=ALU.mult), [tu1, t("zn")], [t("ot", p)])

            pgs = [(p, g) for p in range(8) for g in range(3)]
            for _ in proj_gen(pgs[0][0], pgs[0][1], 0):
                pass
            for k, (p, g) in enumerate(pgs):
                core = core_gen(p, g, k % 2)
                proj = proj_gen(pgs[k + 1][0], pgs[k + 1][1], (k + 1) % 2) if k + 1 < len(pgs) else None
                nstep = 0
                for _ in core:
                    nstep += 1
                    if proj is not None and nstep % 4 == 0:
                        if next(proj, "done") == "done":
                            proj = None
                if proj is not None:
                    for _ in proj:
                        pass
                if g == 2:
                    normalize(p)
            P.barrier()
            for tt in range(4):
                sl = slice(tt * 512, (tt + 1) * 512)
                P.dma("sp", "unpark", I("dma_start", out=H[:, :, sl],
                                        in_=hpark.rearrange("p (c t) -> p c t", c=8)[:, :, sl]),
                      reads=[t("hpark", c) for c in range(8)], writes=[tH(c, tt) for c in range(8)])
            ring["mm"] = [0, 1, 2, 3, 4, 5]
            post_setup(PH + 32768, PH + 49152, PH + 51200)
            mixer_out(l, OT, lambda c, tt: t("ot", c))
            P.barrier()

        outs = []
        for sq_i in range(nseq):
            load_x(sq_i)
            P.barrier()
            for l in layers:
                kind = l % 3
                if do_mixer:
                    if kind == 0:
                        attn_mixer(l, sq_i)
                    elif kind == 1:
                        sc_mixer(l)
                    else:
                        cc_mixer(l)
                if do_ffn:
                    ffn(l)
            ring["mm"] = [0, 1, 2, 3]
            outs += store_y(sq_i)
            P.barrier()
        P.wait_all("sp", outs)
        assert next(keygen, None) is None and not WS["pending"]
        P.emit(block)
    return nc


def _consts():
    c = np.zeros((128, NCST), np.float32)
    c[:, C_ID:C_ID + 128] = np.eye(128, dtype=np.float32)
    sm = np.zeros((128, 128), np.float32)
    for base in (0, 64):
        for m in range(8):
            sm[base + m + 8, base + m] = -1.0
            sm[base + m, base + m + 8] = 1.0
    c[:, C_SMAT:C_SMAT + 128] = sm
    k = np.arange(128)[:, None]
    q = np.arange(128)[None, :]
    c[:, C_MASK:C_MASK + 128] = (k >= q).astype(np.float32)
    c[:, C_MASK + 128:C_MASK + 256] = (k <= q).astype(np.float32)
    inv = (500000.0 ** (-np.arange(0, 16, 2, dtype=np.float32) / 16.0)).astype(np.float32)
    for p in range(128):
        if p % 64 < 16:
            c[p, C_RC] = inv[p % 8] / np.float32(2.0 * math.pi)
    c[:, C_RC + 1] = 1e-6
    c[:, C_RC + 2] = 1e-5
    return c


def _pack_params(inp):
    f = lambda a: np.ascontiguousarray(np.asarray(a, np.float32)).reshape(-1, 128)
    rows = np.concatenate([
        f(inp["mix_norm_pre"]), f(inp["mix_norm_post"]), f(inp["ffn_norm_pre"]), f(inp["ffn_norm_post"]),
        f(inp["sc_w_conv"]), f(inp["cc_b_pw1"]), f(inp["cc_w_dw"]), f(inp["cc_b_dw"]),
        f(inp["cc_ln_g"]), f(inp["cc_ln_b"]), f(inp["cc_b_pw2"]), f(inp["ffn_w_conv"])], axis=0)
    assert rows.shape[0] == 976
    out = np.zeros((1024, 128), np.float32)
    out[:976] = rows
    return out


def _layer_weights(inp, l):
    kind, j = l % 3, l // 3
    a = lambda k, i: np.ascontiguousarray(np.asarray(inp[k][i], np.float32))
    d = {}
    if kind == 0:
        d[f"w{l}_min"] = a("attn_w_qkv", j)
        d[f"w{l}_mout"] = a("attn_w_o", j)
    elif kind == 1:
        d[f"w{l}_min"] = a("sc_w_in", j)
        d[f"w{l}_mout"] = a("sc_w_out", j)
    else:
        d[f"w{l}_min"] = a("cc_w_pw1", j)
        d[f"w{l}_mout"] = a("cc_w_pw2", j)
    d[f"w{l}_fin"] = a("ffn_w_in", l)
    d[f"w{l}_fout"] = a("ffn_w_out", l)
    return d


LAUNCH_GROUPS = [[0, 1, 2, 3]]
_NC_CACHE = {}


def kernel(**inp):
    x = np.ascontiguousarray(np.asarray(inp["x"], np.float32))
    positions = np.ascontiguousarray(np.asarray(inp["positions"], np.int32))
    pth = _pack_params(inp)
    cst = _consts()
    cur = [x[c * SEQ_PER_CORE:(c + 1) * SEQ_PER_CORE].reshape(SEQ_PER_CORE * S, D) for c in range(NCORES)]
    for grp in LAUNCH_GROUPS:
        key = tuple(grp)
        if key not in _NC_CACHE:
            _NC_CACHE[key] = build(list(grp), SEQ_PER_CORE)
        nc = _NC_CACHE[key]
        wd = {}
        for l in grp:
            wd.update(_layer_weights(inp, l))
        in_maps = []
        for c in range(NCORES):
            m = {"x": cur[c], "pos": positions[c * SEQ_PER_CORE:(c + 1) * SEQ_PER_CORE], "pth": pth, "cst": cst}
            m.update(wd)
            in_maps.append(m)
        res = run_bass_kernel_spmd(nc, in_maps, core_ids=list(range(NCORES)))
        cur = [np.asarray(r["y"], np.float32) for r in res.results]
    out = np.stack([c.reshape(SEQ_PER_CORE, S, D) for c in cur], axis=0).reshape(NCORES * SEQ_PER_CORE, S, D)
    return out
```

```python
import contextlib
import math
import numpy as np
import concourse.bass as bass
import concourse.mybir as mybir
from concourse.bass_utils import run_bass_kernel_spmd

F32 = mybir.dt.float32
BF16 = mybir.dt.bfloat16
I32 = mybir.dt.int32
AF = mybir.ActivationFunctionType
ALU = mybir.AluOpType

S = 2048
D = 1024
FF = 2816
NCORES = 8
SEQ_PER_CORE = 4
DEPTH = 4
GROUPS = ((128, 1), (512, 4), (2048, 16))

R_MIXPRE, R_MIXPOST, R_FFNPRE, R_FFNPOST = 0, 32, 64, 96
R_SCCONV = 128
R_CCBPW1 = 152
R_CCWDW = 168
R_CCBDW, R_CCLNG, R_CCLNB, R_CCBPW2 = 416, 424, 432, 440
R_FFNCONV = 448
C_ID, C_SMAT, C_MASK, C_RC = 0, 128, 256, 512
NCST = 640

SEM_CAP = 30000
ENGS = ("pe", "act", "dve", "pool", "sp")


class T:
    __slots__ = ("w", "r")

    def __init__(self):
        self.w = None
        self.r = {}


class TP(T):
    __slots__ = ()


class Prog:
    def __init__(self, nc, stack):
        self.nc = nc
        self.stack = stack
        self.q = {e: [] for e in ENGS}
        self.cnt = {}
        self.seen = {e: {} for e in ENGS}
        self.sems = {}

    def _sem(self, chan, idx):
        lst = self.sems.setdefault(chan, [])
        while len(lst) <= idx:
            lst.append(self.stack.enter_context(self.nc.semaphore(f"s_{chan}_{len(lst)}")))
        return lst[idx]

    def _ev(self, chan, c, step):
        per = SEM_CAP // step
        return self._sem(chan, (c - 1) // per), ((c - 1) % per + 1) * step

    def _deps(self, eng, reads, writes, skip=None):
        deps = {}

        def add(ev):
            if ev is None:
                return
            ch, c = ev
            if (ch == "pe" and eng == "pe") or ch == skip:
                return
            if ch not in ENGS:
                c = self.cnt[ch]
            if deps.get(ch, 0) < c:
                deps[ch] = c
        for t in reads:
            add(t.w)
        for t in writes:
            add(t.w)
            for ch, c in t.r.items():
                add((ch, c))
        out = []
        for ch, c in deps.items():
            if self.seen[eng].get(ch, 0) < c:
                self.seen[eng][ch] = c
                out.append((ch, c))
        return out

    def _mark(self, ev, reads, writes):
        ch, c = ev
        for t in writes:
            t.w = ev
            t.r = {}
        for t in reads:
            if t.r.get(ch, 0) < c:
                t.r[ch] = c

    def op(self, eng, fn, reads=(), writes=()):
        if any(isinstance(t, TP) for t in reads):
            writes = list(writes) + [t for t in reads if isinstance(t, TP)]
            reads = [t for t in reads if not isinstance(t, TP)]
        waits = self._deps(eng, reads, writes)
        c = self.cnt.get(eng, 0) + 1
        self.cnt[eng] = c
        self.q[eng].append((waits, fn, (eng, c, 1)))
        self._mark((eng, c), reads, writes)

    def dma(self, eng, chan, fn, reads=(), writes=()):
        waits = self._deps(eng, reads, writes, skip=chan)
        c = self.cnt.get(chan, 0) + 1
        self.cnt[chan] = c
        self.q[eng].append((waits, fn, (chan, c, 16)))
        self._mark((chan, c), reads, writes)

    def barrier(self):
        for e in ENGS:
            waits = []
            for ch, c in self.cnt.items():
                if ch == e or ch.startswith("w"):
                    continue
                if self.seen[e].get(ch, 0) < c:
                    self.seen[e][ch] = c
                    waits.append((ch, c))
            if waits:
                self.q[e].append((waits, None, None))

    def wait_all(self, eng, tiles):
        waits = self._deps(eng, tiles, ())
        self.q[eng].append((waits, None, None))

    def emit(self, block):
        handles = {"pe": block.tensor, "act": block.scalar, "dve": block.vector,
                   "pool": block.gpsimd, "sp": block.sync}
        for e in ENGS:
            q = self.q[e]
            if not q:
                continue

            def body(h, q=q):
                for waits, fn, inc in q:
                    for ch, c in waits:
                        s, v = self._ev(ch, c, 1 if ch in ENGS else 16)
                        h.wait_ge(s, v)
                    if fn is None:
                        continue
                    ins = None
                    for name, kw in (fn if isinstance(fn, list) else [fn]):
                        ins = getattr(h, name)(**kw)
                    ch, c, step = inc
                    s, v = self._ev(ch, c, step)
                    ins.then_inc(s, step)
            handles[e](body)


def _wkeys(layers, nseq, do_mixer, do_ffn):
    for _ in range(nseq):
        for l in layers:
            kind = l % 3
            if do_mixer:
                if kind == 0:
                    for p in range(8):
                        for g in range(3):
                            yield ("qkv", l, p, g)
                elif kind == 1:
                    for i in range(8):
                        yield ("scin", l, i)
                else:
                    for i in range(8):
                        yield ("pw1", l, i)
                for tt in range(4):
                    for hh in range(2):
                        yield ("mo", l, tt, hh)
            if do_ffn:
                for hf in range(2):
                    for j in range(11):
                        yield ("fin", l, hf, j)
                    for tq in range(2):
                        for o in range(8):
                            yield ("fout", l, hf, tq, o)


def I(name, **kw):
    return (name, kw)


def build(layers, nseq, do_mixer=True, do_ffn=True):
    nc = bass.Bass("TRN2", target_bir_lowering=False)
    dt = nc.dram_tensor
    x = dt("x", [nseq * S, D], F32, kind="ExternalInput").ap()
    y = dt("y", [nseq * S, D], F32, kind="ExternalOutput").ap()
    pos = dt("pos", [nseq, S], I32, kind="ExternalInput").ap()
    pth = dt("pth", [1024, 128], F32, kind="ExternalInput").ap()
    cst = dt("cst", [128, NCST], F32, kind="ExternalInput").ap()
    W = {}
    has_attn = False
    for l in layers:
        kind = l % 3
        if kind == 0:
            has_attn = True
            W[l, "min"] = dt(f"w{l}_min", [D, 9216], F32, kind="ExternalInput").ap()
        elif kind == 1:
            W[l, "min"] = dt(f"w{l}_min", [D, 3072], F32, kind="ExternalInput").ap()
        else:
            W[l, "min"] = dt(f"w{l}_min", [D, 2048], F32, kind="ExternalInput").ap()
        W[l, "mout"] = dt(f"w{l}_mout", [D, D], F32, kind="ExternalInput").ap()
        W[l, "fin"] = dt(f"w{l}_fin", [D, 2 * FF], F32, kind="ExternalInput").ap()
        W[l, "fout"] = dt(f"w{l}_fout", [FF, D], F32, kind="ExternalInput").ap()
    hpark = dt("hpark", [128, 8 * S], F32, kind="Internal").ap() if has_attn else None

    with contextlib.ExitStack() as st:
        ARENA_BYTES = 212480
        arena = st.enter_context(nc.sbuf_tensor("arena", [128, ARENA_BYTES // 4], F32))
        arena_bf = arena.bitcast(BF16)
        arena_i = arena.bitcast(I32)
        pb = [st.enter_context(nc.psum_tensor(f"pb{i}", [128, 512], F32)) for i in range(8)]
        pb_bf = [b.bitcast(BF16) for b in pb]
        block = st.enter_context(nc.Block())
        P = Prog(nc, st)

        def carve(off, n, dtype=F32):
            if dtype == BF16:
                assert off % 2 == 0
                return arena_bf[:, off // 2: off // 2 + n]
            assert off % 4 == 0
            v = arena if dtype == F32 else arena_i
            return v[:, off // 4: off // 4 + n]

        tiles = {}

        def t(*key):
            r = tiles.get(key)
            if r is None:
                r = tiles[key] = T()
            return r

        ptiles = [TP() for _ in range(8)]

        def pbt(i, q0=0, q1=4):
            return [ptiles[i]]

        PTAB = carve(0, 1024)
        IDF = carve(4096, 128)
        ONESB = carve(4608, 128, BF16)
        IDB = carve(4864, 128, BF16)
        SMAT = carve(5120, 128, BF16)
        MASK2 = carve(5376, 256, BF16)
        MASK = MASK2.rearrange("p (j q) -> p j q", j=2)
        RC = carve(5888, 8)
        ONESF = carve(6144, 128)
        BG = carve(6656, 8)
        WS0 = 7168
        NW = 3
        wslots = [carve(WS0 + i * 8192, 4096, BF16) for i in range(NW)]
        A_OFF = WS0 + NW * 8192
        A = carve(A_OFF, 8 * S, BF16).rearrange("p (c t) -> p c t", c=8)
        H_OFF = A_OFF + 32768
        H = carve(H_OFF, 8 * S).rearrange("p (c t) -> p c t", c=8)
        PH = H_OFF + 65536
        assert ARENA_BYTES - PH >= 79904
        EPS_RMS = RC[:, 1:2]
        EPS_LN = RC[:, 2:3]

        def col(r):
            return PTAB[:, r:r + 1]

        def tA(c, tt):
            return t("A", c, tt)

        def tH(c, tt):
            return t("H", c, tt)

        keygen = _wkeys(layers, nseq, do_mixer, do_ffn)
        WS = {"pending": [], "nxt": 0}
        wT = [T() for _ in range(NW)]

        def w_issue():
            key = next(keygen, None)
            if key is None:
                return
            s = WS["nxt"]
            WS["nxt"] = (s + 1) % NW
            slot = wslots[s]
            kind, l = key[0], key[1]
            dmas = []
            if kind == "fin":
                j = key[3]
                v = slot.rearrange("p (c s n) -> p c s n", c=8, s=2)
                src = W[l, "fin"].rearrange("(c p) n -> p c n", p=128)
                for h in range(2):
                    dmas.append((v[:, :, h, :], src[:, :, h * FF + j * 256: h * FF + (j + 1) * 256]))
            elif kind == "fout":
                o = key[4]
                v = slot[:, 0:2816].rearrange("p (k n) -> p k n", k=22)
                src = W[l, "fout"].rearrange("(k p) n -> p k n", p=128)
                dmas.append((v, src[:, :, o * 128:(o + 1) * 128]))
            elif kind in ("qkv", "scin", "pw1"):
                ns = 2 if kind == "pw1" else 3
                v = slot[:, 0:ns * 1024].rearrange("p (c s n) -> p c s n", c=8, s=ns)
                src = W[l, "min"].rearrange("(c p) n -> p c n", p=128)
                for s_ in range(ns):
                    if kind == "qkv":
                        cl = s_ * 3072 + key[3] * 1024 + key[2] * 128
                    else:
                        cl = s_ * 1024 + key[2] * 128
                    dmas.append((v[:, :, s_, :], src[:, :, cl:cl + 128]))
            elif kind == "mo":
                hh = key[3]
                v = slot.rearrange("p (c n) -> p c n", c=8)
                src = W[l, "mout"].rearrange("(c p) n -> p c n", p=128)
                dmas.append((v, src[:, :, hh * 512:(hh + 1) * 512]))
            else:
                raise AssertionError(key)
            for (o_, i_) in dmas:
                P.dma("pool", f"w{s}", I("dma_start", out=o_, in_=i_), writes=[wT[s]])
            WS["pending"].append((key, s))

        def w_acquire(key):
            k, s = WS["pending"].pop(0)
            assert k == key, (k, key)
            return wslots[s], wT[s]

        def w_release():
            w_issue()

        def mm(out, pairs, reads, writes):
            n = len(pairs)
            P.op("pe", [I("matmul", out=out, lhsT=l_, rhs=r_, start=(i == 0), stop=(i == n - 1))
                        for i, (l_, r_) in enumerate(pairs)], reads, writes)

        ring = {"mm": [0, 1, 2, 3], "i": 0}

        def next_bank():
            b = ring["mm"][ring["i"] % len(ring["mm"])]
            ring["i"] += 1
            return b

        flip = {"n": 0}

        def alt_engine():
            flip["n"] += 1
            return "act" if flip["n"] % 2 else "dve"

        def copy_op(eng, out, in_, reads, writes):
            if eng == "act":
                P.op("act", I("activation", out=out, in_=in_, func=AF.Copy), reads, writes)
            else:
                P.op(eng, I("tensor_copy", out=out, in_=in_), reads, writes)

        def rstd_inplace(bank, eps_ap):
            bt = pbt(bank)
            P.op("act", I("activation", out=pb[bank][:], in_=pb[bank][:], func=AF.Ln, bias=eps_ap, scale=1.0 / D),
                 bt + [t("cst")], bt)
            P.op("act", I("activation", out=pb[bank][:], in_=pb[bank][:], func=AF.Exp, scale=-0.5), bt, bt)

        def prenorm(row, sq_off):
            sqs = [carve(sq_off + i * 8192, 4096, BF16).rearrange("p (c n) -> p c n", c=8) for i in range(2)]

            def square(tt):
                sl = slice(tt * 512, (tt + 1) * 512)
                P.op("act", I("activation", out=sqs[tt % 2], in_=H[:, :, sl], func=AF.Square),
                     [tH(c, tt) for c in range(8)], [t("sqpre", tt % 2)])
            square(0)
            square(1)
            for tt in range(4):
                sq = sqs[tt % 2]
                tsq = t("sqpre", tt % 2)
                sl = slice(tt * 512, (tt + 1) * 512)
                bank = 6 + tt % 2
                mm(pb[bank][:], [(ONESB, sq[:, c, :]) for c in range(8)], [tsq, t("cst")], pbt(bank))
                rstd_inplace(bank, EPS_RMS)
                if tt + 2 < 4:
                    square(tt + 2)
                for c in range(8):
                    P.op("dve", I("scalar_tensor_tensor", out=A[:, c, sl], in0=H[:, c, sl], scalar=col(row + c),
                                  in1=pb[bank][:], op0=ALU.mult, op1=ALU.mult),
                         [tH(c, tt), t("ptab")] + pbt(bank), [tA(c, tt)])

        Post = {}

        def post_setup(mb_off, sq_off, mb_off2=None):
            offs = [mb_off, mb_off if mb_off2 is None else mb_off2]
            Post["Mb"] = [carve(o_, 4096).rearrange("p (o n) -> p o n", o=8) for o_ in offs]
            Post["two"] = mb_off2 is not None
            Post["sq"] = [carve(sq_off + i * 1024, 512, BF16) for i in range(2)]
            Post["n"] = 0

        def post_evac(o, bank, tt, grow, brow=None):
            mi = tt % 2 if Post["two"] else 0
            Mb = Post["Mb"][mi]
            sqb = Post["sq"][Post["n"] % 2]
            tsq = t("sqpost", Post["n"] % 2)
            Post["n"] += 1
            sbank = 6 + tt % 2
            bt = pbt(bank)
            if brow is None:
                P.op("act", I("activation", out=sqb, in_=pb[bank][:], func=AF.Square), bt, [tsq])
                P.op("act", I("activation", out=Mb[:, o, :], in_=pb[bank][:], func=AF.Copy, scale=col(grow + o)),
                     bt + [t("ptab")], [t("Mb", mi, o)])
            else:
                P.op("act", I("activation", out=sqb, in_=pb[bank][:], func=AF.Square, bias=col(brow + o)),
                     bt + [t("ptab")], [tsq])
                P.op("act", I("activation", out=Mb[:, o, :], in_=pb[bank][:], func=AF.Identity,
                              scale=col(grow + o), bias=BG[:, o:o + 1]),
                     bt + [t("ptab"), t("bg")], [t("Mb", mi, o)])
            P.op("pe", I("matmul", out=pb[sbank][:], lhsT=ONESB, rhs=sqb, start=(o == 0), stop=(o == 7)),
                 [tsq, t("cst")], pbt(sbank))

        def post_finish(tt):
            mi = tt % 2 if Post["two"] else 0
            Mb = Post["Mb"][mi]
            sbank = 6 + tt % 2
            rstd_inplace(sbank, EPS_RMS)
            sl = slice(tt * 512, (tt + 1) * 512)
            mbt = [t("Mb", mi, o) for o in range(8)]
            P.op("dve", I("tensor_tensor", out=Mb, in0=Mb,
                          in1=pb[sbank][:].unsqueeze(1).broadcast_to([128, 8, 512]), op=ALU.mult),
                 mbt + pbt(sbank), mbt)
            hts = [tH(c, tt) for c in range(8)]
            P.op("pool", I("tensor_tensor", out=H[:, :, sl], in0=H[:, :, sl], in1=Mb, op=ALU.add), mbt + hts, hts)

        def mixer_out(l, Y, ytile, brow=None):
            grow = R_MIXPOST + l * 8
            for tt in range(4):
                sl = slice(tt * 512, (tt + 1) * 512)
                for hh in range(2):
                    slot, wt = w_acquire(("mo", l, tt, hh))
                    v = slot.rearrange("p (c n) -> p c n", c=8)
                    for o4 in range(4):
                        o = hh * 4 + o4
                        bank = next_bank()
                        mm(pb[bank][:], [(v[:, c, o4 * 128:(o4 + 1) * 128], Y[:, c, sl]) for c in range(8)],
                           [wt] + [ytile(c, tt) for c in range(8)], pbt(bank))
                        post_evac(o, bank, tt, grow, brow)
                    w_release()
                post_finish(tt)

        tc_ = t("cst")
        P.dma("sp", "cst", I("dma_start", out=IDF, in_=cst[:, C_ID:C_ID + 128]), writes=[tc_])
        P.dma("sp", "cst", I("dma_start", out=RC, in_=cst[:, C_RC:C_RC + 8]), writes=[tc_])
        P.dma("pool", "cst", I("dma_start", out=IDB, in_=cst[:, C_ID:C_ID + 128]), writes=[tc_])
        P.dma("pool", "cst", I("dma_start", out=SMAT, in_=cst[:, C_SMAT:C_SMAT + 128]), writes=[tc_])
        P.dma("pool", "cst", I("dma_start", out=MASK2, in_=cst[:, C_MASK:C_MASK + 256]), writes=[tc_])
        P.op("dve", I("memset", ap=ONESB, constant=1.0), [], [t("ones")])
        P.op("dve", I("memset", ap=ONESF, constant=1.0), [], [t("ones")])
        pst = carve(PH, 1024).rearrange("p (k f) -> p k f", k=8)
        P.dma("sp", "ld0", I("dma_start", out=pst, in_=pth.rearrange("(k r) f -> r k f", r=128)), writes=[t("pst")])
        for hb in range(2):
            P.op("pe", [I("transpose", out=pb[hb][:, q * 128:(q + 1) * 128], in_=pst[:, hb * 4 + q, :], identity=IDF)
                        for q in range(4)], [t("pst"), tc_], pbt(hb))
            P.op("act", I("activation", out=PTAB[:, hb * 512:(hb + 1) * 512], in_=pb[hb][:], func=AF.Copy),
                 pbt(hb), [t("ptab")])
        for _ in range(NW):
            w_issue()
        P.barrier()

        def load_x(sq_i):
            stg = [carve(PH + i * 4096, 1024) for i in range(2)]
            for j in range(16):
                sb_ = stg[j % 2]
                ts_ = t("stin", j % 2)
                row0 = (sq_i * 16 + j) * 128
                P.dma("sp", f"ld{j % 2}", I("dma_start", out=sb_, in_=x[row0:row0 + 128, :]), writes=[ts_])
                for hb in range(2):
                    bank = next_bank()
                    P.op("pe", [I("transpose", out=pb[bank][:, q * 128:(q + 1) * 128],
                                  in_=sb_[:, (hb * 4 + q) * 128:(hb * 4 + q + 1) * 128], identity=IDF)
                                for q in range(4)], [ts_, tc_], pbt(bank))
                    copy_op(alt_engine(), H[:, hb * 4:hb * 4 + 4, j * 128:(j + 1) * 128],
                            pb[bank][:].rearrange("p (c n) -> p c n", c=4), pbt(bank),
                            [tH(c, j // 4) for c in range(hb * 4, hb * 4 + 4)])

        def store_y(sq_i):
            stg = [carve(PH + 8192 + i * 4096, 1024) for i in range(2)]
            outs = []
            for j in range(16):
                sb_ = stg[j % 2]
                ts_ = t("stout", j % 2)
                for hb in range(2):
                    bank = next_bank()
                    P.op("pe", [I("transpose", out=pb[bank][:, q * 128:(q + 1) * 128],
                                  in_=H[:, hb * 4 + q, j * 128:(j + 1) * 128], identity=IDF) for q in range(4)],
                         [tH(c, j // 4) for c in range(hb * 4, hb * 4 + 4)] + [tc_], pbt(bank))
                    copy_op(alt_engine(), sb_[:, hb * 512:(hb + 1) * 512], pb[bank][:], pbt(bank), [ts_])
                row0 = (sq_i * 16 + j) * 128
                ty = t("yout", sq_i, j)
                P.dma("sp", f"st{j % 2}", I("dma_start", out=y[row0:row0 + 128, :], in_=sb_), reads=[ts_], writes=[ty])
                outs.append(ty)
            return outs

        def ffn(l):
            ring["mm"] = [0, 1, 2, 3, 4, 5]
            Z = carve(PH, 22 * 1024, BF16).rearrange("p (k n) -> p k n", k=22)
            ACC = [[carve(PH + 45056 + (kd * 2 + b) * 4112, 1026) for b in range(2)] for kd in range(2)]
            SG = [carve(PH + 61504 + b * 4096, 1024) for b in range(2)]
            HALO = carve(PH + 80000, 88).rearrange("p (f n) -> p f n", f=44)
            prenorm(R_FFNPRE + l * 8, PH)
            P.barrier()
            npair = 0
            for hf in range(2):
                for j in range(11):
                    slot, wt = w_acquire(("fin", l, hf, j))
                    v = slot.rearrange("p (c s n) -> p c s n", c=8, s=2)
                    for ii in range(2):
                        i = 2 * j + ii
                        buf = npair % 2
                        npair += 1
                        for kd in range(2):
                            ft = kd * 22 + i
                            acc = ACC[kd][buf]
                            tacc = t("acc", kd, buf)
                            r0 = R_FFNCONV + l * 132 + ft
                            r1 = r0 + 44
                            r2 = r0 + 88
                            th = t("halo", ft)
                            P.op("pool", I("memset", ap=acc[:, 1024:1026], constant=0.0), [], [tacc])
                            bks = []
                            for tq in range(2):
                                tt = hf * 2 + tq
                                sl = slice(tt * 512, (tt + 1) * 512)
                                bank = next_bank()
                                bks.append(bank)
                                mm(pb[bank][:], [(v[:, c, kd, ii * 128:(ii + 1) * 128], A[:, c, sl]) for c in range(8)],
                                   [wt] + [tA(c, tt) for c in range(8)], pbt(bank))
                                P.op("act", I("activation", out=acc[:, tq * 512:(tq + 1) * 512], in_=pb[bank][:],
                                              func=AF.Copy, scale=col(r2)), pbt(bank) + [t("ptab")], [tacc])
                            for tq in range(2):
                                bank = bks[tq]
                                a0 = tq * 512
                                P.op("dve", I("scalar_tensor_tensor", out=acc[:, a0 + 1:a0 + 513], in0=pb[bank][:],
                                              scalar=col(r1), in1=acc[:, a0 + 1:a0 + 513], op0=ALU.mult, op1=ALU.add),
                                     pbt(bank) + [tacc, t("ptab")], [tacc])
                                P.op("dve", I("scalar_tensor_tensor", out=acc[:, a0 + 2:a0 + 514], in0=pb[bank][:],
                                              scalar=col(r0), in1=acc[:, a0 + 2:a0 + 514], op0=ALU.mult, op1=ALU.add),
                                     pbt(bank) + [tacc, t("ptab")], [tacc])
                            if hf == 1:
                                P.op("dve", I("tensor_tensor", out=acc[:, 0:2], in0=acc[:, 0:2], in1=HALO[:, ft, :],
                                              op=ALU.add), [tacc, th], [tacc])
                            else:
                                P.op("pool", I("tensor_copy", out=HALO[:, ft, :], in_=acc[:, 1024:1026]), [tacc], [th])
                        sg = SG[buf]
                        tsg = t("sg", buf)
                        P.op("act", I("activation", out=sg, in_=ACC[0][buf][:, 0:1024], func=AF.Silu),
                             [t("acc", 0, buf)], [tsg])
                        P.op("pool", I("tensor_tensor", out=Z[:, i, :], in0=sg, in1=ACC[1][buf][:, 0:1024], op=ALU.mult),
                             [tsg, t("acc", 1, buf)], [t("Z", i)])
                    w_release()
                P.barrier()
                post_setup(PH + 45056, PH + 61504, PH + 63552)
                for tq in range(2):
                    tt = hf * 2 + tq
                    for o in range(8):
                        slot, wt = w_acquire(("fout", l, hf, tq, o))
                        v = slot[:, 0:2816].rearrange("p (k n) -> p k n", k=22)
                        bank = next_bank()
                        mm(pb[bank][:], [(v[:, k, :], Z[:, k, tq * 512:(tq + 1) * 512]) for k in range(22)],
                           [wt] + [t("Z", k) for k in range(22)], pbt(bank))
                        post_evac(o, bank, tt, R_FFNPOST + l * 8)
                        w_release()
                    post_finish(tt)
                P.barrier()

        def sc_mixer(l):
            ring["mm"] = [0, 1, 2, 3, 4, 5]
            Y = carve(PH, 8 * S, BF16).rearrange("p (c t) -> p c t", c=8)
            CH = carve(PH + 32768, 2056)
            BSB = carve(PH + 40992, 2048)
            HSB = [carve(PH + 49184 + b * 2048, 512) for b in range(2)]
            ACC = carve(PH + 53280, 2048)
            prenorm(R_MIXPRE + l * 8, PH)
            P.barrier()
            P.op("dve", I("memset", ap=CH[:, 0:2], constant=0.0), [], [t("ch")])
            nh = 0
            for i in range(8):
                slot, wt = w_acquire(("scin", l, i))
                v = slot[:, 0:3072].rearrange("p (c s n) -> p c s n", c=8, s=3)
                for tt in range(4):
                    sl = slice(tt * 512, (tt + 1) * 512)
                    banks = [next_bank() for _ in range(3)]
                    for s_ in range(3):
                        mm(pb[banks[s_]][:], [(v[:, c, s_, :], A[:, c, sl]) for c in range(8)],
                           [wt] + [tA(c, tt) for c in range(8)], pbt(banks[s_]))
                    hs = HSB[nh % 2]
                    ths = t("hsb", nh % 2)
                    nh += 1
                    P.op("act", I("activation", out=hs, in_=pb[banks[2]][:], func=AF.Copy), pbt(banks[2]), [ths])
                    P.op("dve", I("tensor_tensor", out=CH[:, 2 + tt * 512:2 + (tt + 1) * 512], in0=pb[banks[1]][:],
                                  in1=hs, op=ALU.mult), pbt(banks[1]) + [ths], [t("ch")])
                    P.op("act", I("activation", out=BSB[:, sl], in_=pb[banks[0]][:], func=AF.Copy),
                         pbt(banks[0]), [t("bsb")])
                w_release()
                r0 = R_SCCONV + i
                P.op("act", I("activation", out=ACC, in_=CH[:, 2:2050], func=AF.Copy, scale=col(r0 + 16)),
                     [t("ch"), t("ptab")], [t("scacc")])
                P.op("dve", I("scalar_tensor_tensor", out=ACC, in0=CH[:, 1:2049], scalar=col(r0 + 8), in1=ACC,
                              op0=ALU.mult, op1=ALU.add), [t("ch"), t("scacc"), t("ptab")], [t("scacc")])
                P.op("dve", I("scalar_tensor_tensor", out=ACC, in0=CH[:, 0:2048], scalar=col(r0), in1=ACC,
                              op0=ALU.mult, op1=ALU.add), [t("ch"), t("scacc"), t("ptab")], [t("scacc")])
                P.op("pool", I("tensor_tensor", out=Y[:, i, :], in0=BSB, in1=ACC, op=ALU.mult),
                     [t("bsb"), t("scacc")], [t("Y", i)])
            P.barrier()
            post_setup(PH + 61472, PH + 77856, PH + 32768)
            mixer_out(l, Y, lambda c, tt: t("Y", c))
            P.barrier()

        def cc_mixer(l):
            ring["mm"] = [0, 1, 2, 3]
            X = carve(PH, 8 * S).rearrange("p (c t) -> p c t", c=8)
            GLU = carve(PH + 65536, 2080)
            SIG = [carve(PH + 73856 + b * 2048, 512) for b in range(2)]
            Yc = A
            prenorm(R_MIXPRE + l * 8, PH)
            P.barrier()
            P.op("dve", I("memset", ap=GLU[:, 0:30], constant=0.0), [], [t("glu")])
            P.op("dve", I("tensor_tensor", out=BG, in0=PTAB[:, R_CCBPW2:R_CCBPW2 + 8],
                          in1=PTAB[:, R_MIXPOST + l * 8:R_MIXPOST + l * 8 + 8], op=ALU.mult), [t("ptab")], [t("bg")])
            ns = 0
            for i in range(8):
                slot, wt = w_acquire(("pw1", l, i))
                v = slot[:, 0:2048].rearrange("p (c s n) -> p c s n", c=8, s=2)
                for tt in range(4):
                    sl = slice(tt * 512, (tt + 1) * 512)
                    ba, bg_ = next_bank(), next_bank()
                    mm(pb[ba][:], [(v[:, c, 0, :], A[:, c, sl]) for c in range(8)],
                       [wt] + [tA(c, tt) for c in range(8)], pbt(ba))
                    mm(pb[bg_][:], [(v[:, c, 1, :], A[:, c, sl]) for c in range(8)],
                       [wt] + [tA(c, tt) for c in range(8)], pbt(bg_))
                    sg = SIG[ns % 2]
                    tsg = t("sig", ns % 2)
                    ns += 1
                    P.op("act", I("activation", out=sg, in_=pb[bg_][:], func=AF.Sigmoid, bias=col(R_CCBPW1 + 8 + i)),
                         pbt(bg_) + [t("ptab")], [tsg])
                    P.op("dve", I("scalar_tensor_tensor", out=GLU[:, 30 + tt * 512:30 + (tt + 1) * 512], in0=pb[ba][:],
                                  scalar=col(R_CCBPW1 + i), in1=sg, op0=ALU.add, op1=ALU.mult),
                         pbt(ba) + [tsg, t("ptab")], [t("glu")])
                w_release()
                tx = [t("X", i, tt) for tt in range(4)]
                P.op("act", I("activation", out=X[:, i, :], in_=GLU[:, 30:30 + S], func=AF.Identity,
                              scale=col(R_CCWDW + 240 + i), bias=col(R_CCBDW + i)), [t("glu"), t("ptab")], tx)
                for k in range(30):
                    P.op("dve", I("scalar_tensor_tensor", out=X[:, i, :], in0=GLU[:, k:k + S],
                                  scalar=col(R_CCWDW + k * 8 + i), in1=X[:, i, :], op0=ALU.mult, op1=ALU.add),
                         [t("glu"), t("ptab")] + tx, tx)
            P.barrier()
            SQ = carve(PH + 65536, 4096, BF16).rearrange("p (c n) -> p c n", c=8)
            MEAN = carve(PH + 73728, 512)
            VAR = carve(PH + 75776, 512)
            for tt in range(4):
                sl = slice(tt * 512, (tt + 1) * 512)
                txs = [t("X", c, tt) for c in range(8)]
                P.op("act", I("activation", out=SQ, in_=X[:, :, sl], func=AF.Square), txs, [t("lnsq")])
                b1, b2 = next_bank(), next_bank()
                mm(pb[b1][:], [(ONESF, X[:, c, sl]) for c in range(8)], txs + [t("ones")], pbt(b1))
                mm(pb[b2][:], [(ONESB, SQ[:, c, :]) for c in range(8)], [t("lnsq"), t("ones")], pbt(b2))
                P.op("dve", I("tensor_scalar", out=MEAN, in0=pb[b1][:], scalar1=1.0 / D, scalar2=None, op0=ALU.mult),
                     pbt(b1), [t("mean")])
                P.op("dve", I("tensor_tensor", out=VAR, in0=MEAN, in1=MEAN, op=ALU.mult), [t("mean")], [t("var")])
                P.op("dve", I("scalar_tensor_tensor", out=VAR, in0=pb[b2][:], scalar=1.0 / D, in1=VAR,
                              op0=ALU.mult, op1=ALU.subtract), pbt(b2) + [t("var")], [t("var")])
                P.op("act", I("activation", out=VAR, in_=VAR, func=AF.Ln, bias=EPS_LN, scale=1.0),
                     [t("var"), tc_], [t("var")])
                P.op("act", I("activation", out=VAR, in_=VAR, func=AF.Exp, scale=-0.5), [t("var")], [t("var")])
                P.op("dve", I("tensor_tensor", out=X[:, :, sl], in0=X[:, :, sl],
                              in1=MEAN.unsqueeze(1).broadcast_to([128, 8, 512]), op=ALU.subtract),
                     txs + [t("mean")], txs)
                P.op("dve", I("tensor_tensor", out=X[:, :, sl], in0=X[:, :, sl],
                              in1=VAR.unsqueeze(1).broadcast_to([128, 8, 512]), op=ALU.mult), txs + [t("var")], txs)
                for c in range(8):
                    P.op("act", I("activation", out=Yc[:, c, sl], in_=X[:, c, sl], func=AF.Silu,
                                  scale=col(R_CCLNG + c), bias=col(R_CCLNB + c)), [t("X", c, tt), t("ptab")], [tA(c, tt)])
            P.barrier()
            post_setup(PH, PH + 16384, PH + 18432)
            mixer_out(l, Yc, tA, brow=R_CCBPW2)
            P.barrier()

        def attn_mixer(l, sq_i):
            ring["mm"] = [0, 1, 2]
            HB = H_OFF
            QKV = [[carve(HB + s_ * 12288 + k * 4096, S, BF16) for k in range(3)] for s_ in range(2)]
            VAB = [[carve(HB + 24576 + s_ * 8192 + k * 4096, 2048, BF16).rearrange("p (b n) -> p b n", b=16)
                    for k in range(2)] for s_ in range(2)]
            OT = carve(PH, 8 * S, BF16).rearrange("p (c t) -> p c t", c=8)
            COS = carve(PH + 32768, S)
            SIN = carve(PH + 40960, S)
            PTR = [carve(PH + 49152 + i * 512, 256, BF16).rearrange("p (j q) -> p j q", j=2) for i in range(4)]
            T12 = [[carve(PH + 51200 + (b * 2 + k) * 2048, 512) for k in range(2)] for b in range(2)]
            POSI = carve(PH + 59392, S, I32)
            XF = carve(PH + 67584, S)
            ZN = carve(PH + 59392, S)
            UAB = [[carve(HB + 40960, S), carve(HB + 49152, S)],
                   [carve(HB + 57344, S), carve(PH + 67584, S)]]
            for c in range(8):
                P.dma("sp", "park", I("dma_start", out=hpark[:, c * S:(c + 1) * S], in_=H[:, c, :]),
                      reads=[tH(c, tt) for tt in range(4)], writes=[t("hpark", c)])
            P.dma("sp", "pos", I("dma_start", out=POSI, in_=pos[sq_i:sq_i + 1, :].broadcast_to([128, S])),
                  writes=[t("posi")])
            P.op("dve", I("tensor_copy", out=XF, in_=POSI), [t("posi")], [t("xf")])
            P.op("dve", I("tensor_scalar", out=XF, in0=XF, scalar1=RC[:, 0:1], scalar2=None, op0=ALU.mult),
                 [t("xf"), tc_], [t("xf")])
            P.op("dve", I("tensor_copy", out=POSI, in_=XF), [t("xf")], [t("posi")])
            P.op("dve", I("tensor_copy", out=COS, in_=POSI), [t("posi")], [t("cos")])
            P.op("dve", I("tensor_tensor", out=XF, in0=XF, in1=COS, op=ALU.subtract), [t("xf"), t("cos")], [t("xf")])
            P.op("dve", I("scalar_tensor_tensor", out=SIN, in0=XF, scalar=0.5, in1=XF, op0=ALU.is_gt, op1=ALU.subtract),
                 [t("xf")], [t("sin")])
            P.op("dve", I("scalar_tensor_tensor", out=SIN, in0=SIN, scalar=0.5, in1=SIN, op0=ALU.is_gt, op1=ALU.subtract),
                 [t("sin")], [t("sin")])
            P.op("dve", I("tensor_scalar", out=XF, in0=XF, scalar1=0.25, scalar2=None, op0=ALU.add), [t("xf")], [t("xf")])
            P.op("dve", I("scalar_tensor_tensor", out=COS, in0=XF, scalar=0.5, in1=XF, op0=ALU.is_gt, op1=ALU.subtract),
                 [t("xf"), t("cos")], [t("cos")])
            P.op("dve", I("scalar_tensor_tensor", out=COS, in0=COS, scalar=0.5, in1=COS, op0=ALU.is_gt, op1=ALU.subtract),
                 [t("cos")], [t("cos")])
            TWO_PI = 2.0 * math.pi * (1.0 - 2e-7)
            P.op("act", I("activation", out=SIN, in_=SIN, func=AF.Sin, scale=TWO_PI), [t("sin")], [t("sin")])
            P.op("act", I("activation", out=COS, in_=COS, func=AF.Sin, scale=TWO_PI), [t("cos")], [t("cos")])
            prenorm(R_MIXPRE + l * 8, PH)
            P.barrier()
            for s_ in range(2):
                P.op("pool", I("memset", ap=VAB[s_][0][:, :, 64:128], constant=1.0), [], [t("va", s_)])
                P.op("pool", I("memset", ap=VAB[s_][1][:, :, 0:64], constant=1.0), [], [t("vb", s_)])
            cnt = {"rope": 0, "idx": 0}

            def proj_gen(p, g, set_):
                dil = GROUPS[g][1]
                nb = (S // dil) // 128
                VA, VB = VAB[set_]
                tq_ = [[t("qkv", set_, k, tt_) for tt_ in range(4)] for k in range(3)]
                slot, wt = w_acquire(("qkv", l, p, g))
                v = slot[:, 0:3072].rearrange("p (c s n) -> p c s n", c=8, s=3)
                pend = None
                for s3 in range(3):
                    dst = QKV[set_][s3]
                    for tt in range(4):
                        sl = slice(tt * 512, (tt + 1) * 512)
                        bank = next_bank()
                        bt = pbt(bank)
                        mm(pb[bank][:], [(v[:, c, s3, :], A[:, c, sl]) for c in range(8)],
                           [wt] + [tA(c, tt) for c in range(8)], bt)
                        if pend is not None:
                            pend()
                            pend = None
                        P.op("act", I("activation", out=dst[:, sl], in_=pb[bank][:], func=AF.Copy), bt, [tq_[s3][tt]])
                        if s3 < 2:
                            nr = cnt["rope"]
                            cnt["rope"] += 1
                            t1, t2 = T12[nr % 2]
                            tt1, tt2 = t("t1", nr % 2), t("t2", nr % 2)
                            P.op("dve", I("tensor_tensor", out=t1[0:80, :], in0=pb[bank][0:80, :], in1=COS[0:80, sl],
                                          op=ALU.mult), bt + [t("cos")], [tt1])

                            def rope_tail(dst=dst, sl=sl, t1=t1, t2=t2, tt1=tt1, tt2=tt2, tq=tq_[s3][tt]):
                                mm(pb[3][0:80, :], [(SMAT[:, 0:80], dst[:, sl])], [tq, tc_], pbt(3))
                                P.op("dve", I("tensor_tensor", out=t2[0:80, :], in0=pb[3][0:80, :], in1=SIN[0:80, sl],
                                              op=ALU.mult), pbt(3) + [t("sin")], [tt2])
                                P.op("pool", I("tensor_tensor", out=dst[0:80, sl], in0=t1[0:80, :], in1=t2[0:80, :],
                                               op=ALU.add), [tt1, tt2], [tq])
                            pend = rope_tail
                        yield
                if pend is not None:
                    pend()
                w_release()
                VTv = QKV[set_][2].rearrange("p (m d) -> p d m", d=dil)
                for h8 in range(2):
                    bank = next_bank()
                    ins = []
                    for b8 in range(8):
                        blk = h8 * 8 + b8
                        r, n = blk // nb, blk % nb
                        ins.append(I("transpose", out=pb_bf[bank][:, b8 * 128:(b8 + 1) * 128],
                                     in_=VTv[:, r, n * 128:(n + 1) * 128], identity=IDB))
                    P.op("pe", ins, tq_[2] + [tc_], pbt(bank))
                    src = pb_bf[bank][:].rearrange("p (b n) -> p b n", b=8)
                    P.op("dve", I("tensor_copy", out=VA[:, h8 * 8:h8 * 8 + 8, 0:64], in_=src[:, :, 0:64]),
                         pbt(bank), [t("va", set_)])
                    P.op("act", I("activation", out=VB[:, h8 * 8:h8 * 8 + 8, 64:128], in_=src[:, :, 64:128],
                                  func=AF.Copy), pbt(bank), [t("vb", set_)])
                    yield

            def core_gen(p, g, set_):
                dil = GROUPS[g][1]
                nb = (S // dil) // 128
                QT, KT, VT = QKV[set_]
                VA, VB = VAB[set_]
                U2 = UAB[p % 2]
                tq_ = [[t("qkv", set_, k, tt_) for tt_ in range(4)] for k in range(3)]
                QTv = QT.rearrange("p (m d) -> p d m", d=dil)
                KTv = KT.rearrange("p (m d) -> p d m", d=dil)
                items = [(r, n, hd) for r in range(dil) for n in range(nb) for hd in range(2)]
                st_ = {}

                def scores(it):
                    r, n, hd = it
                    hb = hd * 64
                    gi = cnt["idx"]
                    cnt["idx"] += 1
                    sbank = 4 + gi % 2
                    stv = pb[sbank][:, 0:256].rearrange("p (j q) -> p j q", j=2)
                    stt = pbt(sbank)
                    j0 = 0 if n > 0 else 1
                    P.op("pe", [I("matmul", out=stv[:, jj, :], lhsT=KTv[hb:hb + 64, r, (n - 1 + jj) * 128:(n + jj) * 128],
                                  rhs=QTv[hb:hb + 64, r, n * 128:(n + 1) * 128], start=True, stop=True)
                                for jj in range(j0, 2)], tq_[0] + tq_[1], stt)
                    pt = PTR[gi % 4]
                    tpt = t("pt", gi % 4)
                    P.op("act", I("activation", out=pt[:, j0:2, :], in_=stv[:, j0:2, :], func=AF.Exp, scale=0.125),
                         stt, [tpt])
                    P.op("pool" if gi % 2 == 0 else "dve",
                         I("tensor_tensor", out=pt[:, j0:2, :], in0=pt[:, j0:2, :], in1=MASK[:, j0:2, :],
                           op=ALU.mult), [tpt, tc_], [tpt])
                    return (pt, tpt, j0, gi)

                def pv(it, stf):
                    r, n, hd = it
                    pt, tpt, j0, gi = stf
                    xb = 6 + gi % 2
                    xq = pb[xb][:, 0:128]
                    xt = pbt(xb)
                    Vh = VA if hd == 0 else VB
                    tv = t("va", set_) if hd == 0 else t("vb", set_)
                    P.op("pe", [I("matmul", out=xq, lhsT=Vh[:, r * nb + n - 1 + jj, :], rhs=pt[:, jj, :],
                                  start=(jj == j0), stop=(jj == 1)) for jj in range(j0, 2)], [tpt, tv], xt)
                    tu = t("u", p % 2, hd)
                    Uv = U2[hd].rearrange("p (m d) -> p d m", d=dil)[:, r, n * 128:(n + 1) * 128]
                    if g == 0:
                        P.op("dve", I("tensor_copy", out=Uv, in_=xq), xt, [tu])
                    else:
                        P.op("dve", I("tensor_tensor", out=Uv, in0=xq, in1=Uv, op=ALU.add), xt + [tu], [tu])

                for idx in range(len(items) + 2):
                    if idx < len(items):
                        st_[idx] = scores(items[idx])
                    if idx >= 2:
                        pv(items[idx - 2], st_.pop(idx - 2))
                    yield

            def normalize(p):
                U2 = UAB[p % 2]
                tu0, tu1 = t("u", p % 2, 0), t("u", p % 2, 1)
                P.dma("sp", "zsw", I("dma_start", out=ZN[0:64, :], in_=U2[0][64:128, :]), reads=[tu0], writes=[t("zn")])
                P.dma("sp", "zsw", I("dma_start", out=ZN[64:128, :], in_=U2[1][0:64, :]), reads=[tu1], writes=[t("zn")])
                P.op("act", I("activation", out=ZN, in_=ZN, func=AF.Ln), [t("zn")], [t("zn")])
                P.op("act", I("activation", out=ZN, in_=ZN, func=AF.Exp, scale=-1.0), [t("zn")], [t("zn")])
                P.op("dve", I("tensor_tensor", out=OT[0:64, p, :], in0=U2[0][0:64, :], in1=ZN[0:64, :], op=ALU.mult),
                     [tu0, t("zn")], [t("ot", p)])
                P.op("dve", I("tensor_tensor", out=OT[64:128, p, :], in0=U2[1][64:128, :], in1=ZN[64:128, :],
                              op=ALU.mult), [tu1, t("zn")], [t("ot", p)])

            pgs = [(p, g) for p in range(8) for g in range(3)]
            for _ in proj_gen(pgs[0][0], pgs[0][1], 0):
                pass
            for k, (p, g) in enumerate(pgs):
                core = core_gen(p, g, k % 2)
                proj = proj_gen(pgs[k + 1][0], pgs[k + 1][1], (k + 1) % 2) if k + 1 < len(pgs) else None
                nstep = 0
                for _ in core:
                    nstep += 1
                    if proj is not None and nstep % 4 == 0:
                        if next(proj, "done") == "done":
                            proj = None
                if proj is not None:
                    for _ in proj:
                        pass
                if g == 2:
                    normalize(p)
            P.barrier()
            for tt in range(4):
                sl = slice(tt * 512, (tt + 1) * 512)
                P.dma("sp", "unpark", I("dma_start", out=H[:, :, sl],
                                        in_=hpark.rearrange("p (c t) -> p c t", c=8)[:, :, sl]),
                      reads=[t("hpark", c) for c in range(8)], writes=[tH(c, tt) for c in range(8)])
            ring["mm"] = [0, 1, 2, 3, 4, 5]
            post_setup(PH + 32768, PH + 49152, PH + 51200)
            mixer_out(l, OT, lambda c, tt: t("ot", c))
            P.barrier()

        outs = []
        for sq_i in range(nseq):
            load_x(sq_i)
            P.barrier()
            for l in layers:
                kind = l % 3
                if do_mixer:
                    if kind == 0:
                        attn_mixer(l, sq_i)
                    elif kind == 1:
                        sc_mixer(l)
                    else:
                        cc_mixer(l)
                if do_ffn:
                    ffn(l)
            ring["mm"] = [0, 1, 2, 3]
            outs += store_y(sq_i)
            P.barrier()
        P.wait_all("sp", outs)
        assert next(keygen, None) is None and not WS["pending"]
        P.emit(block)
    return nc


def _consts():
    c = np.zeros((128, NCST), np.float32)
    c[:, C_ID:C_ID + 128] = np.eye(128, dtype=np.float32)
    sm = np.zeros((128, 128), np.float32)
    for base in (0, 64):
        for m in range(8):
            sm[base + m + 8, base + m] = -1.0
            sm[base + m, base + m + 8] = 1.0
    c[:, C_SMAT:C_SMAT + 128] = sm
    k = np.arange(128)[:, None]
    q = np.arange(128)[None, :]
    c[:, C_MASK:C_MASK + 128] = (k >= q).astype(np.float32)
    c[:, C_MASK + 128:C_MASK + 256] = (k <= q).astype(np.float32)
    inv = (500000.0 ** (-np.arange(0, 16, 2, dtype=np.float32) / 16.0)).astype(np.float32)
    for p in range(128):
        if p % 64 < 16:
            c[p, C_RC] = inv[p % 8] / np.float32(2.0 * math.pi)
    c[:, C_RC + 1] = 1e-6
    c[:, C_RC + 2] = 1e-5
    return c


def _pack_params(inp):
    f = lambda a: np.ascontiguousarray(np.asarray(a, np.float32)).reshape(-1, 128)
    rows = np.concatenate([
        f(inp["mix_norm_pre"]), f(inp["mix_norm_post"]), f(inp["ffn_norm_pre"]), f(inp["ffn_norm_post"]),
        f(inp["sc_w_conv"]), f(inp["cc_b_pw1"]), f(inp["cc_w_dw"]), f(inp["cc_b_dw"]),
        f(inp["cc_ln_g"]), f(inp["cc_ln_b"]), f(inp["cc_b_pw2"]), f(inp["ffn_w_conv"])], axis=0)
    assert rows.shape[0] == 976
    out = np.zeros((1024, 128), np.float32)
    out[:976] = rows
    return out


def _layer_weights(inp, l):
    kind, j = l % 3, l // 3
    a = lambda k, i: np.ascontiguousarray(np.asarray(inp[k][i], np.float32))
    d = {}
    if kind == 0:
        d[f"w{l}_min"] = a("attn_w_qkv", j)
        d[f"w{l}_mout"] = a("attn_w_o", j)
    elif kind == 1:
        d[f"w{l}_min"] = a("sc_w_in", j)
        d[f"w{l}_mout"] = a("sc_w_out", j)
    else:
        d[f"w{l}_min"] = a("cc_w_pw1", j)
        d[f"w{l}_mout"] = a("cc_w_pw2", j)
    d[f"w{l}_fin"] = a("ffn_w_in", l)
    d[f"w{l}_fout"] = a("ffn_w_out", l)
    return d


LAUNCH_GROUPS = [[0, 1, 2, 3]]
_NC_CACHE = {}


def kernel(**inp):
    x = np.ascontiguousarray(np.asarray(inp["x"], np.float32))
    positions = np.ascontiguousarray(np.asarray(inp["positions"], np.int32))
    pth = _pack_params(inp)
    cst = _consts()
    cur = [x[c * SEQ_PER_CORE:(c + 1) * SEQ_PER_CORE].reshape(SEQ_PER_CORE * S, D) for c in range(NCORES)]
    for grp in LAUNCH_GROUPS:
        key = tuple(grp)
        if key not in _NC_CACHE:
            _NC_CACHE[key] = build(list(grp), SEQ_PER_CORE)
        nc = _NC_CACHE[key]
        wd = {}
        for l in grp:
            wd.update(_layer_weights(inp, l))
        in_maps = []
        for c in range(NCORES):
            m = {"x": cur[c], "pos": positions[c * SEQ_PER_CORE:(c + 1) * SEQ_PER_CORE], "pth": pth, "cst": cst}
            m.update(wd)
            in_maps.append(m)
        res = run_bass_kernel_spmd(nc, in_maps, core_ids=list(range(NCORES)))
        cur = [np.asarray(r["y"], np.float32) for r in res.results]
    out = np.stack([c.reshape(SEQ_PER_CORE, S, D) for c in cur], axis=0).reshape(NCORES * SEQ_PER_CORE, S, D)
    return out
```

```python
import contextlib
import math
import numpy as np
import concourse.bass as bass
import concourse.mybir as mybir
from concourse.bass_utils import run_bass_kernel_spmd

F32 = mybir.dt.float32
BF16 = mybir.dt.bfloat16
I32 = mybir.dt.int32
AF = mybir.ActivationFunctionType
ALU = mybir.AluOpType

S = 2048
D = 1024
FF = 2816
NCORES = 8
SEQ_PER_CORE = 4
DEPTH = 4
GROUPS = ((128, 1), (512, 4), (2048, 16))

R_MIXPRE, R_MIXPOST, R_FFNPRE, R_FFNPOST = 0, 32, 64, 96
R_SCCONV = 128
R_CCBPW1 = 152
R_CCWDW = 168
R_CCBDW, R_CCLNG, R_CCLNB, R_CCBPW2 = 416, 424, 432, 440
R_FFNCONV = 448
C_ID, C_SMAT, C_MASK, C_RC = 0, 128, 256, 512
NCST = 640

SEM_CAP = 30000
ENGS = ("pe", "act", "dve", "pool", "sp")


class T:
    __slots__ = ("w", "r")

    def __init__(self):
        self.w = None
        self.r = {}


class TP(T):
    __slots__ = ()


class Prog:
    def __init__(self, nc, stack):
        self.nc = nc
        self.stack = stack
        self.q = {e: [] for e in ENGS}
        self.cnt = {}
        self.seen = {e: {} for e in ENGS}
        self.sems = {}

    def _sem(self, chan, idx):
        lst = self.sems.setdefault(chan, [])
        while len(lst) <= idx:
            lst.append(self.stack.enter_context(self.nc.semaphore(f"s_{chan}_{len(lst)}")))
        return lst[idx]

    def _ev(self, chan, c, step):
        per = SEM_CAP // step
        return self._sem(chan, (c - 1) // per), ((c - 1) % per + 1) * step

    def _deps(self, eng, reads, writes, skip=None):
        deps = {}

        def add(ev):
            if ev is None:
                return
            ch, c = ev
            if (ch == "pe" and eng == "pe") or ch == skip:
                return
            if ch not in ENGS:
                c = self.cnt[ch]
            if deps.get(ch, 0) < c:
                deps[ch] = c
        for t in reads:
            add(t.w)
        for t in writes:
            add(t.w)
            for ch, c in t.r.items():
                add((ch, c))
        out = []
        for ch, c in deps.items():
            if self.seen[eng].get(ch, 0) < c:
                self.seen[eng][ch] = c
                out.append((ch, c))
        return out

    def _mark(self, ev, reads, writes):
        ch, c = ev
        for t in writes:
            t.w = ev
            t.r = {}
        for t in reads:
            if t.r.get(ch, 0) < c:
                t.r[ch] = c

    def op(self, eng, fn, reads=(), writes=()):
        if any(isinstance(t, TP) for t in reads):
            writes = list(writes) + [t for t in reads if isinstance(t, TP)]
            reads = [t for t in reads if not isinstance(t, TP)]
        waits = self._deps(eng, reads, writes)
        c = self.cnt.get(eng, 0) + 1
        self.cnt[eng] = c
        self.q[eng].append((waits, fn, (eng, c, 1)))
        self._mark((eng, c), reads, writes)

    def dma(self, eng, chan, fn, reads=(), writes=()):
        waits = self._deps(eng, reads, writes, skip=chan)
        c = self.cnt.get(chan, 0) + 1
        self.cnt[chan] = c
        self.q[eng].append((waits, fn, (chan, c, 16)))
        self._mark((chan, c), reads, writes)

    def barrier(self):
        for e in ENGS:
            waits = []
            for ch, c in self.cnt.items():
                if ch == e or ch.startswith("w"):
                    continue
                if self.seen[e].get(ch, 0) < c:
                    self.seen[e][ch] = c
                    waits.append((ch, c))
            if waits:
                self.q[e].append((waits, None, None))

    def wait_all(self, eng, tiles):
        waits = self._deps(eng, tiles, ())
        self.q[eng].append((waits, None, None))

    def emit(self, block):
        handles = {"pe": block.tensor, "act": block.scalar, "dve": block.vector,
                   "pool": block.gpsimd, "sp": block.sync}
        for e in ENGS:
            q = self.q[e]
            if not q:
                continue

            def body(h, q=q):
                for waits, fn, inc in q:
                    for ch, c in waits:
                        s, v = self._ev(ch, c, 1 if ch in ENGS else 16)
                        h.wait_ge(s, v)
                    if fn is None:
                        continue
                    ins = None
                    for name, kw in (fn if isinstance(fn, list) else [fn]):
                        ins = getattr(h, name)(**kw)
                    ch, c, step = inc
                    s, v = self._ev(ch, c, step)
                    ins.then_inc(s, step)
            handles[e](body)


def _wkeys(layers, nseq, do_mixer, do_ffn):
    for _ in range(nseq):
        for l in layers:
            kind = l % 3
            if do_mixer:
                if kind == 0:
                    for p in range(8):
                        for g in range(3):
                            yield ("qkv", l, p, g)
                elif kind == 1:
                    for i in range(8):
                        yield ("scin", l, i)
                else:
                    for i in range(8):
                        yield ("pw1", l, i)
                for tt in range(4):
                    for hh in range(2):
                        yield ("mo", l, tt, hh)
            if do_ffn:
                for hf in range(2):
                    for j in range(11):
                        yield ("fin", l, hf, j)
                    for tq in range(2):
                        for o in range(8):
                            yield ("fout", l, hf, tq, o)


def I(name, **kw):
    return (name, kw)


def build(layers, nseq, do_mixer=True, do_ffn=True):
    nc = bass.Bass("TRN2", target_bir_lowering=False)
    dt = nc.dram_tensor
    x = dt("x", [nseq * S, D], F32, kind="ExternalInput").ap()
    y = dt("y", [nseq * S, D], F32, kind="ExternalOutput").ap()
    pos = dt("pos", [nseq, S], I32, kind="ExternalInput").ap()
    pth = dt("pth", [1024, 128], F32, kind="ExternalInput").ap()
    cst = dt("cst", [128, NCST], F32, kind="ExternalInput").ap()
    W = {}
    has_attn = False
    for l in layers:
        kind = l % 3
        if kind == 0:
            has_attn = True
            W[l, "min"] = dt(f"w{l}_min", [D, 9216], F32, kind="ExternalInput").ap()
        elif kind == 1:
            W[l, "min"] = dt(f"w{l}_min", [D, 3072], F32, kind="ExternalInput").ap()
        else:
            W[l, "min"] = dt(f"w{l}_min", [D, 2048], F32, kind="ExternalInput").ap()
        W[l, "mout"] = dt(f"w{l}_mout", [D, D], F32, kind="ExternalInput").ap()
        W[l, "fin"] = dt(f"w{l}_fin", [D, 2 * FF], F32, kind="ExternalInput").ap()
        W[l, "fout"] = dt(f"w{l}_fout", [FF, D], F32, kind="ExternalInput").ap()
    hpark = dt("hpark", [128, 8 * S], F32, kind="Internal").ap() if has_attn else None

    with contextlib.ExitStack() as st:
        ARENA_BYTES = 212480
        arena = st.enter_context(nc.sbuf_tensor("arena", [128, ARENA_BYTES // 4], F32))
        arena_bf = arena.bitcast(BF16)
        arena_i = arena.bitcast(I32)
        pb = [st.enter_context(nc.psum_tensor(f"pb{i}", [128, 512], F32)) for i in range(8)]
        pb_bf = [b.bitcast(BF16) for b in pb]
        block = st.enter_context(nc.Block())
        P = Prog(nc, st)

        def carve(off, n, dtype=F32):
            if dtype == BF16:
                assert off % 2 == 0
                return arena_bf[:, off // 2: off // 2 + n]
            assert off % 4 == 0
            v = arena if dtype == F32 else arena_i
            return v[:, off // 4: off // 4 + n]

        tiles = {}

        def t(*key):
            r = tiles.get(key)
            if r is None:
                r = tiles[key] = T()
            return r

        ptiles = [TP() for _ in range(8)]

        def pbt(i, q0=0, q1=4):
            return [ptiles[i]]

        PTAB = carve(0, 1024)
        IDF = carve(4096, 128)
        ONESB = carve(4608, 128, BF16)
        IDB = carve(4864, 128, BF16)
        SMAT = carve(5120, 128, BF16)
        MASK2 = carve(5376, 256, BF16)
        MASK = MASK2.rearrange("p (j q) -> p j q", j=2)
        RC = carve(5888, 8)
        ONESF = carve(6144, 128)
        BG = carve(6656, 8)
        WS0 = 7168
        NW = 3
        wslots = [carve(WS0 + i * 8192, 4096, BF16) for i in range(NW)]
        A_OFF = WS0 + NW * 8192
        A = carve(A_OFF, 8 * S, BF16).rearrange("p (c t) -> p c t", c=8)
        H_OFF = A_OFF + 32768
        H = carve(H_OFF, 8 * S).rearrange("p (c t) -> p c t", c=8)
        PH = H_OFF + 65536
        assert ARENA_BYTES - PH >= 79904
        EPS_RMS = RC[:, 1:2]
        EPS_LN = RC[:, 2:3]

        def col(r):
            return PTAB[:, r:r + 1]

        def tA(c, tt):
            return t("A", c, tt)

        def tH(c, tt):
            return t("H", c, tt)

        keygen = _wkeys(layers, nseq, do_mixer, do_ffn)
        WS = {"pending": [], "nxt": 0}
        wT = [T() for _ in range(NW)]

        def w_issue():
            key = next(keygen, None)
            if key is None:
                return
            s = WS["nxt"]
            WS["nxt"] = (s + 1) % NW
            slot = wslots[s]
            kind, l = key[0], key[1]
            dmas = []
            if kind == "fin":
                j = key[3]
                v = slot.rearrange("p (c s n) -> p c s n", c=8, s=2)
                src = W[l, "fin"].rearrange("(c p) n -> p c n", p=128)
                for h in range(2):
                    dmas.append((v[:, :, h, :], src[:, :, h * FF + j * 256: h * FF + (j + 1) * 256]))
            elif kind == "fout":
                o = key[4]
                v = slot[:, 0:2816].rearrange("p (k n) -> p k n", k=22)
                src = W[l, "fout"].rearrange("(k p) n -> p k n", p=128)
                dmas.append((v, src[:, :, o * 128:(o + 1) * 128]))
            elif kind in ("qkv", "scin", "pw1"):
                ns = 2 if kind == "pw1" else 3
                v = slot[:, 0:ns * 1024].rearrange("p (c s n) -> p c s n", c=8, s=ns)
                src = W[l, "min"].rearrange("(c p) n -> p c n", p=128)
                for s_ in range(ns):
                    if kind == "qkv":
                        cl = s_ * 3072 + key[3] * 1024 + key[2] * 128
                    else:
                        cl = s_ * 1024 + key[2] * 128
                    dmas.append((v[:, :, s_, :], src[:, :, cl:cl + 128]))
            elif kind == "mo":
                hh = key[3]
                v = slot.rearrange("p (c n) -> p c n", c=8)
                src = W[l, "mout"].rearrange("(c p) n -> p c n", p=128)
                dmas.append((v, src[:, :, hh * 512:(hh + 1) * 512]))
            else:
                raise AssertionError(key)
            for (o_, i_) in dmas:
                P.dma("pool", f"w{s}", I("dma_start", out=o_, in_=i_), writes=[wT[s]])
            WS["pending"].append((key, s))

        def w_acquire(key):
            k, s = WS["pending"].pop(0)
            assert k == key, (k, key)
            return wslots[s], wT[s]

        def w_release():
            w_issue()

        def mm(out, pairs, reads, writes):
            n = len(pairs)
            P.op("pe", [I("matmul", out=out, lhsT=l_, rhs=r_, start=(i == 0), stop=(i == n - 1))
                        for i, (l_, r_) in enumerate(pairs)], reads, writes)

        ring = {"mm": [0, 1, 2, 3], "i": 0}

        def next_bank():
            b = ring["mm"][ring["i"] % len(ring["mm"])]
            ring["i"] += 1
            return b

        flip = {"n": 0}

        def alt_engine():
            flip["n"] += 1
            return "act" if flip["n"] % 2 else "dve"

        def copy_op(eng, out, in_, reads, writes):
            if eng == "act":
                P.op("act", I("activation", out=out, in_=in_, func=AF.Copy), reads, writes)
            else:
                P.op(eng, I("tensor_copy", out=out, in_=in_), reads, writes)

        def rstd_inplace(bank, eps_ap):
            bt = pbt(bank)
            P.op("act", I("activation", out=pb[bank][:], in_=pb[bank][:], func=AF.Ln, bias=eps_ap, scale=1.0 / D),
                 bt + [t("cst")], bt)
            P.op("act", I("activation", out=pb[bank][:], in_=pb[bank][:], func=AF.Exp, scale=-0.5), bt, bt)

        def prenorm(row, sq_off):
            sqs = [carve(sq_off + i * 8192, 4096, BF16).rearrange("p (c n) -> p c n", c=8) for i in range(2)]

            def square(tt):
                sl = slice(tt * 512, (tt + 1) * 512)
                P.op("act", I("activation", out=sqs[tt % 2], in_=H[:, :, sl], func=AF.Square),
                     [tH(c, tt) for c in range(8)], [t("sqpre", tt % 2)])
            square(0)
            square(1)
            for tt in range(4):
                sq = sqs[tt % 2]
                tsq = t("sqpre", tt % 2)
                sl = slice(tt * 512, (tt + 1) * 512)
                bank = 6 + tt % 2
                mm(pb[bank][:], [(ONESB, sq[:, c, :]) for c in range(8)], [tsq, t("cst")], pbt(bank))
                rstd_inplace(bank, EPS_RMS)
                if tt + 2 < 4:
                    square(tt + 2)
                for c in range(8):
                    P.op("dve", I("scalar_tensor_tensor", out=A[:, c, sl], in0=H[:, c, sl], scalar=col(row + c),
                                  in1=pb[bank][:], op0=ALU.mult, op1=ALU.mult),
                         [tH(c, tt), t("ptab")] + pbt(bank), [tA(c, tt)])

        Post = {}

        def post_setup(mb_off, sq_off, mb_off2=None):
            offs = [mb_off, mb_off if mb_off2 is None else mb_off2]
            Post["Mb"] = [carve(o_, 4096).rearrange("p (o n) -> p o n", o=8) for o_ in offs]
            Post["two"] = mb_off2 is not None
            Post["sq"] = [carve(sq_off + i * 1024, 512, BF16) for i in range(2)]
            Post["n"] = 0

        def post_evac(o, bank, tt, grow, brow=None):
            mi = tt % 2 if Post["two"] else 0
            Mb = Post["Mb"][mi]
            sqb = Post["sq"][Post["n"] % 2]
            tsq = t("sqpost", Post["n"] % 2)
            Post["n"] += 1
            sbank = 6 + tt % 2
            bt = pbt(bank)
            if brow is None:
                P.op("act", I("activation", out=sqb, in_=pb[bank][:], func=AF.Square), bt, [tsq])
                P.op("act", I("activation", out=Mb[:, o, :], in_=pb[bank][:], func=AF.Copy, scale=col(grow + o)),
                     bt + [t("ptab")], [t("Mb", mi, o)])
            else:
                P.op("act", I("activation", out=sqb, in_=pb[bank][:], func=AF.Square, bias=col(brow + o)),
                     bt + [t("ptab")], [tsq])
                P.op("act", I("activation", out=Mb[:, o, :], in_=pb[bank][:], func=AF.Identity,
                              scale=col(grow + o), bias=BG[:, o:o + 1]),
                     bt + [t("ptab"), t("bg")], [t("Mb", mi, o)])
            P.op("pe", I("matmul", out=pb[sbank][:], lhsT=ONESB, rhs=sqb, start=(o == 0), stop=(o == 7)),
                 [tsq, t("cst")], pbt(sbank))

        def post_finish(tt):
            mi = tt % 2 if Post["two"] else 0
            Mb = Post["Mb"][mi]
            sbank = 6 + tt % 2
            rstd_inplace(sbank, EPS_RMS)
            sl = slice(tt * 512, (tt + 1) * 512)
            mbt = [t("Mb", mi, o) for o in range(8)]
            P.op("dve", I("tensor_tensor", out=Mb, in0=Mb,
                          in1=pb[sbank][:].unsqueeze(1).broadcast_to([128, 8, 512]), op=ALU.mult),
                 mbt + pbt(sbank), mbt)
            hts = [tH(c, tt) for c in range(8)]
            P.op("pool", I("tensor_tensor", out=H[:, :, sl], in0=H[:, :, sl], in1=Mb, op=ALU.add), mbt + hts, hts)

        def mixer_out(l, Y, ytile, brow=None):
            grow = R_MIXPOST + l * 8
            for tt in range(4):
                sl = slice(tt * 512, (tt + 1) * 512)
                for hh in range(2):
                    slot, wt = w_acquire(("mo", l, tt, hh))
                    v = slot.rearrange("p (c n) -> p c n", c=8)
                    for o4 in range(4):
                        o = hh * 4 + o4
                        bank = next_bank()
                        mm(pb[bank][:], [(v[:, c, o4 * 128:(o4 + 1) * 128], Y[:, c, sl]) for c in range(8)],
                           [wt] + [ytile(c, tt) for c in range(8)], pbt(bank))
                        post_evac(o, bank, tt, grow, brow)
                    w_release()
                post_finish(tt)

        tc_ = t("cst")
        P.dma("sp", "cst", I("dma_start", out=IDF, in_=cst[:, C_ID:C_ID + 128]), writes=[tc_])
        P.dma("sp", "cst", I("dma_start", out=RC, in_=cst[:, C_RC:C_RC + 8]), writes=[tc_])
        P.dma("pool", "cst", I("dma_start", out=IDB, in_=cst[:, C_ID:C_ID + 128]), writes=[tc_])
        P.dma("pool", "cst", I("dma_start", out=SMAT, in_=cst[:, C_SMAT:C_SMAT + 128]), writes=[tc_])
        P.dma("pool", "cst", I("dma_start", out=MASK2, in_=cst[:, C_MASK:C_MASK + 256]), writes=[tc_])
        P.op("dve", I("memset", ap=ONESB, constant=1.0), [], [t("ones")])
        P.op("dve", I("memset", ap=ONESF, constant=1.0), [], [t("ones")])
        pst = carve(PH, 1024).rearrange("p (k f) -> p k f", k=8)
        P.dma("sp", "ld0", I("dma_start", out=pst, in_=pth.rearrange("(k r) f -> r k f", r=128)), writes=[t("pst")])
        for hb in range(2):
            P.op("pe", [I("transpose", out=pb[hb][:, q * 128:(q + 1) * 128], in_=pst[:, hb * 4 + q, :], identity=IDF)
                        for q in range(4)], [t("pst"), tc_], pbt(hb))
            P.op("act", I("activation", out=PTAB[:, hb * 512:(hb + 1) * 512], in_=pb[hb][:], func=AF.Copy),
                 pbt(hb), [t("ptab")])
        for _ in range(NW):
            w_issue()
        P.barrier()

        def load_x(sq_i):
            stg = [carve(PH + i * 4096, 1024) for i in range(2)]
            for j in range(16):
                sb_ = stg[j % 2]
                ts_ = t("stin", j % 2)
                row0 = (sq_i * 16 + j) * 128
                P.dma("sp", f"ld{j % 2}", I("dma_start", out=sb_, in_=x[row0:row0 + 128, :]), writes=[ts_])
                for hb in range(2):
                    bank = next_bank()
                    P.op("pe", [I("transpose", out=pb[bank][:, q * 128:(q + 1) * 128],
                                  in_=sb_[:, (hb * 4 + q) * 128:(hb * 4 + q + 1) * 128], identity=IDF)
                                for q in range(4)], [ts_, tc_], pbt(bank))
                    copy_op(alt_engine(), H[:, hb * 4:hb * 4 + 4, j * 128:(j + 1) * 128],
                            pb[bank][:].rearrange("p (c n) -> p c n", c=4), pbt(bank),
                            [tH(c, j // 4) for c in range(hb * 4, hb * 4 + 4)])

        def store_y(sq_i):
            stg = [carve(PH + 8192 + i * 4096, 1024) for i in range(2)]
            outs = []
            for j in range(16):
                sb_ = stg[j % 2]
                ts_ = t("stout", j % 2)
                for hb in range(2):
                    bank = next_bank()
                    P.op("pe", [I("transpose", out=pb[bank][:, q * 128:(q + 1) * 128],
                                  in_=H[:, hb * 4 + q, j * 128:(j + 1) * 128], identity=IDF) for q in range(4)],
                         [tH(c, j // 4) for c in range(hb * 4, hb * 4 + 4)] + [tc_], pbt(bank))
                    copy_op(alt_engine(), sb_[:, hb * 512:(hb + 1) * 512], pb[bank][:], pbt(bank), [ts_])
                row0 = (sq_i * 16 + j) * 128
                ty = t("yout", sq_i, j)
                P.dma("sp", f"st{j % 2}", I("dma_start", out=y[row0:row0 + 128, :], in_=sb_), reads=[ts_], writes=[ty])
                outs.append(ty)
            return outs

        def ffn(l):
            ring["mm"] = [0, 1, 2, 3, 4, 5]
            Z = carve(PH, 22 * 1024, BF16).rearrange("p (k n) -> p k n", k=22)
            ACC = [[carve(PH + 45056 + (kd * 2 + b) * 4112, 1026) for b in range(2)] for kd in range(2)]
            SG = [carve(PH + 61504 + b * 4096, 1024) for b in range(2)]
            HALO = carve(PH + 80000, 88).rearrange("p (f n) -> p f n", f=44)
            prenorm(R_FFNPRE + l * 8, PH)
            P.barrier()
            npair = 0
            for hf in range(2):
                for j in range(11):
                    slot, wt = w_acquire(("fin", l, hf, j))
                    v = slot.rearrange("p (c s n) -> p c s n", c=8, s=2)
                    for ii in range(2):
                        i = 2 * j + ii
                        buf = npair % 2
                        npair += 1
                        for kd in range(2):
                            ft = kd * 22 + i
                            acc = ACC[kd][buf]
                            tacc = t("acc", kd, buf)
                            r0 = R_FFNCONV + l * 132 + ft
                            r1 = r0 + 44
                            r2 = r0 + 88
                            th = t("halo", ft)
                            P.op("dve", I("memset", ap=acc[:, 1024:1026], constant=0.0), [], [tacc])
                            bks = []
                            for tq in range(2):
                                tt = hf * 2 + tq
                                sl = slice(tt * 512, (tt + 1) * 512)
                                bank = next_bank()
                                bks.append(bank)
                                mm(pb[bank][:], [(v[:, c, kd, ii * 128:(ii + 1) * 128], A[:, c, sl]) for c in range(8)],
                                   [wt] + [tA(c, tt) for c in range(8)], pbt(bank))
                                P.op("act", I("activation", out=acc[:, tq * 512:(tq + 1) * 512], in_=pb[bank][:],
                                              func=AF.Copy, scale=col(r2)), pbt(bank) + [t("ptab")], [tacc])
                            for tq in range(2):
                                bank = bks[tq]
                                a0 = tq * 512
                                P.op("dve", I("scalar_tensor_tensor", out=acc[:, a0 + 1:a0 + 513], in0=pb[bank][:],
                                              scalar=col(r1), in1=acc[:, a0 + 1:a0 + 513], op0=ALU.mult, op1=ALU.add),
                                     pbt(bank) + [tacc, t("ptab")], [tacc])
                                P.op("dve", I("scalar_tensor_tensor", out=acc[:, a0 + 2:a0 + 514], in0=pb[bank][:],
                                              scalar=col(r0), in1=acc[:, a0 + 2:a0 + 514], op0=ALU.mult, op1=ALU.add),
                                     pbt(bank) + [tacc, t("ptab")], [tacc])
                            if hf == 1:
                                P.op("dve", I("tensor_tensor", out=acc[:, 0:2], in0=acc[:, 0:2], in1=HALO[:, ft, :],
                                              op=ALU.add), [tacc, th], [tacc])
                            else:
                                P.op("dve", I("tensor_copy", out=HALO[:, ft, :], in_=acc[:, 1024:1026]), [tacc], [th])
                        sg = SG[buf]
                        tsg = t("sg", buf)
                        P.op("act", I("activation", out=sg, in_=ACC[0][buf][:, 0:1024], func=AF.Silu),
                             [t("acc", 0, buf)], [tsg])
                        P.op("pool", I("tensor_tensor", out=Z[:, i, :], in0=sg, in1=ACC[1][buf][:, 0:1024], op=ALU.mult),
                             [tsg, t("acc", 1, buf)], [t("Z", i)])
                    w_release()
                P.barrier()
                post_setup(PH + 45056, PH + 61504, PH + 63552)
                for tq in range(2):
                    tt = hf * 2 + tq
                    for o in range(8):
                        slot, wt = w_acquire(("fout", l, hf, tq, o))
                        v = slot[:, 0:2816].rearrange("p (k n) -> p k n", k=22)
                        bank = next_bank()
                        mm(pb[bank][:], [(v[:, k, :], Z[:, k, tq * 512:(tq + 1) * 512]) for k in range(22)],
                           [wt] + [t("Z", k) for k in range(22)], pbt(bank))
                        post_evac(o, bank, tt, R_FFNPOST + l * 8)
                        w_release()
                    post_finish(tt)
                P.barrier()

        def sc_mixer(l):
            ring["mm"] = [0, 1, 2, 3, 4, 5]
            Y = carve(PH, 8 * S, BF16).rearrange("p (c t) -> p c t", c=8)
            CH = carve(PH + 32768, 2056)
            BSB = carve(PH + 40992, 2048)
            HSB = [carve(PH + 49184 + b * 2048, 512) for b in range(2)]
            ACC = carve(PH + 53280, 2048)
            prenorm(R_MIXPRE + l * 8, PH)
            P.barrier()
            P.op("dve", I("memset", ap=CH[:, 0:2], constant=0.0), [], [t("ch")])
            nh = 0
            for i in range(8):
                slot, wt = w_acquire(("scin", l, i))
                v = slot[:, 0:3072].rearrange("p (c s n) -> p c s n", c=8, s=3)
                for tt in range(4):
                    sl = slice(tt * 512, (tt + 1) * 512)
                    banks = [next_bank() for _ in range(3)]
                    for s_ in range(3):
                        mm(pb[banks[s_]][:], [(v[:, c, s_, :], A[:, c, sl]) for c in range(8)],
                           [wt] + [tA(c, tt) for c in range(8)], pbt(banks[s_]))
                    hs = HSB[nh % 2]
                    ths = t("hsb", nh % 2)
                    nh += 1
                    P.op("act", I("activation", out=hs, in_=pb[banks[2]][:], func=AF.Copy), pbt(banks[2]), [ths])
                    P.op("dve", I("tensor_tensor", out=CH[:, 2 + tt * 512:2 + (tt + 1) * 512], in0=pb[banks[1]][:],
                                  in1=hs, op=ALU.mult), pbt(banks[1]) + [ths], [t("ch")])
                    P.op("act", I("activation", out=BSB[:, sl], in_=pb[banks[0]][:], func=AF.Copy),
                         pbt(banks[0]), [t("bsb")])
                w_release()
                r0 = R_SCCONV + i
                P.op("act", I("activation", out=ACC, in_=CH[:, 2:2050], func=AF.Copy, scale=col(r0 + 16)),
                     [t("ch"), t("ptab")], [t("scacc")])
                P.op("dve", I("scalar_tensor_tensor", out=ACC, in0=CH[:, 1:2049], scalar=col(r0 + 8), in1=ACC,
                              op0=ALU.mult, op1=ALU.add), [t("ch"), t("scacc"), t("ptab")], [t("scacc")])
                P.op("dve", I("scalar_tensor_tensor", out=ACC, in0=CH[:, 0:2048], scalar=col(r0), in1=ACC,
                              op0=ALU.mult, op1=ALU.add), [t("ch"), t("scacc"), t("ptab")], [t("scacc")])
                P.op("pool", I("tensor_tensor", out=Y[:, i, :], in0=BSB, in1=ACC, op=ALU.mult),
                     [t("bsb"), t("scacc")], [t("Y", i)])
            P.barrier()
            post_setup(PH + 61472, PH + 77856, PH + 32768)
            mixer_out(l, Y, lambda c, tt: t("Y", c))
            P.barrier()

        def cc_mixer(l):
            ring["mm"] = [0, 1, 2, 3]
            X = carve(PH, 8 * S).rearrange("p (c t) -> p c t", c=8)
            GLU = carve(PH + 65536, 2080)
            SIG = [carve(PH + 73856 + b * 2048, 512) for b in range(2)]
            Yc = A
            prenorm(R_MIXPRE + l * 8, PH)
            P.barrier()
            P.op("dve", I("memset", ap=GLU[:, 0:30], constant=0.0), [], [t("glu")])
            P.op("dve", I("tensor_tensor", out=BG, in0=PTAB[:, R_CCBPW2:R_CCBPW2 + 8],
                          in1=PTAB[:, R_MIXPOST + l * 8:R_MIXPOST + l * 8 + 8], op=ALU.mult), [t("ptab")], [t("bg")])
            ns = 0
            for i in range(8):
                slot, wt = w_acquire(("pw1", l, i))
                v = slot[:, 0:2048].rearrange("p (c s n) -> p c s n", c=8, s=2)
                for tt in range(4):
                    sl = slice(tt * 512, (tt + 1) * 512)
                    ba, bg_ = next_bank(), next_bank()
                    mm(pb[ba][:], [(v[:, c, 0, :], A[:, c, sl]) for c in range(8)],
                       [wt] + [tA(c, tt) for c in range(8)], pbt(ba))
                    mm(pb[bg_][:], [(v[:, c, 1, :], A[:, c, sl]) for c in range(8)],
                       [wt] + [tA(c, tt) for c in range(8)], pbt(bg_))
                    sg = SIG[ns % 2]
                    tsg = t("sig", ns % 2)
                    ns += 1
                    P.op("act", I("activation", out=sg, in_=pb[bg_][:], func=AF.Sigmoid, bias=col(R_CCBPW1 + 8 + i)),
                         pbt(bg_) + [t("ptab")], [tsg])
                    P.op("dve", I("scalar_tensor_tensor", out=GLU[:, 30 + tt * 512:30 + (tt + 1) * 512], in0=pb[ba][:],
                                  scalar=col(R_CCBPW1 + i), in1=sg, op0=ALU.add, op1=ALU.mult),
                         pbt(ba) + [tsg, t("ptab")], [t("glu")])
                w_release()
                tx = [t("X", i, tt) for tt in range(4)]
                P.op("act", I("activation", out=X[:, i, :], in_=GLU[:, 30:30 + S], func=AF.Identity,
                              scale=col(R_CCWDW + 240 + i), bias=col(R_CCBDW + i)), [t("glu"), t("ptab")], tx)
                for k in range(30):
                    P.op("dve", I("scalar_tensor_tensor", out=X[:, i, :], in0=GLU[:, k:k + S],
                                  scalar=col(R_CCWDW + k * 8 + i), in1=X[:, i, :], op0=ALU.mult, op1=ALU.add),
                         [t("glu"), t("ptab")] + tx, tx)
            P.barrier()
            SQ = carve(PH + 65536, 4096, BF16).rearrange("p (c n) -> p c n", c=8)
            MEAN = carve(PH + 73728, 512)
            VAR = carve(PH + 75776, 512)
            for tt in range(4):
                sl = slice(tt * 512, (tt + 1) * 512)
                txs = [t("X", c, tt) for c in range(8)]
                P.op("act", I("activation", out=SQ, in_=X[:, :, sl], func=AF.Square), txs, [t("lnsq")])
                b1, b2 = next_bank(), next_bank()
                mm(pb[b1][:], [(ONESF, X[:, c, sl]) for c in range(8)], txs + [t("ones")], pbt(b1))
                mm(pb[b2][:], [(ONESB, SQ[:, c, :]) for c in range(8)], [t("lnsq"), t("ones")], pbt(b2))
                P.op("dve", I("tensor_scalar", out=MEAN, in0=pb[b1][:], scalar1=1.0 / D, scalar2=None, op0=ALU.mult),
                     pbt(b1), [t("mean")])
                P.op("dve", I("tensor_tensor", out=VAR, in0=MEAN, in1=MEAN, op=ALU.mult), [t("mean")], [t("var")])
                P.op("dve", I("scalar_tensor_tensor", out=VAR, in0=pb[b2][:], scalar=1.0 / D, in1=VAR,
                              op0=ALU.mult, op1=ALU.subtract), pbt(b2) + [t("var")], [t("var")])
                P.op("act", I("activation", out=VAR, in_=VAR, func=AF.Ln, bias=EPS_LN, scale=1.0),
                     [t("var"), tc_], [t("var")])
                P.op("act", I("activation", out=VAR, in_=VAR, func=AF.Exp, scale=-0.5), [t("var")], [t("var")])
                P.op("dve", I("tensor_tensor", out=X[:, :, sl], in0=X[:, :, sl],
                              in1=MEAN.unsqueeze(1).broadcast_to([128, 8, 512]), op=ALU.subtract),
                     txs + [t("mean")], txs)
                P.op("dve", I("tensor_tensor", out=X[:, :, sl], in0=X[:, :, sl],
                              in1=VAR.unsqueeze(1).broadcast_to([128, 8, 512]), op=ALU.mult), txs + [t("var")], txs)
                for c in range(8):
                    P.op("act", I("activation", out=Yc[:, c, sl], in_=X[:, c, sl], func=AF.Silu,
                                  scale=col(R_CCLNG + c), bias=col(R_CCLNB + c)), [t("X", c, tt), t("ptab")], [tA(c, tt)])
            P.barrier()
            post_setup(PH, PH + 16384, PH + 18432)
            mixer_out(l, Yc, tA, brow=R_CCBPW2)
            P.barrier()

        def attn_mixer(l, sq_i):
            ring["mm"] = [0, 1, 2]
            HB = H_OFF
            QKV = [[carve(HB + s_ * 12288 + k * 4096, S, BF16) for k in range(3)] for s_ in range(2)]
            VAB = [[carve(HB + 24576 + s_ * 8192 + k * 4096, 2048, BF16).rearrange("p (b n) -> p b n", b=16)
                    for k in range(2)] for s_ in range(2)]
            OT = carve(PH, 8 * S, BF16).rearrange("p (c t) -> p c t", c=8)
            COS = carve(PH + 32768, S)
            SIN = carve(PH + 40960, S)
            PTR = [carve(PH + 49152 + i * 512, 256, BF16).rearrange("p (j q) -> p j q", j=2) for i in range(4)]
            T12 = [[carve(PH + 51200 + (b * 2 + k) * 2048, 512) for k in range(2)] for b in range(2)]
            POSI = carve(PH + 59392, S, I32)
            XF = carve(PH + 67584, S)
            ZN = carve(PH + 59392, S)
            UAB = [[carve(HB + 40960, S), carve(HB + 49152, S)],
                   [carve(HB + 57344, S), carve(PH + 67584, S)]]
            for c in range(8):
                P.dma("sp", "park", I("dma_start", out=hpark[:, c * S:(c + 1) * S], in_=H[:, c, :]),
                      reads=[tH(c, tt) for tt in range(4)], writes=[t("hpark", c)])
            P.dma("sp", "pos", I("dma_start", out=POSI, in_=pos[sq_i:sq_i + 1, :].broadcast_to([128, S])),
                  writes=[t("posi")])
            P.op("dve", I("tensor_copy", out=XF, in_=POSI), [t("posi")], [t("xf")])
            P.op("dve", I("tensor_scalar", out=XF, in0=XF, scalar1=RC[:, 0:1], scalar2=None, op0=ALU.mult),
                 [t("xf"), tc_], [t("xf")])
            P.op("dve", I("tensor_copy", out=POSI, in_=XF), [t("xf")], [t("posi")])
            P.op("dve", I("tensor_copy", out=COS, in_=POSI), [t("posi")], [t("cos")])
            P.op("dve", I("tensor_tensor", out=XF, in0=XF, in1=COS, op=ALU.subtract), [t("xf"), t("cos")], [t("xf")])
            P.op("dve", I("scalar_tensor_tensor", out=SIN, in0=XF, scalar=0.5, in1=XF, op0=ALU.is_gt, op1=ALU.subtract),
                 [t("xf")], [t("sin")])
            P.op("dve", I("scalar_tensor_tensor", out=SIN, in0=SIN, scalar=0.5, in1=SIN, op0=ALU.is_gt, op1=ALU.subtract),
                 [t("sin")], [t("sin")])
            P.op("dve", I("tensor_scalar", out=XF, in0=XF, scalar1=0.25, scalar2=None, op0=ALU.add), [t("xf")], [t("xf")])
            P.op("dve", I("scalar_tensor_tensor", out=COS, in0=XF, scalar=0.5, in1=XF, op0=ALU.is_gt, op1=ALU.subtract),
                 [t("xf"), t("cos")], [t("cos")])
            P.op("dve", I("scalar_tensor_tensor", out=COS, in0=COS, scalar=0.5, in1=COS, op0=ALU.is_gt, op1=ALU.subtract),
                 [t("cos")], [t("cos")])
            TWO_PI = 2.0 * math.pi * (1.0 - 2e-7)
            P.op("act", I("activation", out=SIN, in_=SIN, func=AF.Sin, scale=TWO_PI), [t("sin")], [t("sin")])
            P.op("act", I("activation", out=COS, in_=COS, func=AF.Sin, scale=TWO_PI), [t("cos")], [t("cos")])
            prenorm(R_MIXPRE + l * 8, PH)
            P.barrier()
            for s_ in range(2):
                P.op("pool", I("memset", ap=VAB[s_][0][:, :, 64:128], constant=1.0), [], [t("va", s_)])
                P.op("pool", I("memset", ap=VAB[s_][1][:, :, 0:64], constant=1.0), [], [t("vb", s_)])
            cnt = {"rope": 0, "idx": 0}

            def proj_gen(p, g, set_):
                dil = GROUPS[g][1]
                nb = (S // dil) // 128
                VA, VB = VAB[set_]
                tq_ = [[t("qkv", set_, k, tt_) for tt_ in range(4)] for k in range(3)]
                slot, wt = w_acquire(("qkv", l, p, g))
                v = slot[:, 0:3072].rearrange("p (c s n) -> p c s n", c=8, s=3)
                pend = None
                for s3 in range(3):
                    dst = QKV[set_][s3]
                    for tt in range(4):
                        sl = slice(tt * 512, (tt + 1) * 512)
                        bank = next_bank()
                        bt = pbt(bank)
                        mm(pb[bank][:], [(v[:, c, s3, :], A[:, c, sl]) for c in range(8)],
                           [wt] + [tA(c, tt) for c in range(8)], bt)
                        if pend is not None:
                            pend()
                            pend = None
                        P.op("act", I("activation", out=dst[:, sl], in_=pb[bank][:], func=AF.Copy), bt, [tq_[s3][tt]])
                        if s3 < 2:
                            nr = cnt["rope"]
                            cnt["rope"] += 1
                            t1, t2 = T12[nr % 2]
                            tt1, tt2 = t("t1", nr % 2), t("t2", nr % 2)
                            P.op("dve", I("tensor_tensor", out=t1[0:80, :], in0=pb[bank][0:80, :], in1=COS[0:80, sl],
                                          op=ALU.mult), bt + [t("cos")], [tt1])

                            def rope_tail(dst=dst, sl=sl, t1=t1, t2=t2, tt1=tt1, tt2=tt2, tq=tq_[s3][tt]):
                                mm(pb[3][0:80, :], [(SMAT[:, 0:80], dst[:, sl])], [tq, tc_], pbt(3))
                                P.op("dve", I("tensor_tensor", out=t2[0:80, :], in0=pb[3][0:80, :], in1=SIN[0:80, sl],
                                              op=ALU.mult), pbt(3) + [t("sin")], [tt2])
                                P.op("dve", I("tensor_tensor", out=dst[0:80, sl], in0=t1[0:80, :], in1=t2[0:80, :],
                                              op=ALU.add), [tt1, tt2], [tq])
                            pend = rope_tail
                        yield
                if pend is not None:
                    pend()
                w_release()
                VTv = QKV[set_][2].rearrange("p (m d) -> p d m", d=dil)
                for h8 in range(2):
                    bank = next_bank()
                    ins = []
                    for b8 in range(8):
                        blk = h8 * 8 + b8
                        r, n = blk // nb, blk % nb
                        ins.append(I("transpose", out=pb_bf[bank][:, b8 * 128:(b8 + 1) * 128],
                                     in_=VTv[:, r, n * 128:(n + 1) * 128], identity=IDB))
                    P.op("pe", ins, tq_[2] + [tc_], pbt(bank))
                    src = pb_bf[bank][:].rearrange("p (b n) -> p b n", b=8)
                    P.op("dve", I("tensor_copy", out=VA[:, h8 * 8:h8 * 8 + 8, 0:64], in_=src[:, :, 0:64]),
                         pbt(bank), [t("va", set_)])
                    P.op("act", I("activation", out=VB[:, h8 * 8:h8 * 8 + 8, 64:128], in_=src[:, :, 64:128],
                                  func=AF.Copy), pbt(bank), [t("vb", set_)])
                    yield

            def core_gen(p, g, set_):
                dil = GROUPS[g][1]
                nb = (S // dil) // 128
                QT, KT, VT = QKV[set_]
                VA, VB = VAB[set_]
                U2 = UAB[p % 2]
                tq_ = [[t("qkv", set_, k, tt_) for tt_ in range(4)] for k in range(3)]
                QTv = QT.rearrange("p (m d) -> p d m", d=dil)
                KTv = KT.rearrange("p (m d) -> p d m", d=dil)
                items = [(r, n, hd) for r in range(dil) for n in range(nb) for hd in range(2)]
                st_ = {}

                def scores(it):
                    r, n, hd = it
                    hb = hd * 64
                    gi = cnt["idx"]
                    cnt["idx"] += 1
                    sbank = 4 + gi % 2
                    stv = pb[sbank][:, 0:256].rearrange("p (j q) -> p j q", j=2)
                    stt = pbt(sbank)
                    j0 = 0 if n > 0 else 1
                    P.op("pe", [I("matmul", out=stv[:, jj, :], lhsT=KTv[hb:hb + 64, r, (n - 1 + jj) * 128:(n + jj) * 128],
                                  rhs=QTv[hb:hb + 64, r, n * 128:(n + 1) * 128], start=True, stop=True)
                                for jj in range(j0, 2)], tq_[0] + tq_[1], stt)
                    pt = PTR[gi % 4]
                    tpt = t("pt", gi % 4)
                    P.op("act", I("activation", out=pt[:, j0:2, :], in_=stv[:, j0:2, :], func=AF.Exp, scale=0.125),
                         stt, [tpt])
                    P.op("pool",
                         I("tensor_tensor", out=pt[:, j0:2, :], in0=pt[:, j0:2, :], in1=MASK[:, j0:2, :],
                           op=ALU.mult), [tpt, tc_], [tpt])
                    return (pt, tpt, j0, gi)

                def pv(it, stf):
                    r, n, hd = it
                    pt, tpt, j0, gi = stf
                    xb = 6 + gi % 2
                    xq = pb[xb][:, 0:128]
                    xt = pbt(xb)
                    Vh = VA if hd == 0 else VB
                    tv = t("va", set_) if hd == 0 else t("vb", set_)
                    P.op("pe", [I("matmul", out=xq, lhsT=Vh[:, r * nb + n - 1 + jj, :], rhs=pt[:, jj, :],
                                  start=(jj == j0), stop=(jj == 1)) for jj in range(j0, 2)], [tpt, tv], xt)
                    tu = t("u", p % 2, hd)
                    Uv = U2[hd].rearrange("p (m d) -> p d m", d=dil)[:, r, n * 128:(n + 1) * 128]
                    if g == 0:
                        P.op("dve", I("tensor_copy", out=Uv, in_=xq), xt, [tu])
                    else:
                        P.op("dve", I("tensor_tensor", out=Uv, in0=xq, in1=Uv, op=ALU.add), xt + [tu], [tu])

                for idx in range(len(items) + 2):
                    if idx < len(items):
                        st_[idx] = scores(items[idx])
                    if idx >= 2:
                        pv(items[idx - 2], st_.pop(idx - 2))
                    yield

            def normalize(p):
                U2 = UAB[p % 2]
                tu0, tu1 = t("u", p % 2, 0), t("u", p % 2, 1)
                P.dma("sp", "zsw", I("dma_start", out=ZN[0:64, :], in_=U2[0][64:128, :]), reads=[tu0], writes=[t("zn")])
                P.dma("sp", "zsw", I("dma_start", out=ZN[64:128, :], in_=U2[1][0:64, :]), reads=[tu1], writes=[t("zn")])
                P.op("act", I("activation", out=ZN, in_=ZN, func=AF.Ln), [t("zn")], [t("zn")])
                P.op("act", I("activation", out=ZN, in_=ZN, func=AF.Exp, scale=-1.0), [t("zn")], [t("zn")])
                P.op("dve", I("tensor_tensor", out=OT[0:64, p, :], in0=U2[0][0:64, :], in1=ZN[0:64, :], op=ALU.mult),
                     [tu0, t("zn")], [t("ot", p)])
                P.op("dve", I("tensor_tensor", out=OT[64:128, p, :], in0=U2[1][64:128, :], in1=ZN[64:128, :],
                              op=ALU.mult), [tu1, t("zn")], [t("ot", p)])

            pgs = [(p, g) for p in range(8) for g in range(3)]
            for _ in proj_gen(pgs[0][0], pgs[0][1], 0):
                pass
            for k, (p, g) in enumerate(pgs):
                core = core_gen(p, g, k % 2)
                proj = proj_gen(pgs[k + 1][0], pgs[k + 1][1], (k + 1) % 2) if k + 1 < len(pgs) else None
                nstep = 0
                for _ in core:
                    nstep += 1
                    if proj is not None and nstep % 4 == 0:
                        if next(proj, "done") == "done":
                            proj = None
                if proj is not None:
                    for _ in proj:
                        pass
                if g == 2:
                    normalize(p)
            P.barrier()
            for tt in range(4):
                sl = slice(tt * 512, (tt + 1) * 512)
                P.dma("sp", "unpark", I("dma_start", out=H[:, :, sl],
                                        in_=hpark.rearrange("p (c t) -> p c t", c=8)[:, :, sl]),
                      reads=[t("hpark", c) for c in range(8)], writes=[tH(c, tt) for c in range(8)])
            ring["mm"] = [0, 1, 2, 3, 4, 5]
            post_setup(PH + 32768, PH + 49152, PH + 51200)
            mixer_out(l, OT, lambda c, tt: t("ot", c))
            P.barrier()

        outs = []
        for sq_i in range(nseq):
            load_x(sq_i)
            P.barrier()
            for l in layers:
                kind = l % 3
                if do_mixer:
                    if kind == 0:
                        attn_mixer(l, sq_i)
                    elif kind == 1:
                        sc_mixer(l)
                    else:
                        cc_mixer(l)
                if do_ffn:
                    ffn(l)
            ring["mm"] = [0, 1, 2, 3]
            outs += store_y(sq_i)
            P.barrier()
        P.wait_all("sp", outs)
        assert next(keygen, None) is None and not WS["pending"]
        P.emit(block)
    return nc


def _consts():
    c = np.zeros((128, NCST), np.float32)
    c[:, C_ID:C_ID + 128] = np.eye(128, dtype=np.float32)
    sm = np.zeros((128, 128), np.float32)
    for base in (0, 64):
        for m in range(8):
            sm[base + m + 8, base + m] = -1.0
            sm[base + m, base + m + 8] = 1.0
    c[:, C_SMAT:C_SMAT + 128] = sm
    k = np.arange(128)[:, None]
    q = np.arange(128)[None, :]
    c[:, C_MASK:C_MASK + 128] = (k >= q).astype(np.float32)
    c[:, C_MASK + 128:C_MASK + 256] = (k <= q).astype(np.float32)
    inv = (500000.0 ** (-np.arange(0, 16, 2, dtype=np.float32) / 16.0)).astype(np.float32)
    for p in range(128):
        if p % 64 < 16:
            c[p, C_RC] = inv[p % 8] / np.float32(2.0 * math.pi)
    c[:, C_RC + 1] = 1e-6
    c[:, C_RC + 2] = 1e-5
    return c


def _pack_params(inp):
    f = lambda a: np.ascontiguousarray(np.asarray(a, np.float32)).reshape(-1, 128)
    rows = np.concatenate([
        f(inp["mix_norm_pre"]), f(inp["mix_norm_post"]), f(inp["ffn_norm_pre"]), f(inp["ffn_norm_post"]),
        f(inp["sc_w_conv"]), f(inp["cc_b_pw1"]), f(inp["cc_w_dw"]), f(inp["cc_b_dw"]),
        f(inp["cc_ln_g"]), f(inp["cc_ln_b"]), f(inp["cc_b_pw2"]), f(inp["ffn_w_conv"])], axis=0)
    assert rows.shape[0] == 976
    out = np.zeros((1024, 128), np.float32)
    out[:976] = rows
    return out


def _layer_weights(inp, l):
    kind, j = l % 3, l // 3
    a = lambda k, i: np.ascontiguousarray(np.asarray(inp[k][i], np.float32))
    d = {}
    if kind == 0:
        d[f"w{l}_min"] = a("attn_w_qkv", j)
        d[f"w{l}_mout"] = a("attn_w_o", j)
    elif kind == 1:
        d[f"w{l}_min"] = a("sc_w_in", j)
        d[f"w{l}_mout"] = a("sc_w_out", j)
    else:
        d[f"w{l}_min"] = a("cc_w_pw1", j)
        d[f"w{l}_mout"] = a("cc_w_pw2", j)
    d[f"w{l}_fin"] = a("ffn_w_in", l)
    d[f"w{l}_fout"] = a("ffn_w_out", l)
    return d


LAUNCH_GROUPS = [[0, 1, 2, 3]]
_NC_CACHE = {}


def kernel(**inp):
    x = np.ascontiguousarray(np.asarray(inp["x"], np.float32))
    positions = np.ascontiguousarray(np.asarray(inp["positions"], np.int32))
    pth = _pack_params(inp)
    cst = _consts()
    cur = [x[c * SEQ_PER_CORE:(c + 1) * SEQ_PER_CORE].reshape(SEQ_PER_CORE * S, D) for c in range(NCORES)]
    for grp in LAUNCH_GROUPS:
        key = tuple(grp)
        if key not in _NC_CACHE:
            _NC_CACHE[key] = build(list(grp), SEQ_PER_CORE)
        nc = _NC_CACHE[key]
        wd = {}
        for l in grp:
            wd.update(_layer_weights(inp, l))
        in_maps = []
        for c in range(NCORES):
            m = {"x": cur[c], "pos": positions[c * SEQ_PER_CORE:(c + 1) * SEQ_PER_CORE], "pth": pth, "cst": cst}
            m.update(wd)
            in_maps.append(m)
        res = run_bass_kernel_spmd(nc, in_maps, core_ids=list(range(NCORES)))
        cur = [np.asarray(r["y"], np.float32) for r in res.results]
    out = np.stack([c.reshape(SEQ_PER_CORE, S, D) for c in cur], axis=0).reshape(NCORES * SEQ_PER_CORE, S, D)
    return out
```

```python
import contextlib
import math
import numpy as np
import concourse.bass as bass
import concourse.mybir as mybir
from concourse.bass_utils import run_bass_kernel_spmd

F32 = mybir.dt.float32
BF16 = mybir.dt.bfloat16
I32 = mybir.dt.int32
AF = mybir.ActivationFunctionType
ALU = mybir.AluOpType

S = 2048
D = 1024
FF = 2816
NCORES = 8
SEQ_PER_CORE = 4
DEPTH = 4
GROUPS = ((128, 1), (512, 4), (2048, 16))

R_MIXPRE, R_MIXPOST, R_FFNPRE, R_FFNPOST = 0, 32, 64, 96
R_SCCONV = 128
R_CCBPW1 = 152
R_CCWDW = 168
R_CCBDW, R_CCLNG, R_CCLNB, R_CCBPW2 = 416, 424, 432, 440
R_FFNCONV = 448
C_ID, C_SMAT, C_MASK, C_RC = 0, 128, 256, 512
NCST = 640

SEM_CAP = 30000
ENGS = ("pe", "act", "dve", "pool", "sp")


class T:
    __slots__ = ("w", "r")

    def __init__(self):
        self.w = None
        self.r = {}


class TP(T):
    __slots__ = ()


class Prog:
    def __init__(self, nc, stack):
        self.nc = nc
        self.stack = stack
        self.q = {e: [] for e in ENGS}
        self.cnt = {}
        self.seen = {e: {} for e in ENGS}
        self.sems = {}

    def _sem(self, chan, idx):
        lst = self.sems.setdefault(chan, [])
        while len(lst) <= idx:
            lst.append(self.stack.enter_context(self.nc.semaphore(f"s_{chan}_{len(lst)}")))
        return lst[idx]

    def _ev(self, chan, c, step):
        per = SEM_CAP // step
        return self._sem(chan, (c - 1) // per), ((c - 1) % per + 1) * step

    def _deps(self, eng, reads, writes, skip=None):
        deps = {}

        def add(ev):
            if ev is None:
                return
            ch, c = ev
            if (ch == "pe" and eng == "pe") or ch == skip:
                return
            if ch not in ENGS:
                c = self.cnt[ch]
            if deps.get(ch, 0) < c:
                deps[ch] = c
        for t in reads:
            add(t.w)
        for t in writes:
            add(t.w)
            for ch, c in t.r.items():
                add((ch, c))
        out = []
        for ch, c in deps.items():
            if self.seen[eng].get(ch, 0) < c:
                self.seen[eng][ch] = c
                out.append((ch, c))
        return out

    def _mark(self, ev, reads, writes):
        ch, c = ev
        for t in writes:
            t.w = ev
            t.r = {}
        for t in reads:
            if t.r.get(ch, 0) < c:
                t.r[ch] = c

    def op(self, eng, fn, reads=(), writes=()):
        if any(isinstance(t, TP) for t in reads):
            writes = list(writes) + [t for t in reads if isinstance(t, TP)]
            reads = [t for t in reads if not isinstance(t, TP)]
        waits = self._deps(eng, reads, writes)
        c = self.cnt.get(eng, 0) + 1
        self.cnt[eng] = c
        self.q[eng].append((waits, fn, (eng, c, 1)))
        self._mark((eng, c), reads, writes)

    def dma(self, eng, chan, fn, reads=(), writes=()):
        waits = self._deps(eng, reads, writes, skip=chan)
        c = self.cnt.get(chan, 0) + 1
        self.cnt[chan] = c
        self.q[eng].append((waits, fn, (chan, c, 16)))
        self._mark((chan, c), reads, writes)

    def barrier(self):
        for e in ENGS:
            waits = []
            for ch, c in self.cnt.items():
                if ch == e or ch.startswith("w"):
                    continue
                if self.seen[e].get(ch, 0) < c:
                    self.seen[e][ch] = c
                    waits.append((ch, c))
            if waits:
                self.q[e].append((waits, None, None))

    def wait_all(self, eng, tiles):
        waits = self._deps(eng, tiles, ())
        self.q[eng].append((waits, None, None))

    def emit(self, block):
        handles = {"pe": block.tensor, "act": block.scalar, "dve": block.vector,
                   "pool": block.gpsimd, "sp": block.sync}
        for e in ENGS:
            q = self.q[e]
            if not q:
                continue

            def body(h, q=q):
                for waits, fn, inc in q:
                    for ch, c in waits:
                        s, v = self._ev(ch, c, 1 if ch in ENGS else 16)
                        h.wait_ge(s, v)
                    if fn is None:
                        continue
                    ins = None
                    for name, kw in (fn if isinstance(fn, list) else [fn]):
                        ins = getattr(h, name)(**kw)
                    ch, c, step = inc
                    s, v = self._ev(ch, c, step)
                    ins.then_inc(s, step)
            handles[e](body)


def _wkeys(layers, nseq, do_mixer, do_ffn):
    for _ in range(nseq):
        for l in layers:
            kind = l % 3
            if do_mixer:
                if kind == 0:
                    for p in range(8):
                        for g in range(3):
                            yield ("qkv", l, p, g)
                elif kind == 1:
                    for i in range(8):
                        yield ("scin", l, i)
                else:
                    for i in range(8):
                        yield ("pw1", l, i)
                for tt in range(4):
                    for hh in range(2):
                        yield ("mo", l, tt, hh)
            if do_ffn:
                for hf in range(2):
                    for j in range(11):
                        yield ("fin", l, hf, j)
                    for tq in range(2):
                        for o in range(8):
                            yield ("fout", l, hf, tq, o)


def I(name, **kw):
    return (name, kw)


def build(layers, nseq, do_mixer=True, do_ffn=True):
    nc = bass.Bass("TRN2", target_bir_lowering=False)
    dt = nc.dram_tensor
    x = dt("x", [nseq * S, D], F32, kind="ExternalInput").ap()
    y = dt("y", [nseq * S, D], F32, kind="ExternalOutput").ap()
    pos = dt("pos", [nseq, S], I32, kind="ExternalInput").ap()
    pth = dt("pth", [1024, 128], F32, kind="ExternalInput").ap()
    cst = dt("cst", [128, NCST], F32, kind="ExternalInput").ap()
    W = {}
    has_attn = False
    for l in layers:
        kind = l % 3
        if kind == 0:
            has_attn = True
            W[l, "min"] = dt(f"w{l}_min", [D, 9216], F32, kind="ExternalInput").ap()
        elif kind == 1:
            W[l, "min"] = dt(f"w{l}_min", [D, 3072], F32, kind="ExternalInput").ap()
        else:
            W[l, "min"] = dt(f"w{l}_min", [D, 2048], F32, kind="ExternalInput").ap()
        W[l, "mout"] = dt(f"w{l}_mout", [D, D], F32, kind="ExternalInput").ap()
        W[l, "fin"] = dt(f"w{l}_fin", [D, 2 * FF], F32, kind="ExternalInput").ap()
        W[l, "fout"] = dt(f"w{l}_fout", [FF, D], F32, kind="ExternalInput").ap()
    hpark = dt("hpark", [128, 8 * S], F32, kind="Internal").ap() if has_attn else None

    with contextlib.ExitStack() as st:
        ARENA_BYTES = 212480
        arena = st.enter_context(nc.sbuf_tensor("arena", [128, ARENA_BYTES // 4], F32))
        arena_bf = arena.bitcast(BF16)
        arena_i = arena.bitcast(I32)
        pb = [st.enter_context(nc.psum_tensor(f"pb{i}", [128, 512], F32)) for i in range(8)]
        pb_bf = [b.bitcast(BF16) for b in pb]
        block = st.enter_context(nc.Block())
        P = Prog(nc, st)

        def carve(off, n, dtype=F32):
            if dtype == BF16:
                assert off % 2 == 0
                return arena_bf[:, off // 2: off // 2 + n]
            assert off % 4 == 0
            v = arena if dtype == F32 else arena_i
            return v[:, off // 4: off // 4 + n]

        tiles = {}

        def t(*key):
            r = tiles.get(key)
            if r is None:
                r = tiles[key] = T()
            return r

        ptiles = [TP() for _ in range(8)]

        def pbt(i, q0=0, q1=4):
            return [ptiles[i]]

        PTAB = carve(0, 1024)
        IDF = carve(4096, 128)
        ONESB = carve(4608, 128, BF16)
        IDB = carve(4864, 128, BF16)
        SMAT = carve(5120, 128, BF16)
        MASK2 = carve(5376, 256, BF16)
        MASK = MASK2.rearrange("p (j q) -> p j q", j=2)
        RC = carve(5888, 8)
        ONESF = carve(6144, 128)
        BG = carve(6656, 8)
        WS0 = 7168
        NW = 3
        wslots = [carve(WS0 + i * 8192, 4096, BF16) for i in range(NW)]
        A_OFF = WS0 + NW * 8192
        A = carve(A_OFF, 8 * S, BF16).rearrange("p (c t) -> p c t", c=8)
        H_OFF = A_OFF + 32768
        H = carve(H_OFF, 8 * S).rearrange("p (c t) -> p c t", c=8)
        PH = H_OFF + 65536
        assert ARENA_BYTES - PH >= 79904
        EPS_RMS = RC[:, 1:2]
        EPS_LN = RC[:, 2:3]

        def col(r):
            return PTAB[:, r:r + 1]

        def tA(c, tt):
            return t("A", c, tt)

        def tH(c, tt):
            return t("H", c, tt)

        keygen = _wkeys(layers, nseq, do_mixer, do_ffn)
        WS = {"pending": [], "nxt": 0}
        wT = [T() for _ in range(NW)]

        def w_issue():
            key = next(keygen, None)
            if key is None:
                return
            s = WS["nxt"]
            WS["nxt"] = (s + 1) % NW
            slot = wslots[s]
            kind, l = key[0], key[1]
            dmas = []
            if kind == "fin":
                j = key[3]
                v = slot.rearrange("p (c s n) -> p c s n", c=8, s=2)
                src = W[l, "fin"].rearrange("(c p) n -> p c n", p=128)
                for h in range(2):
                    dmas.append((v[:, :, h, :], src[:, :, h * FF + j * 256: h * FF + (j + 1) * 256]))
            elif kind == "fout":
                o = key[4]
                v = slot[:, 0:2816].rearrange("p (k n) -> p k n", k=22)
                src = W[l, "fout"].rearrange("(k p) n -> p k n", p=128)
                dmas.append((v, src[:, :, o * 128:(o + 1) * 128]))
            elif kind in ("qkv", "scin", "pw1"):
                ns = 2 if kind == "pw1" else 3
                v = slot[:, 0:ns * 1024].rearrange("p (c s n) -> p c s n", c=8, s=ns)
                src = W[l, "min"].rearrange("(c p) n -> p c n", p=128)
                for s_ in range(ns):
                    if kind == "qkv":
                        cl = s_ * 3072 + key[3] * 1024 + key[2] * 128
                    else:
                        cl = s_ * 1024 + key[2] * 128
                    dmas.append((v[:, :, s_, :], src[:, :, cl:cl + 128]))
            elif kind == "mo":
                hh = key[3]
                v = slot.rearrange("p (c n) -> p c n", c=8)
                src = W[l, "mout"].rearrange("(c p) n -> p c n", p=128)
                dmas.append((v, src[:, :, hh * 512:(hh + 1) * 512]))
            else:
                raise AssertionError(key)
            for (o_, i_) in dmas:
                P.dma("pool", f"w{s}", I("dma_start", out=o_, in_=i_), writes=[wT[s]])
            WS["pending"].append((key, s))

        def w_acquire(key):
            k, s = WS["pending"].pop(0)
            assert k == key, (k, key)
            return wslots[s], wT[s]

        def w_release():
            w_issue()

        def mm(out, pairs, reads, writes):
            n = len(pairs)
            P.op("pe", [I("matmul", out=out, lhsT=l_, rhs=r_, start=(i == 0), stop=(i == n - 1))
                        for i, (l_, r_) in enumerate(pairs)], reads, writes)

        ring = {"mm": [0, 1, 2, 3], "i": 0}

        def next_bank():
            b = ring["mm"][ring["i"] % len(ring["mm"])]
            ring["i"] += 1
            return b

        flip = {"n": 0}

        def alt_engine():
            flip["n"] += 1
            return "act" if flip["n"] % 2 else "dve"

        def copy_op(eng, out, in_, reads, writes):
            if eng == "act":
                P.op("act", I("activation", out=out, in_=in_, func=AF.Copy), reads, writes)
            else:
                P.op(eng, I("tensor_copy", out=out, in_=in_), reads, writes)

        def rstd_inplace(bank, eps_ap):
            bt = pbt(bank)
            P.op("act", I("activation", out=pb[bank][:], in_=pb[bank][:], func=AF.Ln, bias=eps_ap, scale=1.0 / D),
                 bt + [t("cst")], bt)
            P.op("act", I("activation", out=pb[bank][:], in_=pb[bank][:], func=AF.Exp, scale=-0.5), bt, bt)

        def prenorm(row, sq_off):
            sqs = [carve(sq_off + i * 8192, 4096, BF16).rearrange("p (c n) -> p c n", c=8) for i in range(2)]

            def square(tt):
                sl = slice(tt * 512, (tt + 1) * 512)
                P.op("act", I("activation", out=sqs[tt % 2], in_=H[:, :, sl], func=AF.Square),
                     [tH(c, tt) for c in range(8)], [t("sqpre", tt % 2)])
            square(0)
            square(1)
            for tt in range(4):
                sq = sqs[tt % 2]
                tsq = t("sqpre", tt % 2)
                sl = slice(tt * 512, (tt + 1) * 512)
                bank = 6 + tt % 2
                mm(pb[bank][:], [(ONESB, sq[:, c, :]) for c in range(8)], [tsq, t("cst")], pbt(bank))
                rstd_inplace(bank, EPS_RMS)
                if tt + 2 < 4:
                    square(tt + 2)
                for c in range(8):
                    P.op("dve", I("scalar_tensor_tensor", out=A[:, c, sl], in0=H[:, c, sl], scalar=col(row + c),
                                  in1=pb[bank][:], op0=ALU.mult, op1=ALU.mult),
                         [tH(c, tt), t("ptab")] + pbt(bank), [tA(c, tt)])

        Post = {}

        def post_setup(mb_off, sq_off, mb_off2=None):
            offs = [mb_off, mb_off if mb_off2 is None else mb_off2]
            Post["Mb"] = [carve(o_, 4096).rearrange("p (o n) -> p o n", o=8) for o_ in offs]
            Post["two"] = mb_off2 is not None
            Post["sq"] = [carve(sq_off + i * 1024, 512, BF16) for i in range(2)]
            Post["n"] = 0

        def post_evac(o, bank, tt, grow, brow=None):
            mi = tt % 2 if Post["two"] else 0
            Mb = Post["Mb"][mi]
            sqb = Post["sq"][Post["n"] % 2]
            tsq = t("sqpost", Post["n"] % 2)
            Post["n"] += 1
            sbank = 6 + tt % 2
            bt = pbt(bank)
            if brow is None:
                P.op("act", I("activation", out=sqb, in_=pb[bank][:], func=AF.Square), bt, [tsq])
                P.op("act", I("activation", out=Mb[:, o, :], in_=pb[bank][:], func=AF.Copy, scale=col(grow + o)),
                     bt + [t("ptab")], [t("Mb", mi, o)])
            else:
                P.op("act", I("activation", out=sqb, in_=pb[bank][:], func=AF.Square, bias=col(brow + o)),
                     bt + [t("ptab")], [tsq])
                P.op("act", I("activation", out=Mb[:, o, :], in_=pb[bank][:], func=AF.Identity,
                              scale=col(grow + o), bias=BG[:, o:o + 1]),
                     bt + [t("ptab"), t("bg")], [t("Mb", mi, o)])
            P.op("pe", I("matmul", out=pb[sbank][:], lhsT=ONESB, rhs=sqb, start=(o == 0), stop=(o == 7)),
                 [tsq, t("cst")], pbt(sbank))

        def post_finish(tt):
            mi = tt % 2 if Post["two"] else 0
            Mb = Post["Mb"][mi]
            sbank = 6 + tt % 2
            rstd_inplace(sbank, EPS_RMS)
            sl = slice(tt * 512, (tt + 1) * 512)
            mbt = [t("Mb", mi, o) for o in range(8)]
            P.op("dve", I("tensor_tensor", out=Mb, in0=Mb,
                          in1=pb[sbank][:].unsqueeze(1).broadcast_to([128, 8, 512]), op=ALU.mult),
                 mbt + pbt(sbank), mbt)
            hts = [tH(c, tt) for c in range(8)]
            P.op("pool", I("tensor_tensor", out=H[:, :, sl], in0=H[:, :, sl], in1=Mb, op=ALU.add), mbt + hts, hts)

        def mixer_out(l, Y, ytile, brow=None):
            grow = R_MIXPOST + l * 8
            for tt in range(4):
                sl = slice(tt * 512, (tt + 1) * 512)
                for hh in range(2):
                    slot, wt = w_acquire(("mo", l, tt, hh))
                    v = slot.rearrange("p (c n) -> p c n", c=8)
                    for o4 in range(4):
                        o = hh * 4 + o4
                        bank = next_bank()
                        mm(pb[bank][:], [(v[:, c, o4 * 128:(o4 + 1) * 128], Y[:, c, sl]) for c in range(8)],
                           [wt] + [ytile(c, tt) for c in range(8)], pbt(bank))
                        post_evac(o, bank, tt, grow, brow)
                    w_release()
                post_finish(tt)

        tc_ = t("cst")
        P.dma("sp", "cst", I("dma_start", out=IDF, in_=cst[:, C_ID:C_ID + 128]), writes=[tc_])
        P.dma("sp", "cst", I("dma_start", out=RC, in_=cst[:, C_RC:C_RC + 8]), writes=[tc_])
        P.dma("pool", "cst", I("dma_start", out=IDB, in_=cst[:, C_ID:C_ID + 128]), writes=[tc_])
        P.dma("pool", "cst", I("dma_start", out=SMAT, in_=cst[:, C_SMAT:C_SMAT + 128]), writes=[tc_])
        P.dma("pool", "cst", I("dma_start", out=MASK2, in_=cst[:, C_MASK:C_MASK + 256]), writes=[tc_])
        P.op("dve", I("memset", ap=ONESB, constant=1.0), [], [t("ones")])
        P.op("dve", I("memset", ap=ONESF, constant=1.0), [], [t("ones")])
        pst = carve(PH, 1024).rearrange("p (k f) -> p k f", k=8)
        P.dma("sp", "ld0", I("dma_start", out=pst, in_=pth.rearrange("(k r) f -> r k f", r=128)), writes=[t("pst")])
        for hb in range(2):
            P.op("pe", [I("transpose", out=pb[hb][:, q * 128:(q + 1) * 128], in_=pst[:, hb * 4 + q, :], identity=IDF)
                        for q in range(4)], [t("pst"), tc_], pbt(hb))
            P.op("act", I("activation", out=PTAB[:, hb * 512:(hb + 1) * 512], in_=pb[hb][:], func=AF.Copy),
                 pbt(hb), [t("ptab")])
        for _ in range(NW):
            w_issue()
        P.barrier()

        def load_x(sq_i):
            stg = [carve(PH + i * 4096, 1024) for i in range(2)]
            for j in range(16):
                sb_ = stg[j % 2]
                ts_ = t("stin", j % 2)
                row0 = (sq_i * 16 + j) * 128
                P.dma("sp", f"ld{j % 2}", I("dma_start", out=sb_, in_=x[row0:row0 + 128, :]), writes=[ts_])
                for hb in range(2):
                    bank = next_bank()
                    P.op("pe", [I("transpose", out=pb[bank][:, q * 128:(q + 1) * 128],
                                  in_=sb_[:, (hb * 4 + q) * 128:(hb * 4 + q + 1) * 128], identity=IDF)
                                for q in range(4)], [ts_, tc_], pbt(bank))
                    copy_op(alt_engine(), H[:, hb * 4:hb * 4 + 4, j * 128:(j + 1) * 128],
                            pb[bank][:].rearrange("p (c n) -> p c n", c=4), pbt(bank),
                            [tH(c, j // 4) for c in range(hb * 4, hb * 4 + 4)])

        def store_y(sq_i):
            stg = [carve(PH + 8192 + i * 4096, 1024) for i in range(2)]
            outs = []
            for j in range(16):
                sb_ = stg[j % 2]
                ts_ = t("stout", j % 2)
                for hb in range(2):
                    bank = next_bank()
                    P.op("pe", [I("transpose", out=pb[bank][:, q * 128:(q + 1) * 128],
                                  in_=H[:, hb * 4 + q, j * 128:(j + 1) * 128], identity=IDF) for q in range(4)],
                         [tH(c, j // 4) for c in range(hb * 4, hb * 4 + 4)] + [tc_], pbt(bank))
                    copy_op(alt_engine(), sb_[:, hb * 512:(hb + 1) * 512], pb[bank][:], pbt(bank), [ts_])
                row0 = (sq_i * 16 + j) * 128
                ty = t("yout", sq_i, j)
                P.dma("sp", f"st{j % 2}", I("dma_start", out=y[row0:row0 + 128, :], in_=sb_), reads=[ts_], writes=[ty])
                outs.append(ty)
            return outs

        def ffn(l):
            ring["mm"] = [0, 1, 2, 3, 4, 5]
            Z = carve(PH, 22 * 1024, BF16).rearrange("p (k n) -> p k n", k=22)
            ACC = [[carve(PH + 45056 + (kd * 2 + b) * 4112, 1024) for b in range(2)] for kd in range(2)]
            SG = [carve(PH + 61504 + b * 4096, 1024) for b in range(2)]
            HALO = carve(PH + 80000, 88).rearrange("p (f n) -> p f n", f=44)
            prenorm(R_FFNPRE + l * 8, PH)
            P.barrier()
            npair = 0
            for hf in range(2):
                for j in range(11):
                    slot, wt = w_acquire(("fin", l, hf, j))
                    v = slot.rearrange("p (c s n) -> p c s n", c=8, s=2)
                    for ii in range(2):
                        i = 2 * j + ii
                        buf = npair % 2
                        npair += 1
                        for kd in range(2):
                            ft = kd * 22 + i
                            acc = ACC[kd][buf]
                            tacc = t("acc", kd, buf)
                            r0 = R_FFNCONV + l * 132 + ft
                            r1 = r0 + 44
                            r2 = r0 + 88
                            th = t("halo", ft)
                            bks = []
                            for tq in range(2):
                                tt = hf * 2 + tq
                                sl = slice(tt * 512, (tt + 1) * 512)
                                bank = next_bank()
                                bks.append(bank)
                                mm(pb[bank][:], [(v[:, c, kd, ii * 128:(ii + 1) * 128], A[:, c, sl]) for c in range(8)],
                                   [wt] + [tA(c, tt) for c in range(8)], pbt(bank))
                                P.op("act", I("activation", out=acc[:, tq * 512:(tq + 1) * 512], in_=pb[bank][:],
                                              func=AF.Copy, scale=col(r2)), pbt(bank) + [t("ptab")], [tacc])
                            b0, b1 = bks
                            P.op("dve", I("scalar_tensor_tensor", out=acc[:, 1:513], in0=pb[b0][:],
                                          scalar=col(r1), in1=acc[:, 1:513], op0=ALU.mult, op1=ALU.add),
                                 pbt(b0) + [tacc, t("ptab")], [tacc])
                            P.op("dve", I("scalar_tensor_tensor", out=acc[:, 2:514], in0=pb[b0][:],
                                          scalar=col(r0), in1=acc[:, 2:514], op0=ALU.mult, op1=ALU.add),
                                 pbt(b0) + [tacc, t("ptab")], [tacc])
                            P.op("dve", I("scalar_tensor_tensor", out=acc[:, 513:1024], in0=pb[b1][:, 0:511],
                                          scalar=col(r1), in1=acc[:, 513:1024], op0=ALU.mult, op1=ALU.add),
                                 pbt(b1) + [tacc, t("ptab")], [tacc])
                            P.op("dve", I("scalar_tensor_tensor", out=acc[:, 514:1024], in0=pb[b1][:, 0:510],
                                          scalar=col(r0), in1=acc[:, 514:1024], op0=ALU.mult, op1=ALU.add),
                                 pbt(b1) + [tacc, t("ptab")], [tacc])
                            if hf == 1:
                                P.op("dve", I("tensor_tensor", out=acc[:, 0:2], in0=acc[:, 0:2], in1=HALO[:, ft, :],
                                              op=ALU.add), [tacc, th], [tacc])
                            else:
                                P.op("dve", I("tensor_scalar", out=HALO[:, ft, :], in0=pb[b1][:, 510:512], scalar1=col(r0),
                                              scalar2=None, op0=ALU.mult), pbt(b1) + [t("ptab")], [th])
                                P.op("dve", I("scalar_tensor_tensor", out=HALO[:, ft, 0:1], in0=pb[b1][:, 511:512],
                                              scalar=col(r1), in1=HALO[:, ft, 0:1], op0=ALU.mult, op1=ALU.add),
                                     pbt(b1) + [th, t("ptab")], [th])
                        sg = SG[buf]
                        tsg = t("sg", buf)
                        P.op("act", I("activation", out=sg, in_=ACC[0][buf], func=AF.Silu),
                             [t("acc", 0, buf)], [tsg])
                        P.op("pool", I("tensor_tensor", out=Z[:, i, :], in0=sg, in1=ACC[1][buf], op=ALU.mult),
                             [tsg, t("acc", 1, buf)], [t("Z", i)])
                    w_release()
                P.barrier()
                post_setup(PH + 45056, PH + 61504, PH + 63552)
                for tq in range(2):
                    tt = hf * 2 + tq
                    for o in range(8):
                        slot, wt = w_acquire(("fout", l, hf, tq, o))
                        v = slot[:, 0:2816].rearrange("p (k n) -> p k n", k=22)
                        bank = next_bank()
                        mm(pb[bank][:], [(v[:, k, :], Z[:, k, tq * 512:(tq + 1) * 512]) for k in range(22)],
                           [wt] + [t("Z", k) for k in range(22)], pbt(bank))
                        post_evac(o, bank, tt, R_FFNPOST + l * 8)
                        w_release()
                    post_finish(tt)
                P.barrier()

        def sc_mixer(l):
            ring["mm"] = [0, 1, 2, 3, 4, 5]
            Y = carve(PH, 8 * S, BF16).rearrange("p (c t) -> p c t", c=8)
            CH = carve(PH + 32768, 2056)
            BSB = carve(PH + 40992, 2048)
            HSB = [carve(PH + 49184 + b * 2048, 512) for b in range(2)]
            ACC = carve(PH + 53280, 2048)
            prenorm(R_MIXPRE + l * 8, PH)
            P.barrier()
            P.op("dve", I("memset", ap=CH[:, 0:2], constant=0.0), [], [t("ch")])
            nh = 0
            for i in range(8):
                slot, wt = w_acquire(("scin", l, i))
                v = slot[:, 0:3072].rearrange("p (c s n) -> p c s n", c=8, s=3)
                for tt in range(4):
                    sl = slice(tt * 512, (tt + 1) * 512)
                    banks = [next_bank() for _ in range(3)]
                    for s_ in range(3):
                        mm(pb[banks[s_]][:], [(v[:, c, s_, :], A[:, c, sl]) for c in range(8)],
                           [wt] + [tA(c, tt) for c in range(8)], pbt(banks[s_]))
                    hs = HSB[nh % 2]
                    ths = t("hsb", nh % 2)
                    nh += 1
                    P.op("act", I("activation", out=hs, in_=pb[banks[2]][:], func=AF.Copy), pbt(banks[2]), [ths])
                    P.op("dve", I("tensor_tensor", out=CH[:, 2 + tt * 512:2 + (tt + 1) * 512], in0=pb[banks[1]][:],
                                  in1=hs, op=ALU.mult), pbt(banks[1]) + [ths], [t("ch")])
                    P.op("act", I("activation", out=BSB[:, sl], in_=pb[banks[0]][:], func=AF.Copy),
                         pbt(banks[0]), [t("bsb")])
                w_release()
                r0 = R_SCCONV + i
                P.op("act", I("activation", out=ACC, in_=CH[:, 2:2050], func=AF.Copy, scale=col(r0 + 16)),
                     [t("ch"), t("ptab")], [t("scacc")])
                P.op("dve", I("scalar_tensor_tensor", out=ACC, in0=CH[:, 1:2049], scalar=col(r0 + 8), in1=ACC,
                              op0=ALU.mult, op1=ALU.add), [t("ch"), t("scacc"), t("ptab")], [t("scacc")])
                P.op("dve", I("scalar_tensor_tensor", out=ACC, in0=CH[:, 0:2048], scalar=col(r0), in1=ACC,
                              op0=ALU.mult, op1=ALU.add), [t("ch"), t("scacc"), t("ptab")], [t("scacc")])
                P.op("pool", I("tensor_tensor", out=Y[:, i, :], in0=BSB, in1=ACC, op=ALU.mult),
                     [t("bsb"), t("scacc")], [t("Y", i)])
            P.barrier()
            post_setup(PH + 61472, PH + 77856, PH + 32768)
            mixer_out(l, Y, lambda c, tt: t("Y", c))
            P.barrier()

        def cc_mixer(l):
            ring["mm"] = [0, 1, 2, 3]
            X = carve(PH, 8 * S).rearrange("p (c t) -> p c t", c=8)
            GLU = carve(PH + 65536, 2080)
            SIG = [carve(PH + 73856 + b * 2048, 512) for b in range(2)]
            Yc = A
            prenorm(R_MIXPRE + l * 8, PH)
            P.barrier()
            P.op("dve", I("memset", ap=GLU[:, 0:30], constant=0.0), [], [t("glu")])
            P.op("dve", I("tensor_tensor", out=BG, in0=PTAB[:, R_CCBPW2:R_CCBPW2 + 8],
                          in1=PTAB[:, R_MIXPOST + l * 8:R_MIXPOST + l * 8 + 8], op=ALU.mult), [t("ptab")], [t("bg")])
            ns = 0
            for i in range(8):
                slot, wt = w_acquire(("pw1", l, i))
                v = slot[:, 0:2048].rearrange("p (c s n) -> p c s n", c=8, s=2)
                for tt in range(4):
                    sl = slice(tt * 512, (tt + 1) * 512)
                    ba, bg_ = next_bank(), next_bank()
                    mm(pb[ba][:], [(v[:, c, 0, :], A[:, c, sl]) for c in range(8)],
                       [wt] + [tA(c, tt) for c in range(8)], pbt(ba))
                    mm(pb[bg_][:], [(v[:, c, 1, :], A[:, c, sl]) for c in range(8)],
                       [wt] + [tA(c, tt) for c in range(8)], pbt(bg_))
                    sg = SIG[ns % 2]
                    tsg = t("sig", ns % 2)
                    ns += 1
                    P.op("act", I("activation", out=sg, in_=pb[bg_][:], func=AF.Sigmoid, bias=col(R_CCBPW1 + 8 + i)),
                         pbt(bg_) + [t("ptab")], [tsg])
                    P.op("dve", I("scalar_tensor_tensor", out=GLU[:, 30 + tt * 512:30 + (tt + 1) * 512], in0=pb[ba][:],
                                  scalar=col(R_CCBPW1 + i), in1=sg, op0=ALU.add, op1=ALU.mult),
                         pbt(ba) + [tsg, t("ptab")], [t("glu")])
                w_release()
                tx = [t("X", i, tt) for tt in range(4)]
                P.op("act", I("activation", out=X[:, i, :], in_=GLU[:, 30:30 + S], func=AF.Identity,
                              scale=col(R_CCWDW + 240 + i), bias=col(R_CCBDW + i)), [t("glu"), t("ptab")], tx)
                for k in range(30):
                    P.op("dve", I("scalar_tensor_tensor", out=X[:, i, :], in0=GLU[:, k:k + S],
                                  scalar=col(R_CCWDW + k * 8 + i), in1=X[:, i, :], op0=ALU.mult, op1=ALU.add),
                         [t("glu"), t("ptab")] + tx, tx)
            P.barrier()
            SQ = carve(PH + 65536, 4096, BF16).rearrange("p (c n) -> p c n", c=8)
            MEAN = carve(PH + 73728, 512)
            VAR = carve(PH + 75776, 512)
            for tt in range(4):
                sl = slice(tt * 512, (tt + 1) * 512)
                txs = [t("X", c, tt) for c in range(8)]
                P.op("act", I("activation", out=SQ, in_=X[:, :, sl], func=AF.Square), txs, [t("lnsq")])
                b1, b2 = next_bank(), next_bank()
                mm(pb[b1][:], [(ONESF, X[:, c, sl]) for c in range(8)], txs + [t("ones")], pbt(b1))
                mm(pb[b2][:], [(ONESB, SQ[:, c, :]) for c in range(8)], [t("lnsq"), t("ones")], pbt(b2))
                P.op("dve", I("tensor_scalar", out=MEAN, in0=pb[b1][:], scalar1=1.0 / D, scalar2=None, op0=ALU.mult),
                     pbt(b1), [t("mean")])
                P.op("dve", I("tensor_tensor", out=VAR, in0=MEAN, in1=MEAN, op=ALU.mult), [t("mean")], [t("var")])
                P.op("dve", I("scalar_tensor_tensor", out=VAR, in0=pb[b2][:], scalar=1.0 / D, in1=VAR,
                              op0=ALU.mult, op1=ALU.subtract), pbt(b2) + [t("var")], [t("var")])
                P.op("act", I("activation", out=VAR, in_=VAR, func=AF.Ln, bias=EPS_LN, scale=1.0),
                     [t("var"), tc_], [t("var")])
                P.op("act", I("activation", out=VAR, in_=VAR, func=AF.Exp, scale=-0.5), [t("var")], [t("var")])
                P.op("dve", I("tensor_tensor", out=X[:, :, sl], in0=X[:, :, sl],
                              in1=MEAN.unsqueeze(1).broadcast_to([128, 8, 512]), op=ALU.subtract),
                     txs + [t("mean")], txs)
                P.op("dve", I("tensor_tensor", out=X[:, :, sl], in0=X[:, :, sl],
                              in1=VAR.unsqueeze(1).broadcast_to([128, 8, 512]), op=ALU.mult), txs + [t("var")], txs)
                for c in range(8):
                    P.op("act", I("activation", out=Yc[:, c, sl], in_=X[:, c, sl], func=AF.Silu,
                                  scale=col(R_CCLNG + c), bias=col(R_CCLNB + c)), [t("X", c, tt), t("ptab")], [tA(c, tt)])
            P.barrier()
            post_setup(PH, PH + 16384, PH + 18432)
            mixer_out(l, Yc, tA, brow=R_CCBPW2)
            P.barrier()

        def attn_mixer(l, sq_i):
            ring["mm"] = [0, 1, 2]
            HB = H_OFF
            QKV = [[carve(HB + s_ * 12288 + k * 4096, S, BF16) for k in range(3)] for s_ in range(2)]
            VAB = [[carve(HB + 24576 + s_ * 8192 + k * 4096, 2048, BF16).rearrange("p (b n) -> p b n", b=16)
                    for k in range(2)] for s_ in range(2)]
            OT = carve(PH, 8 * S, BF16).rearrange("p (c t) -> p c t", c=8)
            COS = carve(PH + 32768, S)
            SIN = carve(PH + 40960, S)
            PTR = [carve(PH + 49152 + i * 512, 256, BF16).rearrange("p (j q) -> p j q", j=2) for i in range(4)]
            T12 = [[carve(PH + 51200 + (b * 2 + k) * 2048, 512) for k in range(2)] for b in range(2)]
            POSI = carve(PH + 59392, S, I32)
            XF = carve(PH + 67584, S)
            ZN = carve(PH + 59392, S)
            UAB = [[carve(HB + 40960, S), carve(HB + 49152, S)],
                   [carve(HB + 57344, S), carve(PH + 67584, S)]]
            for c in range(8):
                P.dma("sp", "park", I("dma_start", out=hpark[:, c * S:(c + 1) * S], in_=H[:, c, :]),
                      reads=[tH(c, tt) for tt in range(4)], writes=[t("hpark", c)])
            P.dma("sp", "pos", I("dma_start", out=POSI, in_=pos[sq_i:sq_i + 1, :].broadcast_to([128, S])),
                  writes=[t("posi")])
            P.op("dve", I("tensor_copy", out=XF, in_=POSI), [t("posi")], [t("xf")])
            P.op("dve", I("tensor_scalar", out=XF, in0=XF, scalar1=RC[:, 0:1], scalar2=None, op0=ALU.mult),
                 [t("xf"), tc_], [t("xf")])
            P.op("dve", I("tensor_copy", out=POSI, in_=XF), [t("xf")], [t("posi")])
            P.op("dve", I("tensor_copy", out=COS, in_=POSI), [t("posi")], [t("cos")])
            P.op("dve", I("tensor_tensor", out=XF, in0=XF, in1=COS, op=ALU.subtract), [t("xf"), t("cos")], [t("xf")])
            P.op("dve", I("scalar_tensor_tensor", out=SIN, in0=XF, scalar=0.5, in1=XF, op0=ALU.is_gt, op1=ALU.subtract),
                 [t("xf")], [t("sin")])
            P.op("dve", I("scalar_tensor_tensor", out=SIN, in0=SIN, scalar=0.5, in1=SIN, op0=ALU.is_gt, op1=ALU.subtract),
                 [t("sin")], [t("sin")])
            P.op("dve", I("tensor_scalar", out=XF, in0=XF, scalar1=0.25, scalar2=None, op0=ALU.add), [t("xf")], [t("xf")])
            P.op("dve", I("scalar_tensor_tensor", out=COS, in0=XF, scalar=0.5, in1=XF, op0=ALU.is_gt, op1=ALU.subtract),
                 [t("xf"), t("cos")], [t("cos")])
            P.op("dve", I("scalar_tensor_tensor", out=COS, in0=COS, scalar=0.5, in1=COS, op0=ALU.is_gt, op1=ALU.subtract),
                 [t("cos")], [t("cos")])
            TWO_PI = 2.0 * math.pi * (1.0 - 2e-7)
            P.op("act", I("activation", out=SIN, in_=SIN, func=AF.Sin, scale=TWO_PI), [t("sin")], [t("sin")])
            P.op("act", I("activation", out=COS, in_=COS, func=AF.Sin, scale=TWO_PI), [t("cos")], [t("cos")])
            prenorm(R_MIXPRE + l * 8, PH)
            P.barrier()
            for s_ in range(2):
                P.op("pool", I("memset", ap=VAB[s_][0][:, :, 64:128], constant=1.0), [], [t("va", s_)])
                P.op("pool", I("memset", ap=VAB[s_][1][:, :, 0:64], constant=1.0), [], [t("vb", s_)])
            cnt = {"rope": 0, "idx": 0}

            def proj_gen(p, g, set_):
                dil = GROUPS[g][1]
                nb = (S // dil) // 128
                VA, VB = VAB[set_]
                tq_ = [[t("qkv", set_, k, tt_) for tt_ in range(4)] for k in range(3)]
                slot, wt = w_acquire(("qkv", l, p, g))
                v = slot[:, 0:3072].rearrange("p (c s n) -> p c s n", c=8, s=3)
                pend = None
                for s3 in range(3):
                    dst = QKV[set_][s3]
                    for tt in range(4):
                        sl = slice(tt * 512, (tt + 1) * 512)
                        bank = next_bank()
                        bt = pbt(bank)
                        for c0 in range(0, 8, 2):
                            P.op("pe", [I("matmul", out=pb[bank][:], lhsT=v[:, c, s3, :], rhs=A[:, c, sl],
                                          start=(c == 0), stop=(c == 7)) for c in range(c0, c0 + 2)],
                                 [wt] + [tA(c, tt) for c in range(8)], bt)
                            if c0 == 0 and pend is not None:
                                pend()
                                pend = None
                            if c0 < 6:
                                yield
                        P.op("act", I("activation", out=dst[:, sl], in_=pb[bank][:], func=AF.Copy), bt, [tq_[s3][tt]])
                        if s3 < 2:
                            nr = cnt["rope"]
                            cnt["rope"] += 1
                            t1, t2 = T12[nr % 2]
                            tt1, tt2 = t("t1", nr % 2), t("t2", nr % 2)
                            P.op("dve", I("tensor_tensor", out=t1[0:80, :], in0=pb[bank][0:80, :], in1=COS[0:80, sl],
                                          op=ALU.mult), bt + [t("cos")], [tt1])

                            def rope_tail(dst=dst, sl=sl, t1=t1, t2=t2, tt1=tt1, tt2=tt2, tq=tq_[s3][tt]):
                                mm(pb[3][0:80, :], [(SMAT[:, 0:80], dst[:, sl])], [tq, tc_], pbt(3))
                                P.op("dve", I("tensor_tensor", out=t2[0:80, :], in0=pb[3][0:80, :], in1=SIN[0:80, sl],
                                              op=ALU.mult), pbt(3) + [t("sin")], [tt2])
                                P.op("dve", I("tensor_tensor", out=dst[0:80, sl], in0=t1[0:80, :], in1=t2[0:80, :],
                                              op=ALU.add), [tt1, tt2], [tq])
                            pend = rope_tail
                        yield
                if pend is not None:
                    pend()
                w_release()
                VTv = QKV[set_][2].rearrange("p (m d) -> p d m", d=dil)
                for h8 in range(2):
                    bank = next_bank()
                    ins = []
                    for b8 in range(8):
                        blk = h8 * 8 + b8
                        r, n = blk // nb, blk % nb
                        ins.append(I("transpose", out=pb_bf[bank][:, b8 * 128:(b8 + 1) * 128],
                                     in_=VTv[:, r, n * 128:(n + 1) * 128], identity=IDB))
                    P.op("pe", ins, tq_[2] + [tc_], pbt(bank))
                    src = pb_bf[bank][:].rearrange("p (b n) -> p b n", b=8)
                    P.op("dve", I("tensor_copy", out=VA[:, h8 * 8:h8 * 8 + 8, 0:64], in_=src[:, :, 0:64]),
                         pbt(bank), [t("va", set_)])
                    P.op("act", I("activation", out=VB[:, h8 * 8:h8 * 8 + 8, 64:128], in_=src[:, :, 64:128],
                                  func=AF.Copy), pbt(bank), [t("vb", set_)])
                    yield

            def core_gen(p, g, set_):
                dil = GROUPS[g][1]
                nb = (S // dil) // 128
                QT, KT, VT = QKV[set_]
                VA, VB = VAB[set_]
                U2 = UAB[p % 2]
                tq_ = [[t("qkv", set_, k, tt_) for tt_ in range(4)] for k in range(3)]
                QTv = QT.rearrange("p (m d) -> p d m", d=dil)
                KTv = KT.rearrange("p (m d) -> p d m", d=dil)
                items = [(r, n, hd) for r in range(dil) for n in range(nb) for hd in range(2)]
                st_ = {}

                def scores(it):
                    r, n, hd = it
                    hb = hd * 64
                    gi = cnt["idx"]
                    cnt["idx"] += 1
                    sbank = 4 + gi % 2
                    stv = pb[sbank][:, 0:256].rearrange("p (j q) -> p j q", j=2)
                    stt = pbt(sbank)
                    j0 = 0 if n > 0 else 1
                    P.op("pe", [I("matmul", out=stv[:, jj, :], lhsT=KTv[hb:hb + 64, r, (n - 1 + jj) * 128:(n + jj) * 128],
                                  rhs=QTv[hb:hb + 64, r, n * 128:(n + 1) * 128], start=True, stop=True)
                                for jj in range(j0, 2)], tq_[0] + tq_[1], stt)
                    pt = PTR[gi % 4]
                    tpt = t("pt", gi % 4)
                    P.op("act", I("activation", out=pt[:, j0:2, :], in_=stv[:, j0:2, :], func=AF.Exp, scale=0.125),
                         stt, [tpt])
                    P.op("pool",
                         I("tensor_tensor", out=pt[:, j0:2, :], in0=pt[:, j0:2, :], in1=MASK[:, j0:2, :],
                           op=ALU.mult), [tpt, tc_], [tpt])
                    return (pt, tpt, j0, gi)

                def pv(it, stf):
                    r, n, hd = it
                    pt, tpt, j0, gi = stf
                    xb = 6 + gi % 2
                    xq = pb[xb][:, 0:128]
                    xt = pbt(xb)
                    Vh = VA if hd == 0 else VB
                    tv = t("va", set_) if hd == 0 else t("vb", set_)
                    P.op("pe", [I("matmul", out=xq, lhsT=Vh[:, r * nb + n - 1 + jj, :], rhs=pt[:, jj, :],
                                  start=(jj == j0), stop=(jj == 1)) for jj in range(j0, 2)], [tpt, tv], xt)
                    tu = t("u", p % 2, hd)
                    Uv = U2[hd].rearrange("p (m d) -> p d m", d=dil)[:, r, n * 128:(n + 1) * 128]
                    if g == 0:
                        P.op("dve", I("tensor_copy", out=Uv, in_=xq), xt, [tu])
                    else:
                        P.op("dve", I("tensor_tensor", out=Uv, in0=xq, in1=Uv, op=ALU.add), xt + [tu], [tu])

                SK = 3
                for idx in range(len(items) + SK):
                    if idx < len(items):
                        st_[idx] = scores(items[idx])
                    if idx >= SK:
                        pv(items[idx - SK], st_.pop(idx - SK))
                    yield

            def normalize(p):
                U2 = UAB[p % 2]
                tu0, tu1 = t("u", p % 2, 0), t("u", p % 2, 1)
                P.dma("sp", "zsw", I("dma_start", out=ZN[0:64, :], in_=U2[0][64:128, :]), reads=[tu0], writes=[t("zn")])
                P.dma("sp", "zsw", I("dma_start", out=ZN[64:128, :], in_=U2[1][0:64, :]), reads=[tu1], writes=[t("zn")])
                P.op("act", I("activation", out=ZN, in_=ZN, func=AF.Ln), [t("zn")], [t("zn")])
                P.op("act", I("activation", out=ZN, in_=ZN, func=AF.Exp, scale=-1.0), [t("zn")], [t("zn")])
                P.op("dve", I("tensor_tensor", out=OT[0:64, p, :], in0=U2[0][0:64, :], in1=ZN[0:64, :], op=ALU.mult),
                     [tu0, t("zn")], [t("ot", p)])
                P.op("dve", I("tensor_tensor", out=OT[64:128, p, :], in0=U2[1][64:128, :], in1=ZN[64:128, :],
                              op=ALU.mult), [tu1, t("zn")], [t("ot", p)])

            pgs = [(p, g) for p in range(8) for g in range(3)]
            for _ in proj_gen(pgs[0][0], pgs[0][1], 0):
                pass
            for k, (p, g) in enumerate(pgs):
                core = core_gen(p, g, k % 2)
                proj = proj_gen(pgs[k + 1][0], pgs[k + 1][1], (k + 1) % 2) if k + 1 < len(pgs) else None
                nstep = 0
                for _ in core:
                    nstep += 1
                    if proj is not None:
                        if next(proj, "done") == "done":
                            proj = None
                if proj is not None:
                    for _ in proj:
                        pass
                if g == 2:
                    normalize(p)
            P.barrier()
            for tt in range(4):
                sl = slice(tt * 512, (tt + 1) * 512)
                P.dma("sp", "unpark", I("dma_start", out=H[:, :, sl],
                                        in_=hpark.rearrange("p (c t) -> p c t", c=8)[:, :, sl]),
                      reads=[t("hpark", c) for c in range(8)], writes=[tH(c, tt) for c in range(8)])
            ring["mm"] = [0, 1, 2, 3, 4, 5]
            post_setup(PH + 32768, PH + 49152, PH + 51200)
            mixer_out(l, OT, lambda c, tt: t("ot", c))
            P.barrier()

        outs = []
        for sq_i in range(nseq):
            load_x(sq_i)
            P.barrier()
            for l in layers:
                kind = l % 3
                if do_mixer:
                    if kind == 0:
                        attn_mixer(l, sq_i)
                    elif kind == 1:
                        sc_mixer(l)
                    else:
                        cc_mixer(l)
                if do_ffn:
                    ffn(l)
            ring["mm"] = [0, 1, 2, 3]
            outs += store_y(sq_i)
            P.barrier()
        P.wait_all("sp", outs)
        assert next(keygen, None) is None and not WS["pending"]
        P.emit(block)
    return nc


def _consts():
    c = np.zeros((128, NCST), np.float32)
    c[:, C_ID:C_ID + 128] = np.eye(128, dtype=np.float32)
    sm = np.zeros((128, 128), np.float32)
    for base in (0, 64):
        for m in range(8):
            sm[base + m + 8, base + m] = -1.0
            sm[base + m, base + m + 8] = 1.0
    c[:, C_SMAT:C_SMAT + 128] = sm
    k = np.arange(128)[:, None]
    q = np.arange(128)[None, :]
    c[:, C_MASK:C_MASK + 128] = (k >= q).astype(np.float32)
    c[:, C_MASK + 128:C_MASK + 256] = (k <= q).astype(np.float32)
    inv = (500000.0 ** (-np.arange(0, 16, 2, dtype=np.float32) / 16.0)).astype(np.float32)
    for p in range(128):
        if p % 64 < 16:
            c[p, C_RC] = inv[p % 8] / np.float32(2.0 * math.pi)
    c[:, C_RC + 1] = 1e-6
    c[:, C_RC + 2] = 1e-5
    return c


def _pack_params(inp):
    f = lambda a: np.ascontiguousarray(np.asarray(a, np.float32)).reshape(-1, 128)
    rows = np.concatenate([
        f(inp["mix_norm_pre"]), f(inp["mix_norm_post"]), f(inp["ffn_norm_pre"]), f(inp["ffn_norm_post"]),
        f(inp["sc_w_conv"]), f(inp["cc_b_pw1"]), f(inp["cc_w_dw"]), f(inp["cc_b_dw"]),
        f(inp["cc_ln_g"]), f(inp["cc_ln_b"]), f(inp["cc_b_pw2"]), f(inp["ffn_w_conv"])], axis=0)
    assert rows.shape[0] == 976
    out = np.zeros((1024, 128), np.float32)
    out[:976] = rows
    return out


def _layer_weights(inp, l):
    kind, j = l % 3, l // 3
    a = lambda k, i: np.ascontiguousarray(np.asarray(inp[k][i], np.float32))
    d = {}
    if kind == 0:
        d[f"w{l}_min"] = a("attn_w_qkv", j)
        d[f"w{l}_mout"] = a("attn_w_o", j)
    elif kind == 1:
        d[f"w{l}_min"] = a("sc_w_in", j)
        d[f"w{l}_mout"] = a("sc_w_out", j)
    else:
        d[f"w{l}_min"] = a("cc_w_pw1", j)
        d[f"w{l}_mout"] = a("cc_w_pw2", j)
    d[f"w{l}_fin"] = a("ffn_w_in", l)
    d[f"w{l}_fout"] = a("ffn_w_out", l)
    return d


LAUNCH_GROUPS = [[0, 1, 2, 3]]
_NC_CACHE = {}


def kernel(**inp):
    x = np.ascontiguousarray(np.asarray(inp["x"], np.float32))
    positions = np.ascontiguousarray(np.asarray(inp["positions"], np.int32))
    pth = _pack_params(inp)
    cst = _consts()
    cur = [x[c * SEQ_PER_CORE:(c + 1) * SEQ_PER_CORE].reshape(SEQ_PER_CORE * S, D) for c in range(NCORES)]
    for grp in LAUNCH_GROUPS:
        key = tuple(grp)
        if key not in _NC_CACHE:
            _NC_CACHE[key] = build(list(grp), SEQ_PER_CORE)
        nc = _NC_CACHE[key]
        wd = {}
        for l in grp:
            wd.update(_layer_weights(inp, l))
        in_maps = []
        for c in range(NCORES):
            m = {"x": cur[c], "pos": positions[c * SEQ_PER_CORE:(c + 1) * SEQ_PER_CORE], "pth": pth, "cst": cst}
            m.update(wd)
            in_maps.append(m)
        res = run_bass_kernel_spmd(nc, in_maps, core_ids=list(range(NCORES)))
        cur = [np.asarray(r["y"], np.float32) for r in res.results]
    out = np.stack([c.reshape(SEQ_PER_CORE, S, D) for c in cur], axis=0).reshape(NCORES * SEQ_PER_CORE, S, D)
    return out
```

```python
import contextlib
import math
import numpy as np
import concourse.bass as bass
import concourse.mybir as mybir
from concourse.bass_utils import run_bass_kernel_spmd

F32 = mybir.dt.float32
BF16 = mybir.dt.bfloat16
I32 = mybir.dt.int32
AF = mybir.ActivationFunctionType
ALU = mybir.AluOpType

S = 2048
D = 1024
FF = 2816
NCORES = 8
SEQ_PER_CORE = 4
DEPTH = 4
GROUPS = ((128, 1), (512, 4), (2048, 16))

R_MIXPRE, R_MIXPOST, R_FFNPRE, R_FFNPOST = 0, 32, 64, 96
R_SCCONV = 128
R_CCBPW1 = 152
R_CCWDW = 168
R_CCBDW, R_CCLNG, R_CCLNB, R_CCBPW2 = 416, 424, 432, 440
R_FFNCONV = 448
C_ID, C_SMAT, C_MASK, C_RC = 0, 128, 256, 512
NCST = 640

SEM_CAP = 30000
ENGS = ("pe", "act", "dve", "pool", "sp")


class T:
    __slots__ = ("w", "r")

    def __init__(self):
        self.w = None
        self.r = {}


class TP(T):
    __slots__ = ()


class Prog:
    def __init__(self, nc, stack):
        self.nc = nc
        self.stack = stack
        self.q = {e: [] for e in ENGS}
        self.cnt = {}
        self.seen = {e: {} for e in ENGS}
        self.sems = {}

    def _sem(self, chan, idx):
        lst = self.sems.setdefault(chan, [])
        while len(lst) <= idx:
            lst.append(self.stack.enter_context(self.nc.semaphore(f"s_{chan}_{len(lst)}")))
        return lst[idx]

    def _ev(self, chan, c, step):
        per = SEM_CAP // step
        return self._sem(chan, (c - 1) // per), ((c - 1) % per + 1) * step

    def _deps(self, eng, reads, writes, skip=None):
        deps = {}

        def add(ev):
            if ev is None:
                return
            ch, c = ev
            if (ch == "pe" and eng == "pe") or ch == skip:
                return
            if ch not in ENGS:
                c = self.cnt[ch]
            if deps.get(ch, 0) < c:
                deps[ch] = c
        for t in reads:
            add(t.w)
        for t in writes:
            add(t.w)
            for ch, c in t.r.items():
                add((ch, c))
        out = []
        for ch, c in deps.items():
            if self.seen[eng].get(ch, 0) < c:
                self.seen[eng][ch] = c
                out.append((ch, c))
        return out

    def _mark(self, ev, reads, writes):
        ch, c = ev
        for t in writes:
            t.w = ev
            t.r = {}
        for t in reads:
            if t.r.get(ch, 0) < c:
                t.r[ch] = c

    def op(self, eng, fn, reads=(), writes=()):
        if any(isinstance(t, TP) for t in reads):
            writes = list(writes) + [t for t in reads if isinstance(t, TP)]
            reads = [t for t in reads if not isinstance(t, TP)]
        waits = self._deps(eng, reads, writes)
        c = self.cnt.get(eng, 0) + 1
        self.cnt[eng] = c
        self.q[eng].append((waits, fn, (eng, c, 1)))
        self._mark((eng, c), reads, writes)

    def dma(self, eng, chan, fn, reads=(), writes=()):
        waits = self._deps(eng, reads, writes, skip=chan)
        c = self.cnt.get(chan, 0) + 1
        self.cnt[chan] = c
        self.q[eng].append((waits, fn, (chan, c, 16)))
        self._mark((chan, c), reads, writes)

    def barrier(self):
        for e in ENGS:
            waits = []
            for ch, c in self.cnt.items():
                if ch == e or ch.startswith("w"):
                    continue
                if self.seen[e].get(ch, 0) < c:
                    self.seen[e][ch] = c
                    waits.append((ch, c))
            if waits:
                self.q[e].append((waits, None, None))

    def wait_all(self, eng, tiles):
        waits = self._deps(eng, tiles, ())
        self.q[eng].append((waits, None, None))

    def emit(self, block):
        handles = {"pe": block.tensor, "act": block.scalar, "dve": block.vector,
                   "pool": block.gpsimd, "sp": block.sync}
        for e in ENGS:
            q = self.q[e]
            if not q:
                continue

            def body(h, q=q):
                for waits, fn, inc in q:
                    for ch, c in waits:
                        s, v = self._ev(ch, c, 1 if ch in ENGS else 16)
                        h.wait_ge(s, v)
                    if fn is None:
                        continue
                    ins = None
                    for name, kw in (fn if isinstance(fn, list) else [fn]):
                        ins = getattr(h, name)(**kw)
                    ch, c, step = inc
                    s, v = self._ev(ch, c, step)
                    ins.then_inc(s, step)
            handles[e](body)


def _wkeys(layers, nseq, do_mixer, do_ffn):
    for _ in range(nseq):
        for l in layers:
            kind = l % 3
            if do_mixer:
                if kind == 0:
                    for p in range(8):
                        for g in range(3):
                            yield ("qkv", l, p, g)
                elif kind == 1:
                    for i in range(8):
                        yield ("scin", l, i)
                else:
                    for i in range(8):
                        yield ("pw1", l, i)
                for tt in range(4):
                    for hh in range(2):
                        yield ("mo", l, tt, hh)
            if do_ffn:
                for hf in range(2):
                    for j in range(11):
                        yield ("fin", l, hf, j)
                    for tq in range(2):
                        for o in range(8):
                            yield ("fout", l, hf, tq, o)


def I(name, **kw):
    return (name, kw)


def build(layers, nseq, do_mixer=True, do_ffn=True):
    nc = bass.Bass("TRN2", target_bir_lowering=False)
    dt = nc.dram_tensor
    x = dt("x", [nseq * S, D], F32, kind="ExternalInput").ap()
    y = dt("y", [nseq * S, D], F32, kind="ExternalOutput").ap()
    pos = dt("pos", [nseq, S], I32, kind="ExternalInput").ap()
    pth = dt("pth", [1024, 128], F32, kind="ExternalInput").ap()
    cst = dt("cst", [128, NCST], F32, kind="ExternalInput").ap()
    W = {}
    has_attn = False
    for l in layers:
        kind = l % 3
        if kind == 0:
            has_attn = True
            W[l, "min"] = dt(f"w{l}_min", [D, 9216], F32, kind="ExternalInput").ap()
        elif kind == 1:
            W[l, "min"] = dt(f"w{l}_min", [D, 3072], F32, kind="ExternalInput").ap()
        else:
            W[l, "min"] = dt(f"w{l}_min", [D, 2048], F32, kind="ExternalInput").ap()
        W[l, "mout"] = dt(f"w{l}_mout", [D, D], F32, kind="ExternalInput").ap()
        W[l, "fin"] = dt(f"w{l}_fin", [D, 2 * FF], F32, kind="ExternalInput").ap()
        W[l, "fout"] = dt(f"w{l}_fout", [FF, D], F32, kind="ExternalInput").ap()
    hpark = dt("hpark", [128, 8 * S], F32, kind="Internal").ap() if has_attn else None

    with contextlib.ExitStack() as st:
        ARENA_BYTES = 212480
        arena = st.enter_context(nc.sbuf_tensor("arena", [128, ARENA_BYTES // 4], F32))
        arena_bf = arena.bitcast(BF16)
        arena_i = arena.bitcast(I32)
        pb = [st.enter_context(nc.psum_tensor(f"pb{i}", [128, 512], F32)) for i in range(8)]
        pb_bf = [b.bitcast(BF16) for b in pb]
        block = st.enter_context(nc.Block())
        P = Prog(nc, st)

        def carve(off, n, dtype=F32):
            if dtype == BF16:
                assert off % 2 == 0
                return arena_bf[:, off // 2: off // 2 + n]
            assert off % 4 == 0
            v = arena if dtype == F32 else arena_i
            return v[:, off // 4: off // 4 + n]

        tiles = {}

        def t(*key):
            r = tiles.get(key)
            if r is None:
                r = tiles[key] = T()
            return r

        ptiles = [TP() for _ in range(8)]

        def pbt(i, q0=0, q1=4):
            return [ptiles[i]]

        PTAB = carve(0, 1024)
        IDF = carve(4096, 128)
        ONESB = carve(4608, 128, BF16)
        IDB = carve(4864, 128, BF16)
        SMAT = carve(5120, 128, BF16)
        MASK2 = carve(5376, 256, BF16)
        MASK = MASK2.rearrange("p (j q) -> p j q", j=2)
        RC = carve(5888, 8)
        ONESF = carve(6144, 128)
        BG = carve(6656, 8)
        WS0 = 7168
        NW = 3
        wslots = [carve(WS0 + i * 8192, 4096, BF16) for i in range(NW)]
        A_OFF = WS0 + NW * 8192
        A = carve(A_OFF, 8 * S, BF16).rearrange("p (c t) -> p c t", c=8)
        H_OFF = A_OFF + 32768
        H = carve(H_OFF, 8 * S).rearrange("p (c t) -> p c t", c=8)
        PH = H_OFF + 65536
        assert ARENA_BYTES - PH >= 79904
        EPS_RMS = RC[:, 1:2]
        EPS_LN = RC[:, 2:3]

        def col(r):
            return PTAB[:, r:r + 1]

        def tA(c, tt):
            return t("A", c, tt)

        def tH(c, tt):
            return t("H", c, tt)

        keygen = _wkeys(layers, nseq, do_mixer, do_ffn)
        WS = {"pending": [], "nxt": 0}
        wT = [T() for _ in range(NW)]

        def w_issue():
            key = next(keygen, None)
            if key is None:
                return
            s = WS["nxt"]
            WS["nxt"] = (s + 1) % NW
            slot = wslots[s]
            kind, l = key[0], key[1]
            dmas = []
            if kind == "fin":
                j = key[3]
                v = slot.rearrange("p (c s n) -> p c s n", c=8, s=2)
                src = W[l, "fin"].rearrange("(c p) n -> p c n", p=128)
                for h in range(2):
                    dmas.append((v[:, :, h, :], src[:, :, h * FF + j * 256: h * FF + (j + 1) * 256]))
            elif kind == "fout":
                o = key[4]
                v = slot[:, 0:2816].rearrange("p (k n) -> p k n", k=22)
                src = W[l, "fout"].rearrange("(k p) n -> p k n", p=128)
                dmas.append((v, src[:, :, o * 128:(o + 1) * 128]))
            elif kind in ("qkv", "scin", "pw1"):
                ns = 2 if kind == "pw1" else 3
                v = slot[:, 0:ns * 1024].rearrange("p (c s n) -> p c s n", c=8, s=ns)
                src = W[l, "min"].rearrange("(c p) n -> p c n", p=128)
                for s_ in range(ns):
                    if kind == "qkv":
                        cl = s_ * 3072 + key[3] * 1024 + key[2] * 128
                    else:
                        cl = s_ * 1024 + key[2] * 128
                    dmas.append((v[:, :, s_, :], src[:, :, cl:cl + 128]))
            elif kind == "mo":
                hh = key[3]
                v = slot.rearrange("p (c n) -> p c n", c=8)
                src = W[l, "mout"].rearrange("(c p) n -> p c n", p=128)
                dmas.append((v, src[:, :, hh * 512:(hh + 1) * 512]))
            else:
                raise AssertionError(key)
            for (o_, i_) in dmas:
                P.dma("pool", f"w{s}", I("dma_start", out=o_, in_=i_), writes=[wT[s]])
            WS["pending"].append((key, s))

        def w_acquire(key):
            k, s = WS["pending"].pop(0)
            assert k == key, (k, key)
            return wslots[s], wT[s]

        def w_release():
            w_issue()

        def mm(out, pairs, reads, writes):
            n = len(pairs)
            P.op("pe", [I("matmul", out=out, lhsT=l_, rhs=r_, start=(i == 0), stop=(i == n - 1))
                        for i, (l_, r_) in enumerate(pairs)], reads, writes)

        ring = {"mm": [0, 1, 2, 3], "i": 0}

        def next_bank():
            b = ring["mm"][ring["i"] % len(ring["mm"])]
            ring["i"] += 1
            return b

        flip = {"n": 0}

        def alt_engine():
            flip["n"] += 1
            return "act" if flip["n"] % 2 else "dve"

        def copy_op(eng, out, in_, reads, writes):
            if eng == "act":
                P.op("act", I("activation", out=out, in_=in_, func=AF.Copy), reads, writes)
            else:
                P.op(eng, I("tensor_copy", out=out, in_=in_), reads, writes)

        def rstd_inplace(bank, eps_ap):
            bt = pbt(bank)
            P.op("act", I("activation", out=pb[bank][:], in_=pb[bank][:], func=AF.Ln, bias=eps_ap, scale=1.0 / D),
                 bt + [t("cst")], bt)
            P.op("act", I("activation", out=pb[bank][:], in_=pb[bank][:], func=AF.Exp, scale=-0.5), bt, bt)

        def prenorm(row, sq_off):
            sqs = [carve(sq_off + i * 8192, 4096, BF16).rearrange("p (c n) -> p c n", c=8) for i in range(2)]

            def square(tt):
                sl = slice(tt * 512, (tt + 1) * 512)
                P.op("act", I("activation", out=sqs[tt % 2], in_=H[:, :, sl], func=AF.Square),
                     [tH(c, tt) for c in range(8)], [t("sqpre", tt % 2)])
            square(0)
            square(1)
            for tt in range(4):
                sq = sqs[tt % 2]
                tsq = t("sqpre", tt % 2)
                sl = slice(tt * 512, (tt + 1) * 512)
                bank = 6 + tt % 2
                mm(pb[bank][:], [(ONESB, sq[:, c, :]) for c in range(8)], [tsq, t("cst")], pbt(bank))
                rstd_inplace(bank, EPS_RMS)
                if tt + 2 < 4:
                    square(tt + 2)
                for c in range(8):
                    P.op("dve", I("scalar_tensor_tensor", out=A[:, c, sl], in0=H[:, c, sl], scalar=col(row + c),
                                  in1=pb[bank][:], op0=ALU.mult, op1=ALU.mult),
                         [tH(c, tt), t("ptab")] + pbt(bank), [tA(c, tt)])

        Post = {}

        def post_setup(mb_off, sq_off, mb_off2=None):
            offs = [mb_off, mb_off if mb_off2 is None else mb_off2]
            Post["Mb"] = [carve(o_, 4096).rearrange("p (o n) -> p o n", o=8) for o_ in offs]
            Post["two"] = mb_off2 is not None
            Post["sq"] = [carve(sq_off + i * 1024, 512, BF16) for i in range(2)]
            Post["n"] = 0

        def post_evac(o, bank, tt, grow, brow=None):
            mi = tt % 2 if Post["two"] else 0
            Mb = Post["Mb"][mi]
            sqb = Post["sq"][Post["n"] % 2]
            tsq = t("sqpost", Post["n"] % 2)
            Post["n"] += 1
            sbank = 6 + tt % 2
            bt = pbt(bank)
            if brow is None:
                P.op("act", I("activation", out=sqb, in_=pb[bank][:], func=AF.Square), bt, [tsq])
                P.op("act", I("activation", out=Mb[:, o, :], in_=pb[bank][:], func=AF.Copy, scale=col(grow + o)),
                     bt + [t("ptab")], [t("Mb", mi, o)])
            else:
                P.op("act", I("activation", out=sqb, in_=pb[bank][:], func=AF.Square, bias=col(brow + o)),
                     bt + [t("ptab")], [tsq])
                P.op("act", I("activation", out=Mb[:, o, :], in_=pb[bank][:], func=AF.Identity,
                              scale=col(grow + o), bias=BG[:, o:o + 1]),
                     bt + [t("ptab"), t("bg")], [t("Mb", mi, o)])
            P.op("pe", I("matmul", out=pb[sbank][:], lhsT=ONESB, rhs=sqb, start=(o == 0), stop=(o == 7)),
                 [tsq, t("cst")], pbt(sbank))

        def post_finish(tt):
            mi = tt % 2 if Post["two"] else 0
            Mb = Post["Mb"][mi]
            sbank = 6 + tt % 2
            rstd_inplace(sbank, EPS_RMS)
            sl = slice(tt * 512, (tt + 1) * 512)
            mbt = [t("Mb", mi, o) for o in range(8)]
            P.op("dve", I("tensor_tensor", out=Mb, in0=Mb,
                          in1=pb[sbank][:].unsqueeze(1).broadcast_to([128, 8, 512]), op=ALU.mult),
                 mbt + pbt(sbank), mbt)
            hts = [tH(c, tt) for c in range(8)]
            P.op("pool", I("tensor_tensor", out=H[:, :, sl], in0=H[:, :, sl], in1=Mb, op=ALU.add), mbt + hts, hts)

        def mixer_out(l, Y, ytile, brow=None):
            grow = R_MIXPOST + l * 8
            for tt in range(4):
                sl = slice(tt * 512, (tt + 1) * 512)
                for hh in range(2):
                    slot, wt = w_acquire(("mo", l, tt, hh))
                    v = slot.rearrange("p (c n) -> p c n", c=8)
                    for o4 in range(4):
                        o = hh * 4 + o4
                        bank = next_bank()
                        mm(pb[bank][:], [(v[:, c, o4 * 128:(o4 + 1) * 128], Y[:, c, sl]) for c in range(8)],
                           [wt] + [ytile(c, tt) for c in range(8)], pbt(bank))
                        post_evac(o, bank, tt, grow, brow)
                    w_release()
                post_finish(tt)

        tc_ = t("cst")
        P.dma("sp", "cst", I("dma_start", out=IDF, in_=cst[:, C_ID:C_ID + 128]), writes=[tc_])
        P.dma("sp", "cst", I("dma_start", out=RC, in_=cst[:, C_RC:C_RC + 8]), writes=[tc_])
        P.dma("pool", "cst", I("dma_start", out=IDB, in_=cst[:, C_ID:C_ID + 128]), writes=[tc_])
        P.dma("pool", "cst", I("dma_start", out=SMAT, in_=cst[:, C_SMAT:C_SMAT + 128]), writes=[tc_])
        P.dma("pool", "cst", I("dma_start", out=MASK2, in_=cst[:, C_MASK:C_MASK + 256]), writes=[tc_])
        P.op("dve", I("memset", ap=ONESB, constant=1.0), [], [t("ones")])
        P.op("dve", I("memset", ap=ONESF, constant=1.0), [], [t("ones")])
        pst = carve(PH, 1024).rearrange("p (k f) -> p k f", k=8)
        P.dma("sp", "ld0", I("dma_start", out=pst, in_=pth.rearrange("(k r) f -> r k f", r=128)), writes=[t("pst")])
        for hb in range(2):
            P.op("pe", [I("transpose", out=pb[hb][:, q * 128:(q + 1) * 128], in_=pst[:, hb * 4 + q, :], identity=IDF)
                        for q in range(4)], [t("pst"), tc_], pbt(hb))
            P.op("act", I("activation", out=PTAB[:, hb * 512:(hb + 1) * 512], in_=pb[hb][:], func=AF.Copy),
                 pbt(hb), [t("ptab")])
        for _ in range(NW):
            w_issue()
        P.barrier()

        def load_x(sq_i):
            stg = [carve(PH + i * 4096, 1024) for i in range(2)]
            for j in range(16):
                sb_ = stg[j % 2]
                ts_ = t("stin", j % 2)
                row0 = (sq_i * 16 + j) * 128
                P.dma("sp", f"ld{j % 2}", I("dma_start", out=sb_, in_=x[row0:row0 + 128, :]), writes=[ts_])
                for hb in range(2):
                    bank = next_bank()
                    P.op("pe", [I("transpose", out=pb[bank][:, q * 128:(q + 1) * 128],
                                  in_=sb_[:, (hb * 4 + q) * 128:(hb * 4 + q + 1) * 128], identity=IDF)
                                for q in range(4)], [ts_, tc_], pbt(bank))
                    copy_op(alt_engine(), H[:, hb * 4:hb * 4 + 4, j * 128:(j + 1) * 128],
                            pb[bank][:].rearrange("p (c n) -> p c n", c=4), pbt(bank),
                            [tH(c, j // 4) for c in range(hb * 4, hb * 4 + 4)])

        def store_y(sq_i):
            stg = [carve(PH + 8192 + i * 4096, 1024) for i in range(2)]
            outs = []
            for j in range(16):
                sb_ = stg[j % 2]
                ts_ = t("stout", j % 2)
                for hb in range(2):
                    bank = next_bank()
                    P.op("pe", [I("transpose", out=pb[bank][:, q * 128:(q + 1) * 128],
                                  in_=H[:, hb * 4 + q, j * 128:(j + 1) * 128], identity=IDF) for q in range(4)],
                         [tH(c, j // 4) for c in range(hb * 4, hb * 4 + 4)] + [tc_], pbt(bank))
                    copy_op(alt_engine(), sb_[:, hb * 512:(hb + 1) * 512], pb[bank][:], pbt(bank), [ts_])
                row0 = (sq_i * 16 + j) * 128
                ty = t("yout", sq_i, j)
                P.dma("sp", f"st{j % 2}", I("dma_start", out=y[row0:row0 + 128, :], in_=sb_), reads=[ts_], writes=[ty])
                outs.append(ty)
            return outs

        def ffn(l):
            ring["mm"] = [0, 1, 2, 3, 4, 5]
            Z = carve(PH, 22 * 1024, BF16).rearrange("p (k n) -> p k n", k=22)
            ACC = [[carve(PH + 45056 + (kd * 2 + b) * 4112, 1024) for b in range(2)] for kd in range(2)]
            SG = [carve(PH + 61504 + b * 4096, 1024) for b in range(2)]
            HALO = carve(PH + 80000, 88).rearrange("p (f n) -> p f n", f=44)
            prenorm(R_FFNPRE + l * 8, PH)
            P.barrier()
            npair = 0
            for hf in range(2):
                for j in range(11):
                    slot, wt = w_acquire(("fin", l, hf, j))
                    v = slot.rearrange("p (c s n) -> p c s n", c=8, s=2)
                    for ii in range(2):
                        i = 2 * j + ii
                        buf = npair % 2
                        npair += 1
                        for kd in range(2):
                            ft = kd * 22 + i
                            acc = ACC[kd][buf]
                            tacc = t("acc", kd, buf)
                            r0 = R_FFNCONV + l * 132 + ft
                            r1 = r0 + 44
                            r2 = r0 + 88
                            th = t("halo", ft)
                            bks = []
                            for tq in range(2):
                                tt = hf * 2 + tq
                                sl = slice(tt * 512, (tt + 1) * 512)
                                bank = next_bank()
                                bks.append(bank)
                                mm(pb[bank][:], [(v[:, c, kd, ii * 128:(ii + 1) * 128], A[:, c, sl]) for c in range(8)],
                                   [wt] + [tA(c, tt) for c in range(8)], pbt(bank))
                                P.op("act", I("activation", out=acc[:, tq * 512:(tq + 1) * 512], in_=pb[bank][:],
                                              func=AF.Copy, scale=col(r2)), pbt(bank) + [t("ptab")], [tacc])
                            b0, b1 = bks
                            P.op("dve", I("scalar_tensor_tensor", out=acc[:, 1:513], in0=pb[b0][:],
                                          scalar=col(r1), in1=acc[:, 1:513], op0=ALU.mult, op1=ALU.add),
                                 pbt(b0) + [tacc, t("ptab")], [tacc])
                            P.op("dve", I("scalar_tensor_tensor", out=acc[:, 2:514], in0=pb[b0][:],
                                          scalar=col(r0), in1=acc[:, 2:514], op0=ALU.mult, op1=ALU.add),
                                 pbt(b0) + [tacc, t("ptab")], [tacc])
                            P.op("dve", I("scalar_tensor_tensor", out=acc[:, 513:1024], in0=pb[b1][:, 0:511],
                                          scalar=col(r1), in1=acc[:, 513:1024], op0=ALU.mult, op1=ALU.add),
                                 pbt(b1) + [tacc, t("ptab")], [tacc])
                            P.op("dve", I("scalar_tensor_tensor", out=acc[:, 514:1024], in0=pb[b1][:, 0:510],
                                          scalar=col(r0), in1=acc[:, 514:1024], op0=ALU.mult, op1=ALU.add),
                                 pbt(b1) + [tacc, t("ptab")], [tacc])
                            if hf == 1:
                                P.op("dve", I("tensor_tensor", out=acc[:, 0:2], in0=acc[:, 0:2], in1=HALO[:, ft, :],
                                              op=ALU.add), [tacc, th], [tacc])
                            else:
                                P.op("dve", I("tensor_scalar", out=HALO[:, ft, :], in0=pb[b1][:, 510:512], scalar1=col(r0),
                                              scalar2=None, op0=ALU.mult), pbt(b1) + [t("ptab")], [th])
                                P.op("dve", I("scalar_tensor_tensor", out=HALO[:, ft, 0:1], in0=pb[b1][:, 511:512],
                                              scalar=col(r1), in1=HALO[:, ft, 0:1], op0=ALU.mult, op1=ALU.add),
                                     pbt(b1) + [th, t("ptab")], [th])
                        sg = SG[buf]
                        tsg = t("sg", buf)
                        P.op("act", I("activation", out=sg, in_=ACC[0][buf], func=AF.Silu),
                             [t("acc", 0, buf)], [tsg])
                        P.op("pool", I("tensor_tensor", out=Z[:, i, :], in0=sg, in1=ACC[1][buf], op=ALU.mult),
                             [tsg, t("acc", 1, buf)], [t("Z", i)])
                    w_release()
                P.barrier()
                post_setup(PH + 45056, PH + 61504, PH + 63552)
                for tq in range(2):
                    tt = hf * 2 + tq
                    for o in range(8):
                        slot, wt = w_acquire(("fout", l, hf, tq, o))
                        v = slot[:, 0:2816].rearrange("p (k n) -> p k n", k=22)
                        bank = next_bank()
                        mm(pb[bank][:], [(v[:, k, :], Z[:, k, tq * 512:(tq + 1) * 512]) for k in range(22)],
                           [wt] + [t("Z", k) for k in range(22)], pbt(bank))
                        post_evac(o, bank, tt, R_FFNPOST + l * 8)
                        w_release()
                    post_finish(tt)
                P.barrier()

        def sc_mixer(l):
            ring["mm"] = [0, 1, 2, 3, 4, 5]
            Y = carve(PH, 8 * S, BF16).rearrange("p (c t) -> p c t", c=8)
            CH = carve(PH + 32768, 2056)
            BSB = carve(PH + 40992, 2048)
            HSB = [carve(PH + 49184 + b * 2048, 512) for b in range(2)]
            ACC = carve(PH + 53280, 2048)
            prenorm(R_MIXPRE + l * 8, PH)
            P.barrier()
            P.op("dve", I("memset", ap=CH[:, 0:2], constant=0.0), [], [t("ch")])
            nh = 0
            for i in range(8):
                slot, wt = w_acquire(("scin", l, i))
                v = slot[:, 0:3072].rearrange("p (c s n) -> p c s n", c=8, s=3)
                for tt in range(4):
                    sl = slice(tt * 512, (tt + 1) * 512)
                    banks = [next_bank() for _ in range(3)]
                    for s_ in range(3):
                        mm(pb[banks[s_]][:], [(v[:, c, s_, :], A[:, c, sl]) for c in range(8)],
                           [wt] + [tA(c, tt) for c in range(8)], pbt(banks[s_]))
                    hs = HSB[nh % 2]
                    ths = t("hsb", nh % 2)
                    nh += 1
                    P.op("act", I("activation", out=hs, in_=pb[banks[2]][:], func=AF.Copy), pbt(banks[2]), [ths])
                    P.op("dve", I("tensor_tensor", out=CH[:, 2 + tt * 512:2 + (tt + 1) * 512], in0=pb[banks[1]][:],
                                  in1=hs, op=ALU.mult), pbt(banks[1]) + [ths], [t("ch")])
                    P.op("act", I("activation", out=BSB[:, sl], in_=pb[banks[0]][:], func=AF.Copy),
                         pbt(banks[0]), [t("bsb")])
                w_release()
                r0 = R_SCCONV + i
                P.op("act", I("activation", out=ACC, in_=CH[:, 2:2050], func=AF.Copy, scale=col(r0 + 16)),
                     [t("ch"), t("ptab")], [t("scacc")])
                P.op("dve", I("scalar_tensor_tensor", out=ACC, in0=CH[:, 1:2049], scalar=col(r0 + 8), in1=ACC,
                              op0=ALU.mult, op1=ALU.add), [t("ch"), t("scacc"), t("ptab")], [t("scacc")])
                P.op("dve", I("scalar_tensor_tensor", out=ACC, in0=CH[:, 0:2048], scalar=col(r0), in1=ACC,
                              op0=ALU.mult, op1=ALU.add), [t("ch"), t("scacc"), t("ptab")], [t("scacc")])
                P.op("pool", I("tensor_tensor", out=Y[:, i, :], in0=BSB, in1=ACC, op=ALU.mult),
                     [t("bsb"), t("scacc")], [t("Y", i)])
            P.barrier()
            post_setup(PH + 61472, PH + 77856, PH + 32768)
            mixer_out(l, Y, lambda c, tt: t("Y", c))
            P.barrier()

        def cc_mixer(l):
            ring["mm"] = [0, 1, 2, 3, 4, 5]
            X = carve(PH, 8 * S).rearrange("p (c t) -> p c t", c=8)
            GB = carve(PH + 65536, 2080, BF16)
            DG = [carve(PH + 69696 + b_ * 7936, 31 * 128, BF16).rearrange("p (k n) -> p k n", k=31) for b_ in range(1)]
            SIG = [carve(PH + 77632 + b * 2048, 512) for b in range(2)]
            Yc = A
            prenorm(R_MIXPRE + l * 8, PH)
            P.barrier()
            P.op("dve", I("memset", ap=GB[:, 0:30], constant=0.0), [], [t("glu")])
            P.op("dve", I("tensor_tensor", out=BG, in0=PTAB[:, R_CCBPW2:R_CCBPW2 + 8],
                          in1=PTAB[:, R_MIXPOST + l * 8:R_MIXPOST + l * 8 + 8], op=ALU.mult), [t("ptab")], [t("bg")])
            ns = 0
            for i in range(8):
                slot, wt = w_acquire(("pw1", l, i))
                v = slot[:, 0:2048].rearrange("p (c s n) -> p c s n", c=8, s=2)
                dg = DG[0]
                P.op("pool", I("tensor_tensor", out=dg, in0=IDB.unsqueeze(1).broadcast_to([128, 31, 128]),
                               in1=PTAB[:, R_CCWDW + i:R_CCWDW + 248:8].unsqueeze(2).broadcast_to([128, 31, 128]),
                               op=ALU.mult), [t("ptab"), tc_], [t("dg")])
                for tt in range(4):
                    sl = slice(tt * 512, (tt + 1) * 512)
                    ba, bg_ = next_bank(), next_bank()
                    mm(pb[ba][:], [(v[:, c, 0, :], A[:, c, sl]) for c in range(8)],
                       [wt] + [tA(c, tt) for c in range(8)], pbt(ba))
                    mm(pb[bg_][:], [(v[:, c, 1, :], A[:, c, sl]) for c in range(8)],
                       [wt] + [tA(c, tt) for c in range(8)], pbt(bg_))
                    sg = SIG[ns % 2]
                    tsg = t("sig", ns % 2)
                    ns += 1
                    P.op("act", I("activation", out=sg, in_=pb[bg_][:], func=AF.Sigmoid, bias=col(R_CCBPW1 + 8 + i)),
                         pbt(bg_) + [t("ptab")], [tsg])
                    P.op("dve", I("scalar_tensor_tensor", out=GB[:, 30 + tt * 512:30 + (tt + 1) * 512], in0=pb[ba][:],
                                  scalar=col(R_CCBPW1 + i), in1=sg, op0=ALU.add, op1=ALU.mult),
                         pbt(ba) + [tsg, t("ptab")], [t("glu", tt)])
                w_release()
                for tt in range(4):
                    bank = next_bank()
                    gl = [t("glu")] + [t("glu", q) for q in range(max(0, tt - 1), tt + 1)]
                    mm(pb[bank][:], [(dg[:, k, :], GB[:, tt * 512 + k:tt * 512 + k + 512]) for k in range(31)],
                       gl + [t("dg")], pbt(bank))
                    P.op("act", I("activation", out=X[:, i, tt * 512:(tt + 1) * 512], in_=pb[bank][:], func=AF.Identity,
                                  bias=col(R_CCBDW + i)), pbt(bank) + [t("ptab")], [t("X", i, tt)])
            P.barrier()
            SQ = carve(PH + 65536, 4096, BF16).rearrange("p (c n) -> p c n", c=8)
            MEAN = carve(PH + 73728, 512)
            VAR = carve(PH + 75776, 512)
            for tt in range(4):
                sl = slice(tt * 512, (tt + 1) * 512)
                txs = [t("X", c, tt) for c in range(8)]
                P.op("act", I("activation", out=SQ, in_=X[:, :, sl], func=AF.Square), txs, [t("lnsq")])
                b1, b2 = next_bank(), next_bank()
                mm(pb[b1][:], [(ONESF, X[:, c, sl]) for c in range(8)], txs + [t("ones")], pbt(b1))
                mm(pb[b2][:], [(ONESB, SQ[:, c, :]) for c in range(8)], [t("lnsq"), t("ones")], pbt(b2))
                P.op("dve", I("tensor_scalar", out=MEAN, in0=pb[b1][:], scalar1=1.0 / D, scalar2=None, op0=ALU.mult),
                     pbt(b1), [t("mean")])
                P.op("dve", I("tensor_tensor", out=VAR, in0=MEAN, in1=MEAN, op=ALU.mult), [t("mean")], [t("var")])
                P.op("dve", I("scalar_tensor_tensor", out=VAR, in0=pb[b2][:], scalar=1.0 / D, in1=VAR,
                              op0=ALU.mult, op1=ALU.subtract), pbt(b2) + [t("var")], [t("var")])
                P.op("act", I("activation", out=VAR, in_=VAR, func=AF.Ln, bias=EPS_LN, scale=1.0),
                     [t("var"), tc_], [t("var")])
                P.op("act", I("activation", out=VAR, in_=VAR, func=AF.Exp, scale=-0.5), [t("var")], [t("var")])
                P.op("dve", I("tensor_tensor", out=X[:, :, sl], in0=X[:, :, sl],
                              in1=MEAN.unsqueeze(1).broadcast_to([128, 8, 512]), op=ALU.subtract),
                     txs + [t("mean")], txs)
                P.op("dve", I("tensor_tensor", out=X[:, :, sl], in0=X[:, :, sl],
                              in1=VAR.unsqueeze(1).broadcast_to([128, 8, 512]), op=ALU.mult), txs + [t("var")], txs)
                for c in range(8):
                    P.op("act", I("activation", out=Yc[:, c, sl], in_=X[:, c, sl], func=AF.Silu,
                                  scale=col(R_CCLNG + c), bias=col(R_CCLNB + c)), [t("X", c, tt), t("ptab")], [tA(c, tt)])
            P.barrier()
            post_setup(PH, PH + 16384, PH + 18432)
            mixer_out(l, Yc, tA, brow=R_CCBPW2)
            P.barrier()

        def attn_mixer(l, sq_i):
            ring["mm"] = [0, 1, 2]
            HB = H_OFF
            QKV = [[carve(HB + s_ * 12288 + k * 4096, S, BF16) for k in range(3)] for s_ in range(2)]
            VAB = [[carve(HB + 24576 + s_ * 8192 + k * 4096, 2048, BF16).rearrange("p (b n) -> p b n", b=16)
                    for k in range(2)] for s_ in range(2)]
            OT = carve(PH, 8 * S, BF16).rearrange("p (c t) -> p c t", c=8)
            COS = carve(PH + 32768, S)
            SIN = carve(PH + 40960, S)
            PTR = [carve(PH + 49152 + i * 512, 256, BF16).rearrange("p (j q) -> p j q", j=2) for i in range(4)]
            T12 = [[carve(PH + 51200 + (b * 2 + k) * 2048, 512) for k in range(2)] for b in range(2)]
            POSI = carve(PH + 59392, S, I32)
            XF = carve(PH + 67584, S)
            ZN = carve(PH + 59392, S)
            UAB = [[carve(HB + 40960, S), carve(HB + 49152, S)],
                   [carve(HB + 57344, S), carve(PH + 67584, S)]]
            for c in range(8):
                P.dma("sp", "park", I("dma_start", out=hpark[:, c * S:(c + 1) * S], in_=H[:, c, :]),
                      reads=[tH(c, tt) for tt in range(4)], writes=[t("hpark", c)])
            P.dma("sp", "pos", I("dma_start", out=POSI, in_=pos[sq_i:sq_i + 1, :].broadcast_to([128, S])),
                  writes=[t("posi")])
            P.op("dve", I("tensor_copy", out=XF, in_=POSI), [t("posi")], [t("xf")])
            P.op("dve", I("tensor_scalar", out=XF, in0=XF, scalar1=RC[:, 0:1], scalar2=None, op0=ALU.mult),
                 [t("xf"), tc_], [t("xf")])
            P.op("dve", I("tensor_copy", out=POSI, in_=XF), [t("xf")], [t("posi")])
            P.op("dve", I("tensor_copy", out=COS, in_=POSI), [t("posi")], [t("cos")])
            P.op("dve", I("tensor_tensor", out=XF, in0=XF, in1=COS, op=ALU.subtract), [t("xf"), t("cos")], [t("xf")])
            P.op("dve", I("scalar_tensor_tensor", out=SIN, in0=XF, scalar=0.5, in1=XF, op0=ALU.is_gt, op1=ALU.subtract),
                 [t("xf")], [t("sin")])
            P.op("dve", I("scalar_tensor_tensor", out=SIN, in0=SIN, scalar=0.5, in1=SIN, op0=ALU.is_gt, op1=ALU.subtract),
                 [t("sin")], [t("sin")])
            P.op("dve", I("tensor_scalar", out=XF, in0=XF, scalar1=0.25, scalar2=None, op0=ALU.add), [t("xf")], [t("xf")])
            P.op("dve", I("scalar_tensor_tensor", out=COS, in0=XF, scalar=0.5, in1=XF, op0=ALU.is_gt, op1=ALU.subtract),
                 [t("xf"), t("cos")], [t("cos")])
            P.op("dve", I("scalar_tensor_tensor", out=COS, in0=COS, scalar=0.5, in1=COS, op0=ALU.is_gt, op1=ALU.subtract),
                 [t("cos")], [t("cos")])
            TWO_PI = 2.0 * math.pi * (1.0 - 2e-7)
            P.op("act", I("activation", out=SIN, in_=SIN, func=AF.Sin, scale=TWO_PI), [t("sin")], [t("sin")])
            P.op("act", I("activation", out=COS, in_=COS, func=AF.Sin, scale=TWO_PI), [t("cos")], [t("cos")])
            prenorm(R_MIXPRE + l * 8, PH)
            P.barrier()
            for s_ in range(2):
                P.op("pool", I("memset", ap=VAB[s_][0][:, :, 64:128], constant=1.0), [], [t("va", s_)])
                P.op("pool", I("memset", ap=VAB[s_][1][:, :, 0:64], constant=1.0), [], [t("vb", s_)])
            cnt = {"rope": 0, "idx": 0}

            def proj_gen(p, g, set_):
                dil = GROUPS[g][1]
                nb = (S // dil) // 128
                VA, VB = VAB[set_]
                tq_ = [[t("qkv", set_, k, tt_) for tt_ in range(4)] for k in range(3)]
                slot, wt = w_acquire(("qkv", l, p, g))
                v = slot[:, 0:3072].rearrange("p (c s n) -> p c s n", c=8, s=3)
                pend = None
                for s3 in range(3):
                    dst = QKV[set_][s3]
                    for tt in range(4):
                        sl = slice(tt * 512, (tt + 1) * 512)
                        bank = next_bank()
                        bt = pbt(bank)
                        for c0 in range(0, 8, 2):
                            P.op("pe", [I("matmul", out=pb[bank][:], lhsT=v[:, c, s3, :], rhs=A[:, c, sl],
                                          start=(c == 0), stop=(c == 7)) for c in range(c0, c0 + 2)],
                                 [wt] + [tA(c, tt) for c in range(8)], bt)
                            if c0 == 0 and pend is not None:
                                pend()
                                pend = None
                            if c0 < 6:
                                yield
                        P.op("act", I("activation", out=dst[:, sl], in_=pb[bank][:], func=AF.Copy), bt, [tq_[s3][tt]])
                        if s3 < 2:
                            nr = cnt["rope"]
                            cnt["rope"] += 1
                            t1, t2 = T12[nr % 2]
                            tt1, tt2 = t("t1", nr % 2), t("t2", nr % 2)
                            P.op("dve", I("tensor_tensor", out=t1[0:80, :], in0=pb[bank][0:80, :], in1=COS[0:80, sl],
                                          op=ALU.mult), bt + [t("cos")], [tt1])

                            def rope_tail(dst=dst, sl=sl, t1=t1, t2=t2, tt1=tt1, tt2=tt2, tq=tq_[s3][tt]):
                                mm(pb[3][0:80, :], [(SMAT[:, 0:80], dst[:, sl])], [tq, tc_], pbt(3))
                                P.op("dve", I("tensor_tensor", out=t2[0:80, :], in0=pb[3][0:80, :], in1=SIN[0:80, sl],
                                              op=ALU.mult), pbt(3) + [t("sin")], [tt2])
                                P.op("dve", I("tensor_tensor", out=dst[0:80, sl], in0=t1[0:80, :], in1=t2[0:80, :],
                                              op=ALU.add), [tt1, tt2], [tq])
                            pend = rope_tail
                        yield
                if pend is not None:
                    pend()
                w_release()
                VTv = QKV[set_][2].rearrange("p (m d) -> p d m", d=dil)
                for h8 in range(2):
                    bank = next_bank()
                    ins = []
                    for b8 in range(8):
                        blk = h8 * 8 + b8
                        r, n = blk // nb, blk % nb
                        ins.append(I("transpose", out=pb_bf[bank][:, b8 * 128:(b8 + 1) * 128],
                                     in_=VTv[:, r, n * 128:(n + 1) * 128], identity=IDB))
                    P.op("pe", ins, tq_[2] + [tc_], pbt(bank))
                    src = pb_bf[bank][:].rearrange("p (b n) -> p b n", b=8)
                    P.op("dve", I("tensor_copy", out=VA[:, h8 * 8:h8 * 8 + 8, 0:64], in_=src[:, :, 0:64]),
                         pbt(bank), [t("va", set_)])
                    P.op("act", I("activation", out=VB[:, h8 * 8:h8 * 8 + 8, 64:128], in_=src[:, :, 64:128],
                                  func=AF.Copy), pbt(bank), [t("vb", set_)])
                    yield

            def core_gen(p, g, set_):
                dil = GROUPS[g][1]
                nb = (S // dil) // 128
                QT, KT, VT = QKV[set_]
                VA, VB = VAB[set_]
                U2 = UAB[p % 2]
                tq_ = [[t("qkv", set_, k, tt_) for tt_ in range(4)] for k in range(3)]
                QTv = QT.rearrange("p (m d) -> p d m", d=dil)
                KTv = KT.rearrange("p (m d) -> p d m", d=dil)
                items = [(r, n, hd) for r in range(dil) for n in range(nb) for hd in range(2)]
                st_ = {}

                def scores(it):
                    r, n, hd = it
                    hb = hd * 64
                    gi = cnt["idx"]
                    cnt["idx"] += 1
                    sbank = 4 + gi % 2
                    stv = pb[sbank][:, 0:256].rearrange("p (j q) -> p j q", j=2)
                    stt = pbt(sbank)
                    j0 = 0 if n > 0 else 1
                    P.op("pe", [I("matmul", out=stv[:, jj, :], lhsT=KTv[hb:hb + 64, r, (n - 1 + jj) * 128:(n + jj) * 128],
                                  rhs=QTv[hb:hb + 64, r, n * 128:(n + 1) * 128], start=True, stop=True)
                                for jj in range(j0, 2)], tq_[0] + tq_[1], stt)
                    pt = PTR[gi % 4]
                    tpt = t("pt", gi % 4)
                    P.op("act", I("activation", out=pt[:, j0:2, :], in_=stv[:, j0:2, :], func=AF.Exp, scale=0.125),
                         stt, [tpt])
                    P.op("pool",
                         I("tensor_tensor", out=pt[:, j0:2, :], in0=pt[:, j0:2, :], in1=MASK[:, j0:2, :],
                           op=ALU.mult), [tpt, tc_], [tpt])
                    return (pt, tpt, j0, gi)

                def pv(it, stf):
                    r, n, hd = it
                    pt, tpt, j0, gi = stf
                    xb = 6 + gi % 2
                    xq = pb[xb][:, 0:128]
                    xt = pbt(xb)
                    Vh = VA if hd == 0 else VB
                    tv = t("va", set_) if hd == 0 else t("vb", set_)
                    P.op("pe", [I("matmul", out=xq, lhsT=Vh[:, r * nb + n - 1 + jj, :], rhs=pt[:, jj, :],
                                  start=(jj == j0), stop=(jj == 1)) for jj in range(j0, 2)], [tpt, tv], xt)
                    tu = t("u", p % 2, hd)
                    Uv = U2[hd].rearrange("p (m d) -> p d m", d=dil)[:, r, n * 128:(n + 1) * 128]
                    if g == 0:
                        P.op("dve", I("tensor_copy", out=Uv, in_=xq), xt, [tu])
                    else:
                        P.op("dve", I("tensor_tensor", out=Uv, in0=xq, in1=Uv, op=ALU.add), xt + [tu], [tu])

                SK = 3
                for idx in range(len(items) + SK):
                    if idx < len(items):
                        st_[idx] = scores(items[idx])
                    if idx >= SK:
                        pv(items[idx - SK], st_.pop(idx - SK))
                    yield

            def normalize(p):
                U2 = UAB[p % 2]
                tu0, tu1 = t("u", p % 2, 0), t("u", p % 2, 1)
                P.dma("sp", "zsw", I("dma_start", out=ZN[0:64, :], in_=U2[0][64:128, :]), reads=[tu0], writes=[t("zn")])
                P.dma("sp", "zsw", I("dma_start", out=ZN[64:128, :], in_=U2[1][0:64, :]), reads=[tu1], writes=[t("zn")])
                P.op("act", I("activation", out=ZN, in_=ZN, func=AF.Ln), [t("zn")], [t("zn")])
                P.op("act", I("activation", out=ZN, in_=ZN, func=AF.Exp, scale=-1.0), [t("zn")], [t("zn")])
                P.op("dve", I("tensor_tensor", out=OT[0:64, p, :], in0=U2[0][0:64, :], in1=ZN[0:64, :], op=ALU.mult),
                     [tu0, t("zn")], [t("ot", p)])
                P.op("dve", I("tensor_tensor", out=OT[64:128, p, :], in0=U2[1][64:128, :], in1=ZN[64:128, :],
                              op=ALU.mult), [tu1, t("zn")], [t("ot", p)])

            pgs = [(p, g) for p in range(8) for g in range(3)]
            for _ in proj_gen(pgs[0][0], pgs[0][1], 0):
                pass
            for k, (p, g) in enumerate(pgs):
                core = core_gen(p, g, k % 2)
                proj = proj_gen(pgs[k + 1][0], pgs[k + 1][1], (k + 1) % 2) if k + 1 < len(pgs) else None
                nstep = 0
                for _ in core:
                    nstep += 1
                    if proj is not None:
                        if next(proj, "done") == "done":
                            proj = None
                if proj is not None:
                    for _ in proj:
                        pass
                if g == 2:
                    normalize(p)
            P.barrier()
            for tt in range(4):
                sl = slice(tt * 512, (tt + 1) * 512)
                P.dma("sp", "unpark", I("dma_start", out=H[:, :, sl],
                                        in_=hpark.rearrange("p (c t) -> p c t", c=8)[:, :, sl]),
                      reads=[t("hpark", c) for c in range(8)], writes=[tH(c, tt) for c in range(8)])
            ring["mm"] = [0, 1, 2, 3, 4, 5]
            post_setup(PH + 32768, PH + 49152, PH + 51200)
            mixer_out(l, OT, lambda c, tt: t("ot", c))
            P.barrier()

        outs = []
        for sq_i in range(nseq):
            load_x(sq_i)
            P.barrier()
            for l in layers:
                kind = l % 3
                if do_mixer:
                    if kind == 0:
                        attn_mixer(l, sq_i)
                    elif kind == 1:
                        sc_mixer(l)
                    else:
                        cc_mixer(l)
                if do_ffn:
                    ffn(l)
            ring["mm"] = [0, 1, 2, 3]
            outs += store_y(sq_i)
            P.barrier()
        P.wait_all("sp", outs)
        assert next(keygen, None) is None and not WS["pending"]
        P.emit(block)
    return nc


def _consts():
    c = np.zeros((128, NCST), np.float32)
    c[:, C_ID:C_ID + 128] = np.eye(128, dtype=np.float32)
    sm = np.zeros((128, 128), np.float32)
    for base in (0, 64):
        for m in range(8):
            sm[base + m + 8, base + m] = -1.0
            sm[base + m, base + m + 8] = 1.0
    c[:, C_SMAT:C_SMAT + 128] = sm
    k = np.arange(128)[:, None]
    q = np.arange(128)[None, :]
    c[:, C_MASK:C_MASK + 128] = (k >= q).astype(np.float32)
    c[:, C_MASK + 128:C_MASK + 256] = (k <= q).astype(np.float32)
    inv = (500000.0 ** (-np.arange(0, 16, 2, dtype=np.float32) / 16.0)).astype(np.float32)
    for p in range(128):
        if p % 64 < 16:
            c[p, C_RC] = inv[p % 8] / np.float32(2.0 * math.pi)
    c[:, C_RC + 1] = 1e-6
    c[:, C_RC + 2] = 1e-5
    return c


def _pack_params(inp):
    f = lambda a: np.ascontiguousarray(np.asarray(a, np.float32)).reshape(-1, 128)
    rows = np.concatenate([
        f(inp["mix_norm_pre"]), f(inp["mix_norm_post"]), f(inp["ffn_norm_pre"]), f(inp["ffn_norm_post"]),
        f(inp["sc_w_conv"]), f(inp["cc_b_pw1"]), f(inp["cc_w_dw"]), f(inp["cc_b_dw"]),
        f(inp["cc_ln_g"]), f(inp["cc_ln_b"]), f(inp["cc_b_pw2"]), f(inp["ffn_w_conv"])], axis=0)
    assert rows.shape[0] == 976
    out = np.zeros((1024, 128), np.float32)
    out[:976] = rows
    return out


def _layer_weights(inp, l):
    kind, j = l % 3, l // 3
    a = lambda k, i: np.ascontiguousarray(np.asarray(inp[k][i], np.float32))
    d = {}
    if kind == 0:
        d[f"w{l}_min"] = a("attn_w_qkv", j)
        d[f"w{l}_mout"] = a("attn_w_o", j)
    elif kind == 1:
        d[f"w{l}_min"] = a("sc_w_in", j)
        d[f"w{l}_mout"] = a("sc_w_out", j)
    else:
        d[f"w{l}_min"] = a("cc_w_pw1", j)
        d[f"w{l}_mout"] = a("cc_w_pw2", j)
    d[f"w{l}_fin"] = a("ffn_w_in", l)
    d[f"w{l}_fout"] = a("ffn_w_out", l)
    return d


LAUNCH_GROUPS = [[0, 1, 2, 3]]
_NC_CACHE = {}


def kernel(**inp):
    x = np.ascontiguousarray(np.asarray(inp["x"], np.float32))
    positions = np.ascontiguousarray(np.asarray(inp["positions"], np.int32))
    pth = _pack_params(inp)
    cst = _consts()
    cur = [x[c * SEQ_PER_CORE:(c + 1) * SEQ_PER_CORE].reshape(SEQ_PER_CORE * S, D) for c in range(NCORES)]
    for grp in LAUNCH_GROUPS:
        key = tuple(grp)
        if key not in _NC_CACHE:
            _NC_CACHE[key] = build(list(grp), SEQ_PER_CORE)
        nc = _NC_CACHE[key]
        wd = {}
        for l in grp:
            wd.update(_layer_weights(inp, l))
        in_maps = []
        for c in range(NCORES):
            m = {"x": cur[c], "pos": positions[c * SEQ_PER_CORE:(c + 1) * SEQ_PER_CORE], "pth": pth, "cst": cst}
            m.update(wd)
            in_maps.append(m)
        res = run_bass_kernel_spmd(nc, in_maps, core_ids=list(range(NCORES)))
        cur = [np.asarray(r["y"], np.float32) for r in res.results]
    out = np.stack([c.reshape(SEQ_PER_CORE, S, D) for c in cur], axis=0).reshape(NCORES * SEQ_PER_CORE, S, D)
    return out
```

```python
import contextlib
import math
import numpy as np
import concourse.bass as bass
import concourse.mybir as mybir
from concourse.bass_utils import run_bass_kernel_spmd

F32 = mybir.dt.float32
BF16 = mybir.dt.bfloat16
I32 = mybir.dt.int32
AF = mybir.ActivationFunctionType
ALU = mybir.AluOpType

S = 2048
D = 1024
FF = 2816
NCORES = 8
SEQ_PER_CORE = 4
DEPTH = 4
GROUPS = ((128, 1), (512, 4), (2048, 16))

R_MIXPRE, R_MIXPOST, R_FFNPRE, R_FFNPOST = 0, 32, 64, 96
R_SCCONV = 128
R_CCBPW1 = 152
R_CCWDW = 168
R_CCBDW, R_CCLNG, R_CCLNB, R_CCBPW2 = 416, 424, 432, 440
R_FFNCONV = 448
C_ID, C_SMAT, C_MASK, C_RC = 0, 128, 256, 512
NCST = 640

SEM_CAP = 30000
ENGS = ("pe", "act", "dve", "pool", "sp")


class T:
    __slots__ = ("w", "r")

    def __init__(self):
        self.w = None
        self.r = {}


class TP(T):
    __slots__ = ()


class Prog:
    def __init__(self, nc, stack):
        self.nc = nc
        self.stack = stack
        self.q = {e: [] for e in ENGS}
        self.cnt = {}
        self.seen = {e: {} for e in ENGS}
        self.sems = {}

    def _sem(self, chan, idx):
        lst = self.sems.setdefault(chan, [])
        while len(lst) <= idx:
            lst.append(self.stack.enter_context(self.nc.semaphore(f"s_{chan}_{len(lst)}")))
        return lst[idx]

    def _ev(self, chan, c, step):
        per = SEM_CAP // step
        return self._sem(chan, (c - 1) // per), ((c - 1) % per + 1) * step

    def _deps(self, eng, reads, writes, skip=None):
        deps = {}

        def add(ev):
            if ev is None:
                return
            ch, c = ev
            if (ch == "pe" and eng == "pe") or ch == skip:
                return
            if ch not in ENGS:
                c = self.cnt[ch]
            if deps.get(ch, 0) < c:
                deps[ch] = c
        for t in reads:
            add(t.w)
        for t in writes:
            add(t.w)
            for ch, c in t.r.items():
                add((ch, c))
        out = []
        for ch, c in deps.items():
            if self.seen[eng].get(ch, 0) < c:
                self.seen[eng][ch] = c
                out.append((ch, c))
        return out

    def _mark(self, ev, reads, writes):
        ch, c = ev
        for t in writes:
            t.w = ev
            t.r = {}
        for t in reads:
            if t.r.get(ch, 0) < c:
                t.r[ch] = c

    def op(self, eng, fn, reads=(), writes=()):
        if any(isinstance(t, TP) for t in reads):
            writes = list(writes) + [t for t in reads if isinstance(t, TP)]
            reads = [t for t in reads if not isinstance(t, TP)]
        waits = self._deps(eng, reads, writes)
        c = self.cnt.get(eng, 0) + 1
        self.cnt[eng] = c
        self.q[eng].append((waits, fn, (eng, c, 1)))
        self._mark((eng, c), reads, writes)

    def dma(self, eng, chan, fn, reads=(), writes=()):
        waits = self._deps(eng, reads, writes, skip=chan)
        c = self.cnt.get(chan, 0) + 1
        self.cnt[chan] = c
        self.q[eng].append((waits, fn, (chan, c, 16)))
        self._mark((chan, c), reads, writes)

    def barrier(self):
        for e in ENGS:
            waits = []
            for ch, c in self.cnt.items():
                if ch == e or ch.startswith("w"):
                    continue
                if self.seen[e].get(ch, 0) < c:
                    self.seen[e][ch] = c
                    waits.append((ch, c))
            if waits:
                self.q[e].append((waits, None, None))

    def fence(self, engs, tiles):
        for e in engs:
            waits = self._deps(e, (), tiles)
            if waits:
                self.q[e].append((waits, None, None))

    def wait_all(self, eng, tiles):
        waits = self._deps(eng, tiles, ())
        self.q[eng].append((waits, None, None))

    def emit(self, block):
        handles = {"pe": block.tensor, "act": block.scalar, "dve": block.vector,
                   "pool": block.gpsimd, "sp": block.sync}
        for e in ENGS:
            q = self.q[e]
            if not q:
                continue

            def body(h, q=q):
                for waits, fn, inc in q:
                    for ch, c in waits:
                        s, v = self._ev(ch, c, 1 if ch in ENGS else 16)
                        h.wait_ge(s, v)
                    if fn is None:
                        continue
                    ins = None
                    for name, kw in (fn if isinstance(fn, list) else [fn]):
                        ins = getattr(h, name)(**kw)
                    ch, c, step = inc
                    s, v = self._ev(ch, c, step)
                    ins.then_inc(s, step)
            handles[e](body)


def _wkeys(layers, nseq, do_mixer, do_ffn):
    for _ in range(nseq):
        for l in layers:
            kind = l % 3
            if do_mixer:
                if kind == 0:
                    for p in range(8):
                        for g in range(3):
                            yield ("qkv", l, p, g)
                elif kind == 1:
                    for i in range(8):
                        yield ("scin", l, i)
                else:
                    for i in range(8):
                        yield ("pw1", l, i)
                for tt in range(4):
                    for hh in range(2):
                        yield ("mo", l, tt, hh)
            if do_ffn:
                for hf in range(2):
                    for j in range(11):
                        yield ("fin", l, hf, j)
                    for tq in range(2):
                        for o in range(8):
                            yield ("fout", l, hf, tq, o)


def I(name, **kw):
    return (name, kw)


def build(layers, nseq, do_mixer=True, do_ffn=True):
    nc = bass.Bass("TRN2", target_bir_lowering=False)
    dt = nc.dram_tensor
    x = dt("x", [nseq * S, D], F32, kind="ExternalInput").ap()
    y = dt("y", [nseq * S, D], F32, kind="ExternalOutput").ap()
    pos = dt("pos", [nseq, S], I32, kind="ExternalInput").ap()
    pth = dt("pth", [1024, 128], F32, kind="ExternalInput").ap()
    cst = dt("cst", [128, NCST], F32, kind="ExternalInput").ap()
    W = {}
    has_attn = False
    for l in layers:
        kind = l % 3
        if kind == 0:
            has_attn = True
            W[l, "min"] = dt(f"w{l}_min", [D, 9216], F32, kind="ExternalInput").ap()
        elif kind == 1:
            W[l, "min"] = dt(f"w{l}_min", [D, 3072], F32, kind="ExternalInput").ap()
        else:
            W[l, "min"] = dt(f"w{l}_min", [D, 2048], F32, kind="ExternalInput").ap()
        W[l, "mout"] = dt(f"w{l}_mout", [D, D], F32, kind="ExternalInput").ap()
        W[l, "fin"] = dt(f"w{l}_fin", [D, 2 * FF], F32, kind="ExternalInput").ap()
        W[l, "fout"] = dt(f"w{l}_fout", [FF, D], F32, kind="ExternalInput").ap()
    hpark = dt("hpark", [128, 8 * S], F32, kind="Internal").ap() if has_attn else None

    with contextlib.ExitStack() as st:
        ARENA_BYTES = 212480
        arena = st.enter_context(nc.sbuf_tensor("arena", [128, ARENA_BYTES // 4], F32))
        arena_bf = arena.bitcast(BF16)
        arena_i = arena.bitcast(I32)
        pb = [st.enter_context(nc.psum_tensor(f"pb{i}", [128, 512], F32)) for i in range(8)]
        pb_bf = [b.bitcast(BF16) for b in pb]
        block = st.enter_context(nc.Block())
        P = Prog(nc, st)

        def carve(off, n, dtype=F32):
            if dtype == BF16:
                assert off % 2 == 0
                return arena_bf[:, off // 2: off // 2 + n]
            assert off % 4 == 0
            v = arena if dtype == F32 else arena_i
            return v[:, off // 4: off // 4 + n]

        tiles = {}

        def t(*key):
            r = tiles.get(key)
            if r is None:
                r = tiles[key] = T()
            return r

        ptiles = [TP() for _ in range(8)]

        def pbt(i, q0=0, q1=4):
            return [ptiles[i]]

        PTAB = carve(0, 1024)
        IDF = carve(4096, 128)
        ONESB = carve(4608, 128, BF16)
        IDB = carve(4864, 128, BF16)
        SMAT = carve(5120, 128, BF16)
        MASK3 = carve(5376, 384, BF16)
        MASKCP = MASK3[:, 128:384]
        RC = carve(6688, 8)
        ONESF = carve(6144, 128)
        BG = carve(6656, 8)
        WS0 = 7168
        NW = 3
        wslots = [carve(WS0 + i * 8192, 4096, BF16) for i in range(NW)]
        A_OFF = WS0 + NW * 8192
        A = carve(A_OFF, 8 * S, BF16).rearrange("p (c t) -> p c t", c=8)
        H_OFF = A_OFF + 32768
        H = carve(H_OFF, 8 * S).rearrange("p (c t) -> p c t", c=8)
        PH = H_OFF + 65536
        assert ARENA_BYTES - PH >= 79904
        EPS_RMS = RC[:, 1:2]
        EPS_LN = RC[:, 2:3]

        def col(r):
            return PTAB[:, r:r + 1]

        def tA(c, tt):
            return t("A", c, tt)

        def tH(c, tt):
            return t("H", c, tt)

        keygen = _wkeys(layers, nseq, do_mixer, do_ffn)
        WS = {"pending": [], "nxt": 0}
        wT = [T() for _ in range(NW)]

        def w_issue():
            key = next(keygen, None)
            if key is None:
                return
            s = WS["nxt"]
            WS["nxt"] = (s + 1) % NW
            slot = wslots[s]
            kind, l = key[0], key[1]
            dmas = []
            if kind == "fin":
                j = key[3]
                v = slot.rearrange("p (c s n) -> p c s n", c=8, s=2)
                src = W[l, "fin"].rearrange("(c p) n -> p c n", p=128)
                for h in range(2):
                    dmas.append((v[:, :, h, :], src[:, :, h * FF + j * 256: h * FF + (j + 1) * 256]))
            elif kind == "fout":
                o = key[4]
                v = slot[:, 0:2816].rearrange("p (k n) -> p k n", k=22)
                src = W[l, "fout"].rearrange("(k p) n -> p k n", p=128)
                dmas.append((v, src[:, :, o * 128:(o + 1) * 128]))
            elif kind in ("qkv", "scin", "pw1"):
                ns = 2 if kind == "pw1" else 3
                v = slot[:, 0:ns * 1024].rearrange("p (c s n) -> p c s n", c=8, s=ns)
                src = W[l, "min"].rearrange("(c p) n -> p c n", p=128)
                for s_ in range(ns):
                    if kind == "qkv":
                        cl = s_ * 3072 + key[3] * 1024 + key[2] * 128
                    else:
                        cl = s_ * 1024 + key[2] * 128
                    dmas.append((v[:, :, s_, :], src[:, :, cl:cl + 128]))
            elif kind == "mo":
                hh = key[3]
                v = slot.rearrange("p (c n) -> p c n", c=8)
                src = W[l, "mout"].rearrange("(c p) n -> p c n", p=128)
                dmas.append((v, src[:, :, hh * 512:(hh + 1) * 512]))
            else:
                raise AssertionError(key)
            for (o_, i_) in dmas:
                P.dma("pool", f"w{s}", I("dma_start", out=o_, in_=i_), writes=[wT[s]])
            WS["pending"].append((key, s))

        def w_acquire(key):
            k, s = WS["pending"].pop(0)
            assert k == key, (k, key)
            return wslots[s], wT[s]

        def w_release():
            w_issue()

        def mm(out, pairs, reads, writes):
            n = len(pairs)
            P.op("pe", [I("matmul", out=out, lhsT=l_, rhs=r_, start=(i == 0), stop=(i == n - 1))
                        for i, (l_, r_) in enumerate(pairs)], reads, writes)

        ring = {"mm": [0, 1, 2, 3], "i": 0}

        def next_bank():
            b = ring["mm"][ring["i"] % len(ring["mm"])]
            ring["i"] += 1
            return b

        flip = {"n": 0}

        def alt_engine():
            flip["n"] += 1
            return "act" if flip["n"] % 2 else "dve"

        def copy_op(eng, out, in_, reads, writes):
            if eng == "act":
                P.op("act", I("activation", out=out, in_=in_, func=AF.Copy), reads, writes)
            else:
                P.op(eng, I("tensor_copy", out=out, in_=in_), reads, writes)

        def rstd_inplace(bank, eps_ap):
            bt = pbt(bank)
            P.op("act", I("activation", out=pb[bank][:], in_=pb[bank][:], func=AF.Ln, bias=eps_ap, scale=1.0 / D),
                 bt + [t("cst")], bt)
            P.op("act", I("activation", out=pb[bank][:], in_=pb[bank][:], func=AF.Exp, scale=-0.5), bt, bt)

        def prenorm(row, sq_off):
            sqs = [carve(sq_off + i * 8192, 4096, BF16).rearrange("p (c n) -> p c n", c=8) for i in range(2)]

            def square(tt):
                sl = slice(tt * 512, (tt + 1) * 512)
                P.op("act", I("activation", out=sqs[tt % 2], in_=H[:, :, sl], func=AF.Square),
                     [tH(c, tt) for c in range(8)], [t("sqpre", tt % 2)])
            square(0)
            square(1)
            for tt in range(4):
                sq = sqs[tt % 2]
                tsq = t("sqpre", tt % 2)
                sl = slice(tt * 512, (tt + 1) * 512)
                bank = 6 + tt % 2
                mm(pb[bank][:], [(ONESB, sq[:, c, :]) for c in range(8)], [tsq, t("cst")], pbt(bank))
                rstd_inplace(bank, EPS_RMS)
                if tt + 2 < 4:
                    square(tt + 2)
                for c in range(8):
                    P.op("dve", I("scalar_tensor_tensor", out=A[:, c, sl], in0=H[:, c, sl], scalar=col(row + c),
                                  in1=pb[bank][:], op0=ALU.mult, op1=ALU.mult),
                         [tH(c, tt), t("ptab")] + pbt(bank), [tA(c, tt)])

        Post = {}

        def post_setup(mb_off, sq_off, mb_off2=None):
            offs = [mb_off, mb_off if mb_off2 is None else mb_off2]
            Post["Mb"] = [carve(o_, 4096).rearrange("p (o n) -> p o n", o=8) for o_ in offs]
            Post["two"] = mb_off2 is not None
            Post["sq"] = [carve(sq_off + i * 1024, 512, BF16) for i in range(2)]
            Post["n"] = 0

        def post_evac(o, bank, tt, grow, brow=None):
            mi = tt % 2 if Post["two"] else 0
            Mb = Post["Mb"][mi]
            sqb = Post["sq"][Post["n"] % 2]
            tsq = t("sqpost", Post["n"] % 2)
            Post["n"] += 1
            sbank = 6 + tt % 2
            bt = pbt(bank)
            if brow is None:
                P.op("act", I("activation", out=sqb, in_=pb[bank][:], func=AF.Square), bt, [tsq])
                P.op("act", I("activation", out=Mb[:, o, :], in_=pb[bank][:], func=AF.Copy, scale=col(grow + o)),
                     bt + [t("ptab")], [t("Mb", mi, o)])
            else:
                P.op("act", I("activation", out=sqb, in_=pb[bank][:], func=AF.Square, bias=col(brow + o)),
                     bt + [t("ptab")], [tsq])
                P.op("act", I("activation", out=Mb[:, o, :], in_=pb[bank][:], func=AF.Identity,
                              scale=col(grow + o), bias=BG[:, o:o + 1]),
                     bt + [t("ptab"), t("bg")], [t("Mb", mi, o)])
            P.op("pe", I("matmul", out=pb[sbank][:], lhsT=ONESB, rhs=sqb, start=(o == 0), stop=(o == 7)),
                 [tsq, t("cst")], pbt(sbank))

        def post_finish(tt):
            mi = tt % 2 if Post["two"] else 0
            Mb = Post["Mb"][mi]
            sbank = 6 + tt % 2
            rstd_inplace(sbank, EPS_RMS)
            sl = slice(tt * 512, (tt + 1) * 512)
            mbt = [t("Mb", mi, o) for o in range(8)]
            P.op("dve", I("tensor_tensor", out=Mb, in0=Mb,
                          in1=pb[sbank][:].unsqueeze(1).broadcast_to([128, 8, 512]), op=ALU.mult),
                 mbt + pbt(sbank), mbt)
            hts = [tH(c, tt) for c in range(8)]
            P.op("pool", I("tensor_tensor", out=H[:, 0:5, sl], in0=H[:, 0:5, sl], in1=Mb[:, 0:5, :], op=ALU.add),
                 mbt[0:5] + hts[0:5], hts[0:5])
            P.op("dve", I("tensor_tensor", out=H[:, 5:8, sl], in0=H[:, 5:8, sl], in1=Mb[:, 5:8, :], op=ALU.add),
                 mbt[5:8] + hts[5:8], hts[5:8])

        def mixer_out(l, Y, ytile, brow=None):
            grow = R_MIXPOST + l * 8
            for tt in range(4):
                sl = slice(tt * 512, (tt + 1) * 512)
                for hh in range(2):
                    slot, wt = w_acquire(("mo", l, tt, hh))
                    v = slot.rearrange("p (c n) -> p c n", c=8)
                    for o4 in range(4):
                        o = hh * 4 + o4
                        bank = next_bank()
                        mm(pb[bank][:], [(v[:, c, o4 * 128:(o4 + 1) * 128], Y[:, c, sl]) for c in range(8)],
                           [wt] + [ytile(c, tt) for c in range(8)], pbt(bank))
                        post_evac(o, bank, tt, grow, brow)
                    w_release()
                post_finish(tt)

        tc_ = t("cst")
        P.dma("sp", "cst", I("dma_start", out=IDF, in_=cst[:, C_ID:C_ID + 128]), writes=[tc_])
        P.dma("sp", "cst", I("dma_start", out=RC, in_=cst[:, C_RC:C_RC + 8]), writes=[tc_])
        P.dma("pool", "cst", I("dma_start", out=IDB, in_=cst[:, C_ID:C_ID + 128]), writes=[tc_])
        P.dma("pool", "cst", I("dma_start", out=SMAT, in_=cst[:, C_SMAT:C_SMAT + 128]), writes=[tc_])
        P.dma("pool", "cst", I("dma_start", out=MASK3[:, 0:256], in_=cst[:, C_MASK:C_MASK + 256]), writes=[tc_])
        P.dma("pool", "cst", I("dma_start", out=MASK3[:, 256:384], in_=cst[:, C_MASK:C_MASK + 128]), writes=[tc_])
        P.op("dve", I("memset", ap=ONESB, constant=1.0), [], [t("ones")])
        P.op("dve", I("memset", ap=ONESF, constant=1.0), [], [t("ones")])
        pst = carve(PH, 1024).rearrange("p (k f) -> p k f", k=8)
        P.dma("sp", "ld0", I("dma_start", out=pst, in_=pth.rearrange("(k r) f -> r k f", r=128)), writes=[t("pst")])
        for hb in range(2):
            P.op("pe", [I("transpose", out=pb[hb][:, q * 128:(q + 1) * 128], in_=pst[:, hb * 4 + q, :], identity=IDF)
                        for q in range(4)], [t("pst"), tc_], pbt(hb))
            P.op("act", I("activation", out=PTAB[:, hb * 512:(hb + 1) * 512], in_=pb[hb][:], func=AF.Copy),
                 pbt(hb), [t("ptab")])
        for _ in range(NW):
            w_issue()
        P.barrier()

        def load_x(sq_i):
            stg = [carve(PH + i * 4096, 1024) for i in range(2)]
            for j in range(16):
                sb_ = stg[j % 2]
                ts_ = t("stin", j % 2)
                row0 = (sq_i * 16 + j) * 128
                P.dma("sp", f"ld{j % 2}", I("dma_start", out=sb_, in_=x[row0:row0 + 128, :]), writes=[ts_])
                for hb in range(2):
                    bank = next_bank()
                    P.op("pe", [I("transpose", out=pb[bank][:, q * 128:(q + 1) * 128],
                                  in_=sb_[:, (hb * 4 + q) * 128:(hb * 4 + q + 1) * 128], identity=IDF)
                                for q in range(4)], [ts_, tc_], pbt(bank))
                    copy_op(alt_engine(), H[:, hb * 4:hb * 4 + 4, j * 128:(j + 1) * 128],
                            pb[bank][:].rearrange("p (c n) -> p c n", c=4), pbt(bank),
                            [tH(c, j // 4) for c in range(hb * 4, hb * 4 + 4)])

        def store_y(sq_i):
            stg = [carve(PH + 8192 + i * 4096, 1024) for i in range(2)]
            outs = []
            for j in range(16):
                sb_ = stg[j % 2]
                ts_ = t("stout", j % 2)
                for hb in range(2):
                    bank = next_bank()
                    P.op("pe", [I("transpose", out=pb[bank][:, q * 128:(q + 1) * 128],
                                  in_=H[:, hb * 4 + q, j * 128:(j + 1) * 128], identity=IDF) for q in range(4)],
                         [tH(c, j // 4) for c in range(hb * 4, hb * 4 + 4)] + [tc_], pbt(bank))
                    copy_op(alt_engine(), sb_[:, hb * 512:(hb + 1) * 512], pb[bank][:], pbt(bank), [ts_])
                row0 = (sq_i * 16 + j) * 128
                ty = t("yout", sq_i, j)
                P.dma("sp", f"st{j % 2}", I("dma_start", out=y[row0:row0 + 128, :], in_=sb_), reads=[ts_], writes=[ty])
                outs.append(ty)
            return outs

        def ffn(l):
            ring["mm"] = [0, 1, 2, 3, 4, 5]
            Z = carve(PH, 22 * 1024, BF16).rearrange("p (k n) -> p k n", k=22)
            ACC = [[carve(PH + 45056 + (kd * 2 + b) * 4112, 1024) for b in range(2)] for kd in range(2)]
            SG = [carve(PH + 61504 + b * 4096, 1024) for b in range(2)]
            HALO = carve(PH + 80000, 88).rearrange("p (f n) -> p f n", f=44)
            prenorm(R_FFNPRE + l * 8, PH)
            P.fence(("pool",), [t("sqpre", 0), t("sqpre", 1)])
            npair = 0
            for hf in range(2):
                for j in range(11):
                    slot, wt = w_acquire(("fin", l, hf, j))
                    v = slot.rearrange("p (c s n) -> p c s n", c=8, s=2)
                    for ii in range(2):
                        i = 2 * j + ii
                        buf = npair % 2
                        npair += 1
                        for kd in range(2):
                            ft = kd * 22 + i
                            acc = ACC[kd][buf]
                            tacc = t("acc", kd, buf)
                            r0 = R_FFNCONV + l * 132 + ft
                            r1 = r0 + 44
                            r2 = r0 + 88
                            th = t("halo", ft)
                            bks = []
                            for tq in range(2):
                                tt = hf * 2 + tq
                                sl = slice(tt * 512, (tt + 1) * 512)
                                bank = next_bank()
                                bks.append(bank)
                                mm(pb[bank][:], [(v[:, c, kd, ii * 128:(ii + 1) * 128], A[:, c, sl]) for c in range(8)],
                                   [wt] + [tA(c, tt) for c in range(8)], pbt(bank))
                                P.op("act", I("activation", out=acc[:, tq * 512:(tq + 1) * 512], in_=pb[bank][:],
                                              func=AF.Copy, scale=col(r2)), pbt(bank) + [t("ptab")], [tacc])
                            b0, b1 = bks
                            P.op("dve", I("scalar_tensor_tensor", out=acc[:, 1:513], in0=pb[b0][:],
                                          scalar=col(r1), in1=acc[:, 1:513], op0=ALU.mult, op1=ALU.add),
                                 pbt(b0) + [tacc, t("ptab")], [tacc])
                            P.op("dve", I("scalar_tensor_tensor", out=acc[:, 2:514], in0=pb[b0][:],
                                          scalar=col(r0), in1=acc[:, 2:514], op0=ALU.mult, op1=ALU.add),
                                 pbt(b0) + [tacc, t("ptab")], [tacc])
                            P.op("dve", I("scalar_tensor_tensor", out=acc[:, 513:1024], in0=pb[b1][:, 0:511],
                                          scalar=col(r1), in1=acc[:, 513:1024], op0=ALU.mult, op1=ALU.add),
                                 pbt(b1) + [tacc, t("ptab")], [tacc])
                            P.op("dve", I("scalar_tensor_tensor", out=acc[:, 514:1024], in0=pb[b1][:, 0:510],
                                          scalar=col(r0), in1=acc[:, 514:1024], op0=ALU.mult, op1=ALU.add),
                                 pbt(b1) + [tacc, t("ptab")], [tacc])
                            if hf == 1:
                                P.op("dve", I("tensor_tensor", out=acc[:, 0:2], in0=acc[:, 0:2], in1=HALO[:, ft, :],
                                              op=ALU.add), [tacc, th], [tacc])
                            else:
                                P.op("dve", I("tensor_scalar", out=HALO[:, ft, :], in0=pb[b1][:, 510:512], scalar1=col(r0),
                                              scalar2=None, op0=ALU.mult), pbt(b1) + [t("ptab")], [th])
                                P.op("dve", I("scalar_tensor_tensor", out=HALO[:, ft, 0:1], in0=pb[b1][:, 511:512],
                                              scalar=col(r1), in1=HALO[:, ft, 0:1], op0=ALU.mult, op1=ALU.add),
                                     pbt(b1) + [th, t("ptab")], [th])
                        sg = SG[buf]
                        tsg = t("sg", buf)
                        P.op("act", I("activation", out=sg, in_=ACC[0][buf], func=AF.Silu),
                             [t("acc", 0, buf)], [tsg])
                        P.op("pool", I("tensor_tensor", out=Z[:, i, :], in0=sg, in1=ACC[1][buf], op=ALU.mult),
                             [tsg, t("acc", 1, buf)], [t("Z", i)])
                    w_release()
                acc_tiles = [t("acc", kd, b_) for kd in range(2) for b_ in range(2)] + [t("sg", b_) for b_ in range(2)]
                P.fence(("act", "dve", "pool"), acc_tiles)
                post_setup(PH + 45056, PH + 61504, PH + 63552)
                for tq in range(2):
                    tt = hf * 2 + tq
                    for o in range(8):
                        slot, wt = w_acquire(("fout", l, hf, tq, o))
                        v = slot[:, 0:2816].rearrange("p (k n) -> p k n", k=22)
                        bank = next_bank()
                        mm(pb[bank][:], [(v[:, k, :], Z[:, k, tq * 512:(tq + 1) * 512]) for k in range(22)],
                           [wt] + [t("Z", k) for k in range(22)], pbt(bank))
                        post_evac(o, bank, tt, R_FFNPOST + l * 8)
                        w_release()
                    post_finish(tt)
                if hf == 0:
                    mb_tiles = [t("Mb", mi, o) for mi in range(2) for o in range(8)] + [t("sqpost", 0), t("sqpost", 1)]
                    P.fence(("act", "dve", "pool"), mb_tiles)
                else:
                    P.barrier()

        def sc_mixer(l):
            ring["mm"] = [0, 1, 2, 3, 4, 5]
            Y = carve(PH, 8 * S, BF16).rearrange("p (c t) -> p c t", c=8)
            CH = carve(PH + 32768, 2056)
            BSB = carve(PH + 40992, 2048)
            HSB = [carve(PH + 49184 + b * 2048, 512) for b in range(2)]
            ACC = carve(PH + 53280, 2048)
            prenorm(R_MIXPRE + l * 8, PH)
            P.barrier()
            P.op("dve", I("memset", ap=CH[:, 0:2], constant=0.0), [], [t("ch")])
            nh = 0
            for i in range(8):
                slot, wt = w_acquire(("scin", l, i))
                v = slot[:, 0:3072].rearrange("p (c s n) -> p c s n", c=8, s=3)
                for tt in range(4):
                    sl = slice(tt * 512, (tt + 1) * 512)
                    banks = [next_bank() for _ in range(3)]
                    for s_ in range(3):
                        mm(pb[banks[s_]][:], [(v[:, c, s_, :], A[:, c, sl]) for c in range(8)],
                           [wt] + [tA(c, tt) for c in range(8)], pbt(banks[s_]))
                    hs = HSB[nh % 2]
                    ths = t("hsb", nh % 2)
                    nh += 1
                    P.op("act", I("activation", out=hs, in_=pb[banks[2]][:], func=AF.Copy), pbt(banks[2]), [ths])
                    P.op("dve", I("tensor_tensor", out=CH[:, 2 + tt * 512:2 + (tt + 1) * 512], in0=pb[banks[1]][:],
                                  in1=hs, op=ALU.mult), pbt(banks[1]) + [ths], [t("ch")])
                    P.op("act", I("activation", out=BSB[:, sl], in_=pb[banks[0]][:], func=AF.Copy),
                         pbt(banks[0]), [t("bsb")])
                w_release()
                r0 = R_SCCONV + i
                P.op("act", I("activation", out=ACC, in_=CH[:, 2:2050], func=AF.Copy, scale=col(r0 + 16)),
                     [t("ch"), t("ptab")], [t("scacc")])
                P.op("dve", I("scalar_tensor_tensor", out=ACC, in0=CH[:, 1:2049], scalar=col(r0 + 8), in1=ACC,
                              op0=ALU.mult, op1=ALU.add), [t("ch"), t("scacc"), t("ptab")], [t("scacc")])
                P.op("dve", I("scalar_tensor_tensor", out=ACC, in0=CH[:, 0:2048], scalar=col(r0), in1=ACC,
                              op0=ALU.mult, op1=ALU.add), [t("ch"), t("scacc"), t("ptab")], [t("scacc")])
                P.op("pool", I("tensor_tensor", out=Y[:, i, :], in0=BSB, in1=ACC, op=ALU.mult),
                     [t("bsb"), t("scacc")], [t("Y", i)])
            P.barrier()
            post_setup(PH + 61472, PH + 77856, PH + 32768)
            mixer_out(l, Y, lambda c, tt: t("Y", c))
            P.barrier()

        def cc_mixer(l):
            ring["mm"] = [0, 1, 2, 3, 4, 5]
            X = carve(PH, 8 * S).rearrange("p (c t) -> p c t", c=8)
            GB = carve(PH + 65536, 2080, BF16)
            DG = [carve(PH + 69696 + b_ * 7936, 31 * 128, BF16).rearrange("p (k n) -> p k n", k=31) for b_ in range(1)]
            SIG = [carve(PH + 77632 + b * 2048, 512) for b in range(2)]
            Yc = A
            prenorm(R_MIXPRE + l * 8, PH)
            P.barrier()
            P.op("dve", I("memset", ap=GB[:, 0:30], constant=0.0), [], [t("glu")])
            P.op("dve", I("tensor_tensor", out=BG, in0=PTAB[:, R_CCBPW2:R_CCBPW2 + 8],
                          in1=PTAB[:, R_MIXPOST + l * 8:R_MIXPOST + l * 8 + 8], op=ALU.mult), [t("ptab")], [t("bg")])
            ns = 0
            for i in range(8):
                slot, wt = w_acquire(("pw1", l, i))
                v = slot[:, 0:2048].rearrange("p (c s n) -> p c s n", c=8, s=2)
                dg = DG[0]
                P.op("pool", I("tensor_tensor", out=dg, in0=IDB.unsqueeze(1).broadcast_to([128, 31, 128]),
                               in1=PTAB[:, R_CCWDW + i:R_CCWDW + 248:8].unsqueeze(2).broadcast_to([128, 31, 128]),
                               op=ALU.mult), [t("ptab"), tc_], [t("dg")])
                for tt in range(4):
                    sl = slice(tt * 512, (tt + 1) * 512)
                    ba, bg_ = next_bank(), next_bank()
                    mm(pb[ba][:], [(v[:, c, 0, :], A[:, c, sl]) for c in range(8)],
                       [wt] + [tA(c, tt) for c in range(8)], pbt(ba))
                    mm(pb[bg_][:], [(v[:, c, 1, :], A[:, c, sl]) for c in range(8)],
                       [wt] + [tA(c, tt) for c in range(8)], pbt(bg_))
                    sg = SIG[ns % 2]
                    tsg = t("sig", ns % 2)
                    ns += 1
                    P.op("act", I("activation", out=sg, in_=pb[bg_][:], func=AF.Sigmoid, bias=col(R_CCBPW1 + 8 + i)),
                         pbt(bg_) + [t("ptab")], [tsg])
                    P.op("dve", I("scalar_tensor_tensor", out=GB[:, 30 + tt * 512:30 + (tt + 1) * 512], in0=pb[ba][:],
                                  scalar=col(R_CCBPW1 + i), in1=sg, op0=ALU.add, op1=ALU.mult),
                         pbt(ba) + [tsg, t("ptab")], [t("glu", tt)])
                w_release()
                for tt in range(4):
                    bank = next_bank()
                    gl = [t("glu")] + [t("glu", q) for q in range(max(0, tt - 1), tt + 1)]
                    mm(pb[bank][:], [(dg[:, k, :], GB[:, tt * 512 + k:tt * 512 + k + 512]) for k in range(31)],
                       gl + [t("dg")], pbt(bank))
                    P.op("act", I("activation", out=X[:, i, tt * 512:(tt + 1) * 512], in_=pb[bank][:], func=AF.Identity,
                                  bias=col(R_CCBDW + i)), pbt(bank) + [t("ptab")], [t("X", i, tt)])
            P.barrier()
            SQ = carve(PH + 65536, 4096, BF16).rearrange("p (c n) -> p c n", c=8)
            MEAN = carve(PH + 73728, 512)
            VAR = carve(PH + 75776, 512)
            for tt in range(4):
                sl = slice(tt * 512, (tt + 1) * 512)
                txs = [t("X", c, tt) for c in range(8)]
                P.op("act", I("activation", out=SQ, in_=X[:, :, sl], func=AF.Square), txs, [t("lnsq")])
                b1, b2 = next_bank(), next_bank()
                mm(pb[b1][:], [(ONESF, X[:, c, sl]) for c in range(8)], txs + [t("ones")], pbt(b1))
                mm(pb[b2][:], [(ONESB, SQ[:, c, :]) for c in range(8)], [t("lnsq"), t("ones")], pbt(b2))
                P.op("dve", I("tensor_scalar", out=MEAN, in0=pb[b1][:], scalar1=1.0 / D, scalar2=None, op0=ALU.mult),
                     pbt(b1), [t("mean")])
                P.op("dve", I("tensor_tensor", out=VAR, in0=MEAN, in1=MEAN, op=ALU.mult), [t("mean")], [t("var")])
                P.op("dve", I("scalar_tensor_tensor", out=VAR, in0=pb[b2][:], scalar=1.0 / D, in1=VAR,
                              op0=ALU.mult, op1=ALU.subtract), pbt(b2) + [t("var")], [t("var")])
                P.op("act", I("activation", out=VAR, in_=VAR, func=AF.Ln, bias=EPS_LN, scale=1.0),
                     [t("var"), tc_], [t("var")])
                P.op("act", I("activation", out=VAR, in_=VAR, func=AF.Exp, scale=-0.5), [t("var")], [t("var")])
                P.op("dve", I("tensor_tensor", out=X[:, :, sl], in0=X[:, :, sl],
                              in1=MEAN.unsqueeze(1).broadcast_to([128, 8, 512]), op=ALU.subtract),
                     txs + [t("mean")], txs)
                P.op("dve", I("tensor_tensor", out=X[:, :, sl], in0=X[:, :, sl],
                              in1=VAR.unsqueeze(1).broadcast_to([128, 8, 512]), op=ALU.mult), txs + [t("var")], txs)
                for c in range(8):
                    P.op("act", I("activation", out=Yc[:, c, sl], in_=X[:, c, sl], func=AF.Silu,
                                  scale=col(R_CCLNG + c), bias=col(R_CCLNB + c)), [t("X", c, tt), t("ptab")], [tA(c, tt)])
            P.barrier()
            post_setup(PH, PH + 16384, PH + 18432)
            mixer_out(l, Yc, tA, brow=R_CCBPW2)
            P.barrier()

        def attn_mixer(l, sq_i):
            ring["mm"] = [0, 1, 2]
            HB = H_OFF
            QKV = [[carve(HB + s_ * 12288 + k * 4096, S, BF16) for k in range(3)] for s_ in range(2)]
            VAB = [[carve(HB + 24576 + s_ * 8192 + k * 4096, 2048, BF16).rearrange("p (b n) -> p b n", b=16)
                    for k in range(2)] for s_ in range(2)]
            OT = carve(PH, 8 * S, BF16).rearrange("p (c t) -> p c t", c=8)
            COS = carve(PH + 32768, S)
            SIN = carve(PH + 40960, S)
            PTR = [carve(PH + 49152 + i * 512, 256, BF16).rearrange("p (j q) -> p j q", j=2) for i in range(4)]
            T12 = [[carve(PH + 51200 + (b * 2 + k) * 2048, 512) for k in range(2)] for b in range(2)]
            POSI = carve(PH + 59392, S, I32)
            XF = carve(PH + 67584, S)
            ZN = carve(PH + 59392, S)
            UAB = [[carve(HB + 40960, S), carve(HB + 49152, S)],
                   [carve(HB + 57344, S), carve(PH + 67584, S)]]
            for c in range(8):
                P.dma("sp", "park", I("dma_start", out=hpark[:, c * S:(c + 1) * S], in_=H[:, c, :]),
                      reads=[tH(c, tt) for tt in range(4)], writes=[t("hpark", c)])
            P.dma("sp", "pos", I("dma_start", out=POSI, in_=pos[sq_i:sq_i + 1, :].broadcast_to([128, S])),
                  writes=[t("posi")])
            P.op("dve", I("tensor_copy", out=XF, in_=POSI), [t("posi")], [t("xf")])
            P.op("dve", I("tensor_scalar", out=XF, in0=XF, scalar1=RC[:, 0:1], scalar2=None, op0=ALU.mult),
                 [t("xf"), tc_], [t("xf")])
            P.op("dve", I("tensor_copy", out=POSI, in_=XF), [t("xf")], [t("posi")])
            P.op("dve", I("tensor_copy", out=COS, in_=POSI), [t("posi")], [t("cos")])
            P.op("dve", I("tensor_tensor", out=XF, in0=XF, in1=COS, op=ALU.subtract), [t("xf"), t("cos")], [t("xf")])
            P.op("dve", I("scalar_tensor_tensor", out=SIN, in0=XF, scalar=0.5, in1=XF, op0=ALU.is_gt, op1=ALU.subtract),
                 [t("xf")], [t("sin")])
            P.op("dve", I("scalar_tensor_tensor", out=SIN, in0=SIN, scalar=0.5, in1=SIN, op0=ALU.is_gt, op1=ALU.subtract),
                 [t("sin")], [t("sin")])
            P.op("dve", I("tensor_scalar", out=XF, in0=XF, scalar1=0.25, scalar2=None, op0=ALU.add), [t("xf")], [t("xf")])
            P.op("dve", I("scalar_tensor_tensor", out=COS, in0=XF, scalar=0.5, in1=XF, op0=ALU.is_gt, op1=ALU.subtract),
                 [t("xf"), t("cos")], [t("cos")])
            P.op("dve", I("scalar_tensor_tensor", out=COS, in0=COS, scalar=0.5, in1=COS, op0=ALU.is_gt, op1=ALU.subtract),
                 [t("cos")], [t("cos")])
            TWO_PI = 2.0 * math.pi * (1.0 - 2e-7)
            P.op("act", I("activation", out=SIN, in_=SIN, func=AF.Sin, scale=TWO_PI), [t("sin")], [t("sin")])
            P.op("act", I("activation", out=COS, in_=COS, func=AF.Sin, scale=TWO_PI), [t("cos")], [t("cos")])
            prenorm(R_MIXPRE + l * 8, PH)
            P.barrier()
            for s_ in range(2):
                P.op("pool", I("memset", ap=VAB[s_][0][:, :, 64:128], constant=1.0), [], [t("va", s_)])
                P.op("pool", I("memset", ap=VAB[s_][1][:, :, 0:64], constant=1.0), [], [t("vb", s_)])
            cnt = {"rope": 0, "idx": 0}

            def proj_gen(p, g, set_):
                dil = GROUPS[g][1]
                nb = (S // dil) // 128
                VA, VB = VAB[set_]
                tq_ = [[t("qkv", set_, k, tt_) for tt_ in range(4)] for k in range(3)]
                slot, wt = w_acquire(("qkv", l, p, g))
                v = slot[:, 0:3072].rearrange("p (c s n) -> p c s n", c=8, s=3)
                pend = None
                for s3 in range(3):
                    dst = QKV[set_][s3]
                    for tt in range(4):
                        sl = slice(tt * 512, (tt + 1) * 512)
                        bank = next_bank()
                        bt = pbt(bank)
                        for c0 in range(0, 8, 2):
                            P.op("pe", [I("matmul", out=pb[bank][:], lhsT=v[:, c, s3, :], rhs=A[:, c, sl],
                                          start=(c == 0), stop=(c == 7)) for c in range(c0, c0 + 2)],
                                 [wt] + [tA(c, tt) for c in range(8)], bt)
                            if c0 == 0 and pend is not None:
                                pend()
                                pend = None
                            if c0 < 6:
                                yield
                        P.op("act", I("activation", out=dst[:, sl], in_=pb[bank][:], func=AF.Copy), bt, [tq_[s3][tt]])
                        if s3 < 2:
                            nr = cnt["rope"]
                            cnt["rope"] += 1
                            t1, t2 = T12[nr % 2]
                            tt1, tt2 = t("t1", nr % 2), t("t2", nr % 2)
                            P.op("dve", I("tensor_tensor", out=t1[0:80, :], in0=pb[bank][0:80, :], in1=COS[0:80, sl],
                                          op=ALU.mult), bt + [t("cos")], [tt1])

                            def rope_tail(dst=dst, sl=sl, t1=t1, t2=t2, tt1=tt1, tt2=tt2, tq=tq_[s3][tt]):
                                mm(pb[3][0:80, :], [(SMAT[:, 0:80], dst[:, sl])], [tq, tc_], pbt(3))
                                P.op("dve", I("tensor_tensor", out=t2[0:80, :], in0=pb[3][0:80, :], in1=SIN[0:80, sl],
                                              op=ALU.mult), pbt(3) + [t("sin")], [tt2])
                                P.op("dve", I("tensor_tensor", out=dst[0:80, sl], in0=t1[0:80, :], in1=t2[0:80, :],
                                              op=ALU.add), [tt1, tt2], [tq])
                            pend = rope_tail
                        yield
                if pend is not None:
                    pend()
                w_release()
                VTv = QKV[set_][2].rearrange("p (m d) -> p d m", d=dil)
                for h8 in range(2):
                    bank = next_bank()
                    ins = []
                    for b8 in range(8):
                        blk = h8 * 8 + b8
                        r, n = blk // nb, blk % nb
                        ins.append(I("transpose", out=pb_bf[bank][:, b8 * 128:(b8 + 1) * 128],
                                     in_=VTv[:, r, n * 128:(n + 1) * 128], identity=IDB))
                    P.op("pe", ins, tq_[2] + [tc_], pbt(bank))
                    src = pb_bf[bank][:].rearrange("p (b n) -> p b n", b=8)
                    P.op("dve", I("tensor_copy", out=VA[:, h8 * 8:h8 * 8 + 8, 0:64], in_=src[:, :, 0:64]),
                         pbt(bank), [t("va", set_)])
                    P.op("act", I("activation", out=VB[:, h8 * 8:h8 * 8 + 8, 64:128], in_=src[:, :, 64:128],
                                  func=AF.Copy), pbt(bank), [t("vb", set_)])
                    yield

            def core_gen(p, g, set_):
                dil = GROUPS[g][1]
                nb = (S // dil) // 128
                QT, KT, VT = QKV[set_]
                VA, VB = VAB[set_]
                U2 = UAB[p % 2]
                tq_ = [[t("qkv", set_, k, tt_) for tt_ in range(4)] for k in range(3)]
                QTv = QT.rearrange("p (m d) -> p d m", d=dil)
                KTv = KT.rearrange("p (m d) -> p d m", d=dil)
                items = [(r, n, hd) for r in range(dil) for n in range(nb) for hd in range(2)]
                st_ = {}

                def scores(it):
                    r, n, hd = it
                    hb = hd * 64
                    gi = cnt["idx"]
                    cnt["idx"] += 1
                    sbank = 4 + gi % 2
                    w = 256 if n + 1 < nb else 128
                    stv = pb[sbank][:, 0:w]
                    stt = pbt(sbank)
                    P.op("pe", I("matmul", out=stv, lhsT=KTv[hb:hb + 64, r, n * 128:(n + 1) * 128],
                                 rhs=QTv[hb:hb + 64, r, n * 128:n * 128 + w], start=True, stop=True),
                         tq_[0] + tq_[1], stt)
                    pt = PTR[gi % 4].rearrange("p j q -> p (j q)")
                    tpt = t("pt", gi % 4)
                    P.op("act", I("activation", out=pt[:, 0:w], in_=stv, func=AF.Exp, scale=0.125), stt, [tpt])
                    P.op("pool", I("tensor_tensor", out=pt[:, 0:w], in0=pt[:, 0:w], in1=MASKCP[:, 0:w], op=ALU.mult),
                         [tpt, tc_], [tpt])
                    return (pt, tpt, w, gi)

                def pv(it, stf):
                    r, n, hd = it
                    pt, tpt, w, gi = stf
                    xb = 6 + gi % 2
                    xq = pb[xb][:, 0:w]
                    xt = pbt(xb)
                    Vh = VA if hd == 0 else VB
                    tv = t("va", set_) if hd == 0 else t("vb", set_)
                    P.op("pe", I("matmul", out=xq, lhsT=Vh[:, r * nb + n, :], rhs=pt[:, 0:w], start=True, stop=True),
                         [tpt, tv], xt)
                    tu = t("u", p % 2, hd)
                    Uv = U2[hd].rearrange("p (m d) -> p d m", d=dil)[:, r, n * 128:n * 128 + w]
                    P.op("dve", I("tensor_tensor", out=Uv, in0=xq, in1=Uv, op=ALU.add), xt + [tu], [tu])

                if g == 0:
                    for hd in range(2):
                        P.op("pool", I("memset", ap=U2[hd], constant=0.0), [], [t("u", p % 2, hd)])
                SK = 2
                for idx in range(0, len(items) + SK, 2):
                    for k2 in range(2):
                        if idx + k2 < len(items):
                            st_[idx + k2] = scores(items[idx + k2])
                    for k2 in range(2):
                        if 0 <= idx + k2 - SK < len(items):
                            pv(items[idx + k2 - SK], st_.pop(idx + k2 - SK))
                    yield
                    yield

            def normalize(p):
                U2 = UAB[p % 2]
                tu0, tu1 = t("u", p % 2, 0), t("u", p % 2, 1)
                P.dma("sp", "zsw", I("dma_start", out=ZN[0:64, :], in_=U2[0][64:128, :]), reads=[tu0], writes=[t("zn")])
                P.dma("sp", "zsw", I("dma_start", out=ZN[64:128, :], in_=U2[1][0:64, :]), reads=[tu1], writes=[t("zn")])
                P.op("act", I("activation", out=ZN, in_=ZN, func=AF.Ln), [t("zn")], [t("zn")])
                P.op("act", I("activation", out=ZN, in_=ZN, func=AF.Exp, scale=-1.0), [t("zn")], [t("zn")])
                P.op("dve", I("tensor_tensor", out=OT[0:64, p, :], in0=U2[0][0:64, :], in1=ZN[0:64, :], op=ALU.mult),
                     [tu0, t("zn")], [t("ot", p)])
                P.op("dve", I("tensor_tensor", out=OT[64:128, p, :], in0=U2[1][64:128, :], in1=ZN[64:128, :],
                              op=ALU.mult), [tu1, t("zn")], [t("ot", p)])

            pgs = [(p, g) for p in range(8) for g in range(3)]
            for _ in proj_gen(pgs[0][0], pgs[0][1], 0):
                pass
            for k, (p, g) in enumerate(pgs):
                core = core_gen(p, g, k % 2)
                proj = proj_gen(pgs[k + 1][0], pgs[k + 1][1], (k + 1) % 2) if k + 1 < len(pgs) else None
                nstep = 0
                for _ in core:
                    nstep += 1
                    if proj is not None:
                        if next(proj, "done") == "done":
                            proj = None
                if proj is not None:
                    for _ in proj:
                        pass
                if g == 2:
                    normalize(p)
            P.barrier()
            for tt in range(4):
                sl = slice(tt * 512, (tt + 1) * 512)
                P.dma("sp", "unpark", I("dma_start", out=H[:, :, sl],
                                        in_=hpark.rearrange("p (c t) -> p c t", c=8)[:, :, sl]),
                      reads=[t("hpark", c) for c in range(8)], writes=[tH(c, tt) for c in range(8)])
            ring["mm"] = [0, 1, 2, 3, 4, 5]
            post_setup(PH + 32768, PH + 49152, PH + 51200)
            mixer_out(l, OT, lambda c, tt: t("ot", c))
            P.barrier()

        outs = []
        for sq_i in range(nseq):
            load_x(sq_i)
            P.barrier()
            for l in layers:
                kind = l % 3
                if do_mixer:
                    if kind == 0:
                        attn_mixer(l, sq_i)
                    elif kind == 1:
                        sc_mixer(l)
                    else:
                        cc_mixer(l)
                if do_ffn:
                    ffn(l)
            ring["mm"] = [0, 1, 2, 3]
            outs += store_y(sq_i)
        P.wait_all("sp", outs)
        assert next(keygen, None) is None and not WS["pending"]
        P.emit(block)
    return nc


def _consts():
    c = np.zeros((128, NCST), np.float32)
    c[:, C_ID:C_ID + 128] = np.eye(128, dtype=np.float32)
    sm = np.zeros((128, 128), np.float32)
    for base in (0, 64):
        for m in range(8):
            sm[base + m + 8, base + m] = -1.0
            sm[base + m, base + m + 8] = 1.0
    c[:, C_SMAT:C_SMAT + 128] = sm
    k = np.arange(128)[:, None]
    q = np.arange(128)[None, :]
    c[:, C_MASK:C_MASK + 128] = (k >= q).astype(np.float32)
    c[:, C_MASK + 128:C_MASK + 256] = (k <= q).astype(np.float32)
    inv = (500000.0 ** (-np.arange(0, 16, 2, dtype=np.float32) / 16.0)).astype(np.float32)
    for p in range(128):
        if p % 64 < 16:
            c[p, C_RC] = inv[p % 8] / np.float32(2.0 * math.pi)
    c[:, C_RC + 1] = 1e-6
    c[:, C_RC + 2] = 1e-5
    return c


def _pack_params(inp):
    f = lambda a: np.ascontiguousarray(np.asarray(a, np.float32)).reshape(-1, 128)
    rows = np.concatenate([
        f(inp["mix_norm_pre"]), f(inp["mix_norm_post"]), f(inp["ffn_norm_pre"]), f(inp["ffn_norm_post"]),
        f(inp["sc_w_conv"]), f(inp["cc_b_pw1"]), f(inp["cc_w_dw"]), f(inp["cc_b_dw"]),
        f(inp["cc_ln_g"]), f(inp["cc_ln_b"]), f(inp["cc_b_pw2"]), f(inp["ffn_w_conv"])], axis=0)
    assert rows.shape[0] == 976
    out = np.zeros((1024, 128), np.float32)
    out[:976] = rows
    return out


def _layer_weights(inp, l):
    kind, j = l % 3, l // 3
    a = lambda k, i: np.ascontiguousarray(np.asarray(inp[k][i], np.float32))
    d = {}
    if kind == 0:
        d[f"w{l}_min"] = a("attn_w_qkv", j)
        d[f"w{l}_mout"] = a("attn_w_o", j)
    elif kind == 1:
        d[f"w{l}_min"] = a("sc_w_in", j)
        d[f"w{l}_mout"] = a("sc_w_out", j)
    else:
        d[f"w{l}_min"] = a("cc_w_pw1", j)
        d[f"w{l}_mout"] = a("cc_w_pw2", j)
    d[f"w{l}_fin"] = a("ffn_w_in", l)
    d[f"w{l}_fout"] = a("ffn_w_out", l)
    return d


LAUNCH_GROUPS = [[0, 1, 2, 3]]
_NC_CACHE = {}


def kernel(**inp):
    x = np.ascontiguousarray(np.asarray(inp["x"], np.float32))
    positions = np.ascontiguousarray(np.asarray(inp["positions"], np.int32))
    pth = _pack_params(inp)
    cst = _consts()
    cur = [x[c * SEQ_PER_CORE:(c + 1) * SEQ_PER_CORE].reshape(SEQ_PER_CORE * S, D) for c in range(NCORES)]
    for grp in LAUNCH_GROUPS:
        key = tuple(grp)
        if key not in _NC_CACHE:
            _NC_CACHE[key] = build(list(grp), SEQ_PER_CORE)
        nc = _NC_CACHE[key]
        wd = {}
        for l in grp:
            wd.update(_layer_weights(inp, l))
        in_maps = []
        for c in range(NCORES):
            m = {"x": cur[c], "pos": positions[c * SEQ_PER_CORE:(c + 1) * SEQ_PER_CORE], "pth": pth, "cst": cst}
            m.update(wd)
            in_maps.append(m)
        res = run_bass_kernel_spmd(nc, in_maps, core_ids=list(range(NCORES)))
        cur = [np.asarray(r["y"], np.float32) for r in res.results]
    out = np.stack([c.reshape(SEQ_PER_CORE, S, D) for c in cur], axis=0).reshape(NCORES * SEQ_PER_CORE, S, D)
    return out
```

```python
import contextlib
import math
import numpy as np
import concourse.bass as bass
import concourse.mybir as mybir
from concourse.bass_utils import run_bass_kernel_spmd

F32 = mybir.dt.float32
BF16 = mybir.dt.bfloat16
I32 = mybir.dt.int32
AF = mybir.ActivationFunctionType
ALU = mybir.AluOpType

S = 2048
D = 1024
FF = 2816
NCORES = 8
SEQ_PER_CORE = 4
DEPTH = 4
GROUPS = ((128, 1), (512, 4), (2048, 16))

R_MIXPRE, R_MIXPOST, R_FFNPRE, R_FFNPOST = 0, 32, 64, 96
R_SCCONV = 128
R_CCBPW1 = 152
R_CCWDW = 168
R_CCBDW, R_CCLNG, R_CCLNB, R_CCBPW2 = 416, 424, 432, 440
R_FFNCONV = 448
C_ID, C_SMAT, C_MASK, C_RC = 0, 128, 256, 512
NCST = 640

SEM_CAP = 30000
ENGS = ("pe", "act", "dve", "pool", "sp")


class T:
    __slots__ = ("w", "r")

    def __init__(self):
        self.w = None
        self.r = {}


class TP(T):
    __slots__ = ()


class Prog:
    def __init__(self, nc, stack):
        self.nc = nc
        self.stack = stack
        self.q = {e: [] for e in ENGS}
        self.cnt = {}
        self.seen = {e: {} for e in ENGS}
        self.sems = {}

    def _sem(self, chan, idx):
        lst = self.sems.setdefault(chan, [])
        while len(lst) <= idx:
            lst.append(self.stack.enter_context(self.nc.semaphore(f"s_{chan}_{len(lst)}")))
        return lst[idx]

    def _ev(self, chan, c, step):
        per = SEM_CAP // step
        return self._sem(chan, (c - 1) // per), ((c - 1) % per + 1) * step

    def _deps(self, eng, reads, writes, skip=None):
        deps = {}

        def add(ev):
            if ev is None:
                return
            ch, c = ev
            if (ch == "pe" and eng == "pe") or ch == skip:
                return
            if ch not in ENGS:
                c = self.cnt[ch]
            if deps.get(ch, 0) < c:
                deps[ch] = c
        for t in reads:
            add(t.w)
        for t in writes:
            add(t.w)
            for ch, c in t.r.items():
                add((ch, c))
        out = []
        for ch, c in deps.items():
            if self.seen[eng].get(ch, 0) < c:
                self.seen[eng][ch] = c
                out.append((ch, c))
        return out

    def _mark(self, ev, reads, writes):
        ch, c = ev
        for t in writes:
            t.w = ev
            t.r = {}
        for t in reads:
            if t.r.get(ch, 0) < c:
                t.r[ch] = c

    def op(self, eng, fn, reads=(), writes=()):
        if any(isinstance(t, TP) for t in reads):
            writes = list(writes) + [t for t in reads if isinstance(t, TP)]
            reads = [t for t in reads if not isinstance(t, TP)]
        waits = self._deps(eng, reads, writes)
        c = self.cnt.get(eng, 0) + 1
        self.cnt[eng] = c
        self.q[eng].append((waits, fn, (eng, c, 1)))
        self._mark((eng, c), reads, writes)

    def dma(self, eng, chan, fn, reads=(), writes=()):
        waits = self._deps(eng, reads, writes, skip=chan)
        c = self.cnt.get(chan, 0) + 1
        self.cnt[chan] = c
        self.q[eng].append((waits, fn, (chan, c, 16)))
        self._mark((chan, c), reads, writes)

    def barrier(self):
        for e in ENGS:
            waits = []
            for ch, c in self.cnt.items():
                if ch == e or ch.startswith("w"):
                    continue
                if self.seen[e].get(ch, 0) < c:
                    self.seen[e][ch] = c
                    waits.append((ch, c))
            if waits:
                self.q[e].append((waits, None, None))

    def fence(self, engs, tiles):
        for e in engs:
            waits = self._deps(e, (), tiles)
            if waits:
                self.q[e].append((waits, None, None))

    def wait_all(self, eng, tiles):
        waits = self._deps(eng, tiles, ())
        self.q[eng].append((waits, None, None))

    def emit(self, block):
        handles = {"pe": block.tensor, "act": block.scalar, "dve": block.vector,
                   "pool": block.gpsimd, "sp": block.sync}
        for e in ENGS:
            q = self.q[e]
            if not q:
                continue

            def body(h, q=q):
                for waits, fn, inc in q:
                    for ch, c in waits:
                        s, v = self._ev(ch, c, 1 if ch in ENGS else 16)
                        h.wait_ge(s, v)
                    if fn is None:
                        continue
                    ins = None
                    for name, kw in (fn if isinstance(fn, list) else [fn]):
                        ins = getattr(h, name)(**kw)
                    ch, c, step = inc
                    s, v = self._ev(ch, c, step)
                    ins.then_inc(s, step)
            handles[e](body)


def _wkeys(layers, nseq, do_mixer, do_ffn):
    for _ in range(nseq):
        for l in layers:
            kind = l % 3
            if do_mixer:
                if kind == 0:
                    for p in range(8):
                        for g in range(3):
                            yield ("qkv", l, p, g)
                elif kind == 1:
                    for i in range(8):
                        yield ("scin", l, i)
                else:
                    for i in range(8):
                        yield ("pw1", l, i)
                for tt in range(4):
                    for hh in range(2):
                        yield ("mo", l, tt, hh)
            if do_ffn:
                for hf in range(2):
                    for j in range(11):
                        yield ("fin", l, hf, j)
                    for tq in range(2):
                        for o in range(8):
                            yield ("fout", l, hf, tq, o)


def I(name, **kw):
    return (name, kw)


def build(layers, nseq, do_mixer=True, do_ffn=True):
    nc = bass.Bass("TRN2", target_bir_lowering=False)
    dt = nc.dram_tensor
    x = dt("x", [nseq * S, D], F32, kind="ExternalInput").ap()
    y = dt("y", [nseq * S, D], F32, kind="ExternalOutput").ap()
    pos = dt("pos", [nseq, S], I32, kind="ExternalInput").ap()
    pth = dt("pth", [1024, 128], F32, kind="ExternalInput").ap()
    cst = dt("cst", [128, NCST], F32, kind="ExternalInput").ap()
    W = {}
    has_attn = False
    for l in layers:
        kind = l % 3
        if kind == 0:
            has_attn = True
            W[l, "min"] = dt(f"w{l}_min", [D, 9216], F32, kind="ExternalInput").ap()
        elif kind == 1:
            W[l, "min"] = dt(f"w{l}_min", [D, 3072], F32, kind="ExternalInput").ap()
        else:
            W[l, "min"] = dt(f"w{l}_min", [D, 2048], F32, kind="ExternalInput").ap()
        W[l, "mout"] = dt(f"w{l}_mout", [D, D], F32, kind="ExternalInput").ap()
        W[l, "fin"] = dt(f"w{l}_fin", [D, 2 * FF], F32, kind="ExternalInput").ap()
        W[l, "fout"] = dt(f"w{l}_fout", [FF, D], F32, kind="ExternalInput").ap()
    hpark = dt("hpark", [128, 8 * S], F32, kind="Internal").ap() if has_attn else None

    with contextlib.ExitStack() as st:
        ARENA_BYTES = 212480
        arena = st.enter_context(nc.sbuf_tensor("arena", [128, ARENA_BYTES // 4], F32))
        arena_bf = arena.bitcast(BF16)
        arena_i = arena.bitcast(I32)
        pb = [st.enter_context(nc.psum_tensor(f"pb{i}", [128, 512], F32)) for i in range(8)]
        pb_bf = [b.bitcast(BF16) for b in pb]
        block = st.enter_context(nc.Block())
        P = Prog(nc, st)

        def carve(off, n, dtype=F32):
            if dtype == BF16:
                assert off % 2 == 0
                return arena_bf[:, off // 2: off // 2 + n]
            assert off % 4 == 0
            v = arena if dtype == F32 else arena_i
            return v[:, off // 4: off // 4 + n]

        tiles = {}

        def t(*key):
            r = tiles.get(key)
            if r is None:
                r = tiles[key] = T()
            return r

        ptiles = [TP() for _ in range(8)]

        def pbt(i, q0=0, q1=4):
            return [ptiles[i]]

        PTAB = carve(0, 1024)
        IDF = carve(4096, 128)
        ONESB = carve(4608, 128, BF16)
        IDB = carve(4864, 128, BF16)
        SMAT = carve(5120, 128, BF16)
        MASK3 = carve(5376, 384, BF16)
        MASKCP = MASK3[:, 128:384]
        RC = carve(6688, 8)
        ONESF = carve(6144, 128)
        BG = carve(6656, 8)
        WS0 = 7168
        NW = 3
        wslots = [carve(WS0 + i * 8192, 4096, BF16) for i in range(NW)]
        A_OFF = WS0 + NW * 8192
        A = carve(A_OFF, 8 * S, BF16).rearrange("p (c t) -> p c t", c=8)
        H_OFF = A_OFF + 32768
        H = carve(H_OFF, 8 * S).rearrange("p (c t) -> p c t", c=8)
        PH = H_OFF + 65536
        assert ARENA_BYTES - PH >= 79904
        EPS_RMS = RC[:, 1:2]
        EPS_LN = RC[:, 2:3]

        def col(r):
            return PTAB[:, r:r + 1]

        def tA(c, tt):
            return t("A", c, tt)

        def tH(c, tt):
            return t("H", c, tt)

        keygen = _wkeys(layers, nseq, do_mixer, do_ffn)
        WS = {"pending": [], "nxt": 0}
        wT = [T() for _ in range(NW)]

        def w_issue():
            key = next(keygen, None)
            if key is None:
                return
            s = WS["nxt"]
            WS["nxt"] = (s + 1) % NW
            slot = wslots[s]
            kind, l = key[0], key[1]
            dmas = []
            if kind == "fin":
                j = key[3]
                v = slot.rearrange("p (c s n) -> p c s n", c=8, s=2)
                src = W[l, "fin"].rearrange("(c p) n -> p c n", p=128)
                for h in range(2):
                    dmas.append((v[:, :, h, :], src[:, :, h * FF + j * 256: h * FF + (j + 1) * 256]))
            elif kind == "fout":
                o = key[4]
                v = slot[:, 0:2816].rearrange("p (k n) -> p k n", k=22)
                src = W[l, "fout"].rearrange("(k p) n -> p k n", p=128)
                dmas.append((v, src[:, :, o * 128:(o + 1) * 128]))
            elif kind in ("qkv", "scin", "pw1"):
                ns = 2 if kind == "pw1" else 3
                v = slot[:, 0:ns * 1024].rearrange("p (c s n) -> p c s n", c=8, s=ns)
                src = W[l, "min"].rearrange("(c p) n -> p c n", p=128)
                for s_ in range(ns):
                    if kind == "qkv":
                        cl = s_ * 3072 + key[3] * 1024 + key[2] * 128
                    else:
                        cl = s_ * 1024 + key[2] * 128
                    dmas.append((v[:, :, s_, :], src[:, :, cl:cl + 128]))
            elif kind == "mo":
                hh = key[3]
                v = slot.rearrange("p (c n) -> p c n", c=8)
                src = W[l, "mout"].rearrange("(c p) n -> p c n", p=128)
                dmas.append((v, src[:, :, hh * 512:(hh + 1) * 512]))
            else:
                raise AssertionError(key)
            for (o_, i_) in dmas:
                P.dma("pool", f"w{s}", I("dma_start", out=o_, in_=i_), writes=[wT[s]])
            WS["pending"].append((key, s))

        def w_acquire(key):
            k, s = WS["pending"].pop(0)
            assert k == key, (k, key)
            return wslots[s], wT[s]

        def w_release():
            w_issue()

        def mm(out, pairs, reads, writes):
            n = len(pairs)
            P.op("pe", [I("matmul", out=out, lhsT=l_, rhs=r_, start=(i == 0), stop=(i == n - 1))
                        for i, (l_, r_) in enumerate(pairs)], reads, writes)

        ring = {"mm": [0, 1, 2, 3], "i": 0}

        def next_bank():
            b = ring["mm"][ring["i"] % len(ring["mm"])]
            ring["i"] += 1
            return b

        flip = {"n": 0}

        def alt_engine():
            flip["n"] += 1
            return "act" if flip["n"] % 2 else "dve"

        def copy_op(eng, out, in_, reads, writes):
            if eng == "act":
                P.op("act", I("activation", out=out, in_=in_, func=AF.Copy), reads, writes)
            else:
                P.op(eng, I("tensor_copy", out=out, in_=in_), reads, writes)

        def rstd_inplace(bank, eps_ap):
            bt = pbt(bank)
            P.op("act", I("activation", out=pb[bank][:], in_=pb[bank][:], func=AF.Ln, bias=eps_ap, scale=1.0 / D),
                 bt + [t("cst")], bt)
            P.op("act", I("activation", out=pb[bank][:], in_=pb[bank][:], func=AF.Exp, scale=-0.5), bt, bt)

        def prenorm(row, sq_off):
            sqs = [carve(sq_off + i * 8192, 4096, BF16).rearrange("p (c n) -> p c n", c=8) for i in range(2)]

            def square(tt):
                sl = slice(tt * 512, (tt + 1) * 512)
                P.op("act", I("activation", out=sqs[tt % 2], in_=H[:, :, sl], func=AF.Square),
                     [tH(c, tt) for c in range(8)], [t("sqpre", tt % 2)])
            square(0)
            square(1)
            for tt in range(4):
                sq = sqs[tt % 2]
                tsq = t("sqpre", tt % 2)
                sl = slice(tt * 512, (tt + 1) * 512)
                bank = 6 + tt % 2
                mm(pb[bank][:], [(ONESB, sq[:, c, :]) for c in range(8)], [tsq, t("cst")], pbt(bank))
                rstd_inplace(bank, EPS_RMS)
                if tt + 2 < 4:
                    square(tt + 2)
                for c in range(8):
                    P.op("dve", I("scalar_tensor_tensor", out=A[:, c, sl], in0=H[:, c, sl], scalar=col(row + c),
                                  in1=pb[bank][:], op0=ALU.mult, op1=ALU.mult),
                         [tH(c, tt), t("ptab")] + pbt(bank), [tA(c, tt)])

        Post = {}

        def post_setup(mb_off, sq_off, mb_off2=None):
            offs = [mb_off, mb_off if mb_off2 is None else mb_off2]
            Post["Mb"] = [carve(o_, 4096).rearrange("p (o n) -> p o n", o=8) for o_ in offs]
            Post["two"] = mb_off2 is not None
            Post["sq"] = [carve(sq_off + i * 1024, 512, BF16) for i in range(2)]
            Post["n"] = 0

        def post_evac(o, bank, tt, grow, brow=None):
            mi = tt % 2 if Post["two"] else 0
            Mb = Post["Mb"][mi]
            sqb = Post["sq"][Post["n"] % 2]
            tsq = t("sqpost", Post["n"] % 2)
            Post["n"] += 1
            sbank = 6 + tt % 2
            bt = pbt(bank)
            if brow is None:
                P.op("act", I("activation", out=sqb, in_=pb[bank][:], func=AF.Square), bt, [tsq])
                P.op("act", I("activation", out=Mb[:, o, :], in_=pb[bank][:], func=AF.Copy, scale=col(grow + o)),
                     bt + [t("ptab")], [t("Mb", mi, o)])
            else:
                P.op("act", I("activation", out=sqb, in_=pb[bank][:], func=AF.Square, bias=col(brow + o)),
                     bt + [t("ptab")], [tsq])
                P.op("act", I("activation", out=Mb[:, o, :], in_=pb[bank][:], func=AF.Identity,
                              scale=col(grow + o), bias=BG[:, o:o + 1]),
                     bt + [t("ptab"), t("bg")], [t("Mb", mi, o)])
            P.op("pe", I("matmul", out=pb[sbank][:], lhsT=ONESB, rhs=sqb, start=(o == 0), stop=(o == 7)),
                 [tsq, t("cst")], pbt(sbank))

        def post_finish(tt):
            mi = tt % 2 if Post["two"] else 0
            Mb = Post["Mb"][mi]
            sbank = 6 + tt % 2
            rstd_inplace(sbank, EPS_RMS)
            sl = slice(tt * 512, (tt + 1) * 512)
            mbt = [t("Mb", mi, o) for o in range(8)]
            P.op("dve", I("tensor_tensor", out=Mb, in0=Mb,
                          in1=pb[sbank][:].unsqueeze(1).broadcast_to([128, 8, 512]), op=ALU.mult),
                 mbt + pbt(sbank), mbt)
            hts = [tH(c, tt) for c in range(8)]
            P.op("pool", I("tensor_tensor", out=H[:, 0:5, sl], in0=H[:, 0:5, sl], in1=Mb[:, 0:5, :], op=ALU.add),
                 mbt[0:5] + hts[0:5], hts[0:5])
            P.op("dve", I("tensor_tensor", out=H[:, 5:8, sl], in0=H[:, 5:8, sl], in1=Mb[:, 5:8, :], op=ALU.add),
                 mbt[5:8] + hts[5:8], hts[5:8])

        def mixer_out(l, Y, ytile, brow=None):
            grow = R_MIXPOST + l * 8
            for tt in range(4):
                sl = slice(tt * 512, (tt + 1) * 512)
                for hh in range(2):
                    slot, wt = w_acquire(("mo", l, tt, hh))
                    v = slot.rearrange("p (c n) -> p c n", c=8)
                    for o4 in range(4):
                        o = hh * 4 + o4
                        bank = next_bank()
                        mm(pb[bank][:], [(v[:, c, o4 * 128:(o4 + 1) * 128], Y[:, c, sl]) for c in range(8)],
                           [wt] + [ytile(c, tt) for c in range(8)], pbt(bank))
                        post_evac(o, bank, tt, grow, brow)
                    w_release()
                post_finish(tt)

        tc_ = t("cst")
        P.dma("sp", "cst", I("dma_start", out=IDF, in_=cst[:, C_ID:C_ID + 128]), writes=[tc_])
        P.dma("sp", "cst", I("dma_start", out=RC, in_=cst[:, C_RC:C_RC + 8]), writes=[tc_])
        P.dma("pool", "cst", I("dma_start", out=IDB, in_=cst[:, C_ID:C_ID + 128]), writes=[tc_])
        P.dma("pool", "cst", I("dma_start", out=SMAT, in_=cst[:, C_SMAT:C_SMAT + 128]), writes=[tc_])
        P.dma("pool", "cst", I("dma_start", out=MASK3[:, 0:256], in_=cst[:, C_MASK:C_MASK + 256]), writes=[tc_])
        P.dma("pool", "cst", I("dma_start", out=MASK3[:, 256:384], in_=cst[:, C_MASK:C_MASK + 128]), writes=[tc_])
        P.op("dve", I("memset", ap=ONESB, constant=1.0), [], [t("ones")])
        P.op("dve", I("memset", ap=ONESF, constant=1.0), [], [t("ones")])
        pst = carve(PH, 1024).rearrange("p (k f) -> p k f", k=8)
        P.dma("sp", "ld0", I("dma_start", out=pst, in_=pth.rearrange("(k r) f -> r k f", r=128)), writes=[t("pst")])
        for hb in range(2):
            P.op("pe", [I("transpose", out=pb[hb][:, q * 128:(q + 1) * 128], in_=pst[:, hb * 4 + q, :], identity=IDF)
                        for q in range(4)], [t("pst"), tc_], pbt(hb))
            P.op("act", I("activation", out=PTAB[:, hb * 512:(hb + 1) * 512], in_=pb[hb][:], func=AF.Copy),
                 pbt(hb), [t("ptab")])
        for _ in range(NW):
            w_issue()
        P.barrier()

        def load_x(sq_i):
            stg = [carve(PH + i * 4096, 1024) for i in range(2)]
            for j in range(16):
                sb_ = stg[j % 2]
                ts_ = t("stin", j % 2)
                row0 = (sq_i * 16 + j) * 128
                P.dma("sp", f"ld{j % 2}", I("dma_start", out=sb_, in_=x[row0:row0 + 128, :]), writes=[ts_])
                for hb in range(2):
                    bank = next_bank()
                    P.op("pe", [I("transpose", out=pb[bank][:, q * 128:(q + 1) * 128],
                                  in_=sb_[:, (hb * 4 + q) * 128:(hb * 4 + q + 1) * 128], identity=IDF)
                                for q in range(4)], [ts_, tc_], pbt(bank))
                    copy_op(alt_engine(), H[:, hb * 4:hb * 4 + 4, j * 128:(j + 1) * 128],
                            pb[bank][:].rearrange("p (c n) -> p c n", c=4), pbt(bank),
                            [tH(c, j // 4) for c in range(hb * 4, hb * 4 + 4)])

        def store_y(sq_i):
            stg = [carve(PH + 8192 + i * 4096, 1024) for i in range(2)]
            outs = []
            for j in range(16):
                sb_ = stg[j % 2]
                ts_ = t("stout", j % 2)
                for hb in range(2):
                    bank = next_bank()
                    P.op("pe", [I("transpose", out=pb[bank][:, q * 128:(q + 1) * 128],
                                  in_=H[:, hb * 4 + q, j * 128:(j + 1) * 128], identity=IDF) for q in range(4)],
                         [tH(c, j // 4) for c in range(hb * 4, hb * 4 + 4)] + [tc_], pbt(bank))
                    copy_op(alt_engine(), sb_[:, hb * 512:(hb + 1) * 512], pb[bank][:], pbt(bank), [ts_])
                row0 = (sq_i * 16 + j) * 128
                ty = t("yout", sq_i, j)
                P.dma("sp", f"st{j % 2}", I("dma_start", out=y[row0:row0 + 128, :], in_=sb_), reads=[ts_], writes=[ty])
                outs.append(ty)
            return outs

        def ffn(l):
            ring["mm"] = [0, 1, 2, 3, 4, 5]
            Z = carve(PH, 22 * 1024, BF16).rearrange("p (k n) -> p k n", k=22)
            ACC = [[carve(PH + 45056 + (kd * 2 + b) * 4112, 1024) for b in range(2)] for kd in range(2)]
            SG = [carve(PH + 61504 + b * 4096, 1024) for b in range(2)]
            HALO = carve(PH + 80000, 88).rearrange("p (f n) -> p f n", f=44)
            prenorm(R_FFNPRE + l * 8, PH)
            P.fence(("pool",), [t("sqpre", 0), t("sqpre", 1)])
            npair = 0
            for hf in range(2):
                for j in range(11):
                    slot, wt = w_acquire(("fin", l, hf, j))
                    v = slot.rearrange("p (c s n) -> p c s n", c=8, s=2)
                    for ii in range(2):
                        i = 2 * j + ii
                        buf = npair % 2
                        npair += 1
                        for kd in range(2):
                            ft = kd * 22 + i
                            acc = ACC[kd][buf]
                            tacc = t("acc", kd, buf)
                            r0 = R_FFNCONV + l * 132 + ft
                            r1 = r0 + 44
                            r2 = r0 + 88
                            th = t("halo", ft)
                            bks = []
                            for tq in range(2):
                                tt = hf * 2 + tq
                                sl = slice(tt * 512, (tt + 1) * 512)
                                bank = next_bank()
                                bks.append(bank)
                                mm(pb[bank][:], [(v[:, c, kd, ii * 128:(ii + 1) * 128], A[:, c, sl]) for c in range(8)],
                                   [wt] + [tA(c, tt) for c in range(8)], pbt(bank))
                                P.op("act", I("activation", out=acc[:, tq * 512:(tq + 1) * 512], in_=pb[bank][:],
                                              func=AF.Copy, scale=col(r2)), pbt(bank) + [t("ptab")], [tacc])
                            b0, b1 = bks
                            P.op("dve", I("scalar_tensor_tensor", out=acc[:, 1:513], in0=pb[b0][:],
                                          scalar=col(r1), in1=acc[:, 1:513], op0=ALU.mult, op1=ALU.add),
                                 pbt(b0) + [tacc, t("ptab")], [tacc])
                            P.op("dve", I("scalar_tensor_tensor", out=acc[:, 2:514], in0=pb[b0][:],
                                          scalar=col(r0), in1=acc[:, 2:514], op0=ALU.mult, op1=ALU.add),
                                 pbt(b0) + [tacc, t("ptab")], [tacc])
                            P.op("dve", I("scalar_tensor_tensor", out=acc[:, 513:1024], in0=pb[b1][:, 0:511],
                                          scalar=col(r1), in1=acc[:, 513:1024], op0=ALU.mult, op1=ALU.add),
                                 pbt(b1) + [tacc, t("ptab")], [tacc])
                            P.op("dve", I("scalar_tensor_tensor", out=acc[:, 514:1024], in0=pb[b1][:, 0:510],
                                          scalar=col(r0), in1=acc[:, 514:1024], op0=ALU.mult, op1=ALU.add),
                                 pbt(b1) + [tacc, t("ptab")], [tacc])
                            if hf == 1:
                                P.op("dve", I("tensor_tensor", out=acc[:, 0:2], in0=acc[:, 0:2], in1=HALO[:, ft, :],
                                              op=ALU.add), [tacc, th], [tacc])
                            else:
                                P.op("dve", I("tensor_scalar", out=HALO[:, ft, :], in0=pb[b1][:, 510:512], scalar1=col(r0),
                                              scalar2=None, op0=ALU.mult), pbt(b1) + [t("ptab")], [th])
                                P.op("dve", I("scalar_tensor_tensor", out=HALO[:, ft, 0:1], in0=pb[b1][:, 511:512],
                                              scalar=col(r1), in1=HALO[:, ft, 0:1], op0=ALU.mult, op1=ALU.add),
                                     pbt(b1) + [th, t("ptab")], [th])
                        sg = SG[buf]
                        tsg = t("sg", buf)
                        P.op("act", I("activation", out=sg, in_=ACC[0][buf], func=AF.Silu),
                             [t("acc", 0, buf)], [tsg])
                        P.op("pool", I("tensor_tensor", out=Z[:, i, :], in0=sg, in1=ACC[1][buf], op=ALU.mult),
                             [tsg, t("acc", 1, buf)], [t("Z", i)])
                    w_release()
                acc_tiles = [t("acc", kd, b_) for kd in range(2) for b_ in range(2)] + [t("sg", b_) for b_ in range(2)]
                P.fence(("act", "dve", "pool"), acc_tiles)
                post_setup(PH + 45056, PH + 61504, PH + 63552)
                for tq in range(2):
                    tt = hf * 2 + tq
                    for o in range(8):
                        slot, wt = w_acquire(("fout", l, hf, tq, o))
                        v = slot[:, 0:2816].rearrange("p (k n) -> p k n", k=22)
                        bank = next_bank()
                        P.op("pe", [I("matmul", out=pb[bank][:], lhsT=v[:, k, :], rhs=Z[:, k, tq * 512:(tq + 1) * 512],
                                      start=(k == 0), stop=False) for k in range(18)],
                             [wt] + [t("Z", k) for k in range(18)], pbt(bank))
                        P.op("pe", [I("matmul", out=pb[bank][:], lhsT=v[:, k, :], rhs=Z[:, k, tq * 512:(tq + 1) * 512],
                                      start=False, stop=(k == 21)) for k in range(18, 22)],
                             [wt] + [t("Z", k) for k in range(18, 22)], pbt(bank))
                        post_evac(o, bank, tt, R_FFNPOST + l * 8)
                        w_release()
                    post_finish(tt)
                if hf == 0:
                    mb_tiles = [t("Mb", mi, o) for mi in range(2) for o in range(8)] + [t("sqpost", 0), t("sqpost", 1)]
                    P.fence(("act", "dve", "pool"), mb_tiles)
                else:
                    P.barrier()

        def sc_mixer(l):
            ring["mm"] = [0, 1, 2, 3, 4, 5]
            Y = carve(PH, 8 * S, BF16).rearrange("p (c t) -> p c t", c=8)
            CH = carve(PH + 32768, 2056)
            BSB = carve(PH + 40992, 2048)
            HSB = [carve(PH + 49184 + b * 2048, 512) for b in range(2)]
            ACC = carve(PH + 53280, 2048)
            prenorm(R_MIXPRE + l * 8, PH)
            P.barrier()
            P.op("dve", I("memset", ap=CH[:, 0:2], constant=0.0), [], [t("ch")])
            nh = 0
            for i in range(8):
                slot, wt = w_acquire(("scin", l, i))
                v = slot[:, 0:3072].rearrange("p (c s n) -> p c s n", c=8, s=3)
                for tt in range(4):
                    sl = slice(tt * 512, (tt + 1) * 512)
                    banks = [next_bank() for _ in range(3)]
                    for s_ in range(3):
                        mm(pb[banks[s_]][:], [(v[:, c, s_, :], A[:, c, sl]) for c in range(8)],
                           [wt] + [tA(c, tt) for c in range(8)], pbt(banks[s_]))
                    hs = HSB[nh % 2]
                    ths = t("hsb", nh % 2)
                    nh += 1
                    P.op("act", I("activation", out=hs, in_=pb[banks[2]][:], func=AF.Copy), pbt(banks[2]), [ths])
                    P.op("dve", I("tensor_tensor", out=CH[:, 2 + tt * 512:2 + (tt + 1) * 512], in0=pb[banks[1]][:],
                                  in1=hs, op=ALU.mult), pbt(banks[1]) + [ths], [t("ch")])
                    P.op("act", I("activation", out=BSB[:, sl], in_=pb[banks[0]][:], func=AF.Copy),
                         pbt(banks[0]), [t("bsb")])
                w_release()
                r0 = R_SCCONV + i
                P.op("act", I("activation", out=ACC, in_=CH[:, 2:2050], func=AF.Copy, scale=col(r0 + 16)),
                     [t("ch"), t("ptab")], [t("scacc")])
                P.op("dve", I("scalar_tensor_tensor", out=ACC, in0=CH[:, 1:2049], scalar=col(r0 + 8), in1=ACC,
                              op0=ALU.mult, op1=ALU.add), [t("ch"), t("scacc"), t("ptab")], [t("scacc")])
                P.op("dve", I("scalar_tensor_tensor", out=ACC, in0=CH[:, 0:2048], scalar=col(r0), in1=ACC,
                              op0=ALU.mult, op1=ALU.add), [t("ch"), t("scacc"), t("ptab")], [t("scacc")])
                P.op("pool", I("tensor_tensor", out=Y[:, i, :], in0=BSB, in1=ACC, op=ALU.mult),
                     [t("bsb"), t("scacc")], [t("Y", i)])
            P.barrier()
            post_setup(PH + 61472, PH + 77856, PH + 32768)
            mixer_out(l, Y, lambda c, tt: t("Y", c))
            P.barrier()

        def cc_mixer(l):
            ring["mm"] = [0, 1, 2, 3, 4, 5]
            X = carve(PH, 8 * S).rearrange("p (c t) -> p c t", c=8)
            GB = carve(PH + 65536, 2080, BF16)
            DG = [carve(PH + 69696 + b_ * 7936, 31 * 128, BF16).rearrange("p (k n) -> p k n", k=31) for b_ in range(1)]
            SIG = [carve(PH + 77632 + b * 2048, 512) for b in range(2)]
            Yc = A
            prenorm(R_MIXPRE + l * 8, PH)
            P.barrier()
            P.op("dve", I("memset", ap=GB[:, 0:30], constant=0.0), [], [t("glu")])
            P.op("dve", I("tensor_tensor", out=BG, in0=PTAB[:, R_CCBPW2:R_CCBPW2 + 8],
                          in1=PTAB[:, R_MIXPOST + l * 8:R_MIXPOST + l * 8 + 8], op=ALU.mult), [t("ptab")], [t("bg")])
            ns = 0
            for i in range(8):
                slot, wt = w_acquire(("pw1", l, i))
                v = slot[:, 0:2048].rearrange("p (c s n) -> p c s n", c=8, s=2)
                dg = DG[0]
                P.op("pool", I("tensor_tensor", out=dg, in0=IDB.unsqueeze(1).broadcast_to([128, 31, 128]),
                               in1=PTAB[:, R_CCWDW + i:R_CCWDW + 248:8].unsqueeze(2).broadcast_to([128, 31, 128]),
                               op=ALU.mult), [t("ptab"), tc_], [t("dg")])
                for tt in range(4):
                    sl = slice(tt * 512, (tt + 1) * 512)
                    ba, bg_ = next_bank(), next_bank()
                    mm(pb[ba][:], [(v[:, c, 0, :], A[:, c, sl]) for c in range(8)],
                       [wt] + [tA(c, tt) for c in range(8)], pbt(ba))
                    mm(pb[bg_][:], [(v[:, c, 1, :], A[:, c, sl]) for c in range(8)],
                       [wt] + [tA(c, tt) for c in range(8)], pbt(bg_))
                    sg = SIG[ns % 2]
                    tsg = t("sig", ns % 2)
                    ns += 1
                    P.op("act", I("activation", out=sg, in_=pb[bg_][:], func=AF.Sigmoid, bias=col(R_CCBPW1 + 8 + i)),
                         pbt(bg_) + [t("ptab")], [tsg])
                    P.op("dve", I("scalar_tensor_tensor", out=GB[:, 30 + tt * 512:30 + (tt + 1) * 512], in0=pb[ba][:],
                                  scalar=col(R_CCBPW1 + i), in1=sg, op0=ALU.add, op1=ALU.mult),
                         pbt(ba) + [tsg, t("ptab")], [t("glu", tt)])
                w_release()
                for tt in range(4):
                    bank = next_bank()
                    gl = [t("glu")] + [t("glu", q) for q in range(max(0, tt - 1), tt + 1)]
                    mm(pb[bank][:], [(dg[:, k, :], GB[:, tt * 512 + k:tt * 512 + k + 512]) for k in range(31)],
                       gl + [t("dg")], pbt(bank))
                    P.op("act", I("activation", out=X[:, i, tt * 512:(tt + 1) * 512], in_=pb[bank][:], func=AF.Identity,
                                  bias=col(R_CCBDW + i)), pbt(bank) + [t("ptab")], [t("X", i, tt)])
            P.barrier()
            SQ = carve(PH + 65536, 4096, BF16).rearrange("p (c n) -> p c n", c=8)
            MEAN = carve(PH + 73728, 512)
            VAR = carve(PH + 75776, 512)
            for tt in range(4):
                sl = slice(tt * 512, (tt + 1) * 512)
                txs = [t("X", c, tt) for c in range(8)]
                P.op("act", I("activation", out=SQ, in_=X[:, :, sl], func=AF.Square), txs, [t("lnsq")])
                b1, b2 = next_bank(), next_bank()
                mm(pb[b1][:], [(ONESF, X[:, c, sl]) for c in range(8)], txs + [t("ones")], pbt(b1))
                mm(pb[b2][:], [(ONESB, SQ[:, c, :]) for c in range(8)], [t("lnsq"), t("ones")], pbt(b2))
                P.op("dve", I("tensor_scalar", out=MEAN, in0=pb[b1][:], scalar1=1.0 / D, scalar2=None, op0=ALU.mult),
                     pbt(b1), [t("mean")])
                P.op("dve", I("tensor_tensor", out=VAR, in0=MEAN, in1=MEAN, op=ALU.mult), [t("mean")], [t("var")])
                P.op("dve", I("scalar_tensor_tensor", out=VAR, in0=pb[b2][:], scalar=1.0 / D, in1=VAR,
                              op0=ALU.mult, op1=ALU.subtract), pbt(b2) + [t("var")], [t("var")])
                P.op("act", I("activation", out=VAR, in_=VAR, func=AF.Ln, bias=EPS_LN, scale=1.0),
                     [t("var"), tc_], [t("var")])
                P.op("act", I("activation", out=VAR, in_=VAR, func=AF.Exp, scale=-0.5), [t("var")], [t("var")])
                P.op("dve", I("tensor_tensor", out=X[:, :, sl], in0=X[:, :, sl],
                              in1=MEAN.unsqueeze(1).broadcast_to([128, 8, 512]), op=ALU.subtract),
                     txs + [t("mean")], txs)
                P.op("dve", I("tensor_tensor", out=X[:, :, sl], in0=X[:, :, sl],
                              in1=VAR.unsqueeze(1).broadcast_to([128, 8, 512]), op=ALU.mult), txs + [t("var")], txs)
                for c in range(8):
                    P.op("act", I("activation", out=Yc[:, c, sl], in_=X[:, c, sl], func=AF.Silu,
                                  scale=col(R_CCLNG + c), bias=col(R_CCLNB + c)), [t("X", c, tt), t("ptab")], [tA(c, tt)])
            P.barrier()
            post_setup(PH, PH + 16384, PH + 18432)
            mixer_out(l, Yc, tA, brow=R_CCBPW2)
            P.barrier()

        def attn_mixer(l, sq_i):
            ring["mm"] = [0, 1, 2]
            HB = H_OFF
            QKV = [[carve(HB + s_ * 12288 + k * 4096, S, BF16) for k in range(3)] for s_ in range(2)]
            VAB = [[carve(HB + 24576 + s_ * 8192 + k * 4096, 2048, BF16).rearrange("p (b n) -> p b n", b=16)
                    for k in range(2)] for s_ in range(2)]
            OT = carve(PH, 8 * S, BF16).rearrange("p (c t) -> p c t", c=8)
            COS = carve(PH + 32768, S)
            SIN = carve(PH + 40960, S)
            PTR = [carve(PH + 49152 + i * 512, 256, BF16).rearrange("p (j q) -> p j q", j=2) for i in range(4)]
            T12 = [[carve(PH + 51200 + (b * 2 + k) * 2048, 512) for k in range(2)] for b in range(2)]
            POSI = carve(PH + 59392, S, I32)
            XF = carve(PH + 67584, S)
            ZN = carve(PH + 59392, S)
            UAB = [[carve(HB + 40960, S), carve(HB + 49152, S)],
                   [carve(HB + 57344, S), carve(PH + 67584, S)]]
            for c in (0, 1, 3, 4, 5, 6, 2, 7):
                P.dma("sp", f"park{c}", I("dma_start", out=hpark[:, c * S:(c + 1) * S], in_=H[:, c, :]),
                      reads=[tH(c, tt) for tt in range(4)], writes=[t("hpark", c)])
            P.dma("sp", "pos", I("dma_start", out=POSI, in_=pos[sq_i:sq_i + 1, :].broadcast_to([128, S])),
                  writes=[t("posi")])
            P.op("dve", I("tensor_copy", out=XF, in_=POSI), [t("posi")], [t("xf")])
            P.op("dve", I("tensor_scalar", out=XF, in0=XF, scalar1=RC[:, 0:1], scalar2=None, op0=ALU.mult),
                 [t("xf"), tc_], [t("xf")])
            P.op("dve", I("tensor_copy", out=POSI, in_=XF), [t("xf")], [t("posi")])
            P.op("dve", I("tensor_copy", out=COS, in_=POSI), [t("posi")], [t("cos")])
            P.op("dve", I("tensor_tensor", out=XF, in0=XF, in1=COS, op=ALU.subtract), [t("xf"), t("cos")], [t("xf")])
            P.op("dve", I("scalar_tensor_tensor", out=SIN, in0=XF, scalar=0.5, in1=XF, op0=ALU.is_gt, op1=ALU.subtract),
                 [t("xf")], [t("sin")])
            P.op("dve", I("scalar_tensor_tensor", out=SIN, in0=SIN, scalar=0.5, in1=SIN, op0=ALU.is_gt, op1=ALU.subtract),
                 [t("sin")], [t("sin")])
            P.op("dve", I("tensor_scalar", out=XF, in0=XF, scalar1=0.25, scalar2=None, op0=ALU.add), [t("xf")], [t("xf")])
            P.op("dve", I("scalar_tensor_tensor", out=COS, in0=XF, scalar=0.5, in1=XF, op0=ALU.is_gt, op1=ALU.subtract),
                 [t("xf"), t("cos")], [t("cos")])
            P.op("dve", I("scalar_tensor_tensor", out=COS, in0=COS, scalar=0.5, in1=COS, op0=ALU.is_gt, op1=ALU.subtract),
                 [t("cos")], [t("cos")])
            TWO_PI = 2.0 * math.pi * (1.0 - 2e-7)
            P.op("act", I("activation", out=SIN, in_=SIN, func=AF.Sin, scale=TWO_PI), [t("sin")], [t("sin")])
            P.op("act", I("activation", out=COS, in_=COS, func=AF.Sin, scale=TWO_PI), [t("cos")], [t("cos")])
            prenorm(R_MIXPRE + l * 8, PH)

            def inherit(key, olds):
                nt = T()
                for o_ in olds:
                    for ev in ([o_.w] if o_.w else []) + list(o_.r.items()):
                        if nt.r.get(ev[0], 0) < ev[1]:
                            nt.r[ev[0]] = ev[1]
                tiles[key] = nt

            def hch(lo, hi):
                return [tH(c, tt) for c in range(lo // 8192, (hi - 1) // 8192 + 1) for tt in range(4)]
            for s_ in range(2):
                for k in range(3):
                    for tt_ in range(4):
                        o0 = s_ * 12288 + k * 4096 + tt_ * 1024
                        inherit(("qkv", s_, k, tt_), hch(o0, o0 + 1024))
                inherit(("va", s_), hch(24576 + s_ * 8192, 24576 + s_ * 8192 + 4096))
                inherit(("vb", s_), hch(24576 + s_ * 8192 + 4096, 24576 + s_ * 8192 + 8192))
            inherit(("u", 0, 0), hch(40960, 49152))
            inherit(("u", 0, 1), hch(49152, 57344))
            inherit(("u", 1, 0), hch(57344, 65536))
            inherit(("u", 1, 1), [t("xf"), t("posi")])
            inherit(("zn",), [t("posi"), t("xf")])
            for p_ in range(8):
                inherit(("ot", p_), [t("sqpre", 0), t("sqpre", 1)])
            for s_ in range(2):
                P.op("pool", I("memset", ap=VAB[s_][0][:, :, 64:128], constant=1.0), [], [t("va", s_)])
                P.op("pool", I("memset", ap=VAB[s_][1][:, :, 0:64], constant=1.0), [], [t("vb", s_)])
                if s_ == 0:
                    for hd in range(2):
                        P.op("pool", I("memset", ap=UAB[0][hd], constant=0.0), [], [t("u", 0, hd)])
            cnt = {"rope": 0, "idx": 0}

            def proj_gen(p, g, set_):
                dil = GROUPS[g][1]
                nb = (S // dil) // 128
                VA, VB = VAB[set_]
                tq_ = [[t("qkv", set_, k, tt_) for tt_ in range(4)] for k in range(3)]
                slot, wt = w_acquire(("qkv", l, p, g))
                v = slot[:, 0:3072].rearrange("p (c s n) -> p c s n", c=8, s=3)
                pend = None
                for s3 in range(3):
                    dst = QKV[set_][s3]
                    for tt in range(4):
                        sl = slice(tt * 512, (tt + 1) * 512)
                        bank = next_bank()
                        bt = pbt(bank)
                        for c0 in range(0, 8, 2):
                            P.op("pe", [I("matmul", out=pb[bank][:], lhsT=v[:, c, s3, :], rhs=A[:, c, sl],
                                          start=(c == 0), stop=(c == 7)) for c in range(c0, c0 + 2)],
                                 [wt] + [tA(c, tt) for c in range(8)], bt)
                            if c0 == 0 and pend is not None:
                                pend()
                                pend = None
                            if c0 < 6:
                                yield
                        P.op("act", I("activation", out=dst[:, sl], in_=pb[bank][:], func=AF.Copy), bt, [tq_[s3][tt]])
                        if s3 < 2:
                            nr = cnt["rope"]
                            cnt["rope"] += 1
                            t1, t2 = T12[nr % 2]
                            tt1, tt2 = t("t1", nr % 2), t("t2", nr % 2)
                            P.op("dve", I("tensor_tensor", out=t1[0:80, :], in0=pb[bank][0:80, :], in1=COS[0:80, sl],
                                          op=ALU.mult), bt + [t("cos")], [tt1])

                            def rope_tail(dst=dst, sl=sl, t1=t1, t2=t2, tt1=tt1, tt2=tt2, tq=tq_[s3][tt]):
                                mm(pb[3][0:80, :], [(SMAT[:, 0:80], dst[:, sl])], [tq, tc_], pbt(3))
                                P.op("dve", I("tensor_tensor", out=t2[0:80, :], in0=pb[3][0:80, :], in1=SIN[0:80, sl],
                                              op=ALU.mult), pbt(3) + [t("sin")], [tt2])
                                P.op("dve", I("tensor_tensor", out=dst[0:80, sl], in0=t1[0:80, :], in1=t2[0:80, :],
                                              op=ALU.add), [tt1, tt2], [tq])
                            pend = rope_tail
                        yield
                if pend is not None:
                    pend()
                w_release()
                VTv = QKV[set_][2].rearrange("p (m d) -> p d m", d=dil)
                for h8 in range(2):
                    bank = next_bank()
                    ins = []
                    for b8 in range(8):
                        blk = h8 * 8 + b8
                        r, n = blk // nb, blk % nb
                        ins.append(I("transpose", out=pb_bf[bank][:, b8 * 128:(b8 + 1) * 128],
                                     in_=VTv[:, r, n * 128:(n + 1) * 128], identity=IDB))
                    P.op("pe", ins, tq_[2] + [tc_], pbt(bank))
                    src = pb_bf[bank][:].rearrange("p (b n) -> p b n", b=8)
                    P.op("dve", I("tensor_copy", out=VA[:, h8 * 8:h8 * 8 + 8, 0:64], in_=src[:, :, 0:64]),
                         pbt(bank), [t("va", set_)])
                    P.op("act", I("activation", out=VB[:, h8 * 8:h8 * 8 + 8, 64:128], in_=src[:, :, 64:128],
                                  func=AF.Copy), pbt(bank), [t("vb", set_)])
                    yield

            def core_gen(p, g, set_):
                dil = GROUPS[g][1]
                nb = (S // dil) // 128
                QT, KT, VT = QKV[set_]
                VA, VB = VAB[set_]
                U2 = UAB[p % 2]
                tq_ = [[t("qkv", set_, k, tt_) for tt_ in range(4)] for k in range(3)]
                QTv = QT.rearrange("p (m d) -> p d m", d=dil)
                KTv = KT.rearrange("p (m d) -> p d m", d=dil)
                items = [(r, n, hd) for r in range(dil) for n in range(nb) for hd in range(2)]
                st_ = {}

                def scores(it):
                    r, n, hd = it
                    hb = hd * 64
                    gi = cnt["idx"]
                    cnt["idx"] += 1
                    sbank = 4 + gi % 2
                    w = 256 if n + 1 < nb else 128
                    stv = pb[sbank][:, 0:w]
                    stt = pbt(sbank)
                    P.op("pe", I("matmul", out=stv, lhsT=KTv[hb:hb + 64, r, n * 128:(n + 1) * 128],
                                 rhs=QTv[hb:hb + 64, r, n * 128:n * 128 + w], start=True, stop=True),
                         tq_[0] + tq_[1], stt)
                    pt = PTR[gi % 4].rearrange("p j q -> p (j q)")
                    tpt = t("pt", gi % 4)
                    P.op("act", I("activation", out=pt[:, 0:w], in_=stv, func=AF.Exp, scale=0.125), stt, [tpt])
                    P.op("pool", I("tensor_tensor", out=pt[:, 0:w], in0=pt[:, 0:w], in1=MASKCP[:, 0:w], op=ALU.mult),
                         [tpt, tc_], [tpt])
                    return (pt, tpt, w, gi)

                def pv(it, stf):
                    r, n, hd = it
                    pt, tpt, w, gi = stf
                    xb = 6 + gi % 2
                    xq = pb[xb][:, 0:w]
                    xt = pbt(xb)
                    Vh = VA if hd == 0 else VB
                    tv = t("va", set_) if hd == 0 else t("vb", set_)
                    P.op("pe", I("matmul", out=xq, lhsT=Vh[:, r * nb + n, :], rhs=pt[:, 0:w], start=True, stop=True),
                         [tpt, tv], xt)
                    tu = t("u", p % 2, hd)
                    Uv = U2[hd].rearrange("p (m d) -> p d m", d=dil)[:, r, n * 128:n * 128 + w]
                    P.op("dve", I("tensor_tensor", out=Uv, in0=xq, in1=Uv, op=ALU.add), xt + [tu], [tu])

                SK = 2
                for idx in range(0, len(items) + SK, 2):
                    for k2 in range(2):
                        if idx + k2 < len(items):
                            st_[idx + k2] = scores(items[idx + k2])
                    for k2 in range(2):
                        if 0 <= idx + k2 - SK < len(items):
                            pv(items[idx + k2 - SK], st_.pop(idx + k2 - SK))
                    yield
                    yield

            def normalize(p):
                U2 = UAB[p % 2]
                tu0, tu1 = t("u", p % 2, 0), t("u", p % 2, 1)
                P.dma("sp", "zsw", I("dma_start", out=ZN[0:64, :], in_=U2[0][64:128, :]), reads=[tu0], writes=[t("zn")])
                P.dma("sp", "zsw", I("dma_start", out=ZN[64:128, :], in_=U2[1][0:64, :]), reads=[tu1], writes=[t("zn")])
                P.op("act", I("activation", out=ZN, in_=ZN, func=AF.Ln), [t("zn")], [t("zn")])
                P.op("act", I("activation", out=ZN, in_=ZN, func=AF.Exp, scale=-1.0), [t("zn")], [t("zn")])
                P.op("dve", I("tensor_tensor", out=OT[0:64, p, :], in0=U2[0][0:64, :], in1=ZN[0:64, :], op=ALU.mult),
                     [tu0, t("zn")], [t("ot", p)])
                P.op("dve", I("tensor_tensor", out=OT[64:128, p, :], in0=U2[1][64:128, :], in1=ZN[64:128, :],
                              op=ALU.mult), [tu1, t("zn")], [t("ot", p)])

            pgs = [(p, g) for p in range(8) for g in range(3)]
            for _ in proj_gen(pgs[0][0], pgs[0][1], 0):
                pass
            for k, (p, g) in enumerate(pgs):
                core = core_gen(p, g, k % 2)
                proj = proj_gen(pgs[k + 1][0], pgs[k + 1][1], (k + 1) % 2) if k + 1 < len(pgs) else None
                nstep = 0
                for _ in core:
                    nstep += 1
                    if proj is not None:
                        if next(proj, "done") == "done":
                            proj = None
                if proj is not None:
                    for _ in proj:
                        pass
                if k == 0:
                    for hd in range(2):
                        P.op("pool", I("memset", ap=UAB[1][hd], constant=0.0), [], [t("u", 1, hd)])
                if g == 2:
                    normalize(p)
                    if p + 2 < 8:
                        for hd in range(2):
                            P.op("pool", I("memset", ap=UAB[p % 2][hd], constant=0.0), [], [t("u", p % 2, hd)])
            P.barrier()
            for tt in range(4):
                sl = slice(tt * 512, (tt + 1) * 512)
                P.dma("sp", "unpark", I("dma_start", out=H[:, :, sl],
                                        in_=hpark.rearrange("p (c t) -> p c t", c=8)[:, :, sl]),
                      reads=[t("hpark", c) for c in range(8)], writes=[tH(c, tt) for c in range(8)])
            ring["mm"] = [0, 1, 2, 3, 4, 5]
            post_setup(PH + 32768, PH + 49152, PH + 51200)
            mixer_out(l, OT, lambda c, tt: t("ot", c))
            P.barrier()

        outs = []
        for sq_i in range(nseq):
            load_x(sq_i)
            P.barrier()
            for l in layers:
                kind = l % 3
                if do_mixer:
                    if kind == 0:
                        attn_mixer(l, sq_i)
                    elif kind == 1:
                        sc_mixer(l)
                    else:
                        cc_mixer(l)
                if do_ffn:
                    ffn(l)
            ring["mm"] = [0, 1, 2, 3]
            outs += store_y(sq_i)
        P.wait_all("sp", outs)
        assert next(keygen, None) is None and not WS["pending"]
        P.emit(block)
    return nc


def _consts():
    c = np.zeros((128, NCST), np.float32)
    c[:, C_ID:C_ID + 128] = np.eye(128, dtype=np.float32)
    sm = np.zeros((128, 128), np.float32)
    for base in (0, 64):
        for m in range(8):
            sm[base + m + 8, base + m] = -1.0
            sm[base + m, base + m + 8] = 1.0
    c[:, C_SMAT:C_SMAT + 128] = sm
    k = np.arange(128)[:, None]
    q = np.arange(128)[None, :]
    c[:, C_MASK:C_MASK + 128] = (k >= q).astype(np.float32)
    c[:, C_MASK + 128:C_MASK + 256] = (k <= q).astype(np.float32)
    inv = (500000.0 ** (-np.arange(0, 16, 2, dtype=np.float32) / 16.0)).astype(np.float32)
    for p in range(128):
        if p % 64 < 16:
            c[p, C_RC] = inv[p % 8] / np.float32(2.0 * math.pi)
    c[:, C_RC + 1] = 1e-6
    c[:, C_RC + 2] = 1e-5
    return c


def _pack_params(inp):
    f = lambda a: np.ascontiguousarray(np.asarray(a, np.float32)).reshape(-1, 128)
    rows = np.concatenate([
        f(inp["mix_norm_pre"]), f(inp["mix_norm_post"]), f(inp["ffn_norm_pre"]), f(inp["ffn_norm_post"]),
        f(inp["sc_w_conv"]), f(inp["cc_b_pw1"]), f(inp["cc_w_dw"]), f(inp["cc_b_dw"]),
        f(inp["cc_ln_g"]), f(inp["cc_ln_b"]), f(inp["cc_b_pw2"]), f(inp["ffn_w_conv"])], axis=0)
    assert rows.shape[0] == 976
    out = np.zeros((1024, 128), np.float32)
    out[:976] = rows
    return out


def _layer_weights(inp, l):
    kind, j = l % 3, l // 3
    a = lambda k, i: np.ascontiguousarray(np.asarray(inp[k][i], np.float32))
    d = {}
    if kind == 0:
        d[f"w{l}_min"] = a("attn_w_qkv", j)
        d[f"w{l}_mout"] = a("attn_w_o", j)
    elif kind == 1:
        d[f"w{l}_min"] = a("sc_w_in", j)
        d[f"w{l}_mout"] = a("sc_w_out", j)
    else:
        d[f"w{l}_min"] = a("cc_w_pw1", j)
        d[f"w{l}_mout"] = a("cc_w_pw2", j)
    d[f"w{l}_fin"] = a("ffn_w_in", l)
    d[f"w{l}_fout"] = a("ffn_w_out", l)
    return d


LAUNCH_GROUPS = [[0, 1, 2, 3]]
_NC_CACHE = {}


def kernel(**inp):
    x = np.ascontiguousarray(np.asarray(inp["x"], np.float32))
    positions = np.ascontiguousarray(np.asarray(inp["positions"], np.int32))
    pth = _pack_params(inp)
    cst = _consts()
    cur = [x[c * SEQ_PER_CORE:(c + 1) * SEQ_PER_CORE].reshape(SEQ_PER_CORE * S, D) for c in range(NCORES)]
    for grp in LAUNCH_GROUPS:
        key = tuple(grp)
        if key not in _NC_CACHE:
            _NC_CACHE[key] = build(list(grp), SEQ_PER_CORE)
        nc = _NC_CACHE[key]
        wd = {}
        for l in grp:
            wd.update(_layer_weights(inp, l))
        in_maps = []
        for c in range(NCORES):
            m = {"x": cur[c], "pos": positions[c * SEQ_PER_CORE:(c + 1) * SEQ_PER_CORE], "pth": pth, "cst": cst}
            m.update(wd)
            in_maps.append(m)
        res = run_bass_kernel_spmd(nc, in_maps, core_ids=list(range(NCORES)))
        cur = [np.asarray(r["y"], np.float32) for r in res.results]
    out = np.stack([c.reshape(SEQ_PER_CORE, S, D) for c in cur], axis=0).reshape(NCORES * SEQ_PER_CORE, S, D)
    return out
```
